# Optimizing a Trainium2 kernel written in Bass

```python
import math
import jax
import jax.numpy as jnp
from jax import lax
import numpy as np

D_MODEL = 2048
BATCH = 4
SEQ = 2048
DEPTH = 2
DEC_BATCH = 128
DEC_SEQ = 1
PAST_LEN = 16384
PAGE_SIZE = 128

N_BRANCH = 4
BRANCH_W = D_MODEL // 2
DKA = 128
DVA = 128
HA = BRANCH_W // DVA
CONV_A = 4
A_QKV = HA * (2 * DKA + DVA)
WB = BRANCH_W
CONV_B = 31
WC = BRANCH_W
CONV_C = 3
DKD = 128
DVD = 256
HD = BRANCH_W // DVD
FORGET_BIAS = 3.0
CHUNK = 64
MEM_LEN = 256
XH = 4
XDH = 128
D_FF = 4 * D_MODEL
LN_EPS = 1e-5
RMS_EPS = 1e-6
NEG_BIG = -1e30
DN_ALPHA = (2 * DEPTH) ** 0.25
DN_BETA = (8 * DEPTH) ** -0.25

SPLIT_SIZES = (A_QKV, HA * DVA, HA, HA,
               2 * WB,
               WC, WC, WC,
               HD * DKD, HD * DKD, HD * DVD, HD * DVD, HD, HD,
               N_BRANCH * D_MODEL)
N_IN = sum(SPLIT_SIZES)
MLSTM_F_OFFSET = sum(SPLIT_SIZES[:13])

kernel_name = "hybrid_deltanet_conformer_shortconv_mlstm_step"


def layer_norm(x, g, b):
    xf = x.astype(jnp.float32)
    mu = jnp.mean(xf, axis=-1, keepdims=True)
    var = jnp.mean(jnp.square(xf - mu), axis=-1, keepdims=True)
    return ((xf - mu) * lax.rsqrt(var + LN_EPS) * g + b).astype(x.dtype)


def rms_norm(x, g):
    xf = x.astype(jnp.float32)
    return (xf * lax.rsqrt(jnp.mean(xf * xf, axis=-1, keepdims=True) + RMS_EPS) * g).astype(x.dtype)


def l2norm(x):
    xf = x.astype(jnp.float32)
    return xf * lax.rsqrt(jnp.sum(xf * xf, axis=-1, keepdims=True) + RMS_EPS)


def causal_dwconv(x, buf, w):
    xx = jnp.concatenate([buf.astype(x.dtype), x], axis=1)
    y = lax.conv_general_dilated(xx, w[:, None, :].astype(x.dtype), window_strides=(1,), padding="VALID",
                                 dimension_numbers=("NWC", "WIO", "NWC"),
                                 feature_group_count=x.shape[-1])
    return y, xx[:, -(w.shape[0] - 1):]


def _pad_time(a, pad, value):
    if pad == 0:
        return a
    widths = [(0, 0), (0, pad)] + [(0, 0)] * (a.ndim - 2)
    return jnp.pad(a, widths, constant_values=value)


def _to_chunks(a, n, l):
    a = a.reshape((a.shape[0], n, l) + a.shape[2:])
    return a.transpose((1, 0, 3, 2) + tuple(range(4, a.ndim)))


def _from_chunks(a):
    a = a.transpose((1, 0, 3, 2) + tuple(range(4, a.ndim)))
    return a.reshape((a.shape[0], a.shape[1] * a.shape[2]) + a.shape[3:])


def gated_delta_rule(q, k, v, beta, g, S0):
    f32 = jnp.float32
    T = q.shape[1]
    L = min(CHUNK, T)
    pad = (-T) % L
    q, k, v, beta, g = (_pad_time(a.astype(f32), pad, 0.0) for a in (q, k, v, beta, g))
    N = (T + pad) // L
    q, k, v, beta, g = (_to_chunks(a, N, L) for a in (q, k, v, beta, g))
    dv = v.shape[-1]
    incl = jnp.tril(jnp.ones((L, L), bool))
    strict = jnp.tril(jnp.ones((L, L), bool), -1)
    gc = jnp.cumsum(g, axis=-1)
    decay = jnp.exp(jnp.where(incl, gc[..., :, None] - gc[..., None, :], -jnp.inf))
    a_low = jnp.where(strict, beta[..., :, None] * jnp.einsum("nbhld,nbhmd->nbhlm", k, k) * decay, 0.0)
    rhs = jnp.concatenate([v * beta[..., None], k * (beta * jnp.exp(gc))[..., None]], axis=-1)
    sol = lax.linalg.triangular_solve(jnp.eye(L, dtype=f32) + a_low, rhs, left_side=True, lower=True)
    u, w = sol[..., :dv], sol[..., dv:]
    qk = jnp.einsum("nbhld,nbhmd->nbhlm", q, k) * decay
    q_dec = q * jnp.exp(gc)[..., None]
    k_dec = k * jnp.exp(gc[..., -1:] - gc)[..., None]
    g_last = jnp.exp(gc[..., -1])

    def step(S, inp):
        u_c, w_c, qk_c, qd_c, kd_c, gl_c = inp
        delta = u_c - jnp.einsum("bhlk,bhkv->bhlv", w_c, S)
        o = jnp.einsum("bhlk,bhkv->bhlv", qd_c, S) + jnp.einsum("bhlm,bhmv->bhlv", qk_c, delta)
        S = S * gl_c[..., None, None] + jnp.einsum("bhlk,bhlv->bhkv", kd_c, delta)
        return S, o

    S, o = lax.scan(step, S0.astype(f32), (u, w, qk, q_dec, k_dec, g_last))
    return _from_chunks(o)[:, :T], S


def mlstm_chunked(q, k, v, log_i, log_f, C0, n0, m0):
    f32 = jnp.float32
    T = q.shape[1]
    L = min(CHUNK, T)
    pad = (-T) % L
    q, k, v, log_f = (_pad_time(a.astype(f32), pad, 0.0) for a in (q, k, v, log_f))
    log_i = _pad_time(log_i.astype(f32), pad, NEG_BIG)
    N = (T + pad) // L
    q, k, v, log_i, log_f = (_to_chunks(a, N, L) for a in (q, k, v, log_i, log_f))
    incl = jnp.tril(jnp.ones((L, L), bool))
    b = jnp.cumsum(log_f, axis=-1)
    log_d = jnp.where(incl, b[..., :, None] - b[..., None, :] + log_i[..., None, :], -jnp.inf)
    m_intra = jnp.max(log_d, axis=-1)
    b_last = b[..., -1]
    log_w = b_last[..., None] - b + log_i
    m_w = jnp.max(log_w, axis=-1)
    qk = jnp.einsum("nbhld,nbhmd->nbhlm", q, k)

    def step(carry, inp):
        C, n, m = carry
        b_c, ld_c, mi_c, qk_c, q_c, k_c, v_c, lw_c, mw_c, bl_c = inp
        m_tok = jnp.maximum(b_c + m[..., None], mi_c)
        inter = jnp.exp(b_c + m[..., None] - m_tok)
        dmat = jnp.exp(ld_c - m_tok[..., None]) * qk_c
        num = inter[..., None] * jnp.einsum("bhlk,bhkv->bhlv", q_c, C) + jnp.einsum("bhlm,bhmv->bhlv", dmat, v_c)
        den = inter * jnp.einsum("bhlk,bhk->bhl", q_c, n) + jnp.sum(dmat, axis=-1)
        h = num / jnp.maximum(jnp.abs(den), jnp.exp(-m_tok))[..., None]
        m_new = jnp.maximum(bl_c + m, mw_c)
        scale = jnp.exp(bl_c + m - m_new)
        wgt = jnp.exp(lw_c - m_new[..., None])
        C = scale[..., None, None] * C + jnp.einsum("bhlk,bhlv->bhkv", k_c * wgt[..., None], v_c)
        n = scale[..., None] * n + jnp.einsum("bhlk,bhl->bhk", k_c, wgt)
        return (C, n, m_new), h

    (C, n, m), h = lax.scan(step, (C0.astype(f32), n0.astype(f32), m0.astype(f32)),
                            (b, log_d, m_intra, qk, q, k, v, log_w, m_w, b_last))
    return _from_chunks(h)[:, :T], C, n, m


def token_mixers(x, st, p):
    f32 = jnp.float32
    conv_a, S_a, conv_b, conv_c, C_d, n_d, m_d = st
    bn, T, _ = x.shape
    proj = x @ p["w_in"] + p["b_in"]
    (qkv_a, z_a, beta_a, dec_a, glu_b, bg_c, cg_c, h_c,
     q_d, k_d, v_d, o_d, i_d, f_d, gate_pre) = jnp.split(proj, np.cumsum(SPLIT_SIZES)[:-1].tolist(), axis=-1)

    qkv_a, conv_a_new = causal_dwconv(qkv_a, conv_a, p["a_conv_w"])
    qkv_a = jax.nn.silu(qkv_a)
    q_a, k_a, v_a = jnp.split(qkv_a, [HA * DKA, 2 * HA * DKA], axis=-1)
    q_a = l2norm(q_a.reshape(bn, T, HA, DKA)) * (DKA ** -0.5)
    k_a = l2norm(k_a.reshape(bn, T, HA, DKA))
    v_a = v_a.reshape(bn, T, HA, DVA)
    beta = jax.nn.sigmoid(beta_a.astype(f32))
    g = -jnp.exp(p["a_A_log"].astype(f32)) * jax.nn.softplus(dec_a.astype(f32) + p["a_dt_bias"])
    o_a, S_a_new = gated_delta_rule(q_a, k_a, v_a, beta, g, S_a)
    o_a = rms_norm(o_a, p["a_norm_w"]) * jax.nn.silu(z_a.reshape(bn, T, HA, DVA).astype(f32))
    out_a = o_a.reshape(bn, T, HA * DVA).astype(x.dtype)

    a_b, g_b = jnp.split(glu_b, 2, axis=-1)
    u_b, conv_b_new = causal_dwconv(a_b * jax.nn.sigmoid(g_b), conv_b, p["b_conv_w"])
    out_b = jax.nn.silu(layer_norm(u_b + p["b_conv_b"], p["b_ln_g"], p["b_ln_b"]))

    u_c, conv_c_new = causal_dwconv(cg_c * h_c, conv_c, p["c_conv_w"])
    out_c = bg_c * u_c

    q = q_d.reshape(bn, T, HD, DKD)
    k = k_d.reshape(bn, T, HD, DKD) * (DKD ** -0.5)
    v = v_d.reshape(bn, T, HD, DVD)
    h, C_new, n_new, m_new = mlstm_chunked(q, k, v, i_d.astype(f32), jax.nn.log_sigmoid(f_d.astype(f32)), C_d, n_d, m_d)
    out_d = (rms_norm(h, p["d_norm_w"]) * jax.nn.sigmoid(o_d.reshape(bn, T, HD, DVD).astype(f32)))
    out_d = out_d.reshape(bn, T, HD * DVD).astype(x.dtype)

    branches = jnp.stack([out_a, out_b, out_c, out_d], axis=2)
    proj_br = jnp.einsum("btnw,nwd->btnd", branches, p["w_branch"])
    gates = jax.nn.sigmoid(gate_pre.reshape(bn, T, N_BRANCH, D_MODEL))
    y = jnp.sum(gates * proj_br, axis=2) @ p["w_out"]
    return y, (conv_a_new, S_a_new, conv_b_new, conv_c_new, C_new, n_new, m_new)


def memory_attention(x, mem_k, mem_v, p):
    bn, T, _ = x.shape
    q = (x @ p["xq_w"]).reshape(bn, T, XH, XDH)
    s = jnp.einsum("bthd,bmhd->bhtm", q, mem_k.astype(x.dtype)).astype(jnp.float32) * (XDH ** -0.5)
    a = jax.nn.softmax(s, axis=-1).astype(x.dtype)
    o = jnp.einsum("bhtm,bmhd->bthd", a, mem_v.astype(x.dtype)).reshape(bn, T, XH * XDH)
    return o @ p["xo_w"]


def sq_relu_mlp(x, p):
    h = jnp.square(jax.nn.relu(x @ p["ffn_w1"] + p["ffn_b1"]))
    return h @ p["ffn_w2"] + p["ffn_b2"]


def decoder_layer(x, mem_k, mem_v, st, p):
    y, st_new = token_mixers(x, st, p)
    x = layer_norm(DN_ALPHA * x + y, p["ln1_g"], p["ln1_b"])
    x = layer_norm(DN_ALPHA * x + memory_attention(x, mem_k, mem_v, p), p["ln2_g"], p["ln2_b"])
    x = layer_norm(DN_ALPHA * x + sq_relu_mlp(x, p), p["ln3_g"], p["ln3_b"])
    return x, st_new


def _stack_states(states):
    return [jnp.stack(col) for col in zip(*states)]


def setup_inputs(seed: int = 0) -> dict:
    key = jax.random.key(seed)
    ks = iter(jax.random.split(key, 48))

    def nrm(shape, scale):
        return jax.random.normal(next(ks), shape, jnp.float32) * scale

    def gain(shape):
        return 1.0 + nrm(shape, 0.02)

    dx = XH * XDH
    x_prompt = nrm((BATCH, SEQ, D_MODEL), 1.0)
    x_sample = nrm((DEC_BATCH, DEC_SEQ, D_MODEL), 1.0)
    mem_prompt = nrm((BATCH, MEM_LEN, D_MODEL), 1.0)
    cache_mem_k = nrm((DEPTH, DEC_BATCH, MEM_LEN, XH, XDH), 1.0)
    cache_mem_v = nrm((DEPTH, DEC_BATCH, MEM_LEN, XH, XDH), DN_BETA)
    state_delta_conv = nrm((DEPTH, DEC_BATCH, CONV_A - 1, A_QKV), 1.0)
    state_delta_S = nrm((DEPTH, DEC_BATCH, HA, DKA, DVA), DKA ** -0.5)
    state_glu_conv = nrm((DEPTH, DEC_BATCH, CONV_B - 1, WB), 0.5)
    state_short_conv = nrm((DEPTH, DEC_BATCH, CONV_C - 1, WC), 0.5)
    state_mlstm_C = nrm((DEPTH, DEC_BATCH, HD, DKD, DVD), 0.3)
    state_mlstm_n = nrm((DEPTH, DEC_BATCH, HD, DKD), 0.3)
    state_mlstm_m = nrm((DEPTH, DEC_BATCH, HD), 1.0)

    w_in = nrm((DEPTH, D_MODEL, N_IN), D_MODEL ** -0.5)
    b_in = nrm((DEPTH, N_IN), 0.01).at[:, MLSTM_F_OFFSET:MLSTM_F_OFFSET + HD].add(FORGET_BIAS)
    a_conv_w = nrm((DEPTH, CONV_A, A_QKV), CONV_A ** -0.5)
    a_A_log = jnp.log(jax.random.uniform(next(ks), (DEPTH, HA), jnp.float32, 1.0, 16.0))
    dt = jnp.exp(jax.random.uniform(next(ks), (DEPTH, HA), jnp.float32, math.log(1e-3), math.log(1e-1)))
    a_dt_bias = dt + jnp.log(-jnp.expm1(-dt))
    a_norm_w = gain((DEPTH, DVA))
    b_conv_w = nrm((DEPTH, CONV_B, WB), CONV_B ** -0.5)
    b_conv_b = nrm((DEPTH, WB), 0.02)
    b_ln_g = gain((DEPTH, WB))
    b_ln_b = nrm((DEPTH, WB), 0.02)
    c_conv_w = nrm((DEPTH, CONV_C, WC), CONV_C ** -0.5)
    d_norm_w = gain((DEPTH, DVD))
    w_branch = nrm((DEPTH, N_BRANCH, BRANCH_W, D_MODEL), BRANCH_W ** -0.5 * DN_BETA)
    w_out = nrm((DEPTH, D_MODEL, D_MODEL), D_MODEL ** -0.5 * DN_BETA)
    ln1_g = gain((DEPTH, D_MODEL))
    ln1_b = nrm((DEPTH, D_MODEL), 0.02)
    xq_w = nrm((DEPTH, D_MODEL, dx), D_MODEL ** -0.5)
    xk_w = nrm((DEPTH, D_MODEL, dx), D_MODEL ** -0.5)
    xv_w = nrm((DEPTH, D_MODEL, dx), D_MODEL ** -0.5 * DN_BETA)
    xo_w = nrm((DEPTH, dx, D_MODEL), dx ** -0.5 * DN_BETA)
    ln2_g = gain((DEPTH, D_MODEL))
    ln2_b = nrm((DEPTH, D_MODEL), 0.02)
    ffn_w1 = nrm((DEPTH, D_MODEL, D_FF), D_MODEL ** -0.5 * DN_BETA)
    ffn_b1 = nrm((DEPTH, D_FF), 0.02)
    ffn_w2 = nrm((DEPTH, D_FF, D_MODEL), D_FF ** -0.5 * DN_BETA)
    ffn_b2 = nrm((DEPTH, D_MODEL), 0.02)
    ln3_g = gain((DEPTH, D_MODEL))
    ln3_b = nrm((DEPTH, D_MODEL), 0.02)
    return {"x_prompt": x_prompt, "x_sample": x_sample, "mem_prompt": mem_prompt,
            "cache_mem_k": cache_mem_k, "cache_mem_v": cache_mem_v,
            "state_delta_conv": state_delta_conv, "state_delta_S": state_delta_S,
            "state_glu_conv": state_glu_conv, "state_short_conv": state_short_conv,
            "state_mlstm_C": state_mlstm_C, "state_mlstm_n": state_mlstm_n, "state_mlstm_m": state_mlstm_m,
            "w_in": w_in, "b_in": b_in, "a_conv_w": a_conv_w, "a_A_log": a_A_log, "a_dt_bias": a_dt_bias,
            "a_norm_w": a_norm_w, "b_conv_w": b_conv_w, "b_conv_b": b_conv_b, "b_ln_g": b_ln_g, "b_ln_b": b_ln_b,
            "c_conv_w": c_conv_w, "d_norm_w": d_norm_w, "w_branch": w_branch, "w_out": w_out,
            "ln1_g": ln1_g, "ln1_b": ln1_b, "xq_w": xq_w, "xk_w": xk_w, "xv_w": xv_w, "xo_w": xo_w,
            "ln2_g": ln2_g, "ln2_b": ln2_b, "ffn_w1": ffn_w1, "ffn_b1": ffn_b1, "ffn_w2": ffn_w2, "ffn_b2": ffn_b2,
            "ln3_g": ln3_g, "ln3_b": ln3_b}


def reference(x_prompt, x_sample, mem_prompt, cache_mem_k, cache_mem_v,
              state_delta_conv, state_delta_S, state_glu_conv, state_short_conv,
              state_mlstm_C, state_mlstm_n, state_mlstm_m,
              w_in, b_in, a_conv_w, a_A_log, a_dt_bias, a_norm_w,
              b_conv_w, b_conv_b, b_ln_g, b_ln_b, c_conv_w, d_norm_w, w_branch, w_out,
              ln1_g, ln1_b, xq_w, xk_w, xv_w, xo_w, ln2_g, ln2_b,
              ffn_w1, ffn_b1, ffn_w2, ffn_b2, ln3_g, ln3_b):
    f32 = jnp.float32
    weights = dict(w_in=w_in, b_in=b_in, a_conv_w=a_conv_w, a_A_log=a_A_log, a_dt_bias=a_dt_bias,
                   a_norm_w=a_norm_w, b_conv_w=b_conv_w, b_conv_b=b_conv_b, b_ln_g=b_ln_g, b_ln_b=b_ln_b,
                   c_conv_w=c_conv_w, d_norm_w=d_norm_w, w_branch=w_branch, w_out=w_out,
                   ln1_g=ln1_g, ln1_b=ln1_b, xq_w=xq_w, xk_w=xk_w, xv_w=xv_w, xo_w=xo_w,
                   ln2_g=ln2_g, ln2_b=ln2_b, ffn_w1=ffn_w1, ffn_b1=ffn_b1, ffn_w2=ffn_w2, ffn_b2=ffn_b2,
                   ln3_g=ln3_g, ln3_b=ln3_b)
    bp = x_prompt.shape[0]
    n_mem = mem_prompt.shape[1]
    zero_state = (jnp.zeros((bp, CONV_A - 1, A_QKV), x_prompt.dtype),
                  jnp.zeros((bp, HA, DKA, DVA), f32),
                  jnp.zeros((bp, CONV_B - 1, WB), x_prompt.dtype),
                  jnp.zeros((bp, CONV_C - 1, WC), x_prompt.dtype),
                  jnp.zeros((bp, HD, DKD, DVD), f32),
                  jnp.zeros((bp, HD, DKD), f32),
                  jnp.zeros((bp, HD), f32))
    xp, xs = x_prompt, x_sample
    mem_ks, mem_vs, prompt_states, sample_states = [], [], [], []
    for l in range(DEPTH):
        p = {name: arr[l] for name, arr in weights.items()}
        mk = (mem_prompt @ p["xk_w"]).reshape(bp, n_mem, XH, XDH)
        mv = (mem_prompt @ p["xv_w"]).reshape(bp, n_mem, XH, XDH)
        xp, st_p = decoder_layer(xp, mk, mv, zero_state, p)
        st_in = (state_delta_conv[l], state_delta_S[l], state_glu_conv[l], state_short_conv[l],
                 state_mlstm_C[l], state_mlstm_n[l], state_mlstm_m[l])
        xs, st_s = decoder_layer(xs, cache_mem_k[l], cache_mem_v[l], st_in, p)
        mem_ks.append(mk)
        mem_vs.append(mv)
        prompt_states.append(st_p)
        sample_states.append(st_s)
    mem_k_p = jnp.stack(mem_ks)
    mem_v_p = jnp.stack(mem_vs)
    (delta_conv_p, delta_S_p, glu_conv_p, short_conv_p,
     mlstm_C_p, mlstm_n_p, mlstm_m_p) = _stack_states(prompt_states)
    (delta_conv_s, delta_S_s, glu_conv_s, short_conv_s,
     mlstm_C_s, mlstm_n_s, mlstm_m_s) = _stack_states(sample_states)
    return (xp, xs, mem_k_p, mem_v_p,
            delta_conv_p, delta_S_p, glu_conv_p, short_conv_p, mlstm_C_p, mlstm_n_p, mlstm_m_p,
            delta_conv_s, delta_S_s, glu_conv_s, short_conv_s, mlstm_C_s, mlstm_n_s, mlstm_m_s)
```

```python
import contextlib
import numpy as np
import concourse.bass as bass
import concourse.mybir as mybir
from concourse.bass_utils import run_bass_kernel_spmd

F32 = mybir.dt.float32
F32R = mybir.dt.float32r
BF16 = mybir.dt.bfloat16
AF = mybir.ActivationFunctionType
ALU = mybir.AluOpType
AX = mybir.AxisListType

EPOCH = 20000
NDS = 8


class View:
    __slots__ = ("key", "ap")

    def __init__(self, key, ap):
        self.key = key
        self.ap = ap


class Buf:
    def __init__(self, t, key):
        self.t = t
        self.key = key

    def __getitem__(self, idx):
        return View(self.key, self.t[idx])

    def sub(self, subkey, idx):
        return View((self.key, subkey), self.t[idx])


class Sched:
    ENG = ["pe", "act", "dve", "pool", "sp"]

    def __init__(self, nc):
        self.nc = nc
        self.stack = contextlib.ExitStack()
        self.prog = {e: [] for e in self.ENG}
        self.count = {e: 0 for e in self.ENG}
        self.waited = {e: {} for e in self.ENG}
        self.last_w = {}
        self.readers = {}
        self.dslot = {e: 0 for e in self.ENG}
        self.dval = {}
        self.sids = {}
        self.nbuf = 0
        self.psum_banks = []
        self.psum_i = 0
        self.nops = 0

    def sb(self, shape, dtype=F32, name=None):
        self.nbuf += 1
        name = name or f"sb{self.nbuf}"
        t = self.stack.enter_context(self.nc.sbuf_tensor(name, list(shape), dtype))
        return Buf(t, name)

    def init_psum(self, n=8):
        for i in range(n):
            t = self.stack.enter_context(self.nc.psum_tensor(f"ps{i}", [128, 512], F32))
            self.psum_banks.append(Buf(t, f"ps{i}"))

    def ps(self):
        b = self.psum_banks[self.psum_i % len(self.psum_banks)]
        self.psum_i += 1
        return b

    def _deps(self, eng, reads, writes):
        deps = set()
        for v in reads:
            t = self.last_w.get(v.key)
            if t is not None:
                deps.add(t)
        for v in writes:
            t = self.last_w.get(v.key)
            if t is not None:
                deps.add(t)
            for r in self.readers.get(v.key, ()):
                if r[2] != eng or r[2] == "dma":
                    deps.add(r)
        for (sid, val, deng) in sorted(deps, key=lambda d: str(d)):
            if deng == eng and eng == "pe":
                continue
            if self.waited[eng].get(sid, 0) >= val:
                continue
            self.waited[eng][sid] = val
            self.prog[eng].append(("wait", sid, val))

    def _commit(self, tok, reads, writes):
        for v in reads:
            self.readers.setdefault(v.key, []).append(tok)
        for v in writes:
            self.last_w[v.key] = tok
            self.readers[v.key] = []

    def op(self, eng, fn, reads=(), writes=()):
        self._deps(eng, reads, writes)
        n = self.count[eng]
        self.count[eng] += 1
        sid = ("c", eng, n // EPOCH)
        val = n % EPOCH + 1
        self.sids[sid] = 1
        tok = (sid, val, eng)
        self.prog[eng].append(("op", fn, sid, 1))
        self._commit(tok, reads, writes)
        self.nops += 1
        return tok

    def dma(self, q, out_ap, in_ap, reads=(), writes=(), **kw):
        eng = q
        self._deps(eng, reads, writes)
        slot = self.dslot[q]
        self.dslot[q] = (slot + 1) % NDS
        sid = ("d", q, slot)
        self.sids[sid] = 1
        prev = self.dval.get(sid, 0)
        if prev > 0 and self.waited[eng].get(sid, 0) < prev:
            self.waited[eng][sid] = prev
            self.prog[eng].append(("wait", sid, prev))
        val = prev + 16
        self.dval[sid] = val
        tok = (sid, val, "dma")
        self.prog[eng].append(("op", lambda e: e.dma_start(out_ap, in_ap, **kw), sid, 16))
        self._commit(tok, reads, writes)
        self.nops += 1
        return tok

    def barrier(self):
        toks = []
        for e in self.ENG:
            n = self.count[e]
            if n > 0:
                toks.append((("c", e, (n - 1) // EPOCH), (n - 1) % EPOCH + 1, e))
        for sid, val in self.dval.items():
            toks.append((sid, val, "dma"))
        for e in self.ENG:
            for (sid, val, deng) in toks:
                if deng == e:
                    continue
                if self.waited[e].get(sid, 0) >= val:
                    continue
                self.waited[e][sid] = val
                self.prog[e].append(("wait", sid, val))

    def finish(self):
        for sid, val in self.dval.items():
            q = sid[1]
            if self.waited[q].get(sid, 0) < val:
                self.prog[q].append(("wait", sid, val))
        nc = self.nc
        sems = {}
        for i, sid in enumerate(self.sids):
            sems[sid] = self.stack.enter_context(nc.semaphore(f"s{i}"))
        prog = self.prog

        def replay(name, e):
            for it in prog[name]:
                if it[0] == "wait":
                    e.wait_ge(sems[it[1]], it[2])
                else:
                    it[1](e).then_inc(sems[it[2]], it[3])

        with nc.Block() as block:
            @block.tensor
            def _(e):
                replay("pe", e)

            @block.scalar
            def _(e):
                replay("act", e)

            @block.vector
            def _(e):
                replay("dve", e)

            @block.gpsimd
            def _(e):
                replay("pool", e)

            @block.sync
            def _(e):
                replay("sp", e)
        self.stack.close()

    def mm(self, out, lhsT, rhs, start=True, stop=True, r=False):
        la, ra = lhsT.ap, rhs.ap
        if r:
            la, ra = la.bitcast(F32R), ra.bitcast(F32R)
        rd = [lhsT, rhs] + ([] if start else [out])
        return self.op("pe", lambda e: e.matmul(out.ap, la, ra, start=start, stop=stop), rd, [out])

    def tr(self, out, in_, ident):
        return self.op("pe", lambda e: e.transpose(out.ap, in_.ap, ident.ap), [in_, ident], [out])

    def act(self, out, in_, func, bias=None, scale=None, accum=None, eng="act"):
        kw = {}
        rd = [in_]
        wr = [out]
        if bias is not None:
            if isinstance(bias, View):
                kw["bias"] = bias.ap
                rd.append(bias)
            else:
                kw["bias"] = bias
        if scale is not None:
            if isinstance(scale, View):
                kw["scale"] = scale.ap
                rd.append(scale)
            else:
                kw["scale"] = scale
        if accum is not None:
            kw["accum_out"] = accum.ap
            wr.append(accum)
        return self.op("act", lambda e: e.activation(out.ap, in_.ap, func, **kw), rd, wr)

    def tt(self, out, a, b, op, eng="dve"):
        return self.op(eng, lambda e: e.tensor_tensor(out.ap, a.ap, b.ap, op), [a, b], [out])

    def ts(self, out, a, s1, op0, s2=None, op1=None, accum=None, eng="dve"):
        rd = [a]
        wr = [out]
        s1a = s1.ap if isinstance(s1, View) else s1
        s2a = s2.ap if isinstance(s2, View) else s2
        if isinstance(s1, View):
            rd.append(s1)
        if isinstance(s2, View):
            rd.append(s2)
        kw = {}
        if op1 is not None:
            kw["op1"] = op1
        if accum is not None:
            kw["accum_out"] = accum.ap
            wr.append(accum)
        return self.op(eng, lambda e: e.tensor_scalar(out.ap, a.ap, s1a, s2a, op0, **kw), rd, wr)

    def stt(self, out, a, s, b, op0, op1, accum=None):
        rd = [a, b]
        wr = [out]
        sa = s.ap if isinstance(s, View) else s
        if isinstance(s, View):
            rd.append(s)
        kw = {}
        if accum is not None:
            kw["accum_out"] = accum.ap
            wr.append(accum)
        return self.op("dve", lambda e: e.scalar_tensor_tensor(out.ap, a.ap, sa, b.ap, op0, op1, **kw), rd, wr)

    def copy(self, out, in_, eng="dve"):
        if eng == "act":
            return self.op("act", lambda e: e.copy(out.ap, in_.ap), [in_], [out])
        return self.op(eng, lambda e: e.tensor_copy(out.ap, in_.ap), [in_], [out])

    def memset(self, out, val, eng="pool"):
        return self.op(eng, lambda e: e.memset(out.ap, val), [], [out])

    def recip(self, out, in_):
        return self.op("dve", lambda e: e.reciprocal(out.ap, in_.ap), [in_], [out])

    def rmax(self, out, in_, eng="dve"):
        return self.op(eng, lambda e: e.reduce_max(out.ap, in_.ap, AX.X), [in_], [out])

D = 2048
NIN = 20504
O_QA, O_KA, O_VA, O_ZA, O_BD = 0, 1024, 2048, 3072, 4096
O_GA, O_GG, O_BG, O_CG, O_HC = 4112, 5136, 6160, 7184, 8208
O_QD, O_KD, O_VD, O_OD, O_IF, O_GATE = 9232, 9744, 10256, 11280, 12304, 12312
ALPHA = 4 ** 0.25
NEG = -1.0e30
NS = 16
TBP = 256
CFG = {}


def dap(t, off, dims):
    return bass.AP(t, off, [list(d) for d in dims])


def build():
    nc = bass.Bass("TRN2", target_bir_lowering=False)

    def din(name, shape):
        return nc.dram_tensor(name, list(shape), F32, kind="ExternalInput")

    def dout(name, shape):
        return nc.dram_tensor(name, list(shape), F32, kind="ExternalOutput")

    I = {}
    for name, shape in [
        ("xp", (2048, D)), ("xs", (NS, D)), ("memp", (256, D)),
        ("cmk", (2, NS, 256, 512)), ("cmv", (2, NS, 256, 512)),
        ("sdc", (2, NS, 3, 3072)), ("sdS", (2, NS, 8, 128, 128)), ("sgc", (2, NS, 30, 1024)),
        ("ssc", (2, NS, 2, 1024)), ("smC", (2, NS, 4, 128, 256)), ("smn", (2, NS, 4, 128)), ("smm", (2, NS, 4)),
        ("w_in", (2, D, NIN)), ("b_in", (2, NIN)), ("a_conv_w", (2, 4, 3072)), ("a_A_log", (2, 8)),
        ("a_dt_bias", (2, 8)), ("a_norm_w", (2, 128)), ("b_conv_w", (2, 31, 1024)), ("b_conv_b", (2, 1024)),
        ("b_ln_g", (2, 1024)), ("b_ln_b", (2, 1024)), ("c_conv_w", (2, 3, 1024)), ("d_norm_w", (2, 256)),
        ("w_branch", (2, 4, 1024, D)), ("w_out", (2, D, D)), ("ln1_g", (2, D)), ("ln1_b", (2, D)),
        ("xq_w", (2, D, 512)), ("xk_w", (2, D, 512)), ("xv_w", (2, D, 512)), ("xo_w", (2, 512, D)),
        ("ln2_g", (2, D)), ("ln2_b", (2, D)), ("ffn_w1", (2, D, 4 * D)), ("ffn_b1", (2, 4 * D)),
        ("ffn_w2", (2, 4 * D, D)), ("ffn_b2", (2, D)), ("ln3_g", (2, D)), ("ln3_b", (2, D)),
    ]:
        I[name] = din(name, shape)
    O = {}
    for name, shape in [
        ("yp", (2048, D)), ("ys", (NS, D)), ("mkp", (2, 256, 512)), ("mvp", (2, 256, 512)),
        ("dcp", (2, 3, 3072)), ("dSp", (2, 8, 128, 128)), ("gcp", (2, 30, 1024)), ("scp", (2, 2, 1024)),
        ("mCp", (2, 4, 128, 256)), ("mnp", (2, 4, 128)), ("mmp", (2, 4)),
        ("dcs", (2, NS, 3, 3072)), ("dSs", (2, NS, 8, 128, 128)), ("gcs", (2, NS, 30, 1024)),
        ("scs", (2, NS, 2, 1024)), ("mCs", (2, NS, 4, 128, 256)), ("mns", (2, NS, 4, 128)), ("mms", (2, NS, 4)),
    ]:
        O[name] = dout(name, shape)

    DBG = {}
    if CFG.get('dump'):
        for nm in ['C', 'B', 'A', 'D']:
            DBG[nm] = dout('dbg_' + nm, (128, 8, TBP))
        for nm in ['mixed', 'x1', 'x2', 'x3']:
            DBG[nm] = dout('dbg_' + nm, (128, 16, TBP))
    dumped = set()
    S = Sched(nc)
    S.init_psum()
    sb = S.sb

    def dump(nm, buf, nch):
        if not CFG.get('dump') or nm in dumped:
            return
        dumped.add(nm)
        S.dma("pool", dap(DBG[nm], 0, [[nch * TBP, 128], [TBP, nch], [1, TBP]]), buf.t[:, 0:nch, :],
              reads=[buf.sub(k, (slice(None), k, slice(None))) for k in range(nch)])

    ones = sb([128, 128], name="ones")
    ident = sb([128, 128], name="ident")
    triu = sb([128, 128], name="triu")
    maskS = sb([128, 128], name="maskS")
    maskL = sb([128, 128], name="maskL")
    zeros = sb([128, 128], name="zeros")
    S.memset(ones[:], 1.0)
    S.memset(zeros[:], 0.0)
    S.op("pool", lambda e: e.affine_select(ident.t[:], ones.t[:], [[-1, 128]], ALU.is_equal, 0.0, base=0, channel_multiplier=1), [ones[:]], [ident[:]])
    S.op("pool", lambda e: e.affine_select(triu.t[:], ones.t[:], [[1, 128]], ALU.is_ge, 0.0, base=0, channel_multiplier=-1), [ones[:]], [triu[:]])
    S.op("pool", lambda e: e.affine_select(maskS.t[:], zeros.t[:], [[1, 128]], ALU.is_gt, NEG, base=0, channel_multiplier=-1), [zeros[:]], [maskS[:]])
    S.op("pool", lambda e: e.affine_select(maskL.t[:], zeros.t[:], [[-1, 128]], ALU.is_ge, NEG, base=0, channel_multiplier=1), [zeros[:]], [maskL[:]])

    def colload(dst, dcol0, t, off, nchunk, n=128):
        S.dma("pool", dst.t[0:n, dcol0:dcol0 + nchunk], dap(t, off, [[1, n], [128, nchunk]]), writes=[dst[:]],
              allow_slow_non_contiguous=True)

    P = []
    for l in range(2):
        p = {}
        bi = sb([128, 162], name=f"bin{l}")
        colload(bi, 0, I["b_in"], l * NIN + 0, 32)
        colload(bi, 32, I["b_in"], l * NIN + O_BD, 1, n=16)
        colload(bi, 33, I["b_in"], l * NIN + O_GA, 64)
        colload(bi, 97, I["b_in"], l * NIN + O_IF, 1, n=8)
        colload(bi, 98, I["b_in"], l * NIN + O_GATE, 64)
        p["bin"] = bi
        acw = sb([128, 24, 4], name=f"acw{l}")
        for k in range(4):
            S.dma("pool", acw.t[:, :, k], dap(I["a_conv_w"], l * 4 * 3072 + k * 3072, [[1, 128], [128, 24]]), writes=[acw[:]], allow_slow_non_contiguous=True)
        bcw = sb([128, 8, 31], name=f"bcw{l}")
        for k in range(31):
            S.dma("pool", bcw.t[:, :, k], dap(I["b_conv_w"], l * 31 * 1024 + k * 1024, [[1, 128], [128, 8]]), writes=[bcw[:]], allow_slow_non_contiguous=True)
        ccw = sb([128, 8, 3], name=f"ccw{l}")
        for k in range(3):
            S.dma("pool", ccw.t[:, :, k], dap(I["c_conv_w"], l * 3 * 1024 + k * 1024, [[1, 128], [128, 8]]), writes=[ccw[:]], allow_slow_non_contiguous=True)
        p["acw"], p["bcw"], p["ccw"] = acw, bcw, ccw
        for nm, nch in [("b_conv_b", 8), ("b_ln_g", 8), ("b_ln_b", 8), ("a_norm_w", 1), ("d_norm_w", 2), ("ln1_g", 16), ("ln1_b", 16),
                        ("ln2_g", 16), ("ln2_b", 16), ("ln3_g", 16), ("ln3_b", 16), ("ffn_b1", 64), ("ffn_b2", 16)]:
            tl = sb([128, nch], name=f"{nm}{l}")
            colload(tl, 0, I[nm], l * nch * 128, nch)
            p[nm] = tl
        negA = sb([128, 8], name=f"negA{l}")
        dtb = sb([128, 8], name=f"dtb{l}")
        S.dma("pool", negA.t[:, :], dap(I["a_A_log"], l * 8, [[0, 128], [1, 8]]), writes=[negA[:]])
        S.dma("pool", dtb.t[:, :], dap(I["a_dt_bias"], l * 8, [[0, 128], [1, 8]]), writes=[dtb[:]])
        S.act(negA[:], negA[:], AF.Exp)
        S.ts(negA[:], negA[:], -1.0, ALU.mult)
        p["negA"], p["dtb"] = negA, dtb
        P.append(p)

    def bcol(l, col0, n=128):
        if col0 < O_BD:
            c = col0 // 128
        elif col0 == O_BD:
            c = 32
        elif col0 < O_IF:
            c = 33 + (col0 - O_GA) // 128
        elif col0 == O_IF:
            c = 97
        else:
            c = 98 + (col0 - O_GATE) // 128
        return P[l]["bin"][0:n, c:c + 1]

    xT = sb([128, 16, TBP], name="xT")
    mixed = sb([128, 16, TBP], name="mixed")
    outn = sb([128, 8, TBP], BF16, name="outn")
    xTb = sb([128, 16, TBP], BF16, name="xTb")
    mixedb = sb([128, 16, TBP], BF16, name="mixedb")
    hb = sb([128, 8, TBP], BF16, name="hb")
    ubuf = sb([128, 8, TBP], name="ubuf")
    tokbuf = sb([128, D], name="tokbuf")
    wsl = [sb([128, 16, 128], name=f"w{i}") for i in range(3)]
    wbf = [sb([128, 16, 128], BF16, name=f"wb{i}") for i in range(3)]
    wi = [0]
    NT = 8
    tmpA = [sb([128, TBP], name=f"tA{i}") for i in range(NT)]
    ti = [0]
    NQ = 12
    tmpQ = [sb([128, 128], name=f"tQ{i}") for i in range(NQ)]
    qi = [0]
    NC_ = 48
    tmpC = [sb([128, 8], name=f"tC{i}") for i in range(NC_)]
    ci_ = [0]
    extb = sb([128, TBP + 30], name="extb")
    persA = sb([128, 16, 24], name="persA")
    lnm = sb([128, TBP], name="lnm"); lnr = sb([128, TBP], name="lnr"); lnm2 = sb([128, TBP], name="lnm2")
    tmpQL = [sb([128, 128], name=f"tQL{i}") for i in range(12)]
    qli = [0]
    persD = sb([128, 16, 12], name="persD")
    exts = sb([128, NS, 31], name="exts")
    t258 = [sb([128, 258], name=f"t258_{i}") for i in range(4)]
    t258i = [0]

    def tA():
        ti[0] += 1
        return tmpA[ti[0] % NT]

    def tQ():
        qi[0] += 1
        return tmpQ[qi[0] % NQ]

    def tQL():
        qli[0] += 1
        return tmpQL[qli[0] % 12]

    def psrc(pv, L, shape_rows):
        if L != 1:
            return pv
        t = tQ()
        v = t[0:shape_rows, 0:1]
        S.act(v, pv, AF.Identity)
        return v

    def tC():
        ci_[0] += 1
        return tmpC[ci_[0] % NC_]

    def t258n():
        t258i[0] += 1
        return t258[t258i[0] % 4]

    def ch(buf, k, sl=slice(None)):
        return buf.sub(k, (slice(None), k, sl))

    ARN = 10240
    arena = sb([128, ARN], name="arena")
    apos = {"p": 0, "s": 0}

    def carve(phase, shape, name):
        n = 1
        for s_ in shape[1:]:
            n *= s_
        a0 = apos[phase]
        apos[phase] += n
        assert apos[phase] <= ARN, (phase, apos[phase])
        v = arena.t[:, a0:a0 + n]
        if len(shape) == 3:
            v = v.rearrange("p (a b) -> p a b", a=shape[1])
        elif len(shape) == 4:
            v = v.rearrange("p (a b c) -> p a b c", a=shape[1], b=shape[2])
        return Buf(v, name)

    histA = [carve("p", [128, 24, 3], f"hA{l}") for l in range(2)]
    histB = [carve("p", [128, 8, 30], f"hB{l}") for l in range(2)]
    histC = [carve("p", [128, 8, 2], f"hC{l}") for l in range(2)]
    Sa = [[carve("p", [128, 128], f"Sa{l}_{h}") for h in range(8)] for l in range(2)]
    Cn = [[carve("p", [128, 258], f"Cn{l}_{h}") for h in range(4)] for l in range(2)]
    mst = [carve("p", [128, 4], f"mst{l}") for l in range(2)]
    KT = [carve("p", [128, 4, 256], f"KT{l}") for l in range(2)]
    Vt = [carve("p", [128, 2, 512], f"Vt{l}") for l in range(2)]
    for l in range(2):
        S.memset(histA[l][:], 0.0); S.memset(histB[l][:], 0.0); S.memset(histC[l][:], 0.0)
        S.memset(mst[l][:], 0.0)
        for h in range(8):
            S.memset(Sa[l][h][:], 0.0)
        for h in range(4):
            S.memset(Cn[l][h][:], 0.0)
    SaS = [carve("s", [128, 128], f"SaS{i}") for i in range(3)]
    CnS = [carve("s", [128, 258], f"CnS{i}") for i in range(3)]
    mstS = [carve("s", [128, 4], f"mstS{i}") for i in range(3)]
    ssi = [0]
    KTs = carve("s", [128, 4, 256], "KTs")
    Kts = carve("s", [128, 2, 512], "Kts")
    Vts = carve("s", [128, 2, 512], "Vts")
    hsA = carve("s", [128, 24, NS, 3], "hsA")
    hsB = carve("s", [128, 8, NS, 30], "hsB")
    hsC = carve("s", [128, 8, NS, 2], "hsC")
    tmpB_new = carve("s", [128, 8, NS], "tmpBn")
    tmpA_new = carve("s", [128, 24, NS], "tmpAn")
    qTb = sb([128, 4, TBP], name="qTb")
    oTb = sb([128, 4, TBP], BF16, name="oTb")

    def load_w(t, base, rstride, row0, col0, n, kcn, bf=False):
        w = wsl[wi[0] % 3]
        wb = wbf[wi[0] % 3]
        wi[0] += 1
        S.dma("sp", w.t[:, 0:kcn, 0:n], dap(t, base + row0 * rstride + col0, [[rstride, 128], [128 * rstride, kcn], [1, n]]),
              writes=[w[:]])
        if not bf:
            return w
        S.copy(wb[:, 0:kcn, 0:n], w[:, 0:kcn, 0:n], eng="pool")
        return wb

    def fm_proj(t, base, rstride, row0, col0, n, kcn, rhs_fn, TB):
        w = load_w(t, base, rstride, row0, col0, n, kcn, bf=True)
        p = S.ps()
        for k in range(kcn):
            S.mm(p[0:n, 0:TB], w[:, k, 0:n], rhs_fn(k), start=(k == 0), stop=(k == kcn - 1), r=False)
        return p

    def win_proj(l, col0, n, TB):
        return fm_proj(I["w_in"], l * D * NIN, NIN, 0, col0, n, 16, lambda k: ch(xTb, k, slice(0, TB)), TB)

    def transpose_to(dst_view, src_view, rows, cols, eng="dve"):
        p = S.ps()
        S.tr(p[0:cols, 0:rows], src_view, ident[0:rows, 0:rows])
        S.copy(dst_view, p[0:cols, 0:rows], eng=eng)

    def layernorm(l, gname, bname, TB):
        pm = S.ps()
        for k in range(16):
            S.mm(pm[:, 0:TB], ones[:, :], ch(xT, k, slice(0, TB)), start=(k == 0), stop=(k == 15), r=False)
        pq = S.ps()
        for k in range(16):
            sq = tA()
            S.act(sq[:, 0:TB], ch(xT, k, slice(0, TB)), AF.Square)
            S.mm(pq[:, 0:TB], ones[:, :], sq[:, 0:TB], start=(k == 0), stop=(k == 15), r=False)
        mean, rstd, m2 = lnm, lnr, lnm2
        S.ts(mean[:, 0:TB], pm[:, 0:TB], 1.0 / D, ALU.mult)
        S.tt(m2[:, 0:TB], mean[:, 0:TB], mean[:, 0:TB], ALU.mult)
        S.stt(rstd[:, 0:TB], pq[:, 0:TB], 1.0 / D, m2[:, 0:TB], ALU.mult, ALU.subtract)
        S.ts(rstd[:, 0:TB], rstd[:, 0:TB], 0.0, ALU.max, 1e-5, ALU.add)
        S.act(rstd[:, 0:TB], rstd[:, 0:TB], AF.Sqrt)
        S.recip(rstd[:, 0:TB], rstd[:, 0:TB])
        for k in range(16):
            t = tA()
            S.tt(t[:, 0:TB], ch(xT, k, slice(0, TB)), mean[:, 0:TB], ALU.subtract)
            S.tt(t[:, 0:TB], t[:, 0:TB], rstd[:, 0:TB], ALU.mult, eng="pool")
            S.act(ch(xT, k, slice(0, TB)), t[:, 0:TB], AF.Identity, bias=P[l][bname][:, k:k + 1], scale=P[l][gname][:, k:k + 1])
            S.copy(ch(xTb, k, slice(0, TB)), ch(xT, k, slice(0, TB)), eng="pool")

    def emit_rows(srcT_fn, nch, R, dst_ap_fn):
        for j0 in range(0, nch, 16):
            nj = min(16, nch - j0)
            for j in range(j0, j0 + nj):
                transpose_to(tokbuf[0:R, (j - j0) * 128:(j - j0 + 1) * 128], srcT_fn(j), 128, R)
            S.dma("pool", dst_ap_fn(j0 * 128, nj * 128), tokbuf.t[0:R, 0:nj * 128], reads=[tokbuf[:]])

    class Conv:
        def __init__(self, mode, W, TB):
            self.mode, self.W, self.TB, self.H = mode, W, TB, W - 1
            if mode == "p":
                self.newv = extb[:, self.H:self.H + TB]
                self.histv = extb[:, 0:self.H]
                self.tail = extb[:, TB:TB + self.H]
            else:
                self.newv = exts[:, :, self.H]
                self.histv = exts[:, :, 0:self.H]

        def tap(self, k):
            if self.mode == "p":
                return extb[:, k:k + self.TB]
            return exts[:, :, k]

    def conv_apply(cv, wtile, j, out_view, bias_view=None):
        W = cv.W
        acc = out_view
        if bias_view is not None:
            S.ts(acc, cv.tap(0), wtile[:, j, 0:1], ALU.mult, bias_view, ALU.add)
        else:
            S.ts(acc, cv.tap(0), wtile[:, j, 0:1], ALU.mult)
        for k in range(1, W):
            S.stt(acc, cv.tap(k), wtile[:, j, k:k + 1], acc, ALU.mult, ALU.add)

    def layer_block(l, TB, mode, chunks, last, hsA=None, hsB=None, hsC=None):
        p_ = P[l]
        shp = (lambda v: v)
        if mode == "p":
            ov = lambda buf: buf[:, 0:TB]
        else:
            ov = lambda buf: buf[:, 0:TB]
        first_branch = [True]

        def branch_merge(n):
            for d in range(16):
                pg = win_proj(l, O_GATE + n * D + d * 128, 128, TB)
                gt = tA()
                S.act(gt[:, 0:TB], pg[:, 0:TB], AF.Sigmoid, bias=bcol(l, O_GATE + n * D + d * 128))
                pb = fm_proj(I["w_branch"], (l * 4 + n) * 1024 * D, D, 0, d * 128, 128, 8, lambda k: ch(outn, k, slice(0, TB)), TB)
                if first_branch[0]:
                    S.tt(ch(mixed, d, slice(0, TB)), pb[:, 0:TB], gt[:, 0:TB], ALU.mult)
                else:
                    S.tt(gt[:, 0:TB], pb[:, 0:TB], gt[:, 0:TB], ALU.mult)
                    if n == 3:
                        S.tt(ch(mixedb, d, slice(0, TB)), ch(mixed, d, slice(0, TB)), gt[:, 0:TB], ALU.add, eng="pool")
                    else:
                        S.tt(ch(mixed, d, slice(0, TB)), ch(mixed, d, slice(0, TB)), gt[:, 0:TB], ALU.add, eng="pool")
            first_branch[0] = False

        def newrow_out(vals_fn, nch, dst_t, lbase, H, C):
            emit_rows(vals_fn, nch, NS, lambda c0, n: dap(dst_t, lbase + (H - 1) * C + c0, [[H * C, NS], [1, n]]))

        def ph_C():
            cv = Conv(mode, 3, TB)
            newC = ubuf
            for j in range(8):
                pc = win_proj(l, O_CG + j * 128, 128, TB)
                cg = tA()
                S.act(cg[:, 0:TB], pc[:, 0:TB], AF.Identity, bias=bcol(l, O_CG + j * 128))
                ph = win_proj(l, O_HC + j * 128, 128, TB)
                if mode == "p":
                    S.copy(cv.histv, histC[l][:, j, :], eng="pool")
                else:
                    S.copy(cv.histv, hsC.sub(j, (slice(None), j, slice(None), slice(None))), eng="pool")
                S.stt(cv.newv, ph[:, 0:TB], bcol(l, O_HC + j * 128), cg[:, 0:TB], ALU.add, ALU.mult)
                acc = tA()
                conv_apply(cv, p_["ccw"], j, acc[:, 0:TB])
                if mode == "p":
                    S.copy(histC[l][:, j, :], cv.tail, eng="pool")
                else:
                    S.copy(ch(newC, j, slice(0, NS)), cv.newv, eng="pool")
                pbg = win_proj(l, O_BG + j * 128, 128, TB)
                S.stt(ch(outn, j, slice(0, TB)), pbg[:, 0:TB], bcol(l, O_BG + j * 128), acc[:, 0:TB], ALU.add, ALU.mult)
            if mode == "s":
                newrow_out(lambda j: ch(newC, j, slice(0, NS)), 8, O["scs"], l * NS * 2 * 1024, 2, 1024)
            branch_merge(2)

        def ph_B():
            cv = Conv(mode, 31, TB)
            newB = mixed
            for j in range(8):
                pg = win_proj(l, O_GG + j * 128, 128, TB)
                sg = tA()
                S.act(sg[:, 0:TB], pg[:, 0:TB], AF.Sigmoid, bias=bcol(l, O_GG + j * 128))
                pa = win_proj(l, O_GA + j * 128, 128, TB)
                if mode == "p":
                    S.copy(cv.histv, histB[l][:, j, :], eng="pool")
                else:
                    S.copy(cv.histv, hsB.sub(j, (slice(None), j, slice(None), slice(None))), eng="pool")
                S.stt(cv.newv, pa[:, 0:TB], bcol(l, O_GA + j * 128), sg[:, 0:TB], ALU.add, ALU.mult)
                conv_apply(cv, p_["bcw"], j, ch(ubuf, j, slice(0, TB)), bias_view=p_["b_conv_b"][:, j:j + 1])
                if mode == "p":
                    S.copy(histB[l][:, j, :], cv.tail, eng="pool")
                else:
                    S.copy(tmpB_new.sub(j, (slice(None), j, slice(None))), cv.newv, eng="pool")
            if mode == "s":
                newrow_out(lambda j: tmpB_new.sub(j, (slice(None), j, slice(None))), 8, O["gcs"], l * NS * 30 * 1024, 30, 1024)
            pm = S.ps()
            for k in range(8):
                S.mm(pm[:, 0:TB], ones[:, :], ch(ubuf, k, slice(0, TB)), start=(k == 0), stop=(k == 7), r=False)
            pq = S.ps()
            for k in range(8):
                sq = tA()
                S.act(sq[:, 0:TB], ch(ubuf, k, slice(0, TB)), AF.Square)
                S.mm(pq[:, 0:TB], ones[:, :], sq[:, 0:TB], start=(k == 0), stop=(k == 7), r=False)
            mean, rstd, m2 = lnm, lnr, lnm2
            S.ts(mean[:, 0:TB], pm[:, 0:TB], 1.0 / 1024, ALU.mult)
            S.tt(m2[:, 0:TB], mean[:, 0:TB], mean[:, 0:TB], ALU.mult)
            S.stt(rstd[:, 0:TB], pq[:, 0:TB], 1.0 / 1024, m2[:, 0:TB], ALU.mult, ALU.subtract)
            S.ts(rstd[:, 0:TB], rstd[:, 0:TB], 0.0, ALU.max, 1e-5, ALU.add)
            S.act(rstd[:, 0:TB], rstd[:, 0:TB], AF.Sqrt)
            S.recip(rstd[:, 0:TB], rstd[:, 0:TB])
            for k in range(8):
                t = tA()
                S.tt(t[:, 0:TB], ch(ubuf, k, slice(0, TB)), mean[:, 0:TB], ALU.subtract)
                S.tt(t[:, 0:TB], t[:, 0:TB], rstd[:, 0:TB], ALU.mult, eng="pool")
                S.act(ch(outn, k, slice(0, TB)), t[:, 0:TB], AF.Silu, bias=p_["b_ln_b"][:, k:k + 1], scale=p_["b_ln_g"][:, k:k + 1])
            branch_merge(1)

        def ph_A():
            pbd = win_proj(l, O_BD, 16, TB)
            bdT = tA()
            S.act(bdT[0:16, 0:TB], pbd[0:16, 0:TB], AF.Identity, bias=bcol(l, O_BD, 16))
            beta_t, g_t, gc_t = [], [], []
            for (c0, L) in chunks:
                p = S.ps()
                S.tr(p[0:L, 0:16], bdT[0:16, c0:c0 + L], ident[0:16, 0:16])
                ci_a = len(beta_t)
                be = Buf(persA.t[:, ci_a, 0:8], ("persA", ci_a, 0)); gg = Buf(persA.t[:, ci_a, 8:16], ("persA", ci_a, 1))
                gcx = Buf(persA.t[:, ci_a, 16:24], ("persA", ci_a, 2)); tmp = tC()
                S.act(be[0:L, 0:8], p[0:L, 0:8], AF.Sigmoid)
                S.tt(tmp[0:L, 0:8], p[0:L, 8:16], p_["dtb"][0:L, 0:8], ALU.add)
                S.act(tmp[0:L, 0:8], tmp[0:L, 0:8], AF.Exp)
                S.act(tmp[0:L, 0:8], tmp[0:L, 0:8], AF.Ln, bias=1.0)
                S.tt(gg[0:L, 0:8], tmp[0:L, 0:8], p_["negA"][0:L, 0:8], ALU.mult)
                pc = S.ps()
                S.mm(pc[0:L, 0:8], triu[0:L, 0:L], gg[0:L, 0:8], r=False)
                S.copy(gcx[0:L, 0:8], pc[0:L, 0:8])
                beta_t.append(be); g_t.append(gg); gc_t.append(gcx)
            newA = tmpA_new
            for h in range(8):
                qkv = []
                for part, off in enumerate((O_QA, O_KA, O_VA)):
                    jj = part * 8 + h
                    cv = Conv(mode, 4, TB)
                    pp = win_proj(l, off + h * 128, 128, TB)
                    if mode == "p":
                        S.copy(cv.histv, histA[l][:, jj, :], eng="pool")
                    else:
                        S.copy(cv.histv, hsA.sub(jj, (slice(None), jj, slice(None), slice(None))), eng="pool")
                    S.act(cv.newv, pp[:, 0:TB], AF.Identity, bias=bcol(l, off + h * 128))
                    acc = tA()
                    conv_apply(cv, p_["acw"], jj, acc[:, 0:TB])
                    if mode == "p":
                        S.copy(histA[l][:, jj, :], cv.tail, eng="pool")
                    else:
                        S.copy(newA.sub(jj, (slice(None), jj, slice(None))), cv.newv, eng="pool")
                    S.act(acc[:, 0:TB], acc[:, 0:TB], AF.Silu)
                    qkv.append(acc)
                qT_, kT_, vT_ = qkv
                for idx, t_ in enumerate((qT_, kT_)):
                    sq = tA()
                    S.act(sq[:, 0:TB], t_[:, 0:TB], AF.Square)
                    pn = S.ps()
                    S.mm(pn[:, 0:TB], ones[:, :], sq[:, 0:TB], r=False)
                    S.ts(sq[:, 0:TB], pn[:, 0:TB], 1e-6, ALU.add)
                    S.act(sq[:, 0:TB], sq[:, 0:TB], AF.Sqrt)
                    S.recip(sq[:, 0:TB], sq[:, 0:TB])
                    if idx == 0:
                        S.stt(t_[:, 0:TB], t_[:, 0:TB], 128 ** -0.5, sq[:, 0:TB], ALU.mult, ALU.mult)
                    else:
                        S.tt(t_[:, 0:TB], t_[:, 0:TB], sq[:, 0:TB], ALU.mult)
                pz = win_proj(l, O_ZA + h * 128, 128, TB)
                sz = tA()
                S.act(sz[:, 0:TB], pz[:, 0:TB], AF.Silu, bias=bcol(l, O_ZA + h * 128))
                for ci, (c0, L) in enumerate(chunks):
                    if mode == "p":
                        Sh = Sa[l][h]
                    else:
                        Sh = SaS[ssi[0] % 3]; ssi[0] += 1
                        S.dma("sp", Sh.t[:, :], dap(I["sdS"], ((l * NS + ci) * 8 + h) * 16384, [[128, 128], [1, 128]]), writes=[Sh[:]])
                    delta_chunk(l, h, ci, c0, L, qT_, kT_, vT_, sz, Sh, beta_t[ci], g_t[ci], gc_t[ci])
                    if mode == "s":
                        S.dma("pool", dap(O["dSs"], ((l * NS + ci) * 8 + h) * 16384, [[128, 128], [1, 128]]), Sh.t[:, :], reads=[Sh[:]])
                    elif last:
                        if ci == len(chunks) - 1:
                            S.dma("pool", dap(O["dSp"], (l * 8 + h) * 16384, [[128, 128], [1, 128]]), Sh.t[:, :], reads=[Sh[:]])
            if mode == "s":
                newrow_out(lambda j: newA.sub(j, (slice(None), j, slice(None))), 24, O["dcs"], l * NS * 3 * 3072, 3, 3072)
            branch_merge(0)

        def ph_D():
            pif = win_proj(l, O_IF, 8, TB)
            ifT = tA()
            S.act(ifT[0:8, 0:TB], pif[0:8, 0:TB], AF.Identity, bias=bcol(l, O_IF, 8))
            lf_t, b_t, ib_t = [], [], []
            for (c0, L) in chunks:
                p = S.ps()
                S.tr(p[0:L, 0:8], ifT[0:8, c0:c0 + L], ident[0:8, 0:8])
                ci_d = len(lf_t)
                lf = Buf(persD.t[:, ci_d, 0:4], ("persD", ci_d, 0)); bb = Buf(persD.t[:, ci_d, 4:8], ("persD", ci_d, 1))
                ib = Buf(persD.t[:, ci_d, 8:12], ("persD", ci_d, 2))
                S.act(lf[0:L, 0:4], p[0:L, 4:8], AF.Exp, scale=-1.0)
                S.act(lf[0:L, 0:4], lf[0:L, 0:4], AF.Ln, bias=1.0)
                S.ts(lf[0:L, 0:4], lf[0:L, 0:4], -1.0, ALU.mult)
                pc = S.ps()
                S.mm(pc[0:L, 0:4], triu[0:L, 0:L], lf[0:L, 0:4], r=False)
                S.copy(bb[0:L, 0:4], pc[0:L, 0:4])
                S.tt(ib[0:L, 0:4], p[0:L, 0:4], bb[0:L, 0:4], ALU.subtract)
                lf_t.append(lf); b_t.append(bb); ib_t.append(ib)
            for h in range(4):
                pq_ = win_proj(l, O_QD + h * 128, 128, TB)
                qT_ = tA()
                S.act(qT_[:, 0:TB], pq_[:, 0:TB], AF.Identity, bias=bcol(l, O_QD + h * 128))
                pk_ = win_proj(l, O_KD + h * 128, 128, TB)
                kT_ = tA()
                S.ts(kT_[:, 0:TB], pk_[:, 0:TB], bcol(l, O_KD + h * 128), ALU.add, 128 ** -0.5, ALU.mult)
                vT_, so_ = [], []
                for e in range(2):
                    pv_ = win_proj(l, O_VD + h * 256 + e * 128, 128, TB)
                    v_ = tA()
                    S.act(v_[:, 0:TB], pv_[:, 0:TB], AF.Identity, bias=bcol(l, O_VD + h * 256 + e * 128))
                    vT_.append(v_)
                for e in range(2):
                    po_ = win_proj(l, O_OD + h * 256 + e * 128, 128, TB)
                    o_ = tA()
                    S.act(o_[:, 0:TB], po_[:, 0:TB], AF.Sigmoid, bias=bcol(l, O_OD + h * 256 + e * 128))
                    so_.append(o_)
                for ci, (c0, L) in enumerate(chunks):
                    if mode == "p":
                        Ch, ms = Cn[l][h], mst[l]
                    else:
                        Ch = CnS[ssi[0] % 3]; ms = mstS[ssi[0] % 3]; ssi[0] += 1
                        base = (l * NS + ci) * 4 + h
                        S.dma("sp", Ch.t[:, 0:256], dap(I["smC"], base * 32768, [[256, 128], [1, 256]]), writes=[Ch[:]])
                        S.dma("sp", Ch.t[:, 256:257], dap(I["smn"], base * 128, [[1, 128], [1, 1]]), writes=[Ch[:]], allow_slow_non_contiguous=True)
                        S.dma("sp", ms.t[:, h:h + 1], dap(I["smm"], base, [[0, 128], [1, 1]]), writes=[ms[:]])
                    mlstm_chunk(l, h, c0, L, qT_, kT_, vT_, so_, Ch, ms, lf_t[ci], b_t[ci], ib_t[ci])
                    if mode == "s" or (last and ci == len(chunks) - 1):
                        if mode == "s":
                            base = (l * NS + ci) * 4 + h
                            oc, on_, om = O["mCs"], O["mns"], O["mms"]
                        else:
                            base = l * 4 + h
                            oc, on_, om = O["mCp"], O["mnp"], O["mmp"]
                        S.dma("pool", dap(oc, base * 32768, [[256, 128], [1, 256]]), Ch.t[:, 0:256], reads=[Ch[:]])
                        S.dma("pool", dap(on_, base * 128, [[1, 128], [1, 1]]), Ch.t[:, 256:257], reads=[Ch[:]], allow_slow_non_contiguous=True)
                        S.dma("pool", dap(om, base, [[1, 1], [1, 1]]), ms.t[0:1, h:h + 1], reads=[ms[:]])
            branch_merge(3)

        def ph_O():
            for d in range(16):
                py = fm_proj(I["w_out"], l * D * D, D, 0, d * 128, 128, 16, lambda k: ch(mixedb, k, slice(0, TB)), TB)
                S.stt(ch(xT, d, slice(0, TB)), ch(xT, d, slice(0, TB)), ALPHA, py[:, 0:TB], ALU.mult, ALU.add)
            layernorm(l, "ln1_g", "ln1_b", TB)

        def ph_X():
            for h in range(4):
                pq_ = fm_proj(I["xq_w"], l * D * 512, 512, 0, h * 128, 128, 16, lambda k: ch(xTb, k, slice(0, TB)), TB)
                S.copy(ch(qTb, h, slice(0, TB)), pq_[:, 0:TB], eng="act")
            for ci, (c0, L) in enumerate(chunks):
                if mode == "p":
                    KTl, Vl = KT[l], Vt[l]
                else:
                    KTl, Vl = KTs, Vts
                    S.dma("sp", Kts.t[:, :, :], dap(I["cmk"], (l * NS + ci) * 256 * 512, [[512, 128], [128 * 512, 2], [1, 512]]), writes=[Kts[:]])
                    S.dma("sp", Vts.t[:, :, :], dap(I["cmv"], (l * NS + ci) * 256 * 512, [[512, 128], [128 * 512, 2], [1, 512]]), writes=[Vts[:]])
                    for hh in range(4):
                        for mc in range(2):
                            transpose_to(KTs[:, hh, mc * 128:(mc + 1) * 128], Kts[:, mc, hh * 128:(hh + 1) * 128], 128, 128, eng="act")
                otok = tokbuf
                for h in range(4):
                    ps_ = S.ps()
                    S.mm(ps_[0:L, 0:256], ch(qTb, h, slice(c0, c0 + L)), KTl[:, h, :], r=False)
                    mx = tC()
                    S.rmax(mx[0:L, 0:1], ps_[0:L, 0:256])
                    S.ts(mx[0:L, 1:2], mx[0:L, 0:1], -(128 ** -0.5), ALU.mult)
                    es = tA()
                    S.act(es[0:L, 0:256], ps_[0:L, 0:256], AF.Exp, bias=mx[0:L, 1:2], scale=128 ** -0.5, accum=mx[0:L, 2:3])
                    S.recip(mx[0:L, 3:4], mx[0:L, 2:3])
                    aT = tA()
                    for mc in range(2):
                        transpose_to(aT[:, mc * 128:mc * 128 + L], es[0:L, mc * 128:(mc + 1) * 128], L, 128, eng="act")
                    po = S.ps()
                    for mc in range(2):
                        S.mm(po[0:L, 0:128], aT[:, mc * 128:mc * 128 + L], Vl[:, mc, h * 128:(h + 1) * 128], start=(mc == 0), stop=(mc == 1), r=False)
                    S.ts(otok[0:L, h * 128:(h + 1) * 128], po[0:L, 0:128], mx[0:L, 3:4], ALU.mult)
                for h in range(4):
                    transpose_to(ch(oTb, h, slice(c0, c0 + L)), otok[0:L, h * 128:(h + 1) * 128], L, 128, eng="act")
            for d in range(16):
                py = fm_proj(I["xo_w"], l * 512 * D, D, 0, d * 128, 128, 4, lambda k: ch(oTb, k, slice(0, TB)), TB)
                S.stt(ch(xT, d, slice(0, TB)), ch(xT, d, slice(0, TB)), ALPHA, py[:, 0:TB], ALU.mult, ALU.add)
            layernorm(l, "ln2_g", "ln2_b", TB)

        def ph_M():
            for g in range(8):
                for jj in range(8):
                    j = g * 8 + jj
                    ph = fm_proj(I["ffn_w1"], l * D * 4 * D, 4 * D, 0, j * 128, 128, 16, lambda k: ch(xTb, k, slice(0, TB)), TB)
                    r_ = tA()
                    S.act(r_[:, 0:TB], ph[:, 0:TB], AF.Relu, bias=p_["ffn_b1"][:, j:j + 1])
                    S.tt(ch(hb, jj, slice(0, TB)), r_[:, 0:TB], r_[:, 0:TB], ALU.mult, eng="pool")
                for d in range(16):
                    pd = fm_proj(I["ffn_w2"], l * 4 * D * D, D, g * 1024, d * 128, 128, 8, lambda k: ch(hb, k, slice(0, TB)), TB)
                    if g == 0:
                        S.copy(ch(mixed, d, slice(0, TB)), pd[:, 0:TB], eng="act")
                    else:
                        S.tt(ch(mixed, d, slice(0, TB)), ch(mixed, d, slice(0, TB)), pd[:, 0:TB], ALU.add)
            for d in range(16):
                S.ts(ch(mixed, d, slice(0, TB)), ch(mixed, d, slice(0, TB)), p_["ffn_b2"][:, d:d + 1], ALU.add)
                S.stt(ch(xT, d, slice(0, TB)), ch(xT, d, slice(0, TB)), ALPHA, ch(mixed, d, slice(0, TB)), ALU.mult, ALU.add)
            layernorm(l, "ln3_g", "ln3_b", TB)

        PH = CFG.get('phases', 'CBADOXM')
        if 'C' in PH:
            ph_C(); dump('C', outn, 8)
        if 'B' in PH:
            ph_B(); dump('B', outn, 8)
        if 'A' in PH:
            ph_A(); dump('A', outn, 8)
        if 'D' in PH:
            ph_D(); dump('D', outn, 8); dump('mixed', mixed, 16)
        if 'O' in PH:
            ph_O(); dump('x1', xT, 16)
        if 'X' in PH:
            ph_X(); dump('x2', xT, 16)
        if 'M' in PH:
            ph_M(); dump('x3', xT, 16)

    def delta_chunk(l, h, ci, c0, L, qT_, kT_, vT_, sz, Sh, be, gg, gcx):
        p_ = P[l]
        beta_c, g_c, gc_c = be[0:L, h:h + 1], gg[0:L, h:h + 1], gcx[0:L, h:h + 1]
        kc, qc, vc = kT_[:, c0:c0 + L], qT_[:, c0:c0 + L], vT_[:, c0:c0 + L]
        gb = tQ()
        S.ts(gb[0:L, :], ones[0:L, :], g_c, ALU.mult)
        pg = S.ps()
        S.mm(pg[:, 0:L], gb[0:L, :], triu[0:L, 0:L], r=False)
        gcr = tQL(); egr = tQL()
        S.copy(gcr[:, 0:L], pg[:, 0:L], eng="act")
        S.act(egr[:, 0:L], pg[:, 0:L], AF.Exp)
        cols = tC()
        eg_c, ekl_c, nb_c, nbk_c = cols[0:L, 0:1], cols[0:L, 1:2], cols[0:L, 2:3], cols[0:L, 3:4]
        S.act(eg_c, gc_c, AF.Exp)
        S.act(ekl_c, gc_c, AF.Exp, bias=gcr[0:L, L - 1:L], scale=-1.0)
        S.ts(nb_c, beta_c, -1.0, ALU.mult)
        S.tt(nbk_c, nb_c, ekl_c, ALU.mult)
        ktok = tQL(); vtok = tQL()
        transpose_to(ktok[0:L, :], kc, 128, L, eng="act")
        transpose_to(vtok[0:L, :], vc, 128, L, eng="act")
        pk = S.ps()
        S.mm(pk[0:L, 0:L], kc, kc, r=False)
        S.mm(pk[0:L, 128:128 + L], kc, qc, r=False)
        qkd = tQL()
        if L > 1:
            Dm = tQ(); Es = tQ(); B0 = tQ(); Ei = tQ()
            S.stt(Dm[0:L, 0:L], gcr[0:L, 0:L], gc_c, maskS[0:L, 0:L], ALU.subtract, ALU.add)
            S.act(Es[0:L, 0:L], Dm[0:L, 0:L], AF.Exp)
            S.stt(B0[0:L, 0:L], pk[0:L, 0:L], beta_c, Es[0:L, 0:L], ALU.mult, ALU.mult)
            S.tt(Ei[0:L, 0:L], Es[0:L, 0:L], ident[0:L, 0:L], ALU.add, eng="pool")
            S.tt(qkd[0:L, 0:L], pk[0:L, 128:128 + L], Ei[0:L, 0:L], ALU.mult)
        else:
            S.copy(qkd[0:L, 0:L], pk[0:L, 128:128 + L], eng="act")
        pks = S.ps()
        S.mm(pks[0:L, 0:128], kc, Sh[:, :], r=False)
        rneg = tQL()
        S.stt(rneg[0:L, :], pks[0:L, 0:128], eg_c, vtok[0:L, :], ALU.mult, ALU.subtract)
        dl = tQL(); dk = tQL()
        if L > 1:
            Bp = B0
            Ap = tQ()
            transpose_to(Ap[0:L, 0:L], B0[0:L, 0:L], L, L, eng="act")
            U = tQ(); Lw = tQ()
            S.tt(U[0:L, 0:L], ident[0:L, 0:L], Bp[0:L, 0:L], ALU.subtract)
            S.tt(Lw[0:L, 0:L], ident[0:L, 0:L], Ap[0:L, 0:L], ALU.subtract, eng="pool")
            nlev = 6
            for j in range(1, nlev + 1):
                pB = S.ps()
                S.mm(pB[0:L, 0:L], Ap[0:L, 0:L], Bp[0:L, 0:L], r=False)
                Bn = tQ()
                S.copy(Bn[0:L, 0:L], pB[0:L, 0:L], eng="act")
                An = None
                if j < nlev:
                    pA = S.ps()
                    S.mm(pA[0:L, 0:L], Bp[0:L, 0:L], Ap[0:L, 0:L], r=False)
                    An = tQ()
                    S.copy(An[0:L, 0:L], pA[0:L, 0:L])
                pU = S.ps()
                S.mm(pU[0:L, 0:L], Lw[0:L, 0:L], Bn[0:L, 0:L], r=False)
                Un = tQ()
                S.tt(Un[0:L, 0:L], U[0:L, 0:L], pU[0:L, 0:L], ALU.add)
                if j < nlev:
                    pL = S.ps()
                    S.mm(pL[0:L, 0:L], U[0:L, 0:L], An[0:L, 0:L], r=False)
                    Ln_ = tQ()
                    S.tt(Ln_[0:L, 0:L], Lw[0:L, 0:L], pL[0:L, 0:L], ALU.add)
                    Lw = Ln_
                    Ap = An
                U = Un
                Bp = Bn
            pT = S.ps()
            S.mm(pT[0:L, 0:128], U[0:L, 0:L], rneg[0:L, :], r=False)
            S.ts(dl[0:L, :], pT[0:L, 0:128], nb_c, ALU.mult)
            S.act(dk[0:L, :], pT[0:L, 0:128], AF.Copy, scale=nbk_c) if False else S.ts(dk[0:L, :], pT[0:L, 0:128], nbk_c, ALU.mult)
        else:
            S.ts(dl[0:L, :], rneg[0:L, :], nb_c, ALU.mult)
            S.ts(dk[0:L, :], rneg[0:L, :], nbk_c, ALU.mult)
        qd = tQL()
        S.tt(qd[:, 0:L], qc, egr[:, 0:L], ALU.mult, eng="pool")
        po = S.ps()
        S.mm(po[0:L, 0:128], qd[:, 0:L], Sh[:, :], start=True, stop=False, r=False)
        S.mm(po[0:L, 0:128], qkd[0:L, 0:L], dl[0:L, :], start=False, stop=True, r=False)
        pS = S.ps()
        S.mm(pS[:, 0:128], ktok[0:L, :], dk[0:L, :], r=False)
        S.stt(Sh[:, :], Sh[:, :], egr[:, L - 1:L], pS[:, 0:128], ALU.mult, ALU.add)
        junk = tQ(); c2 = tC()
        S.act(junk[0:L, :], po[0:L, 0:128], AF.Square, accum=c2[0:L, 0:1])
        S.ts(c2[0:L, 1:2], c2[0:L, 0:1], 1.0 / 128, ALU.mult, 1e-6, ALU.add)
        S.act(c2[0:L, 1:2], c2[0:L, 1:2], AF.Sqrt)
        S.recip(c2[0:L, 2:3], c2[0:L, 1:2])
        on_ = tQL()
        S.ts(on_[0:L, :], po[0:L, 0:128], c2[0:L, 2:3], ALU.mult)
        pT2 = S.ps()
        S.tr(pT2[:, 0:L], on_[0:L, :], ident[0:L, 0:L])
        S.stt(outn.sub(h, (slice(None), h, slice(c0, c0 + L))), psrc(pT2[:, 0:L], L, 128), p_["a_norm_w"][:, 0:1], sz[:, c0:c0 + L], ALU.mult, ALU.mult)

    def mlstm_chunk(l, h, c0, L, qT_, kT_, vT_, so_, Ch, ms, lf, bb, ib):
        p_ = P[l]
        b_c, ib_c, lf_c = bb[0:L, h:h + 1], ib[0:L, h:h + 1], lf[0:L, h:h + 1]
        qc, kc = qT_[:, c0:c0 + L], kT_[:, c0:c0 + L]
        ibb = tQ(); lfb = tQ()
        S.ts(ibb[0:L, :], ones[0:L, :], ib_c, ALU.mult)
        if CFG.get('dstop', 999) == 1: return
        S.ts(lfb[0:L, :], ones[0:L, :], lf_c, ALU.mult)
        if CFG.get('dstop', 999) == 2: return
        pr = S.ps()
        S.mm(pr[:, 0:L], ibb[0:L, :], ident[0:L, 0:L], r=False)
        if CFG.get('dstop', 999) == 3: return
        S.mm(pr[:, 128:128 + L], lfb[0:L, :], triu[0:L, 0:L], r=False)
        if CFG.get('dstop', 999) == 4: return
        ibr = tQ()
        S.copy(ibr[:, 0:L], pr[:, 0:L], eng="act")
        if CFG.get('dstop', 999) == 5: return
        cb = tC()
        S.act(cb[:, 0:1], pr[:, 128 + L - 1:128 + L], AF.Identity)
        if CFG.get('dstop', 999) == 6: return
        S.rmax(cb[:, 1:2], ibr[:, 0:L])
        if CFG.get('dstop', 999) == 7: return
        S.tt(cb[:, 1:2], cb[:, 1:2], cb[:, 0:1], ALU.add)
        if CFG.get('dstop', 999) == 8: return
        S.tt(cb[:, 2:3], cb[:, 0:1], ms[:, h:h + 1], ALU.add)
        if CFG.get('dstop', 999) == 9: return
        S.tt(cb[:, 3:4], cb[:, 2:3], cb[:, 1:2], ALU.max)
        if CFG.get('dstop', 999) == 10: return
        S.tt(cb[:, 5:6], cb[:, 2:3], cb[:, 3:4], ALU.subtract)
        if CFG.get('dstop', 999) == 11: return
        S.act(cb[:, 4:5], cb[:, 5:6], AF.Exp)
        if CFG.get('dstop', 999) == 12: return
        S.tt(cb[:, 5:6], cb[:, 0:1], cb[:, 3:4], ALU.subtract)
        if CFG.get('dstop', 999) == 13: return
        cc = tC()
        S.tt(cc[0:L, 0:1], b_c, ms[0:L, h:h + 1], ALU.add)
        if CFG.get('dstop', 999) == 14: return
        ld = tQ()
        S.stt(ld[0:L, 0:L], ibr[0:L, 0:L], b_c, maskL[0:L, 0:L], ALU.add, ALU.add)
        if CFG.get('dstop', 999) == 15: return
        S.rmax(cc[0:L, 1:2], ld[0:L, 0:L])
        if CFG.get('dstop', 999) == 16: return
        S.tt(cc[0:L, 2:3], cc[0:L, 0:1], cc[0:L, 1:2], ALU.max)
        if CFG.get('dstop', 999) == 17: return
        S.ts(cc[0:L, 3:4], cc[0:L, 2:3], -1.0, ALU.mult)
        if CFG.get('dstop', 999) == 18: return
        S.act(cc[0:L, 4:5], cc[0:L, 0:1], AF.Exp, bias=cc[0:L, 3:4])
        if CFG.get('dstop', 999) == 19: return
        S.act(cc[0:L, 5:6], cc[0:L, 3:4], AF.Exp)
        if CFG.get('dstop', 999) == 20: return
        S.act(cc[0:L, 6:7], ib_c, AF.Exp, bias=cb[0:L, 5:6])
        if CFG.get('dstop', 999) == 21: return
        ed = tQ()
        S.act(ed[0:L, 0:L], ld[0:L, 0:L], AF.Exp, bias=cc[0:L, 3:4])
        if CFG.get('dstop', 999) == 22: return
        pqk = S.ps()
        S.mm(pqk[0:L, 0:L], qc, kc, r=False)
        if CFG.get('dstop', 999) == 23: return
        dm = tQ()
        S.tt(dm[0:L, 0:L], ed[0:L, 0:L], psrc(pqk[0:L, 0:L], L, L), ALU.mult)
        if CFG.get('dstop', 999) == 24: return
        dmT = tQ()
        transpose_to(dmT[0:L, 0:L], dm[0:L, 0:L], L, L, eng="act")
        if CFG.get('dstop', 999) == 25: return
        va = t258n()
        for e in range(2):
            transpose_to(va[0:L, e * 128:(e + 1) * 128], vT_[e][:, c0:c0 + L], 128, L, eng="act")
        S.memset(va[0:L, 256:257], 1.0)
        if CFG.get('dstop', 999) == 26: return
        S.memset(va[0:L, 257:258], 0.0)
        if CFG.get('dstop', 999) == 27: return
        ktok = tQ()
        transpose_to(ktok[0:L, :], kc, 128, L)
        if CFG.get('dstop', 999) == 28: return
        p1 = S.ps()
        S.mm(p1[0:L, 0:258], qc, Ch[:, :], r=False)
        if CFG.get('dstop', 999) == 29: return
        t1 = t258n()
        S.ts(t1[0:L, :], p1[0:L, 0:258], cc[0:L, 4:5], ALU.mult)
        if CFG.get('dstop', 999) == 30: return
        p2 = S.ps()
        S.mm(p2[0:L, 0:258], dmT[0:L, 0:L], va[0:L, :], r=False)
        if CFG.get('dstop', 999) == 31: return
        S.tt(t1[0:L, :], t1[0:L, :], p2[0:L, 0:258], ALU.add)
        if CFG.get('dstop', 999) == 32: return
        c3 = tC()
        S.act(c3[0:L, 5:6], t1[0:L, 256:257], AF.Abs)
        if CFG.get('dstop', 999) == 33: return
        S.tt(c3[0:L, 0:1], c3[0:L, 5:6], cc[0:L, 5:6], ALU.max)
        if CFG.get('dstop', 999) == 34: return
        S.recip(c3[0:L, 1:2], c3[0:L, 0:1])
        if CFG.get('dstop', 999) == 35: return
        hh = t258n()
        S.ts(hh[0:L, 0:256], t1[0:L, 0:256], c3[0:L, 1:2], ALU.mult)
        if CFG.get('dstop', 999) == 36: return
        S.act(t1[0:L, 0:256], hh[0:L, 0:256], AF.Square, accum=c3[0:L, 2:3])
        if CFG.get('dstop', 999) == 37: return
        S.ts(c3[0:L, 3:4], c3[0:L, 2:3], 1.0 / 256, ALU.mult, 1e-6, ALU.add)
        if CFG.get('dstop', 999) == 38: return
        S.act(c3[0:L, 3:4], c3[0:L, 3:4], AF.Sqrt)
        if CFG.get('dstop', 999) == 39: return
        S.recip(c3[0:L, 4:5], c3[0:L, 3:4])
        if CFG.get('dstop', 999) == 40: return
        S.ts(hh[0:L, 0:256], hh[0:L, 0:256], c3[0:L, 4:5], ALU.mult)
        if CFG.get('dstop', 999) == 41: return
        for e in range(2):
            pT = S.ps()
            S.tr(pT[:, 0:L], hh[0:L, e * 128:(e + 1) * 128], ident[0:L, 0:L])
            S.stt(outn.sub(2 * h + e, (slice(None), 2 * h + e, slice(c0, c0 + L))), psrc(pT[:, 0:L], L, 128), p_["d_norm_w"][:, e:e + 1], so_[e][:, c0:c0 + L], ALU.mult, ALU.mult)
        S.ts(va[0:L, :], va[0:L, :], cc[0:L, 6:7], ALU.mult)
        if CFG.get('dstop', 999) == 42: return
        pC = S.ps()
        S.mm(pC[:, 0:258], ktok[0:L, :], va[0:L, :], r=False)
        if CFG.get('dstop', 999) == 43: return
        S.stt(Ch[:, :], Ch[:, :], cb[:, 4:5], pC[:, 0:258], ALU.mult, ALU.add)
        if CFG.get('dstop', 999) == 44: return
        S.copy(ms[:, h:h + 1], cb[:, 3:4])
        if CFG.get('dstop', 999) == 45: return


    memT = mixed
    for t in range(2):
        S.dma("sp", tokbuf.t[:, :], dap(I["memp"], t * 128 * D, [[D, 128], [1, D]]), writes=[tokbuf[:]])
        for k in range(16):
            transpose_to(memT.sub(k, (slice(None), k, slice(t * 128, (t + 1) * 128))), tokbuf[:, k * 128:(k + 1) * 128], 128, 128)
    for l in range(2 if CFG.get('kv', True) else 0):
        Ktok = Buf(ubuf.t[:, 0:4, :].rearrange("p (m x) b -> p m (x b)", m=2), "Ktok")
        for which, wname, oname in ((0, "xk_w", "mkp"), (1, "xv_w", "mvp")):
            for h in range(4):
                w = load_w(I[wname], l * D * 512, 512, 0, h * 128, 128, 16)
                for mc in range(2):
                    p = S.ps()
                    for k in range(16):
                        S.mm(p[:, 0:128], memT.sub(k, (slice(None), k, slice(mc * 128, (mc + 1) * 128))), w[:, k, 0:128], start=(k == 0), stop=(k == 15), r=False)
                    if which == 0:
                        S.copy(Ktok.sub(mc, (slice(None), mc, slice(h * 128, (h + 1) * 128))), p[:, 0:128])
                        transpose_to(KT[l][:, h, mc * 128:(mc + 1) * 128], Ktok.sub(mc, (slice(None), mc, slice(h * 128, (h + 1) * 128))), 128, 128)
                    else:
                        S.copy(Vt[l][:, mc, h * 128:(h + 1) * 128], p[:, 0:128])
            src = Ktok if which == 0 else Vt[l]
            rd = [Ktok.sub(0, (slice(None), 0, slice(None))), Ktok.sub(1, (slice(None), 1, slice(None)))] if which == 0 else [Vt[l][:]]
            S.dma("pool", dap(O[oname], l * 256 * 512, [[512, 128], [128 * 512, 2], [1, 512]]), src.t[:, 0:2, 0:512], reads=rd)

    S.barrier()
    NBLK = CFG.get('nblk', 2048 // TBP)
    NLAY = CFG.get('layers', 2)
    chunks_p = [(c * 128, 128) for c in range(TBP // 128)]
    for blk in range(NBLK):
        for t in range(TBP // 128):
            S.dma("sp", tokbuf.t[:, :], dap(I["xp"], (blk * TBP + t * 128) * D, [[D, 128], [1, D]]), writes=[tokbuf[:]])
            for k in range(16):
                transpose_to(ch(xT, k, slice(t * 128, (t + 1) * 128)), tokbuf[:, k * 128:(k + 1) * 128], 128, 128, eng=("act" if k % 2 else "dve"))
                S.copy(ch(xTb, k, slice(t * 128, (t + 1) * 128)), ch(xT, k, slice(t * 128, (t + 1) * 128)), eng="pool")
        for l in range(NLAY):
            layer_block(l, TBP, "p", chunks_p, (blk == NBLK - 1) and CFG.get("stateout", True))
        for t in range(TBP // 128):
            for k in range(16):
                transpose_to(tokbuf[:, k * 128:(k + 1) * 128], ch(xT, k, slice(t * 128, (t + 1) * 128)), 128, 128, eng=("act" if k % 2 else "dve"))
            S.dma("pool", dap(O["yp"], (blk * TBP + t * 128) * D, [[D, 128], [1, D]]), tokbuf.t[:, :], reads=[tokbuf[:]])
    for l in range(2):
        emit_rows(lambda j: histA[l][:, j, :], 24, 3, lambda c0, n: dap(O["dcp"], l * 3 * 3072 + c0, [[3072, 3], [1, n]]))
        emit_rows(lambda j: histB[l][:, j, :], 8, 30, lambda c0, n: dap(O["gcp"], l * 30 * 1024 + c0, [[1024, 30], [1, n]]))
        emit_rows(lambda j: histC[l][:, j, :], 8, 2, lambda c0, n: dap(O["scp"], l * 2 * 1024 + c0, [[1024, 2], [1, n]]))

    if CFG.get('sample', True):
        S.dma("sp", tokbuf.t[0:NS, :], dap(I["xs"], 0, [[D, NS], [1, D]]), writes=[tokbuf[:]])
        for k in range(16):
            transpose_to(ch(xT, k, slice(0, NS)), tokbuf[0:NS, k * 128:(k + 1) * 128], NS, 128)
            S.copy(ch(xTb, k, slice(0, NS)), ch(xT, k, slice(0, NS)), eng="pool")
        S.barrier()
        for i in range(3):
            S.memset(CnS[i][:], 0.0)
        chunks_s = [(s, 1) for s in range(NS)]
        for l in range(NLAY):
            for (src, H, C, hs, dst) in ((I["sdc"], 3, 3072, hsA, O["dcs"]), (I["sgc"], 30, 1024, hsB, O["gcs"]), (I["ssc"], 2, 1024, hsC, O["scs"])):
                S.dma("pool", dap(dst, l * NS * H * C, [[H * C, NS], [C, H - 1], [1, C]]), dap(src, l * NS * H * C + C, [[H * C, NS], [C, H - 1], [1, C]]))
                spt = max(1, min(NS, 128 // H))
                for s0 in range(0, NS, spt):
                    ns_ = min(spt, NS - s0)
                    R = ns_ * H
                    for cc0 in range(0, C, D):
                        ncol = min(D, C - cc0)
                        S.dma("sp", tokbuf.t[0:R, 0:ncol], dap(src, (l * NS + s0) * H * C + cc0, [[C, R], [1, ncol]]), writes=[tokbuf[:]])
                        for jj in range(ncol // 128):
                            j = cc0 // 128 + jj
                            p = S.ps()
                            S.tr(p[:, 0:R], tokbuf[0:R, jj * 128:(jj + 1) * 128], ident[0:R, 0:R])
                            S.copy(hs.sub(j, (slice(None), j, slice(s0, s0 + ns_), slice(None))), p[:, 0:R].ap.rearrange("p (s h) -> p s h", h=H) if False else View(p.key, p.t[:, 0:R].rearrange("p (s h) -> p s h", h=H)))
            layer_block(l, NS, "s", chunks_s, False, hsA, hsB, hsC)
        for k in range(16):
            transpose_to(tokbuf[0:NS, k * 128:(k + 1) * 128], ch(xT, k, slice(0, NS)), 128, NS)
        S.dma("pool", dap(O["ys"], 0, [[D, NS], [1, D]]), tokbuf.t[0:NS, :], reads=[tokbuf[:]])
    S.finish()
    return nc


_NC = [None]


def kernel(**inp):
    a = {k: np.ascontiguousarray(np.asarray(v, dtype=np.float32)) for k, v in inp.items()}
    if _NC[0] is None:
        _NC[0] = build()
    nc = _NC[0]
    wnames = ["w_in", "b_in", "a_conv_w", "a_A_log", "a_dt_bias", "a_norm_w", "b_conv_w", "b_conv_b", "b_ln_g", "b_ln_b",
              "c_conv_w", "d_norm_w", "w_branch", "w_out", "ln1_g", "ln1_b", "xq_w", "xk_w", "xv_w", "xo_w", "ln2_g", "ln2_b",
              "ffn_w1", "ffn_b1", "ffn_w2", "ffn_b2", "ln3_g", "ln3_b"]
    in_maps = []
    for c in range(8):
        b = c % 4
        sl = slice(c * NS, (c + 1) * NS)
        m = {"xp": a["x_prompt"][b], "xs": a["x_sample"][sl, 0, :], "memp": a["mem_prompt"][b],
             "cmk": a["cache_mem_k"][:, sl].reshape(2, NS, 256, 512), "cmv": a["cache_mem_v"][:, sl].reshape(2, NS, 256, 512),
             "sdc": a["state_delta_conv"][:, sl], "sdS": a["state_delta_S"][:, sl], "sgc": a["state_glu_conv"][:, sl],
             "ssc": a["state_short_conv"][:, sl], "smC": a["state_mlstm_C"][:, sl], "smn": a["state_mlstm_n"][:, sl],
             "smm": a["state_mlstm_m"][:, sl]}
        m = {k: np.ascontiguousarray(v) for k, v in m.items()}
        for w in wnames:
            m[w] = a[w]
        in_maps.append(m)
    res = run_bass_kernel_spmd(nc, in_maps, core_ids=list(range(8)))
    R = res.results

    def pst(name, shape):
        return np.stack([np.asarray(R[b][name]) for b in range(4)], axis=1).reshape(shape)

    def sst(name, shape):
        return np.concatenate([np.asarray(R[c][name]) for c in range(8)], axis=1).reshape(shape)

    y_prompt = np.stack([np.asarray(R[b]["yp"]) for b in range(4)], axis=0)
    y_sample = np.concatenate([np.asarray(R[c]["ys"]) for c in range(8)], axis=0).reshape(128, 1, D)
    outs = (y_prompt, y_sample,
            pst("mkp", (2, 4, 256, 4, 128)), pst("mvp", (2, 4, 256, 4, 128)),
            pst("dcp", (2, 4, 3, 3072)), pst("dSp", (2, 4, 8, 128, 128)), pst("gcp", (2, 4, 30, 1024)),
            pst("scp", (2, 4, 2, 1024)), pst("mCp", (2, 4, 4, 128, 256)), pst("mnp", (2, 4, 4, 128)), pst("mmp", (2, 4, 4)),
            sst("dcs", (2, 128, 3, 3072)), sst("dSs", (2, 128, 8, 128, 128)), sst("gcs", (2, 128, 30, 1024)),
            sst("scs", (2, 128, 2, 1024)), sst("mCs", (2, 128, 4, 128, 256)), sst("mns", (2, 128, 4, 128)), sst("mms", (2, 128, 4)))
    return tuple(np.ascontiguousarray(o.astype(np.float32)) for o in outs)
```

```python
import contextlib
import numpy as np
import concourse.bass as bass
import concourse.mybir as mybir
from concourse.bass_utils import run_bass_kernel_spmd

F32 = mybir.dt.float32
F32R = mybir.dt.float32r
BF16 = mybir.dt.bfloat16
AF = mybir.ActivationFunctionType
ALU = mybir.AluOpType
AX = mybir.AxisListType

EPOCH = 20000
NDS = 8


class View:
    __slots__ = ("key", "ap")

    def __init__(self, key, ap):
        self.key = key
        self.ap = ap


class Buf:
    def __init__(self, t, key):
        self.t = t
        self.key = key

    def __getitem__(self, idx):
        return View(self.key, self.t[idx])

    def sub(self, subkey, idx):
        return View((self.key, subkey), self.t[idx])


class Sched:
    ENG = ["pe", "act", "dve", "pool", "sp"]

    def __init__(self, nc):
        self.nc = nc
        self.stack = contextlib.ExitStack()
        self.prog = {e: [] for e in self.ENG}
        self.count = {e: 0 for e in self.ENG}
        self.waited = {e: {} for e in self.ENG}
        self.last_w = {}
        self.readers = {}
        self.dslot = {e: 0 for e in self.ENG}
        self.dval = {}
        self.sids = {}
        self.nbuf = 0
        self.psum_banks = []
        self.psum_i = 0
        self.nops = 0

    def sb(self, shape, dtype=F32, name=None):
        self.nbuf += 1
        name = name or f"sb{self.nbuf}"
        t = self.stack.enter_context(self.nc.sbuf_tensor(name, list(shape), dtype))
        return Buf(t, name)

    def init_psum(self, n=8):
        for i in range(n):
            t = self.stack.enter_context(self.nc.psum_tensor(f"ps{i}", [128, 512], F32))
            self.psum_banks.append(Buf(t, f"ps{i}"))

    def ps(self):
        b = self.psum_banks[self.psum_i % len(self.psum_banks)]
        self.psum_i += 1
        return b

    def _deps(self, eng, reads, writes):
        deps = set()
        for v in reads:
            t = self.last_w.get(v.key)
            if t is not None:
                deps.add(t)
        for v in writes:
            t = self.last_w.get(v.key)
            if t is not None:
                deps.add(t)
            for r in self.readers.get(v.key, ()):
                if r[2] != eng or r[2] == "dma":
                    deps.add(r)
        for (sid, val, deng) in sorted(deps, key=lambda d: str(d)):
            if deng == eng and eng == "pe":
                continue
            if self.waited[eng].get(sid, 0) >= val:
                continue
            self.waited[eng][sid] = val
            self.prog[eng].append(("wait", sid, val))

    def _commit(self, tok, reads, writes):
        for v in reads:
            self.readers.setdefault(v.key, []).append(tok)
        for v in writes:
            self.last_w[v.key] = tok
            self.readers[v.key] = []

    def op(self, eng, fn, reads=(), writes=()):
        self._deps(eng, reads, writes)
        n = self.count[eng]
        self.count[eng] += 1
        sid = ("c", eng, n // EPOCH)
        val = n % EPOCH + 1
        self.sids[sid] = 1
        tok = (sid, val, eng)
        self.prog[eng].append(("op", fn, sid, 1))
        self._commit(tok, reads, writes)
        self.nops += 1
        return tok

    def dma(self, q, out_ap, in_ap, reads=(), writes=(), **kw):
        eng = q
        self._deps(eng, reads, writes)
        slot = self.dslot[q]
        self.dslot[q] = (slot + 1) % NDS
        sid = ("d", q, slot)
        self.sids[sid] = 1
        prev = self.dval.get(sid, 0)
        if prev > 0 and self.waited[eng].get(sid, 0) < prev:
            self.waited[eng][sid] = prev
            self.prog[eng].append(("wait", sid, prev))
        val = prev + 16
        self.dval[sid] = val
        tok = (sid, val, "dma")
        self.prog[eng].append(("op", lambda e: e.dma_start(out_ap, in_ap, **kw), sid, 16))
        self._commit(tok, reads, writes)
        self.nops += 1
        return tok

    def barrier(self):
        toks = []
        for e in self.ENG:
            n = self.count[e]
            if n > 0:
                toks.append((("c", e, (n - 1) // EPOCH), (n - 1) % EPOCH + 1, e))
        for sid, val in self.dval.items():
            toks.append((sid, val, "dma"))
        for e in self.ENG:
            for (sid, val, deng) in toks:
                if deng == e:
                    continue
                if self.waited[e].get(sid, 0) >= val:
                    continue
                self.waited[e][sid] = val
                self.prog[e].append(("wait", sid, val))

    def finish(self):
        for sid, val in self.dval.items():
            q = sid[1]
            if self.waited[q].get(sid, 0) < val:
                self.prog[q].append(("wait", sid, val))
        nc = self.nc
        sems = {}
        for i, sid in enumerate(self.sids):
            sems[sid] = self.stack.enter_context(nc.semaphore(f"s{i}"))
        prog = self.prog

        def replay(name, e):
            for it in prog[name]:
                if it[0] == "wait":
                    e.wait_ge(sems[it[1]], it[2])
                else:
                    it[1](e).then_inc(sems[it[2]], it[3])

        with nc.Block() as block:
            @block.tensor
            def _(e):
                replay("pe", e)

            @block.scalar
            def _(e):
                replay("act", e)

            @block.vector
            def _(e):
                replay("dve", e)

            @block.gpsimd
            def _(e):
                replay("pool", e)

            @block.sync
            def _(e):
                replay("sp", e)
        self.stack.close()

    def mm(self, out, lhsT, rhs, start=True, stop=True, r=False):
        la, ra = lhsT.ap, rhs.ap
        if r:
            la, ra = la.bitcast(F32R), ra.bitcast(F32R)
        rd = [lhsT, rhs] + ([] if start else [out])
        return self.op("pe", lambda e: e.matmul(out.ap, la, ra, start=start, stop=stop), rd, [out])

    def tr(self, out, in_, ident):
        return self.op("pe", lambda e: e.transpose(out.ap, in_.ap, ident.ap), [in_, ident], [out])

    def act(self, out, in_, func, bias=None, scale=None, accum=None, eng="act"):
        kw = {}
        rd = [in_]
        wr = [out]
        if bias is not None:
            if isinstance(bias, View):
                kw["bias"] = bias.ap
                rd.append(bias)
            else:
                kw["bias"] = bias
        if scale is not None:
            if isinstance(scale, View):
                kw["scale"] = scale.ap
                rd.append(scale)
            else:
                kw["scale"] = scale
        if accum is not None:
            kw["accum_out"] = accum.ap
            wr.append(accum)
        return self.op("act", lambda e: e.activation(out.ap, in_.ap, func, **kw), rd, wr)

    def tt(self, out, a, b, op, eng="dve"):
        return self.op(eng, lambda e: e.tensor_tensor(out.ap, a.ap, b.ap, op), [a, b], [out])

    def ts(self, out, a, s1, op0, s2=None, op1=None, accum=None, eng="dve"):
        rd = [a]
        wr = [out]
        s1a = s1.ap if isinstance(s1, View) else s1
        s2a = s2.ap if isinstance(s2, View) else s2
        if isinstance(s1, View):
            rd.append(s1)
        if isinstance(s2, View):
            rd.append(s2)
        kw = {}
        if op1 is not None:
            kw["op1"] = op1
        if accum is not None:
            kw["accum_out"] = accum.ap
            wr.append(accum)
        return self.op(eng, lambda e: e.tensor_scalar(out.ap, a.ap, s1a, s2a, op0, **kw), rd, wr)

    def stt(self, out, a, s, b, op0, op1, accum=None):
        rd = [a, b]
        wr = [out]
        sa = s.ap if isinstance(s, View) else s
        if isinstance(s, View):
            rd.append(s)
        kw = {}
        if accum is not None:
            kw["accum_out"] = accum.ap
            wr.append(accum)
        return self.op("dve", lambda e: e.scalar_tensor_tensor(out.ap, a.ap, sa, b.ap, op0, op1, **kw), rd, wr)

    def copy(self, out, in_, eng="dve"):
        if eng == "act":
            return self.op("act", lambda e: e.copy(out.ap, in_.ap), [in_], [out])
        return self.op(eng, lambda e: e.tensor_copy(out.ap, in_.ap), [in_], [out])

    def memset(self, out, val, eng="pool"):
        return self.op(eng, lambda e: e.memset(out.ap, val), [], [out])

    def recip(self, out, in_):
        return self.op("dve", lambda e: e.reciprocal(out.ap, in_.ap), [in_], [out])

    def rmax(self, out, in_, eng="dve"):
        return self.op(eng, lambda e: e.reduce_max(out.ap, in_.ap, AX.X), [in_], [out])

D = 2048
NIN = 20504
O_QA, O_KA, O_VA, O_ZA, O_BD = 0, 1024, 2048, 3072, 4096
O_GA, O_GG, O_BG, O_CG, O_HC = 4112, 5136, 6160, 7184, 8208
O_QD, O_KD, O_VD, O_OD, O_IF, O_GATE = 9232, 9744, 10256, 11280, 12304, 12312
ALPHA = 4 ** 0.25
NEG = -1.0e30
NS = 16
TBP = 256
CFG = {}


def dap(t, off, dims):
    return bass.AP(t, off, [list(d) for d in dims])


def build():
    nc = bass.Bass("TRN2", target_bir_lowering=False)

    def din(name, shape):
        return nc.dram_tensor(name, list(shape), F32, kind="ExternalInput")

    def dout(name, shape):
        return nc.dram_tensor(name, list(shape), F32, kind="ExternalOutput")

    I = {}
    for name, shape in [
        ("xp", (2048, D)), ("xs", (NS, D)), ("memp", (256, D)),
        ("cmk", (2, NS, 256, 512)), ("cmv", (2, NS, 256, 512)),
        ("sdc", (2, NS, 3, 3072)), ("sdS", (2, NS, 8, 128, 128)), ("sgc", (2, NS, 30, 1024)),
        ("ssc", (2, NS, 2, 1024)), ("smC", (2, NS, 4, 128, 256)), ("smn", (2, NS, 4, 128)), ("smm", (2, NS, 4)),
        ("w_in", (2, D, NIN)), ("b_in", (2, NIN)), ("a_conv_w", (2, 4, 3072)), ("a_A_log", (2, 8)),
        ("a_dt_bias", (2, 8)), ("a_norm_w", (2, 128)), ("b_conv_w", (2, 31, 1024)), ("b_conv_b", (2, 1024)),
        ("b_ln_g", (2, 1024)), ("b_ln_b", (2, 1024)), ("c_conv_w", (2, 3, 1024)), ("d_norm_w", (2, 256)),
        ("w_branch", (2, 4, 1024, D)), ("w_out", (2, D, D)), ("ln1_g", (2, D)), ("ln1_b", (2, D)),
        ("xq_w", (2, D, 512)), ("xk_w", (2, D, 512)), ("xv_w", (2, D, 512)), ("xo_w", (2, 512, D)),
        ("ln2_g", (2, D)), ("ln2_b", (2, D)), ("ffn_w1", (2, D, 4 * D)), ("ffn_b1", (2, 4 * D)),
        ("ffn_w2", (2, 4 * D, D)), ("ffn_b2", (2, D)), ("ln3_g", (2, D)), ("ln3_b", (2, D)),
    ]:
        I[name] = din(name, shape)
    O = {}
    for name, shape in [
        ("yp", (2048, D)), ("ys", (NS, D)), ("mkp", (2, 256, 512)), ("mvp", (2, 256, 512)),
        ("dcp", (2, 3, 3072)), ("dSp", (2, 8, 128, 128)), ("gcp", (2, 30, 1024)), ("scp", (2, 2, 1024)),
        ("mCp", (2, 4, 128, 256)), ("mnp", (2, 4, 128)), ("mmp", (2, 4)),
        ("dcs", (2, NS, 3, 3072)), ("dSs", (2, NS, 8, 128, 128)), ("gcs", (2, NS, 30, 1024)),
        ("scs", (2, NS, 2, 1024)), ("mCs", (2, NS, 4, 128, 256)), ("mns", (2, NS, 4, 128)), ("mms", (2, NS, 4)),
    ]:
        O[name] = dout(name, shape)

    DBG = {}
    if CFG.get('dump'):
        for nm in ['C', 'B', 'A', 'D']:
            DBG[nm] = dout('dbg_' + nm, (128, 8, TBP))
        for nm in ['mixed', 'x1', 'x2', 'x3']:
            DBG[nm] = dout('dbg_' + nm, (128, 16, TBP))
    dumped = set()
    S = Sched(nc)
    S.init_psum()
    sb = S.sb

    def dump(nm, buf, nch):
        if not CFG.get('dump') or nm in dumped:
            return
        dumped.add(nm)
        S.dma("pool", dap(DBG[nm], 0, [[nch * TBP, 128], [TBP, nch], [1, TBP]]), buf.t[:, 0:nch, :],
              reads=[buf.sub(k, (slice(None), k, slice(None))) for k in range(nch)])

    ones = sb([128, 128], name="ones")
    ident = sb([128, 128], name="ident")
    triu = sb([128, 128], name="triu")
    maskS = sb([128, 128], name="maskS")
    maskL = sb([128, 128], name="maskL")
    zeros = sb([128, 128], name="zeros")
    S.memset(ones[:], 1.0)
    S.memset(zeros[:], 0.0)
    S.op("pool", lambda e: e.affine_select(ident.t[:], ones.t[:], [[-1, 128]], ALU.is_equal, 0.0, base=0, channel_multiplier=1), [ones[:]], [ident[:]])
    S.op("pool", lambda e: e.affine_select(triu.t[:], ones.t[:], [[1, 128]], ALU.is_ge, 0.0, base=0, channel_multiplier=-1), [ones[:]], [triu[:]])
    S.op("pool", lambda e: e.affine_select(maskS.t[:], zeros.t[:], [[1, 128]], ALU.is_gt, NEG, base=0, channel_multiplier=-1), [zeros[:]], [maskS[:]])
    S.op("pool", lambda e: e.affine_select(maskL.t[:], zeros.t[:], [[-1, 128]], ALU.is_ge, NEG, base=0, channel_multiplier=1), [zeros[:]], [maskL[:]])

    def colload(dst, dcol0, t, off, nchunk, n=128):
        S.dma("pool", dst.t[0:n, dcol0:dcol0 + nchunk], dap(t, off, [[1, n], [128, nchunk]]), writes=[dst[:]],
              allow_slow_non_contiguous=True)

    P = []
    for l in range(2):
        p = {}
        bi = sb([128, 162], name=f"bin{l}")
        colload(bi, 0, I["b_in"], l * NIN + 0, 32)
        colload(bi, 32, I["b_in"], l * NIN + O_BD, 1, n=16)
        colload(bi, 33, I["b_in"], l * NIN + O_GA, 64)
        colload(bi, 97, I["b_in"], l * NIN + O_IF, 1, n=8)
        colload(bi, 98, I["b_in"], l * NIN + O_GATE, 64)
        p["bin"] = bi
        acw = sb([128, 24, 4], name=f"acw{l}")
        for k in range(4):
            S.dma("pool", acw.t[:, :, k], dap(I["a_conv_w"], l * 4 * 3072 + k * 3072, [[1, 128], [128, 24]]), writes=[acw[:]], allow_slow_non_contiguous=True)
        bcw = sb([128, 8, 31], name=f"bcw{l}")
        for k in range(31):
            S.dma("pool", bcw.t[:, :, k], dap(I["b_conv_w"], l * 31 * 1024 + k * 1024, [[1, 128], [128, 8]]), writes=[bcw[:]], allow_slow_non_contiguous=True)
        ccw = sb([128, 8, 3], name=f"ccw{l}")
        for k in range(3):
            S.dma("pool", ccw.t[:, :, k], dap(I["c_conv_w"], l * 3 * 1024 + k * 1024, [[1, 128], [128, 8]]), writes=[ccw[:]], allow_slow_non_contiguous=True)
        p["acw"], p["bcw"], p["ccw"] = acw, bcw, ccw
        for nm, nch in [("b_conv_b", 8), ("b_ln_g", 8), ("b_ln_b", 8), ("a_norm_w", 1), ("d_norm_w", 2), ("ln1_g", 16), ("ln1_b", 16),
                        ("ln2_g", 16), ("ln2_b", 16), ("ln3_g", 16), ("ln3_b", 16), ("ffn_b1", 64), ("ffn_b2", 16)]:
            tl = sb([128, nch], name=f"{nm}{l}")
            colload(tl, 0, I[nm], l * nch * 128, nch)
            p[nm] = tl
        negA = sb([128, 8], name=f"negA{l}")
        dtb = sb([128, 8], name=f"dtb{l}")
        S.dma("pool", negA.t[:, :], dap(I["a_A_log"], l * 8, [[0, 128], [1, 8]]), writes=[negA[:]])
        S.dma("pool", dtb.t[:, :], dap(I["a_dt_bias"], l * 8, [[0, 128], [1, 8]]), writes=[dtb[:]])
        S.act(negA[:], negA[:], AF.Exp)
        S.ts(negA[:], negA[:], -1.0, ALU.mult)
        p["negA"], p["dtb"] = negA, dtb
        P.append(p)

    def bcol(l, col0, n=128):
        if col0 < O_BD:
            c = col0 // 128
        elif col0 == O_BD:
            c = 32
        elif col0 < O_IF:
            c = 33 + (col0 - O_GA) // 128
        elif col0 == O_IF:
            c = 97
        else:
            c = 98 + (col0 - O_GATE) // 128
        return P[l]["bin"][0:n, c:c + 1]

    xT = sb([128, 16, TBP], name="xT")
    mixed = sb([128, 16, TBP], name="mixed")
    outn = sb([128, 8, TBP], BF16, name="outn")
    xTb = sb([128, 16, TBP], BF16, name="xTb")
    mixedb = sb([128, 16, TBP], BF16, name="mixedb")
    hb = sb([128, 8, TBP], BF16, name="hb")
    ubuf = sb([128, 8, TBP], name="ubuf")
    tokbuf = sb([128, D], name="tokbuf")
    wsl = [sb([128, 2048], name=f"w{i}") for i in range(3)]
    wbf = [sb([128, 2048], BF16, name=f"wb{i}") for i in range(3)]
    wi = [0]
    NT = 10
    tmpA = [sb([128, TBP], name=f"tA{i}") for i in range(NT)]
    ti = [0]
    NQ = 12
    tmpQ = [sb([128, 128], name=f"tQ{i}") for i in range(NQ)]
    qi = [0]
    NC_ = 48
    tmpC = [sb([128, 8], name=f"tC{i}") for i in range(NC_)]
    ci_ = [0]
    extb = sb([128, TBP + 30], name="extb")
    persA = sb([128, 16, 24], name="persA")
    lnm = sb([128, TBP], name="lnm"); lnr = sb([128, TBP], name="lnr"); lnm2 = sb([128, TBP], name="lnm2")
    tmpQL = [sb([128, 128], name=f"tQL{i}") for i in range(12)]
    qli = [0]
    persD = sb([128, 16, 12], name="persD")
    exts = sb([128, NS, 31], name="exts")
    t258 = [sb([128, 258], name=f"t258_{i}") for i in range(4)]
    t258i = [0]

    def tA():
        ti[0] += 1
        return tmpA[ti[0] % NT]

    def tQ():
        qi[0] += 1
        return tmpQ[qi[0] % NQ]

    def tQL():
        qli[0] += 1
        return tmpQL[qli[0] % 12]

    def psrc(pv, L, shape_rows):
        if L != 1:
            return pv
        t = tQ()
        v = t[0:shape_rows, 0:1]
        S.act(v, pv, AF.Identity)
        return v

    def tC():
        ci_[0] += 1
        return tmpC[ci_[0] % NC_]

    def t258n():
        t258i[0] += 1
        return t258[t258i[0] % 4]

    def ch(buf, k, sl=slice(None)):
        return buf.sub(k, (slice(None), k, sl))

    ARN = 10240
    arena = sb([128, ARN], name="arena")
    apos = {"p": 0, "s": 0}

    def carve(phase, shape, name):
        n = 1
        for s_ in shape[1:]:
            n *= s_
        a0 = apos[phase]
        apos[phase] += n
        assert apos[phase] <= ARN, (phase, apos[phase])
        v = arena.t[:, a0:a0 + n]
        if len(shape) == 3:
            v = v.rearrange("p (a b) -> p a b", a=shape[1])
        elif len(shape) == 4:
            v = v.rearrange("p (a b c) -> p a b c", a=shape[1], b=shape[2])
        return Buf(v, name)

    histA = [carve("p", [128, 24, 3], f"hA{l}") for l in range(2)]
    histB = [carve("p", [128, 8, 30], f"hB{l}") for l in range(2)]
    histC = [carve("p", [128, 8, 2], f"hC{l}") for l in range(2)]
    Sa = [[carve("p", [128, 128], f"Sa{l}_{h}") for h in range(8)] for l in range(2)]
    Cn = [[carve("p", [128, 258], f"Cn{l}_{h}") for h in range(4)] for l in range(2)]
    mst = [carve("p", [128, 4], f"mst{l}") for l in range(2)]
    KT = [carve("p", [128, 4, 256], f"KT{l}") for l in range(2)]
    Vt = [carve("p", [128, 2, 512], f"Vt{l}") for l in range(2)]
    for l in range(2):
        S.memset(histA[l][:], 0.0); S.memset(histB[l][:], 0.0); S.memset(histC[l][:], 0.0)
        S.memset(mst[l][:], 0.0)
        for h in range(8):
            S.memset(Sa[l][h][:], 0.0)
        for h in range(4):
            S.memset(Cn[l][h][:], 0.0)
    SaS = [carve("s", [128, 128], f"SaS{i}") for i in range(3)]
    CnS = [carve("s", [128, 258], f"CnS{i}") for i in range(3)]
    mstS = [carve("s", [128, 4], f"mstS{i}") for i in range(3)]
    ssi = [0]
    KTs = carve("s", [128, 4, 256], "KTs")
    Kts = carve("s", [128, 2, 512], "Kts")
    Vts = carve("s", [128, 2, 512], "Vts")
    hsA = carve("s", [128, 24, NS, 3], "hsA")
    hsB = carve("s", [128, 8, NS, 30], "hsB")
    hsC = carve("s", [128, 8, NS, 2], "hsC")
    tmpB_new = carve("s", [128, 8, NS], "tmpBn")
    tmpA_new = carve("s", [128, 24, NS], "tmpAn")
    qTb = sb([128, 4, TBP], name="qTb")
    oTb = sb([128, 4, TBP], BF16, name="oTb")

    def load_w(t, base, rstride, row0, col0, n, kcn, bf=False, wide=False):
        w = wsl[wi[0] % 3]
        wb = wbf[wi[0] % 3]
        wi[0] += 1
        kk = 4 if wide else 16
        wv = w.t[:, :].rearrange("p (k n) -> p k n", k=kk)
        S.dma("sp", wv[:, 0:kcn, 0:n], dap(t, base + row0 * rstride + col0, [[rstride, 128], [128 * rstride, kcn], [1, n]]),
              writes=[w[:]])
        if not bf:
            return Buf(wv, w.key)
        wbv = wb.t[:, :].rearrange("p (k n) -> p k n", k=kk)
        S.op("pool", lambda e: e.tensor_copy(wbv[:, 0:kcn, 0:n], wv[:, 0:kcn, 0:n]), [w[:]], [wb[:]])
        return Buf(wbv, wb.key)

    def fm_proj(t, base, rstride, row0, col0, n, kcn, rhs_fn, TB):
        w = load_w(t, base, rstride, row0, col0, n, kcn, bf=True)
        p = S.ps()
        for k in range(kcn):
            S.mm(p[0:n, 0:TB], w[:, k, 0:n], rhs_fn(k), start=(k == 0), stop=(k == kcn - 1), r=False)
        return p

    def fm_group(t, base, rstride, row0, col0, ncols, kcn, rhs_fn, TB):
        nch = (ncols + 127) // 128
        pss = [S.ps() for _ in range(nch)]
        for kq in range(0, kcn, 4):
            nk = min(4, kcn - kq)
            w = load_w(t, base, rstride, row0 + kq * 128, col0, ncols, nk, bf=True, wide=True)
            for c in range(nch):
                n = min(128, ncols - c * 128)
                for kk in range(nk):
                    k = kq + kk
                    S.mm(pss[c][0:n, 0:TB], w[:, kk, c * 128:c * 128 + n], rhs_fn(k), start=(k == 0), stop=(k == kcn - 1), r=False)
        return pss

    def win_group(l, col0, ncols, TB):
        return fm_group(I["w_in"], l * D * NIN, NIN, 0, col0, ncols, 16, lambda k: ch(xTb, k, slice(0, TB)), TB)

    def win_proj(l, col0, n, TB):
        return fm_proj(I["w_in"], l * D * NIN, NIN, 0, col0, n, 16, lambda k: ch(xTb, k, slice(0, TB)), TB)

    def transpose_to(dst_view, src_view, rows, cols, eng="dve"):
        p = S.ps()
        S.tr(p[0:cols, 0:rows], src_view, ident[0:rows, 0:rows])
        S.copy(dst_view, p[0:cols, 0:rows], eng=eng)

    def layernorm(l, gname, bname, TB):
        pm = S.ps()
        for k in range(16):
            S.mm(pm[:, 0:TB], ones[:, :], ch(xT, k, slice(0, TB)), start=(k == 0), stop=(k == 15), r=False)
        pq = S.ps()
        for k in range(16):
            sq = tA()
            S.act(sq[:, 0:TB], ch(xT, k, slice(0, TB)), AF.Square)
            S.mm(pq[:, 0:TB], ones[:, :], sq[:, 0:TB], start=(k == 0), stop=(k == 15), r=False)
        mean, rstd, m2 = lnm, lnr, lnm2
        S.ts(mean[:, 0:TB], pm[:, 0:TB], 1.0 / D, ALU.mult)
        S.tt(m2[:, 0:TB], mean[:, 0:TB], mean[:, 0:TB], ALU.mult)
        S.stt(rstd[:, 0:TB], pq[:, 0:TB], 1.0 / D, m2[:, 0:TB], ALU.mult, ALU.subtract)
        S.ts(rstd[:, 0:TB], rstd[:, 0:TB], 0.0, ALU.max, 1e-5, ALU.add)
        S.act(rstd[:, 0:TB], rstd[:, 0:TB], AF.Sqrt)
        S.recip(rstd[:, 0:TB], rstd[:, 0:TB])
        for k in range(16):
            t = tA()
            S.tt(t[:, 0:TB], ch(xT, k, slice(0, TB)), mean[:, 0:TB], ALU.subtract)
            S.tt(t[:, 0:TB], t[:, 0:TB], rstd[:, 0:TB], ALU.mult, eng="pool")
            S.act(ch(xT, k, slice(0, TB)), t[:, 0:TB], AF.Identity, bias=P[l][bname][:, k:k + 1], scale=P[l][gname][:, k:k + 1])
            S.copy(ch(xTb, k, slice(0, TB)), ch(xT, k, slice(0, TB)), eng="pool")

    def emit_rows(srcT_fn, nch, R, dst_ap_fn):
        for j0 in range(0, nch, 16):
            nj = min(16, nch - j0)
            for j in range(j0, j0 + nj):
                transpose_to(tokbuf[0:R, (j - j0) * 128:(j - j0 + 1) * 128], srcT_fn(j), 128, R)
            S.dma("pool", dst_ap_fn(j0 * 128, nj * 128), tokbuf.t[0:R, 0:nj * 128], reads=[tokbuf[:]])

    class Conv:
        def __init__(self, mode, W, TB):
            self.mode, self.W, self.TB, self.H = mode, W, TB, W - 1
            if mode == "p":
                self.newv = extb[:, self.H:self.H + TB]
                self.histv = extb[:, 0:self.H]
                self.tail = extb[:, TB:TB + self.H]
            else:
                self.newv = exts[:, :, self.H]
                self.histv = exts[:, :, 0:self.H]

        def tap(self, k):
            if self.mode == "p":
                return extb[:, k:k + self.TB]
            return exts[:, :, k]

    def conv_apply(cv, wtile, j, out_view, bias_view=None):
        W = cv.W
        acc = out_view
        if bias_view is not None:
            S.ts(acc, cv.tap(0), wtile[:, j, 0:1], ALU.mult, bias_view, ALU.add)
        else:
            S.ts(acc, cv.tap(0), wtile[:, j, 0:1], ALU.mult)
        for k in range(1, W):
            S.stt(acc, cv.tap(k), wtile[:, j, k:k + 1], acc, ALU.mult, ALU.add)

    def layer_block(l, TB, mode, chunks, last, hsA=None, hsB=None, hsC=None):
        p_ = P[l]
        shp = (lambda v: v)
        if mode == "p":
            ov = lambda buf: buf[:, 0:TB]
        else:
            ov = lambda buf: buf[:, 0:TB]
        first_branch = [True]

        def branch_merge(n):
            for dg in range(4):
                pgs = win_group(l, O_GATE + n * D + dg * 512, 512, TB)
                gts = []
                for c in range(4):
                    gt = tA()
                    S.act(gt[:, 0:TB], pgs[c][:, 0:TB], AF.Sigmoid, bias=bcol(l, O_GATE + n * D + (dg * 4 + c) * 128))
                    gts.append(gt)
                pbs = fm_group(I["w_branch"], (l * 4 + n) * 1024 * D, D, 0, dg * 512, 512, 8, lambda k: ch(outn, k, slice(0, TB)), TB)
                for c in range(4):
                    d = dg * 4 + c
                    pb, gt = pbs[c], gts[c]
                    if first_branch[0]:
                        S.tt(ch(mixed, d, slice(0, TB)), pb[:, 0:TB], gt[:, 0:TB], ALU.mult)
                    else:
                        S.tt(gt[:, 0:TB], pb[:, 0:TB], gt[:, 0:TB], ALU.mult)
                        if n == 3:
                            S.tt(ch(mixedb, d, slice(0, TB)), ch(mixed, d, slice(0, TB)), gt[:, 0:TB], ALU.add, eng="pool")
                        else:
                            S.tt(ch(mixed, d, slice(0, TB)), ch(mixed, d, slice(0, TB)), gt[:, 0:TB], ALU.add, eng="pool")
            first_branch[0] = False

        def newrow_out(vals_fn, nch, dst_t, lbase, H, C):
            emit_rows(vals_fn, nch, NS, lambda c0, n: dap(dst_t, lbase + (H - 1) * C + c0, [[H * C, NS], [1, n]]))

        def ph_C():
            cv = Conv(mode, 3, TB)
            newC = ubuf
            for jg in range(2):
                pbgs = win_group(l, O_BG + jg * 512, 512, TB)
                bgs = []
                for c in range(4):
                    t_ = tA()
                    S.act(t_[:, 0:TB], pbgs[c][:, 0:TB], AF.Identity, bias=bcol(l, O_BG + (jg * 4 + c) * 128))
                    bgs.append(t_)
                pcs = win_group(l, O_CG + jg * 512, 512, TB)
                cgs = []
                for c in range(4):
                    t_ = tA()
                    S.act(t_[:, 0:TB], pcs[c][:, 0:TB], AF.Identity, bias=bcol(l, O_CG + (jg * 4 + c) * 128))
                    cgs.append(t_)
                phs = win_group(l, O_HC + jg * 512, 512, TB)
                for c in range(4):
                    j = jg * 4 + c
                    if mode == "p":
                        S.copy(cv.histv, histC[l][:, j, :], eng="pool")
                    else:
                        S.copy(cv.histv, hsC.sub(j, (slice(None), j, slice(None), slice(None))), eng="pool")
                    S.stt(cv.newv, phs[c][:, 0:TB], bcol(l, O_HC + j * 128), cgs[c][:, 0:TB], ALU.add, ALU.mult)
                    conv_apply(cv, p_["ccw"], j, cgs[c][:, 0:TB])
                    if mode == "p":
                        S.copy(histC[l][:, j, :], cv.tail, eng="pool")
                    else:
                        S.copy(ch(newC, j, slice(0, NS)), cv.newv, eng="pool")
                    S.tt(ch(outn, j, slice(0, TB)), bgs[c][:, 0:TB], cgs[c][:, 0:TB], ALU.mult)
            if mode == "s":
                newrow_out(lambda j: ch(newC, j, slice(0, NS)), 8, O["scs"], l * NS * 2 * 1024, 2, 1024)
            branch_merge(2)

        def ph_B():
            cv = Conv(mode, 31, TB)
            newB = mixed
            for jg in range(2):
                pgs = win_group(l, O_GG + jg * 512, 512, TB)
                sgs = []
                for c in range(4):
                    t_ = tA()
                    S.act(t_[:, 0:TB], pgs[c][:, 0:TB], AF.Sigmoid, bias=bcol(l, O_GG + (jg * 4 + c) * 128))
                    sgs.append(t_)
                pas = win_group(l, O_GA + jg * 512, 512, TB)
                for c in range(4):
                    j = jg * 4 + c
                    if mode == "p":
                        S.copy(cv.histv, histB[l][:, j, :], eng="pool")
                    else:
                        S.copy(cv.histv, hsB.sub(j, (slice(None), j, slice(None), slice(None))), eng="pool")
                    S.stt(cv.newv, pas[c][:, 0:TB], bcol(l, O_GA + j * 128), sgs[c][:, 0:TB], ALU.add, ALU.mult)
                    conv_apply(cv, p_["bcw"], j, ch(ubuf, j, slice(0, TB)), bias_view=p_["b_conv_b"][:, j:j + 1])
                    if mode == "p":
                        S.copy(histB[l][:, j, :], cv.tail, eng="pool")
                    else:
                        S.copy(tmpB_new.sub(j, (slice(None), j, slice(None))), cv.newv, eng="pool")
            if mode == "s":
                newrow_out(lambda j: tmpB_new.sub(j, (slice(None), j, slice(None))), 8, O["gcs"], l * NS * 30 * 1024, 30, 1024)
            pm = S.ps()
            for k in range(8):
                S.mm(pm[:, 0:TB], ones[:, :], ch(ubuf, k, slice(0, TB)), start=(k == 0), stop=(k == 7), r=False)
            pq = S.ps()
            for k in range(8):
                sq = tA()
                S.act(sq[:, 0:TB], ch(ubuf, k, slice(0, TB)), AF.Square)
                S.mm(pq[:, 0:TB], ones[:, :], sq[:, 0:TB], start=(k == 0), stop=(k == 7), r=False)
            mean, rstd, m2 = lnm, lnr, lnm2
            S.ts(mean[:, 0:TB], pm[:, 0:TB], 1.0 / 1024, ALU.mult)
            S.tt(m2[:, 0:TB], mean[:, 0:TB], mean[:, 0:TB], ALU.mult)
            S.stt(rstd[:, 0:TB], pq[:, 0:TB], 1.0 / 1024, m2[:, 0:TB], ALU.mult, ALU.subtract)
            S.ts(rstd[:, 0:TB], rstd[:, 0:TB], 0.0, ALU.max, 1e-5, ALU.add)
            S.act(rstd[:, 0:TB], rstd[:, 0:TB], AF.Sqrt)
            S.recip(rstd[:, 0:TB], rstd[:, 0:TB])
            for k in range(8):
                t = tA()
                S.tt(t[:, 0:TB], ch(ubuf, k, slice(0, TB)), mean[:, 0:TB], ALU.subtract)
                S.tt(t[:, 0:TB], t[:, 0:TB], rstd[:, 0:TB], ALU.mult, eng="pool")
                S.act(ch(outn, k, slice(0, TB)), t[:, 0:TB], AF.Silu, bias=p_["b_ln_b"][:, k:k + 1], scale=p_["b_ln_g"][:, k:k + 1])
            branch_merge(1)

        def ph_A():
            pbd = win_proj(l, O_BD, 16, TB)
            bdT = tA()
            S.act(bdT[0:16, 0:TB], pbd[0:16, 0:TB], AF.Identity, bias=bcol(l, O_BD, 16))
            beta_t, g_t, gc_t = [], [], []
            for (c0, L) in chunks:
                p = S.ps()
                S.tr(p[0:L, 0:16], bdT[0:16, c0:c0 + L], ident[0:16, 0:16])
                ci_a = len(beta_t)
                be = Buf(persA.t[:, ci_a, 0:8], ("persA", ci_a, 0)); gg = Buf(persA.t[:, ci_a, 8:16], ("persA", ci_a, 1))
                gcx = Buf(persA.t[:, ci_a, 16:24], ("persA", ci_a, 2)); tmp = tC()
                S.act(be[0:L, 0:8], p[0:L, 0:8], AF.Sigmoid)
                S.tt(tmp[0:L, 0:8], p[0:L, 8:16], p_["dtb"][0:L, 0:8], ALU.add)
                S.act(tmp[0:L, 0:8], tmp[0:L, 0:8], AF.Exp)
                S.act(tmp[0:L, 0:8], tmp[0:L, 0:8], AF.Ln, bias=1.0)
                S.tt(gg[0:L, 0:8], tmp[0:L, 0:8], p_["negA"][0:L, 0:8], ALU.mult)
                pc = S.ps()
                S.mm(pc[0:L, 0:8], triu[0:L, 0:L], gg[0:L, 0:8], r=False)
                S.copy(gcx[0:L, 0:8], pc[0:L, 0:8])
                beta_t.append(be); g_t.append(gg); gc_t.append(gcx)
            newA = tmpA_new
            for h in range(8):
                qkv = []
                for part, off in enumerate((O_QA, O_KA, O_VA)):
                    jj = part * 8 + h
                    cv = Conv(mode, 4, TB)
                    pp = win_proj(l, off + h * 128, 128, TB)
                    if mode == "p":
                        S.copy(cv.histv, histA[l][:, jj, :], eng="pool")
                    else:
                        S.copy(cv.histv, hsA.sub(jj, (slice(None), jj, slice(None), slice(None))), eng="pool")
                    S.act(cv.newv, pp[:, 0:TB], AF.Identity, bias=bcol(l, off + h * 128))
                    acc = tA()
                    conv_apply(cv, p_["acw"], jj, acc[:, 0:TB])
                    if mode == "p":
                        S.copy(histA[l][:, jj, :], cv.tail, eng="pool")
                    else:
                        S.copy(newA.sub(jj, (slice(None), jj, slice(None))), cv.newv, eng="pool")
                    S.act(acc[:, 0:TB], acc[:, 0:TB], AF.Silu)
                    qkv.append(acc)
                qT_, kT_, vT_ = qkv
                for idx, t_ in enumerate((qT_, kT_)):
                    sq = tA()
                    S.act(sq[:, 0:TB], t_[:, 0:TB], AF.Square)
                    pn = S.ps()
                    S.mm(pn[:, 0:TB], ones[:, :], sq[:, 0:TB], r=False)
                    S.ts(sq[:, 0:TB], pn[:, 0:TB], 1e-6, ALU.add)
                    S.act(sq[:, 0:TB], sq[:, 0:TB], AF.Sqrt)
                    S.recip(sq[:, 0:TB], sq[:, 0:TB])
                    if idx == 0:
                        S.stt(t_[:, 0:TB], t_[:, 0:TB], 128 ** -0.5, sq[:, 0:TB], ALU.mult, ALU.mult)
                    else:
                        S.tt(t_[:, 0:TB], t_[:, 0:TB], sq[:, 0:TB], ALU.mult)
                pz = win_proj(l, O_ZA + h * 128, 128, TB)
                sz = tA()
                S.act(sz[:, 0:TB], pz[:, 0:TB], AF.Silu, bias=bcol(l, O_ZA + h * 128))
                for ci, (c0, L) in enumerate(chunks):
                    if mode == "p":
                        Sh = Sa[l][h]
                    else:
                        Sh = SaS[ssi[0] % 3]; ssi[0] += 1
                        S.dma("sp", Sh.t[:, :], dap(I["sdS"], ((l * NS + ci) * 8 + h) * 16384, [[128, 128], [1, 128]]), writes=[Sh[:]])
                    delta_chunk(l, h, ci, c0, L, qT_, kT_, vT_, sz, Sh, beta_t[ci], g_t[ci], gc_t[ci])
                    if mode == "s":
                        S.dma("pool", dap(O["dSs"], ((l * NS + ci) * 8 + h) * 16384, [[128, 128], [1, 128]]), Sh.t[:, :], reads=[Sh[:]])
                    elif last:
                        if ci == len(chunks) - 1:
                            S.dma("pool", dap(O["dSp"], (l * 8 + h) * 16384, [[128, 128], [1, 128]]), Sh.t[:, :], reads=[Sh[:]])
            if mode == "s":
                newrow_out(lambda j: newA.sub(j, (slice(None), j, slice(None))), 24, O["dcs"], l * NS * 3 * 3072, 3, 3072)
            branch_merge(0)

        def ph_D():
            pif = win_proj(l, O_IF, 8, TB)
            ifT = tA()
            S.act(ifT[0:8, 0:TB], pif[0:8, 0:TB], AF.Identity, bias=bcol(l, O_IF, 8))
            lf_t, b_t, ib_t = [], [], []
            for (c0, L) in chunks:
                p = S.ps()
                S.tr(p[0:L, 0:8], ifT[0:8, c0:c0 + L], ident[0:8, 0:8])
                ci_d = len(lf_t)
                lf = Buf(persD.t[:, ci_d, 0:4], ("persD", ci_d, 0)); bb = Buf(persD.t[:, ci_d, 4:8], ("persD", ci_d, 1))
                ib = Buf(persD.t[:, ci_d, 8:12], ("persD", ci_d, 2))
                S.act(lf[0:L, 0:4], p[0:L, 4:8], AF.Exp, scale=-1.0)
                S.act(lf[0:L, 0:4], lf[0:L, 0:4], AF.Ln, bias=1.0)
                S.ts(lf[0:L, 0:4], lf[0:L, 0:4], -1.0, ALU.mult)
                pc = S.ps()
                S.mm(pc[0:L, 0:4], triu[0:L, 0:L], lf[0:L, 0:4], r=False)
                S.copy(bb[0:L, 0:4], pc[0:L, 0:4])
                S.tt(ib[0:L, 0:4], p[0:L, 0:4], bb[0:L, 0:4], ALU.subtract)
                lf_t.append(lf); b_t.append(bb); ib_t.append(ib)
            for h in range(4):
                pq_ = win_proj(l, O_QD + h * 128, 128, TB)
                qT_ = tA()
                S.act(qT_[:, 0:TB], pq_[:, 0:TB], AF.Identity, bias=bcol(l, O_QD + h * 128))
                pk_ = win_proj(l, O_KD + h * 128, 128, TB)
                kT_ = tA()
                S.ts(kT_[:, 0:TB], pk_[:, 0:TB], bcol(l, O_KD + h * 128), ALU.add, 128 ** -0.5, ALU.mult)
                vT_, so_ = [], []
                for e in range(2):
                    pv_ = win_proj(l, O_VD + h * 256 + e * 128, 128, TB)
                    v_ = tA()
                    S.act(v_[:, 0:TB], pv_[:, 0:TB], AF.Identity, bias=bcol(l, O_VD + h * 256 + e * 128))
                    vT_.append(v_)
                for e in range(2):
                    po_ = win_proj(l, O_OD + h * 256 + e * 128, 128, TB)
                    o_ = tA()
                    S.act(o_[:, 0:TB], po_[:, 0:TB], AF.Sigmoid, bias=bcol(l, O_OD + h * 256 + e * 128))
                    so_.append(o_)
                for ci, (c0, L) in enumerate(chunks):
                    if mode == "p":
                        Ch, ms = Cn[l][h], mst[l]
                    else:
                        Ch = CnS[ssi[0] % 3]; ms = mstS[ssi[0] % 3]; ssi[0] += 1
                        base = (l * NS + ci) * 4 + h
                        S.dma("sp", Ch.t[:, 0:256], dap(I["smC"], base * 32768, [[256, 128], [1, 256]]), writes=[Ch[:]])
                        S.dma("sp", Ch.t[:, 256:257], dap(I["smn"], base * 128, [[1, 128], [1, 1]]), writes=[Ch[:]], allow_slow_non_contiguous=True)
                        S.dma("sp", ms.t[:, h:h + 1], dap(I["smm"], base, [[0, 128], [1, 1]]), writes=[ms[:]])
                    mlstm_chunk(l, h, c0, L, qT_, kT_, vT_, so_, Ch, ms, lf_t[ci], b_t[ci], ib_t[ci])
                    if mode == "s" or (last and ci == len(chunks) - 1):
                        if mode == "s":
                            base = (l * NS + ci) * 4 + h
                            oc, on_, om = O["mCs"], O["mns"], O["mms"]
                        else:
                            base = l * 4 + h
                            oc, on_, om = O["mCp"], O["mnp"], O["mmp"]
                        S.dma("pool", dap(oc, base * 32768, [[256, 128], [1, 256]]), Ch.t[:, 0:256], reads=[Ch[:]])
                        S.dma("pool", dap(on_, base * 128, [[1, 128], [1, 1]]), Ch.t[:, 256:257], reads=[Ch[:]], allow_slow_non_contiguous=True)
                        S.dma("pool", dap(om, base, [[1, 1], [1, 1]]), ms.t[0:1, h:h + 1], reads=[ms[:]])
            branch_merge(3)

        def ph_O():
            for dg in range(4):
                pys = fm_group(I["w_out"], l * D * D, D, 0, dg * 512, 512, 16, lambda k: ch(mixedb, k, slice(0, TB)), TB)
                for c in range(4):
                    d = dg * 4 + c
                    S.stt(ch(xT, d, slice(0, TB)), ch(xT, d, slice(0, TB)), ALPHA, pys[c][:, 0:TB], ALU.mult, ALU.add)
            layernorm(l, "ln1_g", "ln1_b", TB)

        def ph_X():
            pqs = fm_group(I["xq_w"], l * D * 512, 512, 0, 0, 512, 16, lambda k: ch(xTb, k, slice(0, TB)), TB)
            for h in range(4):
                S.copy(ch(qTb, h, slice(0, TB)), pqs[h][:, 0:TB], eng="act")
            for ci, (c0, L) in enumerate(chunks):
                if mode == "p":
                    KTl, Vl = KT[l], Vt[l]
                else:
                    KTl, Vl = KTs, Vts
                    S.dma("sp", Kts.t[:, :, :], dap(I["cmk"], (l * NS + ci) * 256 * 512, [[512, 128], [128 * 512, 2], [1, 512]]), writes=[Kts[:]])
                    S.dma("sp", Vts.t[:, :, :], dap(I["cmv"], (l * NS + ci) * 256 * 512, [[512, 128], [128 * 512, 2], [1, 512]]), writes=[Vts[:]])
                    for hh in range(4):
                        for mc in range(2):
                            transpose_to(KTs[:, hh, mc * 128:(mc + 1) * 128], Kts[:, mc, hh * 128:(hh + 1) * 128], 128, 128, eng="act")
                otok = tokbuf
                for h in range(4):
                    ps_ = S.ps()
                    S.mm(ps_[0:L, 0:256], ch(qTb, h, slice(c0, c0 + L)), KTl[:, h, :], r=False)
                    mx = tC()
                    S.rmax(mx[0:L, 0:1], ps_[0:L, 0:256])
                    S.ts(mx[0:L, 1:2], mx[0:L, 0:1], -(128 ** -0.5), ALU.mult)
                    es = tA()
                    S.act(es[0:L, 0:256], ps_[0:L, 0:256], AF.Exp, bias=mx[0:L, 1:2], scale=128 ** -0.5, accum=mx[0:L, 2:3])
                    S.recip(mx[0:L, 3:4], mx[0:L, 2:3])
                    aT = tA()
                    for mc in range(2):
                        transpose_to(aT[:, mc * 128:mc * 128 + L], es[0:L, mc * 128:(mc + 1) * 128], L, 128, eng="act")
                    po = S.ps()
                    for mc in range(2):
                        S.mm(po[0:L, 0:128], aT[:, mc * 128:mc * 128 + L], Vl[:, mc, h * 128:(h + 1) * 128], start=(mc == 0), stop=(mc == 1), r=False)
                    S.ts(otok[0:L, h * 128:(h + 1) * 128], po[0:L, 0:128], mx[0:L, 3:4], ALU.mult)
                for h in range(4):
                    transpose_to(ch(oTb, h, slice(c0, c0 + L)), otok[0:L, h * 128:(h + 1) * 128], L, 128, eng="act")
            for dg in range(4):
                pys = fm_group(I["xo_w"], l * 512 * D, D, 0, dg * 512, 512, 4, lambda k: ch(oTb, k, slice(0, TB)), TB)
                for c in range(4):
                    d = dg * 4 + c
                    S.stt(ch(xT, d, slice(0, TB)), ch(xT, d, slice(0, TB)), ALPHA, pys[c][:, 0:TB], ALU.mult, ALU.add)
            layernorm(l, "ln2_g", "ln2_b", TB)

        def ph_M():
            for g in range(8):
                for jg2 in range(2):
                    phs = fm_group(I["ffn_w1"], l * D * 4 * D, 4 * D, 0, (g * 8 + jg2 * 4) * 128, 512, 16, lambda k: ch(xTb, k, slice(0, TB)), TB)
                    for c in range(4):
                        jj = jg2 * 4 + c
                        j = g * 8 + jj
                        r_ = tA()
                        S.act(r_[:, 0:TB], phs[c][:, 0:TB], AF.Relu, bias=p_["ffn_b1"][:, j:j + 1])
                        S.tt(ch(hb, jj, slice(0, TB)), r_[:, 0:TB], r_[:, 0:TB], ALU.mult, eng="pool")
                for dg in range(4):
                    pds = fm_group(I["ffn_w2"], l * 4 * D * D, D, g * 1024, dg * 512, 512, 8, lambda k: ch(hb, k, slice(0, TB)), TB)
                    for c in range(4):
                        d = dg * 4 + c
                        if g == 0:
                            S.copy(ch(mixed, d, slice(0, TB)), pds[c][:, 0:TB], eng="act")
                        else:
                            S.tt(ch(mixed, d, slice(0, TB)), ch(mixed, d, slice(0, TB)), pds[c][:, 0:TB], ALU.add)
            for d in range(16):
                S.ts(ch(mixed, d, slice(0, TB)), ch(mixed, d, slice(0, TB)), p_["ffn_b2"][:, d:d + 1], ALU.add)
                S.stt(ch(xT, d, slice(0, TB)), ch(xT, d, slice(0, TB)), ALPHA, ch(mixed, d, slice(0, TB)), ALU.mult, ALU.add)
            layernorm(l, "ln3_g", "ln3_b", TB)

        PH = CFG.get('phases', 'CBADOXM')
        if 'C' in PH:
            ph_C(); dump('C', outn, 8)
        if 'B' in PH:
            ph_B(); dump('B', outn, 8)
        if 'A' in PH:
            ph_A(); dump('A', outn, 8)
        if 'D' in PH:
            ph_D(); dump('D', outn, 8); dump('mixed', mixed, 16)
        if 'O' in PH:
            ph_O(); dump('x1', xT, 16)
        if 'X' in PH:
            ph_X(); dump('x2', xT, 16)
        if 'M' in PH:
            ph_M(); dump('x3', xT, 16)

    def delta_chunk(l, h, ci, c0, L, qT_, kT_, vT_, sz, Sh, be, gg, gcx):
        p_ = P[l]
        beta_c, g_c, gc_c = be[0:L, h:h + 1], gg[0:L, h:h + 1], gcx[0:L, h:h + 1]
        kc, qc, vc = kT_[:, c0:c0 + L], qT_[:, c0:c0 + L], vT_[:, c0:c0 + L]
        gb = tQ()
        S.ts(gb[0:L, :], ones[0:L, :], g_c, ALU.mult)
        pg = S.ps()
        S.mm(pg[:, 0:L], gb[0:L, :], triu[0:L, 0:L], r=False)
        gcr = tQL(); egr = tQL()
        S.copy(gcr[:, 0:L], pg[:, 0:L], eng="act")
        S.act(egr[:, 0:L], pg[:, 0:L], AF.Exp)
        cols = tC()
        eg_c, ekl_c, nb_c, nbk_c = cols[0:L, 0:1], cols[0:L, 1:2], cols[0:L, 2:3], cols[0:L, 3:4]
        S.act(eg_c, gc_c, AF.Exp)
        S.act(ekl_c, gc_c, AF.Exp, bias=gcr[0:L, L - 1:L], scale=-1.0)
        S.ts(nb_c, beta_c, -1.0, ALU.mult)
        S.tt(nbk_c, nb_c, ekl_c, ALU.mult)
        ktok = tQL(); vtok = tQL()
        transpose_to(ktok[0:L, :], kc, 128, L, eng="act")
        transpose_to(vtok[0:L, :], vc, 128, L, eng="act")
        pk = S.ps()
        S.mm(pk[0:L, 0:L], kc, kc, r=False)
        S.mm(pk[0:L, 128:128 + L], kc, qc, r=False)
        qkd = tQL()
        if L > 1:
            Dm = tQ(); Es = tQ(); B0 = tQ(); Ei = tQ()
            S.stt(Dm[0:L, 0:L], gcr[0:L, 0:L], gc_c, maskS[0:L, 0:L], ALU.subtract, ALU.add)
            S.act(Es[0:L, 0:L], Dm[0:L, 0:L], AF.Exp)
            S.stt(B0[0:L, 0:L], pk[0:L, 0:L], beta_c, Es[0:L, 0:L], ALU.mult, ALU.mult)
            S.tt(Ei[0:L, 0:L], Es[0:L, 0:L], ident[0:L, 0:L], ALU.add, eng="pool")
            S.tt(qkd[0:L, 0:L], pk[0:L, 128:128 + L], Ei[0:L, 0:L], ALU.mult)
        else:
            S.copy(qkd[0:L, 0:L], pk[0:L, 128:128 + L], eng="act")
        pks = S.ps()
        S.mm(pks[0:L, 0:128], kc, Sh[:, :], r=False)
        rneg = tQL()
        S.stt(rneg[0:L, :], pks[0:L, 0:128], eg_c, vtok[0:L, :], ALU.mult, ALU.subtract)
        dl = tQL(); dk = tQL()
        if L > 1:
            Bp = B0
            Ap = tQ()
            transpose_to(Ap[0:L, 0:L], B0[0:L, 0:L], L, L, eng="act")
            U = tQ(); Lw = tQ()
            S.tt(U[0:L, 0:L], ident[0:L, 0:L], Bp[0:L, 0:L], ALU.subtract)
            S.tt(Lw[0:L, 0:L], ident[0:L, 0:L], Ap[0:L, 0:L], ALU.subtract, eng="pool")
            nlev = 6
            for j in range(1, nlev + 1):
                pB = S.ps()
                S.mm(pB[0:L, 0:L], Ap[0:L, 0:L], Bp[0:L, 0:L], r=False)
                Bn = tQ()
                S.copy(Bn[0:L, 0:L], pB[0:L, 0:L], eng="act")
                An = None
                if j < nlev:
                    pA = S.ps()
                    S.mm(pA[0:L, 0:L], Bp[0:L, 0:L], Ap[0:L, 0:L], r=False)
                    An = tQ()
                    S.copy(An[0:L, 0:L], pA[0:L, 0:L])
                pU = S.ps()
                S.mm(pU[0:L, 0:L], Lw[0:L, 0:L], Bn[0:L, 0:L], r=False)
                Un = tQ()
                S.tt(Un[0:L, 0:L], U[0:L, 0:L], pU[0:L, 0:L], ALU.add)
                if j < nlev:
                    pL = S.ps()
                    S.mm(pL[0:L, 0:L], U[0:L, 0:L], An[0:L, 0:L], r=False)
                    Ln_ = tQ()
                    S.tt(Ln_[0:L, 0:L], Lw[0:L, 0:L], pL[0:L, 0:L], ALU.add)
                    Lw = Ln_
                    Ap = An
                U = Un
                Bp = Bn
            pT = S.ps()
            S.mm(pT[0:L, 0:128], U[0:L, 0:L], rneg[0:L, :], r=False)
            S.ts(dl[0:L, :], pT[0:L, 0:128], nb_c, ALU.mult)
            S.act(dk[0:L, :], pT[0:L, 0:128], AF.Copy, scale=nbk_c) if False else S.ts(dk[0:L, :], pT[0:L, 0:128], nbk_c, ALU.mult)
        else:
            S.ts(dl[0:L, :], rneg[0:L, :], nb_c, ALU.mult)
            S.ts(dk[0:L, :], rneg[0:L, :], nbk_c, ALU.mult)
        qd = tQL()
        S.tt(qd[:, 0:L], qc, egr[:, 0:L], ALU.mult, eng="pool")
        po = S.ps()
        S.mm(po[0:L, 0:128], qd[:, 0:L], Sh[:, :], start=True, stop=False, r=False)
        S.mm(po[0:L, 0:128], qkd[0:L, 0:L], dl[0:L, :], start=False, stop=True, r=False)
        pS = S.ps()
        S.mm(pS[:, 0:128], ktok[0:L, :], dk[0:L, :], r=False)
        S.stt(Sh[:, :], Sh[:, :], egr[:, L - 1:L], pS[:, 0:128], ALU.mult, ALU.add)
        junk = tQ(); c2 = tC()
        S.act(junk[0:L, :], po[0:L, 0:128], AF.Square, accum=c2[0:L, 0:1])
        S.ts(c2[0:L, 1:2], c2[0:L, 0:1], 1.0 / 128, ALU.mult, 1e-6, ALU.add)
        S.act(c2[0:L, 1:2], c2[0:L, 1:2], AF.Sqrt)
        S.recip(c2[0:L, 2:3], c2[0:L, 1:2])
        on_ = tQL()
        S.ts(on_[0:L, :], po[0:L, 0:128], c2[0:L, 2:3], ALU.mult)
        pT2 = S.ps()
        S.tr(pT2[:, 0:L], on_[0:L, :], ident[0:L, 0:L])
        S.stt(outn.sub(h, (slice(None), h, slice(c0, c0 + L))), psrc(pT2[:, 0:L], L, 128), p_["a_norm_w"][:, 0:1], sz[:, c0:c0 + L], ALU.mult, ALU.mult)

    def mlstm_chunk(l, h, c0, L, qT_, kT_, vT_, so_, Ch, ms, lf, bb, ib):
        p_ = P[l]
        b_c, ib_c, lf_c = bb[0:L, h:h + 1], ib[0:L, h:h + 1], lf[0:L, h:h + 1]
        qc, kc = qT_[:, c0:c0 + L], kT_[:, c0:c0 + L]
        ibb = tQ(); lfb = tQ()
        S.ts(ibb[0:L, :], ones[0:L, :], ib_c, ALU.mult)
        if CFG.get('dstop', 999) == 1: return
        S.ts(lfb[0:L, :], ones[0:L, :], lf_c, ALU.mult)
        if CFG.get('dstop', 999) == 2: return
        pr = S.ps()
        S.mm(pr[:, 0:L], ibb[0:L, :], ident[0:L, 0:L], r=False)
        if CFG.get('dstop', 999) == 3: return
        S.mm(pr[:, 128:128 + L], lfb[0:L, :], triu[0:L, 0:L], r=False)
        if CFG.get('dstop', 999) == 4: return
        ibr = tQ()
        S.copy(ibr[:, 0:L], pr[:, 0:L], eng="act")
        if CFG.get('dstop', 999) == 5: return
        cb = tC()
        S.act(cb[:, 0:1], pr[:, 128 + L - 1:128 + L], AF.Identity)
        if CFG.get('dstop', 999) == 6: return
        S.rmax(cb[:, 1:2], ibr[:, 0:L])
        if CFG.get('dstop', 999) == 7: return
        S.tt(cb[:, 1:2], cb[:, 1:2], cb[:, 0:1], ALU.add)
        if CFG.get('dstop', 999) == 8: return
        S.tt(cb[:, 2:3], cb[:, 0:1], ms[:, h:h + 1], ALU.add)
        if CFG.get('dstop', 999) == 9: return
        S.tt(cb[:, 3:4], cb[:, 2:3], cb[:, 1:2], ALU.max)
        if CFG.get('dstop', 999) == 10: return
        S.tt(cb[:, 5:6], cb[:, 2:3], cb[:, 3:4], ALU.subtract)
        if CFG.get('dstop', 999) == 11: return
        S.act(cb[:, 4:5], cb[:, 5:6], AF.Exp)
        if CFG.get('dstop', 999) == 12: return
        S.tt(cb[:, 5:6], cb[:, 0:1], cb[:, 3:4], ALU.subtract)
        if CFG.get('dstop', 999) == 13: return
        cc = tC()
        S.tt(cc[0:L, 0:1], b_c, ms[0:L, h:h + 1], ALU.add)
        if CFG.get('dstop', 999) == 14: return
        ld = tQ()
        S.stt(ld[0:L, 0:L], ibr[0:L, 0:L], b_c, maskL[0:L, 0:L], ALU.add, ALU.add)
        if CFG.get('dstop', 999) == 15: return
        S.rmax(cc[0:L, 1:2], ld[0:L, 0:L])
        if CFG.get('dstop', 999) == 16: return
        S.tt(cc[0:L, 2:3], cc[0:L, 0:1], cc[0:L, 1:2], ALU.max)
        if CFG.get('dstop', 999) == 17: return
        S.ts(cc[0:L, 3:4], cc[0:L, 2:3], -1.0, ALU.mult)
        if CFG.get('dstop', 999) == 18: return
        S.act(cc[0:L, 4:5], cc[0:L, 0:1], AF.Exp, bias=cc[0:L, 3:4])
        if CFG.get('dstop', 999) == 19: return
        S.act(cc[0:L, 5:6], cc[0:L, 3:4], AF.Exp)
        if CFG.get('dstop', 999) == 20: return
        S.act(cc[0:L, 6:7], ib_c, AF.Exp, bias=cb[0:L, 5:6])
        if CFG.get('dstop', 999) == 21: return
        ed = tQ()
        S.act(ed[0:L, 0:L], ld[0:L, 0:L], AF.Exp, bias=cc[0:L, 3:4])
        if CFG.get('dstop', 999) == 22: return
        pqk = S.ps()
        S.mm(pqk[0:L, 0:L], qc, kc, r=False)
        if CFG.get('dstop', 999) == 23: return
        dm = tQ()
        S.tt(dm[0:L, 0:L], ed[0:L, 0:L], psrc(pqk[0:L, 0:L], L, L), ALU.mult)
        if CFG.get('dstop', 999) == 24: return
        dmT = tQ()
        transpose_to(dmT[0:L, 0:L], dm[0:L, 0:L], L, L, eng="act")
        if CFG.get('dstop', 999) == 25: return
        va = t258n()
        for e in range(2):
            transpose_to(va[0:L, e * 128:(e + 1) * 128], vT_[e][:, c0:c0 + L], 128, L, eng="act")
        S.memset(va[0:L, 256:257], 1.0)
        if CFG.get('dstop', 999) == 26: return
        S.memset(va[0:L, 257:258], 0.0)
        if CFG.get('dstop', 999) == 27: return
        ktok = tQ()
        transpose_to(ktok[0:L, :], kc, 128, L)
        if CFG.get('dstop', 999) == 28: return
        p1 = S.ps()
        S.mm(p1[0:L, 0:258], qc, Ch[:, :], r=False)
        if CFG.get('dstop', 999) == 29: return
        t1 = t258n()
        S.ts(t1[0:L, :], p1[0:L, 0:258], cc[0:L, 4:5], ALU.mult)
        if CFG.get('dstop', 999) == 30: return
        p2 = S.ps()
        S.mm(p2[0:L, 0:258], dmT[0:L, 0:L], va[0:L, :], r=False)
        if CFG.get('dstop', 999) == 31: return
        S.tt(t1[0:L, :], t1[0:L, :], p2[0:L, 0:258], ALU.add)
        if CFG.get('dstop', 999) == 32: return
        c3 = tC()
        S.act(c3[0:L, 5:6], t1[0:L, 256:257], AF.Abs)
        if CFG.get('dstop', 999) == 33: return
        S.tt(c3[0:L, 0:1], c3[0:L, 5:6], cc[0:L, 5:6], ALU.max)
        if CFG.get('dstop', 999) == 34: return
        S.recip(c3[0:L, 1:2], c3[0:L, 0:1])
        if CFG.get('dstop', 999) == 35: return
        hh = t258n()
        S.ts(hh[0:L, 0:256], t1[0:L, 0:256], c3[0:L, 1:2], ALU.mult)
        if CFG.get('dstop', 999) == 36: return
        S.act(t1[0:L, 0:256], hh[0:L, 0:256], AF.Square, accum=c3[0:L, 2:3])
        if CFG.get('dstop', 999) == 37: return
        S.ts(c3[0:L, 3:4], c3[0:L, 2:3], 1.0 / 256, ALU.mult, 1e-6, ALU.add)
        if CFG.get('dstop', 999) == 38: return
        S.act(c3[0:L, 3:4], c3[0:L, 3:4], AF.Sqrt)
        if CFG.get('dstop', 999) == 39: return
        S.recip(c3[0:L, 4:5], c3[0:L, 3:4])
        if CFG.get('dstop', 999) == 40: return
        S.ts(hh[0:L, 0:256], hh[0:L, 0:256], c3[0:L, 4:5], ALU.mult)
        if CFG.get('dstop', 999) == 41: return
        for e in range(2):
            pT = S.ps()
            S.tr(pT[:, 0:L], hh[0:L, e * 128:(e + 1) * 128], ident[0:L, 0:L])
            S.stt(outn.sub(2 * h + e, (slice(None), 2 * h + e, slice(c0, c0 + L))), psrc(pT[:, 0:L], L, 128), p_["d_norm_w"][:, e:e + 1], so_[e][:, c0:c0 + L], ALU.mult, ALU.mult)
        S.ts(va[0:L, :], va[0:L, :], cc[0:L, 6:7], ALU.mult)
        if CFG.get('dstop', 999) == 42: return
        pC = S.ps()
        S.mm(pC[:, 0:258], ktok[0:L, :], va[0:L, :], r=False)
        if CFG.get('dstop', 999) == 43: return
        S.stt(Ch[:, :], Ch[:, :], cb[:, 4:5], pC[:, 0:258], ALU.mult, ALU.add)
        if CFG.get('dstop', 999) == 44: return
        S.copy(ms[:, h:h + 1], cb[:, 3:4])
        if CFG.get('dstop', 999) == 45: return


    memT = mixed
    for t in range(2):
        S.dma("sp", tokbuf.t[:, :], dap(I["memp"], t * 128 * D, [[D, 128], [1, D]]), writes=[tokbuf[:]])
        for k in range(16):
            transpose_to(memT.sub(k, (slice(None), k, slice(t * 128, (t + 1) * 128))), tokbuf[:, k * 128:(k + 1) * 128], 128, 128)
    for l in range(2 if CFG.get('kv', True) else 0):
        Ktok = Buf(ubuf.t[:, 0:4, :].rearrange("p (m x) b -> p m (x b)", m=2), "Ktok")
        for which, wname, oname in ((0, "xk_w", "mkp"), (1, "xv_w", "mvp")):
            for h in range(4):
                w = load_w(I[wname], l * D * 512, 512, 0, h * 128, 128, 16)
                for mc in range(2):
                    p = S.ps()
                    for k in range(16):
                        S.mm(p[:, 0:128], memT.sub(k, (slice(None), k, slice(mc * 128, (mc + 1) * 128))), w[:, k, 0:128], start=(k == 0), stop=(k == 15), r=False)
                    if which == 0:
                        S.copy(Ktok.sub(mc, (slice(None), mc, slice(h * 128, (h + 1) * 128))), p[:, 0:128])
                        transpose_to(KT[l][:, h, mc * 128:(mc + 1) * 128], Ktok.sub(mc, (slice(None), mc, slice(h * 128, (h + 1) * 128))), 128, 128)
                    else:
                        S.copy(Vt[l][:, mc, h * 128:(h + 1) * 128], p[:, 0:128])
            src = Ktok if which == 0 else Vt[l]
            rd = [Ktok.sub(0, (slice(None), 0, slice(None))), Ktok.sub(1, (slice(None), 1, slice(None)))] if which == 0 else [Vt[l][:]]
            S.dma("pool", dap(O[oname], l * 256 * 512, [[512, 128], [128 * 512, 2], [1, 512]]), src.t[:, 0:2, 0:512], reads=rd)

    S.barrier()
    NBLK = CFG.get('nblk', 2048 // TBP)
    NLAY = CFG.get('layers', 2)
    chunks_p = [(c * 128, 128) for c in range(TBP // 128)]
    for blk in range(NBLK):
        for t in range(TBP // 128):
            S.dma("sp", tokbuf.t[:, :], dap(I["xp"], (blk * TBP + t * 128) * D, [[D, 128], [1, D]]), writes=[tokbuf[:]])
            for k in range(16):
                transpose_to(ch(xT, k, slice(t * 128, (t + 1) * 128)), tokbuf[:, k * 128:(k + 1) * 128], 128, 128, eng=("act" if k % 2 else "dve"))
                S.copy(ch(xTb, k, slice(t * 128, (t + 1) * 128)), ch(xT, k, slice(t * 128, (t + 1) * 128)), eng="pool")
        for l in range(NLAY):
            layer_block(l, TBP, "p", chunks_p, (blk == NBLK - 1) and CFG.get("stateout", True))
        for t in range(TBP // 128):
            for k in range(16):
                transpose_to(tokbuf[:, k * 128:(k + 1) * 128], ch(xT, k, slice(t * 128, (t + 1) * 128)), 128, 128, eng=("act" if k % 2 else "dve"))
            S.dma("pool", dap(O["yp"], (blk * TBP + t * 128) * D, [[D, 128], [1, D]]), tokbuf.t[:, :], reads=[tokbuf[:]])
    for l in range(2):
        emit_rows(lambda j: histA[l][:, j, :], 24, 3, lambda c0, n: dap(O["dcp"], l * 3 * 3072 + c0, [[3072, 3], [1, n]]))
        emit_rows(lambda j: histB[l][:, j, :], 8, 30, lambda c0, n: dap(O["gcp"], l * 30 * 1024 + c0, [[1024, 30], [1, n]]))
        emit_rows(lambda j: histC[l][:, j, :], 8, 2, lambda c0, n: dap(O["scp"], l * 2 * 1024 + c0, [[1024, 2], [1, n]]))

    if CFG.get('sample', True):
        S.dma("sp", tokbuf.t[0:NS, :], dap(I["xs"], 0, [[D, NS], [1, D]]), writes=[tokbuf[:]])
        for k in range(16):
            transpose_to(ch(xT, k, slice(0, NS)), tokbuf[0:NS, k * 128:(k + 1) * 128], NS, 128)
            S.copy(ch(xTb, k, slice(0, NS)), ch(xT, k, slice(0, NS)), eng="pool")
        S.barrier()
        for i in range(3):
            S.memset(CnS[i][:], 0.0)
        chunks_s = [(s, 1) for s in range(NS)]
        for l in range(NLAY):
            for (src, H, C, hs, dst) in ((I["sdc"], 3, 3072, hsA, O["dcs"]), (I["sgc"], 30, 1024, hsB, O["gcs"]), (I["ssc"], 2, 1024, hsC, O["scs"])):
                S.dma("pool", dap(dst, l * NS * H * C, [[H * C, NS], [C, H - 1], [1, C]]), dap(src, l * NS * H * C + C, [[H * C, NS], [C, H - 1], [1, C]]))
                spt = max(1, min(NS, 128 // H))
                for s0 in range(0, NS, spt):
                    ns_ = min(spt, NS - s0)
                    R = ns_ * H
                    for cc0 in range(0, C, D):
                        ncol = min(D, C - cc0)
                        S.dma("sp", tokbuf.t[0:R, 0:ncol], dap(src, (l * NS + s0) * H * C + cc0, [[C, R], [1, ncol]]), writes=[tokbuf[:]])
                        for jj in range(ncol // 128):
                            j = cc0 // 128 + jj
                            p = S.ps()
                            S.tr(p[:, 0:R], tokbuf[0:R, jj * 128:(jj + 1) * 128], ident[0:R, 0:R])
                            S.copy(hs.sub(j, (slice(None), j, slice(s0, s0 + ns_), slice(None))), p[:, 0:R].ap.rearrange("p (s h) -> p s h", h=H) if False else View(p.key, p.t[:, 0:R].rearrange("p (s h) -> p s h", h=H)))
            layer_block(l, NS, "s", chunks_s, False, hsA, hsB, hsC)
        for k in range(16):
            transpose_to(tokbuf[0:NS, k * 128:(k + 1) * 128], ch(xT, k, slice(0, NS)), 128, NS)
        S.dma("pool", dap(O["ys"], 0, [[D, NS], [1, D]]), tokbuf.t[0:NS, :], reads=[tokbuf[:]])
    S.finish()
    return nc


_NC = [None]


def kernel(**inp):
    a = {k: np.ascontiguousarray(np.asarray(v, dtype=np.float32)) for k, v in inp.items()}
    if _NC[0] is None:
        _NC[0] = build()
    nc = _NC[0]
    wnames = ["w_in", "b_in", "a_conv_w", "a_A_log", "a_dt_bias", "a_norm_w", "b_conv_w", "b_conv_b", "b_ln_g", "b_ln_b",
              "c_conv_w", "d_norm_w", "w_branch", "w_out", "ln1_g", "ln1_b", "xq_w", "xk_w", "xv_w", "xo_w", "ln2_g", "ln2_b",
              "ffn_w1", "ffn_b1", "ffn_w2", "ffn_b2", "ln3_g", "ln3_b"]
    in_maps = []
    for c in range(8):
        b = c % 4
        sl = slice(c * NS, (c + 1) * NS)
        m = {"xp": a["x_prompt"][b], "xs": a["x_sample"][sl, 0, :], "memp": a["mem_prompt"][b],
             "cmk": a["cache_mem_k"][:, sl].reshape(2, NS, 256, 512), "cmv": a["cache_mem_v"][:, sl].reshape(2, NS, 256, 512),
             "sdc": a["state_delta_conv"][:, sl], "sdS": a["state_delta_S"][:, sl], "sgc": a["state_glu_conv"][:, sl],
             "ssc": a["state_short_conv"][:, sl], "smC": a["state_mlstm_C"][:, sl], "smn": a["state_mlstm_n"][:, sl],
             "smm": a["state_mlstm_m"][:, sl]}
        m = {k: np.ascontiguousarray(v) for k, v in m.items()}
        for w in wnames:
            m[w] = a[w]
        in_maps.append(m)
    res = run_bass_kernel_spmd(nc, in_maps, core_ids=list(range(8)))
    R = res.results

    def pst(name, shape):
        return np.stack([np.asarray(R[b][name]) for b in range(4)], axis=1).reshape(shape)

    def sst(name, shape):
        return np.concatenate([np.asarray(R[c][name]) for c in range(8)], axis=1).reshape(shape)

    y_prompt = np.stack([np.asarray(R[b]["yp"]) for b in range(4)], axis=0)
    y_sample = np.concatenate([np.asarray(R[c]["ys"]) for c in range(8)], axis=0).reshape(128, 1, D)
    outs = (y_prompt, y_sample,
            pst("mkp", (2, 4, 256, 4, 128)), pst("mvp", (2, 4, 256, 4, 128)),
            pst("dcp", (2, 4, 3, 3072)), pst("dSp", (2, 4, 8, 128, 128)), pst("gcp", (2, 4, 30, 1024)),
            pst("scp", (2, 4, 2, 1024)), pst("mCp", (2, 4, 4, 128, 256)), pst("mnp", (2, 4, 4, 128)), pst("mmp", (2, 4, 4)),
            sst("dcs", (2, 128, 3, 3072)), sst("dSs", (2, 128, 8, 128, 128)), sst("gcs", (2, 128, 30, 1024)),
            sst("scs", (2, 128, 2, 1024)), sst("mCs", (2, 128, 4, 128, 256)), sst("mns", (2, 128, 4, 128)), sst("mms", (2, 128, 4)))
    return tuple(np.ascontiguousarray(o.astype(np.float32)) for o in outs)
```

```python
import contextlib
import numpy as np
import concourse.bass as bass
import concourse.mybir as mybir
from concourse.bass_utils import run_bass_kernel_spmd

F32 = mybir.dt.float32
F32R = mybir.dt.float32r
BF16 = mybir.dt.bfloat16
AF = mybir.ActivationFunctionType
ALU = mybir.AluOpType
AX = mybir.AxisListType

EPOCH = 20000
NDS = 8


class View:
    __slots__ = ("key", "ap")

    def __init__(self, key, ap):
        self.key = key
        self.ap = ap


class Buf:
    def __init__(self, t, key):
        self.t = t
        self.key = key

    def __getitem__(self, idx):
        return View(self.key, self.t[idx])

    def sub(self, subkey, idx):
        return View((self.key, subkey), self.t[idx])


class Sched:
    ENG = ["pe", "act", "dve", "pool", "sp"]

    def __init__(self, nc):
        self.nc = nc
        self.stack = contextlib.ExitStack()
        self.prog = {e: [] for e in self.ENG}
        self.count = {e: 0 for e in self.ENG}
        self.waited = {e: {} for e in self.ENG}
        self.last_w = {}
        self.readers = {}
        self.dslot = {e: 0 for e in self.ENG}
        self.dval = {}
        self.sids = {}
        self.nbuf = 0
        self.psum_banks = []
        self.psum_i = 0
        self.nops = 0

    def sb(self, shape, dtype=F32, name=None):
        self.nbuf += 1
        name = name or f"sb{self.nbuf}"
        t = self.stack.enter_context(self.nc.sbuf_tensor(name, list(shape), dtype))
        return Buf(t, name)

    def init_psum(self, n=8):
        for i in range(n):
            t = self.stack.enter_context(self.nc.psum_tensor(f"ps{i}", [128, 512], F32))
            self.psum_banks.append(Buf(t, f"ps{i}"))

    def ps(self):
        b = self.psum_banks[self.psum_i % len(self.psum_banks)]
        self.psum_i += 1
        return b

    def _deps(self, eng, reads, writes):
        deps = set()
        for v in reads:
            t = self.last_w.get(v.key)
            if t is not None:
                deps.add(t)
        for v in writes:
            t = self.last_w.get(v.key)
            if t is not None:
                deps.add(t)
            for r in self.readers.get(v.key, ()):
                if r[2] != eng or r[2] == "dma":
                    deps.add(r)
        for (sid, val, deng) in sorted(deps, key=lambda d: str(d)):
            if deng == eng and eng == "pe":
                continue
            if self.waited[eng].get(sid, 0) >= val:
                continue
            self.waited[eng][sid] = val
            self.prog[eng].append(("wait", sid, val))

    def _commit(self, tok, reads, writes):
        for v in reads:
            self.readers.setdefault(v.key, []).append(tok)
        for v in writes:
            self.last_w[v.key] = tok
            self.readers[v.key] = []

    def op(self, eng, fn, reads=(), writes=()):
        self._deps(eng, reads, writes)
        n = self.count[eng]
        self.count[eng] += 1
        sid = ("c", eng, n // EPOCH)
        val = n % EPOCH + 1
        self.sids[sid] = 1
        tok = (sid, val, eng)
        self.prog[eng].append(("op", fn, sid, 1))
        self._commit(tok, reads, writes)
        self.nops += 1
        return tok

    def dma(self, q, out_ap, in_ap, reads=(), writes=(), **kw):
        eng = q
        self._deps(eng, reads, writes)
        slot = self.dslot[q]
        self.dslot[q] = (slot + 1) % NDS
        sid = ("d", q, slot)
        self.sids[sid] = 1
        prev = self.dval.get(sid, 0)
        if prev > 0 and self.waited[eng].get(sid, 0) < prev:
            self.waited[eng][sid] = prev
            self.prog[eng].append(("wait", sid, prev))
        val = prev + 16
        self.dval[sid] = val
        tok = (sid, val, "dma")
        self.prog[eng].append(("op", lambda e: e.dma_start(out_ap, in_ap, **kw), sid, 16))
        self._commit(tok, reads, writes)
        self.nops += 1
        return tok

    def barrier(self):
        toks = []
        for e in self.ENG:
            n = self.count[e]
            if n > 0:
                toks.append((("c", e, (n - 1) // EPOCH), (n - 1) % EPOCH + 1, e))
        for sid, val in self.dval.items():
            toks.append((sid, val, "dma"))
        for e in self.ENG:
            for (sid, val, deng) in toks:
                if deng == e:
                    continue
                if self.waited[e].get(sid, 0) >= val:
                    continue
                self.waited[e][sid] = val
                self.prog[e].append(("wait", sid, val))

    def finish(self):
        for sid, val in self.dval.items():
            q = sid[1]
            if self.waited[q].get(sid, 0) < val:
                self.prog[q].append(("wait", sid, val))
        nc = self.nc
        sems = {}
        for i, sid in enumerate(self.sids):
            sems[sid] = self.stack.enter_context(nc.semaphore(f"s{i}"))
        prog = self.prog

        def replay(name, e):
            for it in prog[name]:
                if it[0] == "wait":
                    e.wait_ge(sems[it[1]], it[2])
                else:
                    it[1](e).then_inc(sems[it[2]], it[3])

        with nc.Block() as block:
            @block.tensor
            def _(e):
                replay("pe", e)

            @block.scalar
            def _(e):
                replay("act", e)

            @block.vector
            def _(e):
                replay("dve", e)

            @block.gpsimd
            def _(e):
                replay("pool", e)

            @block.sync
            def _(e):
                replay("sp", e)
        self.stack.close()

    def mm(self, out, lhsT, rhs, start=True, stop=True, r=False):
        la, ra = lhsT.ap, rhs.ap
        if r:
            la, ra = la.bitcast(F32R), ra.bitcast(F32R)
        rd = [lhsT, rhs] + ([] if start else [out])
        return self.op("pe", lambda e: e.matmul(out.ap, la, ra, start=start, stop=stop), rd, [out])

    def tr(self, out, in_, ident):
        return self.op("pe", lambda e: e.transpose(out.ap, in_.ap, ident.ap), [in_, ident], [out])

    def act(self, out, in_, func, bias=None, scale=None, accum=None, eng="act"):
        kw = {}
        rd = [in_]
        wr = [out]
        if bias is not None:
            if isinstance(bias, View):
                kw["bias"] = bias.ap
                rd.append(bias)
            else:
                kw["bias"] = bias
        if scale is not None:
            if isinstance(scale, View):
                kw["scale"] = scale.ap
                rd.append(scale)
            else:
                kw["scale"] = scale
        if accum is not None:
            kw["accum_out"] = accum.ap
            wr.append(accum)
        return self.op("act", lambda e: e.activation(out.ap, in_.ap, func, **kw), rd, wr)

    def tt(self, out, a, b, op, eng="dve"):
        return self.op(eng, lambda e: e.tensor_tensor(out.ap, a.ap, b.ap, op), [a, b], [out])

    def ts(self, out, a, s1, op0, s2=None, op1=None, accum=None, eng="dve"):
        rd = [a]
        wr = [out]
        s1a = s1.ap if isinstance(s1, View) else s1
        s2a = s2.ap if isinstance(s2, View) else s2
        if isinstance(s1, View):
            rd.append(s1)
        if isinstance(s2, View):
            rd.append(s2)
        kw = {}
        if op1 is not None:
            kw["op1"] = op1
        if accum is not None:
            kw["accum_out"] = accum.ap
            wr.append(accum)
        return self.op(eng, lambda e: e.tensor_scalar(out.ap, a.ap, s1a, s2a, op0, **kw), rd, wr)

    def stt(self, out, a, s, b, op0, op1, accum=None):
        rd = [a, b]
        wr = [out]
        sa = s.ap if isinstance(s, View) else s
        if isinstance(s, View):
            rd.append(s)
        kw = {}
        if accum is not None:
            kw["accum_out"] = accum.ap
            wr.append(accum)
        return self.op("dve", lambda e: e.scalar_tensor_tensor(out.ap, a.ap, sa, b.ap, op0, op1, **kw), rd, wr)

    def copy(self, out, in_, eng="dve"):
        if eng == "act":
            return self.op("act", lambda e: e.copy(out.ap, in_.ap), [in_], [out])
        return self.op(eng, lambda e: e.tensor_copy(out.ap, in_.ap), [in_], [out])

    def memset(self, out, val, eng="pool"):
        return self.op(eng, lambda e: e.memset(out.ap, val), [], [out])

    def recip(self, out, in_):
        return self.op("dve", lambda e: e.reciprocal(out.ap, in_.ap), [in_], [out])

    def rmax(self, out, in_, eng="dve"):
        return self.op(eng, lambda e: e.reduce_max(out.ap, in_.ap, AX.X), [in_], [out])

D = 2048
NIN = 20504
O_QA, O_KA, O_VA, O_ZA, O_BD = 0, 1024, 2048, 3072, 4096
O_GA, O_GG, O_BG, O_CG, O_HC = 4112, 5136, 6160, 7184, 8208
O_QD, O_KD, O_VD, O_OD, O_IF, O_GATE = 9232, 9744, 10256, 11280, 12304, 12312
ALPHA = 4 ** 0.25
NEG = -1.0e30
NS = 16
TBP = 256
CFG = {}


def dap(t, off, dims):
    return bass.AP(t, off, [list(d) for d in dims])


def build():
    nc = bass.Bass("TRN2", target_bir_lowering=False)

    def din(name, shape):
        return nc.dram_tensor(name, list(shape), F32, kind="ExternalInput")

    def dout(name, shape):
        return nc.dram_tensor(name, list(shape), F32, kind="ExternalOutput")

    I = {}
    for name, shape in [
        ("xp", (2048, D)), ("xs", (NS, D)), ("memp", (256, D)),
        ("cmk", (2, NS, 256, 512)), ("cmv", (2, NS, 256, 512)),
        ("sdc", (2, NS, 3, 3072)), ("sdS", (2, NS, 8, 128, 128)), ("sgc", (2, NS, 30, 1024)),
        ("ssc", (2, NS, 2, 1024)), ("smC", (2, NS, 4, 128, 256)), ("smn", (2, NS, 4, 128)), ("smm", (2, NS, 4)),
        ("w_in", (2, D, NIN)), ("b_in", (2, NIN)), ("a_conv_w", (2, 4, 3072)), ("a_A_log", (2, 8)),
        ("a_dt_bias", (2, 8)), ("a_norm_w", (2, 128)), ("b_conv_w", (2, 31, 1024)), ("b_conv_b", (2, 1024)),
        ("b_ln_g", (2, 1024)), ("b_ln_b", (2, 1024)), ("c_conv_w", (2, 3, 1024)), ("d_norm_w", (2, 256)),
        ("w_branch", (2, 4, 1024, D)), ("w_out", (2, D, D)), ("ln1_g", (2, D)), ("ln1_b", (2, D)),
        ("xq_w", (2, D, 512)), ("xk_w", (2, D, 512)), ("xv_w", (2, D, 512)), ("xo_w", (2, 512, D)),
        ("ln2_g", (2, D)), ("ln2_b", (2, D)), ("ffn_w1", (2, D, 4 * D)), ("ffn_b1", (2, 4 * D)),
        ("ffn_w2", (2, 4 * D, D)), ("ffn_b2", (2, D)), ("ln3_g", (2, D)), ("ln3_b", (2, D)),
    ]:
        I[name] = din(name, shape)
    O = {}
    for name, shape in [
        ("yp", (2048, D)), ("ys", (NS, D)), ("mkp", (2, 256, 512)), ("mvp", (2, 256, 512)),
        ("dcp", (2, 3, 3072)), ("dSp", (2, 8, 128, 128)), ("gcp", (2, 30, 1024)), ("scp", (2, 2, 1024)),
        ("mCp", (2, 4, 128, 256)), ("mnp", (2, 4, 128)), ("mmp", (2, 4)),
        ("dcs", (2, NS, 3, 3072)), ("dSs", (2, NS, 8, 128, 128)), ("gcs", (2, NS, 30, 1024)),
        ("scs", (2, NS, 2, 1024)), ("mCs", (2, NS, 4, 128, 256)), ("mns", (2, NS, 4, 128)), ("mms", (2, NS, 4)),
    ]:
        O[name] = dout(name, shape)

    DBG = {}
    if CFG.get('dump'):
        for nm in ['C', 'B', 'A', 'D']:
            DBG[nm] = dout('dbg_' + nm, (128, 8, TBP))
        for nm in ['mixed', 'x1', 'x2', 'x3']:
            DBG[nm] = dout('dbg_' + nm, (128, 16, TBP))
    dumped = set()
    S = Sched(nc)
    S.init_psum()
    sb = S.sb

    def dump(nm, buf, nch):
        if not CFG.get('dump') or nm in dumped:
            return
        dumped.add(nm)
        S.dma("pool", dap(DBG[nm], 0, [[nch * TBP, 128], [TBP, nch], [1, TBP]]), buf.t[:, 0:nch, :],
              reads=[buf.sub(k, (slice(None), k, slice(None))) for k in range(nch)])

    ones = sb([128, 128], name="ones")
    ident = sb([128, 128], name="ident")
    triu = sb([128, 128], name="triu")
    maskS = sb([128, 128], name="maskS")
    maskL = sb([128, 128], name="maskL")
    zeros = sb([128, 128], name="zeros")
    S.memset(ones[:], 1.0)
    S.memset(zeros[:], 0.0)
    S.op("pool", lambda e: e.affine_select(ident.t[:], ones.t[:], [[-1, 128]], ALU.is_equal, 0.0, base=0, channel_multiplier=1), [ones[:]], [ident[:]])
    S.op("pool", lambda e: e.affine_select(triu.t[:], ones.t[:], [[1, 128]], ALU.is_ge, 0.0, base=0, channel_multiplier=-1), [ones[:]], [triu[:]])
    S.op("pool", lambda e: e.affine_select(maskS.t[:], zeros.t[:], [[1, 128]], ALU.is_gt, NEG, base=0, channel_multiplier=-1), [zeros[:]], [maskS[:]])
    S.op("pool", lambda e: e.affine_select(maskL.t[:], zeros.t[:], [[-1, 128]], ALU.is_ge, NEG, base=0, channel_multiplier=1), [zeros[:]], [maskL[:]])

    def colload(dst, dcol0, t, off, nchunk, n=128):
        S.dma("pool", dst.t[0:n, dcol0:dcol0 + nchunk], dap(t, off, [[1, n], [128, nchunk]]), writes=[dst[:]],
              allow_slow_non_contiguous=True)

    P = []
    for l in range(2):
        p = {}
        bi = sb([128, 162], name=f"bin{l}")
        colload(bi, 0, I["b_in"], l * NIN + 0, 32)
        colload(bi, 32, I["b_in"], l * NIN + O_BD, 1, n=16)
        colload(bi, 33, I["b_in"], l * NIN + O_GA, 64)
        colload(bi, 97, I["b_in"], l * NIN + O_IF, 1, n=8)
        colload(bi, 98, I["b_in"], l * NIN + O_GATE, 64)
        p["bin"] = bi
        acw = sb([128, 24, 4], name=f"acw{l}")
        for k in range(4):
            S.dma("pool", acw.t[:, :, k], dap(I["a_conv_w"], l * 4 * 3072 + k * 3072, [[1, 128], [128, 24]]), writes=[acw[:]], allow_slow_non_contiguous=True)
        bcw = sb([128, 8, 31], name=f"bcw{l}")
        for k in range(31):
            S.dma("pool", bcw.t[:, :, k], dap(I["b_conv_w"], l * 31 * 1024 + k * 1024, [[1, 128], [128, 8]]), writes=[bcw[:]], allow_slow_non_contiguous=True)
        ccw = sb([128, 8, 3], name=f"ccw{l}")
        for k in range(3):
            S.dma("pool", ccw.t[:, :, k], dap(I["c_conv_w"], l * 3 * 1024 + k * 1024, [[1, 128], [128, 8]]), writes=[ccw[:]], allow_slow_non_contiguous=True)
        p["acw"], p["bcw"], p["ccw"] = acw, bcw, ccw
        for nm, nch in [("b_conv_b", 8), ("b_ln_g", 8), ("b_ln_b", 8), ("a_norm_w", 1), ("d_norm_w", 2), ("ln1_g", 16), ("ln1_b", 16),
                        ("ln2_g", 16), ("ln2_b", 16), ("ln3_g", 16), ("ln3_b", 16), ("ffn_b1", 64), ("ffn_b2", 16)]:
            tl = sb([128, nch], name=f"{nm}{l}")
            colload(tl, 0, I[nm], l * nch * 128, nch)
            p[nm] = tl
        negA = sb([128, 8], name=f"negA{l}")
        dtb = sb([128, 8], name=f"dtb{l}")
        S.dma("pool", negA.t[:, :], dap(I["a_A_log"], l * 8, [[0, 128], [1, 8]]), writes=[negA[:]])
        S.dma("pool", dtb.t[:, :], dap(I["a_dt_bias"], l * 8, [[0, 128], [1, 8]]), writes=[dtb[:]])
        S.act(negA[:], negA[:], AF.Exp)
        S.ts(negA[:], negA[:], -1.0, ALU.mult)
        p["negA"], p["dtb"] = negA, dtb
        P.append(p)

    def bcol(l, col0, n=128):
        if col0 < O_BD:
            c = col0 // 128
        elif col0 == O_BD:
            c = 32
        elif col0 < O_IF:
            c = 33 + (col0 - O_GA) // 128
        elif col0 == O_IF:
            c = 97
        else:
            c = 98 + (col0 - O_GATE) // 128
        return P[l]["bin"][0:n, c:c + 1]

    xT = sb([128, 16, TBP], name="xT")
    mixed = sb([128, 16, TBP], name="mixed")
    outn = sb([128, 8, TBP], BF16, name="outn")
    xTb = sb([128, 16, TBP], BF16, name="xTb")
    mixedb = sb([128, 16, TBP], BF16, name="mixedb")
    hb = sb([128, 8, TBP], BF16, name="hb")
    ubuf = sb([128, 8, TBP], name="ubuf")
    tokbuf = sb([128, D], name="tokbuf")
    wsl = [sb([128, 2048], name=f"w{i}") for i in range(3)]
    wbf = [sb([128, 2048], BF16, name=f"wb{i}") for i in range(3)]
    wi = [0]
    NT = 10
    tmpA = [sb([128, TBP], name=f"tA{i}") for i in range(NT)]
    ti = [0]
    NQ = 12
    tmpQ = [sb([128, 128], name=f"tQ{i}") for i in range(NQ)]
    qi = [0]
    NC_ = 48
    tmpC = [sb([128, 8], name=f"tC{i}") for i in range(NC_)]
    ci_ = [0]
    extb = sb([128, TBP + 30], name="extb")
    persA = sb([128, 16, 24], name="persA")
    lnm = sb([128, TBP], name="lnm"); lnr = sb([128, TBP], name="lnr"); lnm2 = sb([128, TBP], name="lnm2")
    tmpQL = [sb([128, 128], name=f"tQL{i}") for i in range(12)]
    qli = [0]
    persD = sb([128, 16, 12], name="persD")
    exts = sb([128, NS, 31], name="exts")
    t258 = [sb([128, 258], name=f"t258_{i}") for i in range(4)]
    t258i = [0]

    def tA():
        ti[0] += 1
        return tmpA[ti[0] % NT]

    def tQ():
        qi[0] += 1
        return tmpQ[qi[0] % NQ]

    def tQL():
        qli[0] += 1
        return tmpQL[qli[0] % 12]

    def psrc(pv, L, shape_rows):
        if L != 1:
            return pv
        t = tQ()
        v = t[0:shape_rows, 0:1]
        S.act(v, pv, AF.Identity)
        return v

    def tC():
        ci_[0] += 1
        return tmpC[ci_[0] % NC_]

    def t258n():
        t258i[0] += 1
        return t258[t258i[0] % 4]

    def ch(buf, k, sl=slice(None)):
        return buf.sub(k, (slice(None), k, sl))

    ARN = 10240
    arena = sb([128, ARN], name="arena")
    apos = {"p": 0, "s": 0}

    def carve(phase, shape, name):
        n = 1
        for s_ in shape[1:]:
            n *= s_
        a0 = apos[phase]
        apos[phase] += n
        assert apos[phase] <= ARN, (phase, apos[phase])
        v = arena.t[:, a0:a0 + n]
        if len(shape) == 3:
            v = v.rearrange("p (a b) -> p a b", a=shape[1])
        elif len(shape) == 4:
            v = v.rearrange("p (a b c) -> p a b c", a=shape[1], b=shape[2])
        return Buf(v, name)

    histA = [carve("p", [128, 24, 3], f"hA{l}") for l in range(2)]
    histB = [carve("p", [128, 8, 30], f"hB{l}") for l in range(2)]
    histC = [carve("p", [128, 8, 2], f"hC{l}") for l in range(2)]
    Sa = [[carve("p", [128, 128], f"Sa{l}_{h}") for h in range(8)] for l in range(2)]
    Cn = [[carve("p", [128, 258], f"Cn{l}_{h}") for h in range(4)] for l in range(2)]
    mst = [carve("p", [128, 4], f"mst{l}") for l in range(2)]
    KT = [carve("p", [128, 4, 256], f"KT{l}") for l in range(2)]
    Vt = [carve("p", [128, 2, 512], f"Vt{l}") for l in range(2)]
    for l in range(2):
        S.memset(histA[l][:], 0.0); S.memset(histB[l][:], 0.0); S.memset(histC[l][:], 0.0)
        S.memset(mst[l][:], 0.0)
        for h in range(8):
            S.memset(Sa[l][h][:], 0.0)
        for h in range(4):
            S.memset(Cn[l][h][:], 0.0)
    SaS = [carve("s", [128, 128], f"SaS{i}") for i in range(3)]
    CnS = [carve("s", [128, 258], f"CnS{i}") for i in range(3)]
    mstS = [carve("s", [128, 4], f"mstS{i}") for i in range(3)]
    ssi = [0]
    KTs = carve("s", [128, 4, 256], "KTs")
    Kts = carve("s", [128, 2, 512], "Kts")
    Vts = carve("s", [128, 2, 512], "Vts")
    hsA = carve("s", [128, 24, NS, 3], "hsA")
    hsB = carve("s", [128, 8, NS, 30], "hsB")
    hsC = carve("s", [128, 8, NS, 2], "hsC")
    tmpB_new = carve("s", [128, 8, NS], "tmpBn")
    tmpA_new = carve("s", [128, 24, NS], "tmpAn")
    qTb = sb([128, 4, TBP], name="qTb")
    oTb = sb([128, 4, TBP], BF16, name="oTb")

    def load_w(t, base, rstride, row0, col0, n, kcn, bf=False, wide=False):
        w = wsl[wi[0] % 3]
        wb = wbf[wi[0] % 3]
        wi[0] += 1
        kk = 4 if wide else 16
        wv = w.t[:, :].rearrange("p (k n) -> p k n", k=kk)
        S.dma("sp", wv[:, 0:kcn, 0:n], dap(t, base + row0 * rstride + col0, [[rstride, 128], [128 * rstride, kcn], [1, n]]),
              writes=[w[:]])
        if not bf:
            return Buf(wv, w.key)
        wbv = wb.t[:, :].rearrange("p (k n) -> p k n", k=kk)
        if wi[0] % 2 == 0:
            S.op("act", lambda e: e.copy(wbv[:, 0:kcn, 0:n], wv[:, 0:kcn, 0:n]), [w[:]], [wb[:]])
        else:
            S.op("dve", lambda e: e.tensor_copy(wbv[:, 0:kcn, 0:n], wv[:, 0:kcn, 0:n]), [w[:]], [wb[:]])
        return Buf(wbv, wb.key)

    def fm_proj(t, base, rstride, row0, col0, n, kcn, rhs_fn, TB):
        w = load_w(t, base, rstride, row0, col0, n, kcn, bf=True)
        p = S.ps()
        for k in range(kcn):
            S.mm(p[0:n, 0:TB], w[:, k, 0:n], rhs_fn(k), start=(k == 0), stop=(k == kcn - 1), r=False)
        return p

    def fm_group(t, base, rstride, row0, col0, ncols, kcn, rhs_fn, TB):
        nch = (ncols + 127) // 128
        pss = [S.ps() for _ in range(nch)]
        for kq in range(0, kcn, 4):
            nk = min(4, kcn - kq)
            w = load_w(t, base, rstride, row0 + kq * 128, col0, ncols, nk, bf=True, wide=True)
            for c in range(nch):
                n = min(128, ncols - c * 128)
                for kk in range(nk):
                    k = kq + kk
                    S.mm(pss[c][0:n, 0:TB], w[:, kk, c * 128:c * 128 + n], rhs_fn(k), start=(k == 0), stop=(k == kcn - 1), r=False)
        return pss

    def win_group(l, col0, ncols, TB):
        return fm_group(I["w_in"], l * D * NIN, NIN, 0, col0, ncols, 16, lambda k: ch(xTb, k, slice(0, TB)), TB)

    def win_proj(l, col0, n, TB):
        return fm_proj(I["w_in"], l * D * NIN, NIN, 0, col0, n, 16, lambda k: ch(xTb, k, slice(0, TB)), TB)

    def transpose_to(dst_view, src_view, rows, cols, eng="dve"):
        p = S.ps()
        S.tr(p[0:cols, 0:rows], src_view, ident[0:rows, 0:rows])
        S.copy(dst_view, p[0:cols, 0:rows], eng=eng)

    def layernorm(l, gname, bname, TB):
        pm = S.ps()
        for k in range(16):
            S.mm(pm[:, 0:TB], ones[:, :], ch(xT, k, slice(0, TB)), start=(k == 0), stop=(k == 15), r=False)
        pq = S.ps()
        for k in range(16):
            sq = tA()
            S.act(sq[:, 0:TB], ch(xT, k, slice(0, TB)), AF.Square)
            S.mm(pq[:, 0:TB], ones[:, :], sq[:, 0:TB], start=(k == 0), stop=(k == 15), r=False)
        mean, rstd, m2 = lnm, lnr, lnm2
        S.ts(mean[:, 0:TB], pm[:, 0:TB], 1.0 / D, ALU.mult)
        S.tt(m2[:, 0:TB], mean[:, 0:TB], mean[:, 0:TB], ALU.mult)
        S.stt(rstd[:, 0:TB], pq[:, 0:TB], 1.0 / D, m2[:, 0:TB], ALU.mult, ALU.subtract)
        S.ts(rstd[:, 0:TB], rstd[:, 0:TB], 0.0, ALU.max, 1e-5, ALU.add)
        S.act(rstd[:, 0:TB], rstd[:, 0:TB], AF.Sqrt)
        S.recip(rstd[:, 0:TB], rstd[:, 0:TB])
        for k in range(16):
            t = tA()
            S.tt(t[:, 0:TB], ch(xT, k, slice(0, TB)), mean[:, 0:TB], ALU.subtract)
            S.tt(t[:, 0:TB], t[:, 0:TB], rstd[:, 0:TB], ALU.mult)
            S.act(ch(xT, k, slice(0, TB)), t[:, 0:TB], AF.Identity, bias=P[l][bname][:, k:k + 1], scale=P[l][gname][:, k:k + 1])
            S.act(ch(xTb, k, slice(0, TB)), t[:, 0:TB], AF.Identity, bias=P[l][bname][:, k:k + 1], scale=P[l][gname][:, k:k + 1])

    def emit_rows(srcT_fn, nch, R, dst_ap_fn):
        for j0 in range(0, nch, 16):
            nj = min(16, nch - j0)
            for j in range(j0, j0 + nj):
                transpose_to(tokbuf[0:R, (j - j0) * 128:(j - j0 + 1) * 128], srcT_fn(j), 128, R)
            S.dma("pool", dst_ap_fn(j0 * 128, nj * 128), tokbuf.t[0:R, 0:nj * 128], reads=[tokbuf[:]])

    class Conv:
        def __init__(self, mode, W, TB):
            self.mode, self.W, self.TB, self.H = mode, W, TB, W - 1
            if mode == "p":
                self.newv = extb[:, self.H:self.H + TB]
                self.histv = extb[:, 0:self.H]
                self.tail = extb[:, TB:TB + self.H]
            else:
                self.newv = exts[:, :, self.H]
                self.histv = exts[:, :, 0:self.H]

        def tap(self, k):
            if self.mode == "p":
                return extb[:, k:k + self.TB]
            return exts[:, :, k]

    def conv_apply(cv, wtile, j, out_view, bias_view=None):
        W = cv.W
        acc = out_view
        if bias_view is not None:
            S.ts(acc, cv.tap(0), wtile[:, j, 0:1], ALU.mult, bias_view, ALU.add)
        else:
            S.ts(acc, cv.tap(0), wtile[:, j, 0:1], ALU.mult)
        for k in range(1, W):
            S.stt(acc, cv.tap(k), wtile[:, j, k:k + 1], acc, ALU.mult, ALU.add)

    def layer_block(l, TB, mode, chunks, last, hsA=None, hsB=None, hsC=None):
        p_ = P[l]
        shp = (lambda v: v)
        if mode == "p":
            ov = lambda buf: buf[:, 0:TB]
        else:
            ov = lambda buf: buf[:, 0:TB]
        first_branch = [True]

        def branch_merge(n):
            for dg in range(4):
                pgs = win_group(l, O_GATE + n * D + dg * 512, 512, TB)
                gts = []
                for c in range(4):
                    gt = tA()
                    S.act(gt[:, 0:TB], pgs[c][:, 0:TB], AF.Sigmoid, bias=bcol(l, O_GATE + n * D + (dg * 4 + c) * 128))
                    gts.append(gt)
                pbs = fm_group(I["w_branch"], (l * 4 + n) * 1024 * D, D, 0, dg * 512, 512, 8, lambda k: ch(outn, k, slice(0, TB)), TB)
                for c in range(4):
                    d = dg * 4 + c
                    pb, gt = pbs[c], gts[c]
                    if first_branch[0]:
                        S.tt(ch(mixed, d, slice(0, TB)), pb[:, 0:TB], gt[:, 0:TB], ALU.mult)
                    else:
                        S.tt(gt[:, 0:TB], pb[:, 0:TB], gt[:, 0:TB], ALU.mult)
                        if n == 3:
                            S.tt(ch(mixedb, d, slice(0, TB)), ch(mixed, d, slice(0, TB)), gt[:, 0:TB], ALU.add)
                        else:
                            S.tt(ch(mixed, d, slice(0, TB)), ch(mixed, d, slice(0, TB)), gt[:, 0:TB], ALU.add)
            first_branch[0] = False

        def newrow_out(vals_fn, nch, dst_t, lbase, H, C):
            emit_rows(vals_fn, nch, NS, lambda c0, n: dap(dst_t, lbase + (H - 1) * C + c0, [[H * C, NS], [1, n]]))

        def ph_C():
            cv = Conv(mode, 3, TB)
            newC = ubuf
            for jg in range(2):
                pbgs = win_group(l, O_BG + jg * 512, 512, TB)
                bgs = []
                for c in range(4):
                    t_ = tA()
                    S.act(t_[:, 0:TB], pbgs[c][:, 0:TB], AF.Identity, bias=bcol(l, O_BG + (jg * 4 + c) * 128))
                    bgs.append(t_)
                pcs = win_group(l, O_CG + jg * 512, 512, TB)
                cgs = []
                for c in range(4):
                    t_ = tA()
                    S.act(t_[:, 0:TB], pcs[c][:, 0:TB], AF.Identity, bias=bcol(l, O_CG + (jg * 4 + c) * 128))
                    cgs.append(t_)
                phs = win_group(l, O_HC + jg * 512, 512, TB)
                for c in range(4):
                    j = jg * 4 + c
                    if mode == "p":
                        S.copy(cv.histv, histC[l][:, j, :], eng="pool")
                    else:
                        S.copy(cv.histv, hsC.sub(j, (slice(None), j, slice(None), slice(None))), eng="pool")
                    S.stt(cv.newv, phs[c][:, 0:TB], bcol(l, O_HC + j * 128), cgs[c][:, 0:TB], ALU.add, ALU.mult)
                    conv_apply(cv, p_["ccw"], j, cgs[c][:, 0:TB])
                    if mode == "p":
                        S.copy(histC[l][:, j, :], cv.tail, eng="pool")
                    else:
                        S.copy(ch(newC, j, slice(0, NS)), cv.newv, eng="pool")
                    S.tt(ch(outn, j, slice(0, TB)), bgs[c][:, 0:TB], cgs[c][:, 0:TB], ALU.mult)
            if mode == "s":
                newrow_out(lambda j: ch(newC, j, slice(0, NS)), 8, O["scs"], l * NS * 2 * 1024, 2, 1024)
            branch_merge(2)

        def ph_B():
            cv = Conv(mode, 31, TB)
            newB = mixed
            for jg in range(2):
                pgs = win_group(l, O_GG + jg * 512, 512, TB)
                sgs = []
                for c in range(4):
                    t_ = tA()
                    S.act(t_[:, 0:TB], pgs[c][:, 0:TB], AF.Sigmoid, bias=bcol(l, O_GG + (jg * 4 + c) * 128))
                    sgs.append(t_)
                pas = win_group(l, O_GA + jg * 512, 512, TB)
                for c in range(4):
                    j = jg * 4 + c
                    if mode == "p":
                        S.copy(cv.histv, histB[l][:, j, :], eng="pool")
                    else:
                        S.copy(cv.histv, hsB.sub(j, (slice(None), j, slice(None), slice(None))), eng="pool")
                    S.stt(cv.newv, pas[c][:, 0:TB], bcol(l, O_GA + j * 128), sgs[c][:, 0:TB], ALU.add, ALU.mult)
                    conv_apply(cv, p_["bcw"], j, ch(ubuf, j, slice(0, TB)), bias_view=p_["b_conv_b"][:, j:j + 1])
                    if mode == "p":
                        S.copy(histB[l][:, j, :], cv.tail, eng="pool")
                    else:
                        S.copy(tmpB_new.sub(j, (slice(None), j, slice(None))), cv.newv, eng="pool")
            if mode == "s":
                newrow_out(lambda j: tmpB_new.sub(j, (slice(None), j, slice(None))), 8, O["gcs"], l * NS * 30 * 1024, 30, 1024)
            pm = S.ps()
            for k in range(8):
                S.mm(pm[:, 0:TB], ones[:, :], ch(ubuf, k, slice(0, TB)), start=(k == 0), stop=(k == 7), r=False)
            pq = S.ps()
            for k in range(8):
                sq = tA()
                S.act(sq[:, 0:TB], ch(ubuf, k, slice(0, TB)), AF.Square)
                S.mm(pq[:, 0:TB], ones[:, :], sq[:, 0:TB], start=(k == 0), stop=(k == 7), r=False)
            mean, rstd, m2 = lnm, lnr, lnm2
            S.ts(mean[:, 0:TB], pm[:, 0:TB], 1.0 / 1024, ALU.mult)
            S.tt(m2[:, 0:TB], mean[:, 0:TB], mean[:, 0:TB], ALU.mult)
            S.stt(rstd[:, 0:TB], pq[:, 0:TB], 1.0 / 1024, m2[:, 0:TB], ALU.mult, ALU.subtract)
            S.ts(rstd[:, 0:TB], rstd[:, 0:TB], 0.0, ALU.max, 1e-5, ALU.add)
            S.act(rstd[:, 0:TB], rstd[:, 0:TB], AF.Sqrt)
            S.recip(rstd[:, 0:TB], rstd[:, 0:TB])
            for k in range(8):
                t = tA()
                S.tt(t[:, 0:TB], ch(ubuf, k, slice(0, TB)), mean[:, 0:TB], ALU.subtract)
                S.tt(t[:, 0:TB], t[:, 0:TB], rstd[:, 0:TB], ALU.mult)
                S.act(ch(outn, k, slice(0, TB)), t[:, 0:TB], AF.Silu, bias=p_["b_ln_b"][:, k:k + 1], scale=p_["b_ln_g"][:, k:k + 1])
            branch_merge(1)

        def ph_A():
            pbd = win_proj(l, O_BD, 16, TB)
            bdT = tA()
            S.act(bdT[0:16, 0:TB], pbd[0:16, 0:TB], AF.Identity, bias=bcol(l, O_BD, 16))
            beta_t, g_t, gc_t = [], [], []
            for (c0, L) in chunks:
                p = S.ps()
                S.tr(p[0:L, 0:16], bdT[0:16, c0:c0 + L], ident[0:16, 0:16])
                ci_a = len(beta_t)
                be = Buf(persA.t[:, ci_a, 0:8], ("persA", ci_a, 0)); gg = Buf(persA.t[:, ci_a, 8:16], ("persA", ci_a, 1))
                gcx = Buf(persA.t[:, ci_a, 16:24], ("persA", ci_a, 2)); tmp = tC()
                S.act(be[0:L, 0:8], p[0:L, 0:8], AF.Sigmoid)
                S.tt(tmp[0:L, 0:8], p[0:L, 8:16], p_["dtb"][0:L, 0:8], ALU.add)
                S.act(tmp[0:L, 0:8], tmp[0:L, 0:8], AF.Exp)
                S.act(tmp[0:L, 0:8], tmp[0:L, 0:8], AF.Ln, bias=1.0)
                S.tt(gg[0:L, 0:8], tmp[0:L, 0:8], p_["negA"][0:L, 0:8], ALU.mult)
                pc = S.ps()
                S.mm(pc[0:L, 0:8], triu[0:L, 0:L], gg[0:L, 0:8], r=False)
                S.copy(gcx[0:L, 0:8], pc[0:L, 0:8])
                beta_t.append(be); g_t.append(gg); gc_t.append(gcx)
            newA = tmpA_new
            for h in range(8):
                qkv = []
                for part, off in enumerate((O_QA, O_KA, O_VA)):
                    jj = part * 8 + h
                    cv = Conv(mode, 4, TB)
                    pp = win_proj(l, off + h * 128, 128, TB)
                    if mode == "p":
                        S.copy(cv.histv, histA[l][:, jj, :], eng="pool")
                    else:
                        S.copy(cv.histv, hsA.sub(jj, (slice(None), jj, slice(None), slice(None))), eng="pool")
                    S.act(cv.newv, pp[:, 0:TB], AF.Identity, bias=bcol(l, off + h * 128))
                    acc = tA()
                    conv_apply(cv, p_["acw"], jj, acc[:, 0:TB])
                    if mode == "p":
                        S.copy(histA[l][:, jj, :], cv.tail, eng="pool")
                    else:
                        S.copy(newA.sub(jj, (slice(None), jj, slice(None))), cv.newv, eng="pool")
                    S.act(acc[:, 0:TB], acc[:, 0:TB], AF.Silu)
                    qkv.append(acc)
                qT_, kT_, vT_ = qkv
                for idx, t_ in enumerate((qT_, kT_)):
                    sq = tA()
                    S.act(sq[:, 0:TB], t_[:, 0:TB], AF.Square)
                    pn = S.ps()
                    S.mm(pn[:, 0:TB], ones[:, :], sq[:, 0:TB], r=False)
                    S.ts(sq[:, 0:TB], pn[:, 0:TB], 1e-6, ALU.add)
                    S.act(sq[:, 0:TB], sq[:, 0:TB], AF.Sqrt)
                    S.recip(sq[:, 0:TB], sq[:, 0:TB])
                    if idx == 0:
                        S.stt(t_[:, 0:TB], t_[:, 0:TB], 128 ** -0.5, sq[:, 0:TB], ALU.mult, ALU.mult)
                    else:
                        S.tt(t_[:, 0:TB], t_[:, 0:TB], sq[:, 0:TB], ALU.mult)
                pz = win_proj(l, O_ZA + h * 128, 128, TB)
                sz = tA()
                S.act(sz[:, 0:TB], pz[:, 0:TB], AF.Silu, bias=bcol(l, O_ZA + h * 128))
                for ci, (c0, L) in enumerate(chunks):
                    if mode == "p":
                        Sh = Sa[l][h]
                    else:
                        Sh = SaS[ssi[0] % 3]; ssi[0] += 1
                        S.dma("sp", Sh.t[:, :], dap(I["sdS"], ((l * NS + ci) * 8 + h) * 16384, [[128, 128], [1, 128]]), writes=[Sh[:]])
                    delta_chunk(l, h, ci, c0, L, qT_, kT_, vT_, sz, Sh, beta_t[ci], g_t[ci], gc_t[ci])
                    if mode == "s":
                        S.dma("pool", dap(O["dSs"], ((l * NS + ci) * 8 + h) * 16384, [[128, 128], [1, 128]]), Sh.t[:, :], reads=[Sh[:]])
                    elif last:
                        if ci == len(chunks) - 1:
                            S.dma("pool", dap(O["dSp"], (l * 8 + h) * 16384, [[128, 128], [1, 128]]), Sh.t[:, :], reads=[Sh[:]])
            if mode == "s":
                newrow_out(lambda j: newA.sub(j, (slice(None), j, slice(None))), 24, O["dcs"], l * NS * 3 * 3072, 3, 3072)
            branch_merge(0)

        def ph_D():
            pif = win_proj(l, O_IF, 8, TB)
            ifT = tA()
            S.act(ifT[0:8, 0:TB], pif[0:8, 0:TB], AF.Identity, bias=bcol(l, O_IF, 8))
            lf_t, b_t, ib_t = [], [], []
            for (c0, L) in chunks:
                p = S.ps()
                S.tr(p[0:L, 0:8], ifT[0:8, c0:c0 + L], ident[0:8, 0:8])
                ci_d = len(lf_t)
                lf = Buf(persD.t[:, ci_d, 0:4], ("persD", ci_d, 0)); bb = Buf(persD.t[:, ci_d, 4:8], ("persD", ci_d, 1))
                ib = Buf(persD.t[:, ci_d, 8:12], ("persD", ci_d, 2))
                S.act(lf[0:L, 0:4], p[0:L, 4:8], AF.Exp, scale=-1.0)
                S.act(lf[0:L, 0:4], lf[0:L, 0:4], AF.Ln, bias=1.0)
                S.ts(lf[0:L, 0:4], lf[0:L, 0:4], -1.0, ALU.mult)
                pc = S.ps()
                S.mm(pc[0:L, 0:4], triu[0:L, 0:L], lf[0:L, 0:4], r=False)
                S.copy(bb[0:L, 0:4], pc[0:L, 0:4])
                S.tt(ib[0:L, 0:4], p[0:L, 0:4], bb[0:L, 0:4], ALU.subtract)
                lf_t.append(lf); b_t.append(bb); ib_t.append(ib)
            for h in range(4):
                pq_ = win_proj(l, O_QD + h * 128, 128, TB)
                qT_ = tA()
                S.act(qT_[:, 0:TB], pq_[:, 0:TB], AF.Identity, bias=bcol(l, O_QD + h * 128))
                pk_ = win_proj(l, O_KD + h * 128, 128, TB)
                kT_ = tA()
                S.ts(kT_[:, 0:TB], pk_[:, 0:TB], bcol(l, O_KD + h * 128), ALU.add, 128 ** -0.5, ALU.mult)
                vT_, so_ = [], []
                for e in range(2):
                    pv_ = win_proj(l, O_VD + h * 256 + e * 128, 128, TB)
                    v_ = tA()
                    S.act(v_[:, 0:TB], pv_[:, 0:TB], AF.Identity, bias=bcol(l, O_VD + h * 256 + e * 128))
                    vT_.append(v_)
                for e in range(2):
                    po_ = win_proj(l, O_OD + h * 256 + e * 128, 128, TB)
                    o_ = tA()
                    S.act(o_[:, 0:TB], po_[:, 0:TB], AF.Sigmoid, bias=bcol(l, O_OD + h * 256 + e * 128))
                    so_.append(o_)
                for ci, (c0, L) in enumerate(chunks):
                    if mode == "p":
                        Ch, ms = Cn[l][h], mst[l]
                    else:
                        Ch = CnS[ssi[0] % 3]; ms = mstS[ssi[0] % 3]; ssi[0] += 1
                        base = (l * NS + ci) * 4 + h
                        S.dma("sp", Ch.t[:, 0:256], dap(I["smC"], base * 32768, [[256, 128], [1, 256]]), writes=[Ch[:]])
                        S.dma("sp", Ch.t[:, 256:257], dap(I["smn"], base * 128, [[1, 128], [1, 1]]), writes=[Ch[:]], allow_slow_non_contiguous=True)
                        S.dma("sp", ms.t[:, h:h + 1], dap(I["smm"], base, [[0, 128], [1, 1]]), writes=[ms[:]])
                    mlstm_chunk(l, h, c0, L, qT_, kT_, vT_, so_, Ch, ms, lf_t[ci], b_t[ci], ib_t[ci])
                    if mode == "s" or (last and ci == len(chunks) - 1):
                        if mode == "s":
                            base = (l * NS + ci) * 4 + h
                            oc, on_, om = O["mCs"], O["mns"], O["mms"]
                        else:
                            base = l * 4 + h
                            oc, on_, om = O["mCp"], O["mnp"], O["mmp"]
                        S.dma("pool", dap(oc, base * 32768, [[256, 128], [1, 256]]), Ch.t[:, 0:256], reads=[Ch[:]])
                        S.dma("pool", dap(on_, base * 128, [[1, 128], [1, 1]]), Ch.t[:, 256:257], reads=[Ch[:]], allow_slow_non_contiguous=True)
                        S.dma("pool", dap(om, base, [[1, 1], [1, 1]]), ms.t[0:1, h:h + 1], reads=[ms[:]])
            branch_merge(3)

        def ph_O():
            for dg in range(4):
                pys = fm_group(I["w_out"], l * D * D, D, 0, dg * 512, 512, 16, lambda k: ch(mixedb, k, slice(0, TB)), TB)
                for c in range(4):
                    d = dg * 4 + c
                    S.stt(ch(xT, d, slice(0, TB)), ch(xT, d, slice(0, TB)), ALPHA, pys[c][:, 0:TB], ALU.mult, ALU.add)
            layernorm(l, "ln1_g", "ln1_b", TB)

        def ph_X():
            pqs = fm_group(I["xq_w"], l * D * 512, 512, 0, 0, 512, 16, lambda k: ch(xTb, k, slice(0, TB)), TB)
            for h in range(4):
                S.copy(ch(qTb, h, slice(0, TB)), pqs[h][:, 0:TB], eng="act")
            for ci, (c0, L) in enumerate(chunks):
                if mode == "p":
                    KTl, Vl = KT[l], Vt[l]
                else:
                    KTl, Vl = KTs, Vts
                    S.dma("sp", Kts.t[:, :, :], dap(I["cmk"], (l * NS + ci) * 256 * 512, [[512, 128], [128 * 512, 2], [1, 512]]), writes=[Kts[:]])
                    S.dma("sp", Vts.t[:, :, :], dap(I["cmv"], (l * NS + ci) * 256 * 512, [[512, 128], [128 * 512, 2], [1, 512]]), writes=[Vts[:]])
                    for hh in range(4):
                        for mc in range(2):
                            transpose_to(KTs[:, hh, mc * 128:(mc + 1) * 128], Kts[:, mc, hh * 128:(hh + 1) * 128], 128, 128, eng="act")
                otok = tokbuf
                for h in range(4):
                    ps_ = S.ps()
                    S.mm(ps_[0:L, 0:256], ch(qTb, h, slice(c0, c0 + L)), KTl[:, h, :], r=False)
                    mx = tC()
                    S.rmax(mx[0:L, 0:1], ps_[0:L, 0:256])
                    S.ts(mx[0:L, 1:2], mx[0:L, 0:1], -(128 ** -0.5), ALU.mult)
                    es = tA()
                    S.act(es[0:L, 0:256], ps_[0:L, 0:256], AF.Exp, bias=mx[0:L, 1:2], scale=128 ** -0.5, accum=mx[0:L, 2:3])
                    S.recip(mx[0:L, 3:4], mx[0:L, 2:3])
                    aT = tA()
                    for mc in range(2):
                        transpose_to(aT[:, mc * 128:mc * 128 + L], es[0:L, mc * 128:(mc + 1) * 128], L, 128, eng="act")
                    po = S.ps()
                    for mc in range(2):
                        S.mm(po[0:L, 0:128], aT[:, mc * 128:mc * 128 + L], Vl[:, mc, h * 128:(h + 1) * 128], start=(mc == 0), stop=(mc == 1), r=False)
                    S.ts(otok[0:L, h * 128:(h + 1) * 128], po[0:L, 0:128], mx[0:L, 3:4], ALU.mult)
                for h in range(4):
                    transpose_to(ch(oTb, h, slice(c0, c0 + L)), otok[0:L, h * 128:(h + 1) * 128], L, 128, eng="act")
            for dg in range(4):
                pys = fm_group(I["xo_w"], l * 512 * D, D, 0, dg * 512, 512, 4, lambda k: ch(oTb, k, slice(0, TB)), TB)
                for c in range(4):
                    d = dg * 4 + c
                    S.stt(ch(xT, d, slice(0, TB)), ch(xT, d, slice(0, TB)), ALPHA, pys[c][:, 0:TB], ALU.mult, ALU.add)
            layernorm(l, "ln2_g", "ln2_b", TB)

        def ph_M():
            for g in range(8):
                for jg2 in range(2):
                    phs = fm_group(I["ffn_w1"], l * D * 4 * D, 4 * D, 0, (g * 8 + jg2 * 4) * 128, 512, 16, lambda k: ch(xTb, k, slice(0, TB)), TB)
                    for c in range(4):
                        jj = jg2 * 4 + c
                        j = g * 8 + jj
                        r_ = tA()
                        S.act(r_[:, 0:TB], phs[c][:, 0:TB], AF.Relu, bias=p_["ffn_b1"][:, j:j + 1])
                        S.act(ch(hb, jj, slice(0, TB)), r_[:, 0:TB], AF.Square)
                for dg in range(4):
                    pds = fm_group(I["ffn_w2"], l * 4 * D * D, D, g * 1024, dg * 512, 512, 8, lambda k: ch(hb, k, slice(0, TB)), TB)
                    for c in range(4):
                        d = dg * 4 + c
                        if g == 0:
                            S.copy(ch(mixed, d, slice(0, TB)), pds[c][:, 0:TB], eng="act")
                        else:
                            S.tt(ch(mixed, d, slice(0, TB)), ch(mixed, d, slice(0, TB)), pds[c][:, 0:TB], ALU.add)
            for d in range(16):
                S.ts(ch(mixed, d, slice(0, TB)), ch(mixed, d, slice(0, TB)), p_["ffn_b2"][:, d:d + 1], ALU.add)
                S.stt(ch(xT, d, slice(0, TB)), ch(xT, d, slice(0, TB)), ALPHA, ch(mixed, d, slice(0, TB)), ALU.mult, ALU.add)
            layernorm(l, "ln3_g", "ln3_b", TB)

        PH = CFG.get('phases', 'CBADOXM')
        if 'C' in PH:
            ph_C(); dump('C', outn, 8)
        if 'B' in PH:
            ph_B(); dump('B', outn, 8)
        if 'A' in PH:
            ph_A(); dump('A', outn, 8)
        if 'D' in PH:
            ph_D(); dump('D', outn, 8); dump('mixed', mixed, 16)
        if 'O' in PH:
            ph_O(); dump('x1', xT, 16)
        if 'X' in PH:
            ph_X(); dump('x2', xT, 16)
        if 'M' in PH:
            ph_M(); dump('x3', xT, 16)

    def delta_chunk(l, h, ci, c0, L, qT_, kT_, vT_, sz, Sh, be, gg, gcx):
        p_ = P[l]
        beta_c, g_c, gc_c = be[0:L, h:h + 1], gg[0:L, h:h + 1], gcx[0:L, h:h + 1]
        kc, qc, vc = kT_[:, c0:c0 + L], qT_[:, c0:c0 + L], vT_[:, c0:c0 + L]
        gb = tQ()
        S.ts(gb[0:L, :], ones[0:L, :], g_c, ALU.mult)
        pg = S.ps()
        S.mm(pg[:, 0:L], gb[0:L, :], triu[0:L, 0:L], r=False)
        gcr = tQL(); egr = tQL()
        S.copy(gcr[:, 0:L], pg[:, 0:L], eng="act")
        S.act(egr[:, 0:L], pg[:, 0:L], AF.Exp)
        cols = tC()
        eg_c, ekl_c, nb_c, nbk_c = cols[0:L, 0:1], cols[0:L, 1:2], cols[0:L, 2:3], cols[0:L, 3:4]
        S.act(eg_c, gc_c, AF.Exp)
        S.act(ekl_c, gc_c, AF.Exp, bias=gcr[0:L, L - 1:L], scale=-1.0)
        S.ts(nb_c, beta_c, -1.0, ALU.mult)
        S.tt(nbk_c, nb_c, ekl_c, ALU.mult)
        ktok = tQL(); vtok = tQL()
        transpose_to(ktok[0:L, :], kc, 128, L, eng="act")
        transpose_to(vtok[0:L, :], vc, 128, L, eng="act")
        pk = S.ps()
        S.mm(pk[0:L, 0:L], kc, kc, r=False)
        S.mm(pk[0:L, 128:128 + L], kc, qc, r=False)
        qkd = tQL()
        if L > 1:
            Dm = tQ(); Es = tQ(); B0 = tQ(); Ei = tQ()
            S.stt(Dm[0:L, 0:L], gcr[0:L, 0:L], gc_c, maskS[0:L, 0:L], ALU.subtract, ALU.add)
            S.act(Es[0:L, 0:L], Dm[0:L, 0:L], AF.Exp)
            S.stt(B0[0:L, 0:L], pk[0:L, 0:L], beta_c, Es[0:L, 0:L], ALU.mult, ALU.mult)
            S.tt(Ei[0:L, 0:L], Es[0:L, 0:L], ident[0:L, 0:L], ALU.add, eng="pool")
            S.tt(qkd[0:L, 0:L], pk[0:L, 128:128 + L], Ei[0:L, 0:L], ALU.mult)
        else:
            S.copy(qkd[0:L, 0:L], pk[0:L, 128:128 + L], eng="act")
        pks = S.ps()
        S.mm(pks[0:L, 0:128], kc, Sh[:, :], r=False)
        rneg = tQL()
        S.stt(rneg[0:L, :], pks[0:L, 0:128], eg_c, vtok[0:L, :], ALU.mult, ALU.subtract)
        dl = tQL(); dk = tQL()
        if L > 1:
            Bp = B0
            Ap = tQ()
            transpose_to(Ap[0:L, 0:L], B0[0:L, 0:L], L, L, eng="act")
            U = tQ(); Lw = tQ()
            S.tt(U[0:L, 0:L], ident[0:L, 0:L], Bp[0:L, 0:L], ALU.subtract)
            S.tt(Lw[0:L, 0:L], ident[0:L, 0:L], Ap[0:L, 0:L], ALU.subtract, eng="pool")
            nlev = 6
            for j in range(1, nlev + 1):
                pB = S.ps()
                S.mm(pB[0:L, 0:L], Ap[0:L, 0:L], Bp[0:L, 0:L], r=False)
                Bn = tQ()
                S.copy(Bn[0:L, 0:L], pB[0:L, 0:L], eng="act")
                An = None
                if j < nlev:
                    pA = S.ps()
                    S.mm(pA[0:L, 0:L], Bp[0:L, 0:L], Ap[0:L, 0:L], r=False)
                    An = tQ()
                    S.copy(An[0:L, 0:L], pA[0:L, 0:L])
                pU = S.ps()
                S.mm(pU[0:L, 0:L], Lw[0:L, 0:L], Bn[0:L, 0:L], r=False)
                Un = tQ()
                S.tt(Un[0:L, 0:L], U[0:L, 0:L], pU[0:L, 0:L], ALU.add)
                if j < nlev:
                    pL = S.ps()
                    S.mm(pL[0:L, 0:L], U[0:L, 0:L], An[0:L, 0:L], r=False)
                    Ln_ = tQ()
                    S.tt(Ln_[0:L, 0:L], Lw[0:L, 0:L], pL[0:L, 0:L], ALU.add)
                    Lw = Ln_
                    Ap = An
                U = Un
                Bp = Bn
            pT = S.ps()
            S.mm(pT[0:L, 0:128], U[0:L, 0:L], rneg[0:L, :], r=False)
            S.ts(dl[0:L, :], pT[0:L, 0:128], nb_c, ALU.mult)
            S.act(dk[0:L, :], pT[0:L, 0:128], AF.Copy, scale=nbk_c) if False else S.ts(dk[0:L, :], pT[0:L, 0:128], nbk_c, ALU.mult)
        else:
            S.ts(dl[0:L, :], rneg[0:L, :], nb_c, ALU.mult)
            S.ts(dk[0:L, :], rneg[0:L, :], nbk_c, ALU.mult)
        qd = tQL()
        S.tt(qd[:, 0:L], qc, egr[:, 0:L], ALU.mult, eng="pool")
        po = S.ps()
        S.mm(po[0:L, 0:128], qd[:, 0:L], Sh[:, :], start=True, stop=False, r=False)
        S.mm(po[0:L, 0:128], qkd[0:L, 0:L], dl[0:L, :], start=False, stop=True, r=False)
        pS = S.ps()
        S.mm(pS[:, 0:128], ktok[0:L, :], dk[0:L, :], r=False)
        S.stt(Sh[:, :], Sh[:, :], egr[:, L - 1:L], pS[:, 0:128], ALU.mult, ALU.add)
        junk = tQ(); c2 = tC()
        S.act(junk[0:L, :], po[0:L, 0:128], AF.Square, accum=c2[0:L, 0:1])
        S.ts(c2[0:L, 1:2], c2[0:L, 0:1], 1.0 / 128, ALU.mult, 1e-6, ALU.add)
        S.act(c2[0:L, 1:2], c2[0:L, 1:2], AF.Sqrt)
        S.recip(c2[0:L, 2:3], c2[0:L, 1:2])
        on_ = tQL()
        S.ts(on_[0:L, :], po[0:L, 0:128], c2[0:L, 2:3], ALU.mult)
        pT2 = S.ps()
        S.tr(pT2[:, 0:L], on_[0:L, :], ident[0:L, 0:L])
        S.stt(outn.sub(h, (slice(None), h, slice(c0, c0 + L))), psrc(pT2[:, 0:L], L, 128), p_["a_norm_w"][:, 0:1], sz[:, c0:c0 + L], ALU.mult, ALU.mult)

    def mlstm_chunk(l, h, c0, L, qT_, kT_, vT_, so_, Ch, ms, lf, bb, ib):
        p_ = P[l]
        b_c, ib_c, lf_c = bb[0:L, h:h + 1], ib[0:L, h:h + 1], lf[0:L, h:h + 1]
        qc, kc = qT_[:, c0:c0 + L], kT_[:, c0:c0 + L]
        ibb = tQ(); lfb = tQ()
        S.ts(ibb[0:L, :], ones[0:L, :], ib_c, ALU.mult)
        if CFG.get('dstop', 999) == 1: return
        S.ts(lfb[0:L, :], ones[0:L, :], lf_c, ALU.mult)
        if CFG.get('dstop', 999) == 2: return
        pr = S.ps()
        S.mm(pr[:, 0:L], ibb[0:L, :], ident[0:L, 0:L], r=False)
        if CFG.get('dstop', 999) == 3: return
        S.mm(pr[:, 128:128 + L], lfb[0:L, :], triu[0:L, 0:L], r=False)
        if CFG.get('dstop', 999) == 4: return
        ibr = tQ()
        S.copy(ibr[:, 0:L], pr[:, 0:L], eng="act")
        if CFG.get('dstop', 999) == 5: return
        cb = tC()
        S.act(cb[:, 0:1], pr[:, 128 + L - 1:128 + L], AF.Identity)
        if CFG.get('dstop', 999) == 6: return
        S.rmax(cb[:, 1:2], ibr[:, 0:L])
        if CFG.get('dstop', 999) == 7: return
        S.tt(cb[:, 1:2], cb[:, 1:2], cb[:, 0:1], ALU.add)
        if CFG.get('dstop', 999) == 8: return
        S.tt(cb[:, 2:3], cb[:, 0:1], ms[:, h:h + 1], ALU.add)
        if CFG.get('dstop', 999) == 9: return
        S.tt(cb[:, 3:4], cb[:, 2:3], cb[:, 1:2], ALU.max)
        if CFG.get('dstop', 999) == 10: return
        S.tt(cb[:, 5:6], cb[:, 2:3], cb[:, 3:4], ALU.subtract)
        if CFG.get('dstop', 999) == 11: return
        S.act(cb[:, 4:5], cb[:, 5:6], AF.Exp)
        if CFG.get('dstop', 999) == 12: return
        S.tt(cb[:, 5:6], cb[:, 0:1], cb[:, 3:4], ALU.subtract)
        if CFG.get('dstop', 999) == 13: return
        cc = tC()
        S.tt(cc[0:L, 0:1], b_c, ms[0:L, h:h + 1], ALU.add)
        if CFG.get('dstop', 999) == 14: return
        ld = tQ()
        S.stt(ld[0:L, 0:L], ibr[0:L, 0:L], b_c, maskL[0:L, 0:L], ALU.add, ALU.add)
        if CFG.get('dstop', 999) == 15: return
        S.rmax(cc[0:L, 1:2], ld[0:L, 0:L])
        if CFG.get('dstop', 999) == 16: return
        S.tt(cc[0:L, 2:3], cc[0:L, 0:1], cc[0:L, 1:2], ALU.max)
        if CFG.get('dstop', 999) == 17: return
        S.ts(cc[0:L, 3:4], cc[0:L, 2:3], -1.0, ALU.mult)
        if CFG.get('dstop', 999) == 18: return
        S.act(cc[0:L, 4:5], cc[0:L, 0:1], AF.Exp, bias=cc[0:L, 3:4])
        if CFG.get('dstop', 999) == 19: return
        S.act(cc[0:L, 5:6], cc[0:L, 3:4], AF.Exp)
        if CFG.get('dstop', 999) == 20: return
        S.act(cc[0:L, 6:7], ib_c, AF.Exp, bias=cb[0:L, 5:6])
        if CFG.get('dstop', 999) == 21: return
        ed = tQ()
        S.act(ed[0:L, 0:L], ld[0:L, 0:L], AF.Exp, bias=cc[0:L, 3:4])
        if CFG.get('dstop', 999) == 22: return
        pqk = S.ps()
        S.mm(pqk[0:L, 0:L], qc, kc, r=False)
        if CFG.get('dstop', 999) == 23: return
        dm = tQ()
        S.tt(dm[0:L, 0:L], ed[0:L, 0:L], psrc(pqk[0:L, 0:L], L, L), ALU.mult)
        if CFG.get('dstop', 999) == 24: return
        dmT = tQ()
        transpose_to(dmT[0:L, 0:L], dm[0:L, 0:L], L, L, eng="act")
        if CFG.get('dstop', 999) == 25: return
        va = t258n()
        for e in range(2):
            transpose_to(va[0:L, e * 128:(e + 1) * 128], vT_[e][:, c0:c0 + L], 128, L, eng="act")
        S.memset(va[0:L, 256:257], 1.0)
        if CFG.get('dstop', 999) == 26: return
        S.memset(va[0:L, 257:258], 0.0)
        if CFG.get('dstop', 999) == 27: return
        ktok = tQ()
        transpose_to(ktok[0:L, :], kc, 128, L)
        if CFG.get('dstop', 999) == 28: return
        p1 = S.ps()
        S.mm(p1[0:L, 0:258], qc, Ch[:, :], r=False)
        if CFG.get('dstop', 999) == 29: return
        t1 = t258n()
        S.ts(t1[0:L, :], p1[0:L, 0:258], cc[0:L, 4:5], ALU.mult)
        if CFG.get('dstop', 999) == 30: return
        p2 = S.ps()
        S.mm(p2[0:L, 0:258], dmT[0:L, 0:L], va[0:L, :], r=False)
        if CFG.get('dstop', 999) == 31: return
        S.tt(t1[0:L, :], t1[0:L, :], p2[0:L, 0:258], ALU.add)
        if CFG.get('dstop', 999) == 32: return
        c3 = tC()
        S.act(c3[0:L, 5:6], t1[0:L, 256:257], AF.Abs)
        if CFG.get('dstop', 999) == 33: return
        S.tt(c3[0:L, 0:1], c3[0:L, 5:6], cc[0:L, 5:6], ALU.max)
        if CFG.get('dstop', 999) == 34: return
        S.recip(c3[0:L, 1:2], c3[0:L, 0:1])
        if CFG.get('dstop', 999) == 35: return
        hh = t258n()
        S.ts(hh[0:L, 0:256], t1[0:L, 0:256], c3[0:L, 1:2], ALU.mult)
        if CFG.get('dstop', 999) == 36: return
        S.act(t1[0:L, 0:256], hh[0:L, 0:256], AF.Square, accum=c3[0:L, 2:3])
        if CFG.get('dstop', 999) == 37: return
        S.ts(c3[0:L, 3:4], c3[0:L, 2:3], 1.0 / 256, ALU.mult, 1e-6, ALU.add)
        if CFG.get('dstop', 999) == 38: return
        S.act(c3[0:L, 3:4], c3[0:L, 3:4], AF.Sqrt)
        if CFG.get('dstop', 999) == 39: return
        S.recip(c3[0:L, 4:5], c3[0:L, 3:4])
        if CFG.get('dstop', 999) == 40: return
        S.ts(hh[0:L, 0:256], hh[0:L, 0:256], c3[0:L, 4:5], ALU.mult)
        if CFG.get('dstop', 999) == 41: return
        for e in range(2):
            pT = S.ps()
            S.tr(pT[:, 0:L], hh[0:L, e * 128:(e + 1) * 128], ident[0:L, 0:L])
            S.stt(outn.sub(2 * h + e, (slice(None), 2 * h + e, slice(c0, c0 + L))), psrc(pT[:, 0:L], L, 128), p_["d_norm_w"][:, e:e + 1], so_[e][:, c0:c0 + L], ALU.mult, ALU.mult)
        S.ts(va[0:L, :], va[0:L, :], cc[0:L, 6:7], ALU.mult)
        if CFG.get('dstop', 999) == 42: return
        pC = S.ps()
        S.mm(pC[:, 0:258], ktok[0:L, :], va[0:L, :], r=False)
        if CFG.get('dstop', 999) == 43: return
        S.stt(Ch[:, :], Ch[:, :], cb[:, 4:5], pC[:, 0:258], ALU.mult, ALU.add)
        if CFG.get('dstop', 999) == 44: return
        S.copy(ms[:, h:h + 1], cb[:, 3:4])
        if CFG.get('dstop', 999) == 45: return


    memT = mixed
    for t in range(2):
        S.dma("sp", tokbuf.t[:, :], dap(I["memp"], t * 128 * D, [[D, 128], [1, D]]), writes=[tokbuf[:]])
        for k in range(16):
            transpose_to(memT.sub(k, (slice(None), k, slice(t * 128, (t + 1) * 128))), tokbuf[:, k * 128:(k + 1) * 128], 128, 128)
    for l in range(2 if CFG.get('kv', True) else 0):
        Ktok = Buf(ubuf.t[:, 0:4, :].rearrange("p (m x) b -> p m (x b)", m=2), "Ktok")
        for which, wname, oname in ((0, "xk_w", "mkp"), (1, "xv_w", "mvp")):
            for h in range(4):
                w = load_w(I[wname], l * D * 512, 512, 0, h * 128, 128, 16)
                for mc in range(2):
                    p = S.ps()
                    for k in range(16):
                        S.mm(p[:, 0:128], memT.sub(k, (slice(None), k, slice(mc * 128, (mc + 1) * 128))), w[:, k, 0:128], start=(k == 0), stop=(k == 15), r=False)
                    if which == 0:
                        S.copy(Ktok.sub(mc, (slice(None), mc, slice(h * 128, (h + 1) * 128))), p[:, 0:128])
                        transpose_to(KT[l][:, h, mc * 128:(mc + 1) * 128], Ktok.sub(mc, (slice(None), mc, slice(h * 128, (h + 1) * 128))), 128, 128)
                    else:
                        S.copy(Vt[l][:, mc, h * 128:(h + 1) * 128], p[:, 0:128])
            src = Ktok if which == 0 else Vt[l]
            rd = [Ktok.sub(0, (slice(None), 0, slice(None))), Ktok.sub(1, (slice(None), 1, slice(None)))] if which == 0 else [Vt[l][:]]
            S.dma("pool", dap(O[oname], l * 256 * 512, [[512, 128], [128 * 512, 2], [1, 512]]), src.t[:, 0:2, 0:512], reads=rd)

    S.barrier()
    NBLK = CFG.get('nblk', 2048 // TBP)
    NLAY = CFG.get('layers', 2)
    chunks_p = [(c * 128, 128) for c in range(TBP // 128)]
    for blk in range(NBLK):
        for t in range(TBP // 128):
            S.dma("sp", tokbuf.t[:, :], dap(I["xp"], (blk * TBP + t * 128) * D, [[D, 128], [1, D]]), writes=[tokbuf[:]])
            for k in range(16):
                transpose_to(ch(xT, k, slice(t * 128, (t + 1) * 128)), tokbuf[:, k * 128:(k + 1) * 128], 128, 128, eng=("act" if k % 2 else "dve"))
                S.copy(ch(xTb, k, slice(t * 128, (t + 1) * 128)), ch(xT, k, slice(t * 128, (t + 1) * 128)), eng="pool")
        for l in range(NLAY):
            layer_block(l, TBP, "p", chunks_p, (blk == NBLK - 1) and CFG.get("stateout", True))
        for t in range(TBP // 128):
            for k in range(16):
                transpose_to(tokbuf[:, k * 128:(k + 1) * 128], ch(xT, k, slice(t * 128, (t + 1) * 128)), 128, 128, eng=("act" if k % 2 else "dve"))
            S.dma("pool", dap(O["yp"], (blk * TBP + t * 128) * D, [[D, 128], [1, D]]), tokbuf.t[:, :], reads=[tokbuf[:]])
    for l in range(2):
        emit_rows(lambda j: histA[l][:, j, :], 24, 3, lambda c0, n: dap(O["dcp"], l * 3 * 3072 + c0, [[3072, 3], [1, n]]))
        emit_rows(lambda j: histB[l][:, j, :], 8, 30, lambda c0, n: dap(O["gcp"], l * 30 * 1024 + c0, [[1024, 30], [1, n]]))
        emit_rows(lambda j: histC[l][:, j, :], 8, 2, lambda c0, n: dap(O["scp"], l * 2 * 1024 + c0, [[1024, 2], [1, n]]))

    if CFG.get('sample', True):
        S.dma("sp", tokbuf.t[0:NS, :], dap(I["xs"], 0, [[D, NS], [1, D]]), writes=[tokbuf[:]])
        for k in range(16):
            transpose_to(ch(xT, k, slice(0, NS)), tokbuf[0:NS, k * 128:(k + 1) * 128], NS, 128)
            S.copy(ch(xTb, k, slice(0, NS)), ch(xT, k, slice(0, NS)), eng="pool")
        S.barrier()
        for i in range(3):
            S.memset(CnS[i][:], 0.0)
        chunks_s = [(s, 1) for s in range(NS)]
        for l in range(NLAY):
            for (src, H, C, hs, dst) in ((I["sdc"], 3, 3072, hsA, O["dcs"]), (I["sgc"], 30, 1024, hsB, O["gcs"]), (I["ssc"], 2, 1024, hsC, O["scs"])):
                S.dma("pool", dap(dst, l * NS * H * C, [[H * C, NS], [C, H - 1], [1, C]]), dap(src, l * NS * H * C + C, [[H * C, NS], [C, H - 1], [1, C]]))
                spt = max(1, min(NS, 128 // H))
                for s0 in range(0, NS, spt):
                    ns_ = min(spt, NS - s0)
                    R = ns_ * H
                    for cc0 in range(0, C, D):
                        ncol = min(D, C - cc0)
                        S.dma("sp", tokbuf.t[0:R, 0:ncol], dap(src, (l * NS + s0) * H * C + cc0, [[C, R], [1, ncol]]), writes=[tokbuf[:]])
                        for jj in range(ncol // 128):
                            j = cc0 // 128 + jj
                            p = S.ps()
                            S.tr(p[:, 0:R], tokbuf[0:R, jj * 128:(jj + 1) * 128], ident[0:R, 0:R])
                            S.copy(hs.sub(j, (slice(None), j, slice(s0, s0 + ns_), slice(None))), p[:, 0:R].ap.rearrange("p (s h) -> p s h", h=H) if False else View(p.key, p.t[:, 0:R].rearrange("p (s h) -> p s h", h=H)))
            layer_block(l, NS, "s", chunks_s, False, hsA, hsB, hsC)
        for k in range(16):
            transpose_to(tokbuf[0:NS, k * 128:(k + 1) * 128], ch(xT, k, slice(0, NS)), 128, NS)
        S.dma("pool", dap(O["ys"], 0, [[D, NS], [1, D]]), tokbuf.t[0:NS, :], reads=[tokbuf[:]])
    S.finish()
    return nc


_NC = [None]


def kernel(**inp):
    a = {k: np.ascontiguousarray(np.asarray(v, dtype=np.float32)) for k, v in inp.items()}
    if _NC[0] is None:
        _NC[0] = build()
    nc = _NC[0]
    wnames = ["w_in", "b_in", "a_conv_w", "a_A_log", "a_dt_bias", "a_norm_w", "b_conv_w", "b_conv_b", "b_ln_g", "b_ln_b",
              "c_conv_w", "d_norm_w", "w_branch", "w_out", "ln1_g", "ln1_b", "xq_w", "xk_w", "xv_w", "xo_w", "ln2_g", "ln2_b",
              "ffn_w1", "ffn_b1", "ffn_w2", "ffn_b2", "ln3_g", "ln3_b"]
    in_maps = []
    for c in range(8):
        b = c % 4
        sl = slice(c * NS, (c + 1) * NS)
        m = {"xp": a["x_prompt"][b], "xs": a["x_sample"][sl, 0, :], "memp": a["mem_prompt"][b],
             "cmk": a["cache_mem_k"][:, sl].reshape(2, NS, 256, 512), "cmv": a["cache_mem_v"][:, sl].reshape(2, NS, 256, 512),
             "sdc": a["state_delta_conv"][:, sl], "sdS": a["state_delta_S"][:, sl], "sgc": a["state_glu_conv"][:, sl],
             "ssc": a["state_short_conv"][:, sl], "smC": a["state_mlstm_C"][:, sl], "smn": a["state_mlstm_n"][:, sl],
             "smm": a["state_mlstm_m"][:, sl]}
        m = {k: np.ascontiguousarray(v) for k, v in m.items()}
        for w in wnames:
            m[w] = a[w]
        in_maps.append(m)
    res = run_bass_kernel_spmd(nc, in_maps, core_ids=list(range(8)))
    R = res.results

    def pst(name, shape):
        return np.stack([np.asarray(R[b][name]) for b in range(4)], axis=1).reshape(shape)

    def sst(name, shape):
        return np.concatenate([np.asarray(R[c][name]) for c in range(8)], axis=1).reshape(shape)

    y_prompt = np.stack([np.asarray(R[b]["yp"]) for b in range(4)], axis=0)
    y_sample = np.concatenate([np.asarray(R[c]["ys"]) for c in range(8)], axis=0).reshape(128, 1, D)
    outs = (y_prompt, y_sample,
            pst("mkp", (2, 4, 256, 4, 128)), pst("mvp", (2, 4, 256, 4, 128)),
            pst("dcp", (2, 4, 3, 3072)), pst("dSp", (2, 4, 8, 128, 128)), pst("gcp", (2, 4, 30, 1024)),
            pst("scp", (2, 4, 2, 1024)), pst("mCp", (2, 4, 4, 128, 256)), pst("mnp", (2, 4, 4, 128)), pst("mmp", (2, 4, 4)),
            sst("dcs", (2, 128, 3, 3072)), sst("dSs", (2, 128, 8, 128, 128)), sst("gcs", (2, 128, 30, 1024)),
            sst("scs", (2, 128, 2, 1024)), sst("mCs", (2, 128, 4, 128, 256)), sst("mns", (2, 128, 4, 128)), sst("mms", (2, 128, 4)))
    return tuple(np.ascontiguousarray(o.astype(np.float32)) for o in outs)
```

```python
import contextlib
import numpy as np
import concourse.bass as bass
import concourse.mybir as mybir
from concourse.bass_utils import run_bass_kernel_spmd

F32 = mybir.dt.float32
F32R = mybir.dt.float32r
BF16 = mybir.dt.bfloat16
AF = mybir.ActivationFunctionType
ALU = mybir.AluOpType
AX = mybir.AxisListType

EPOCH = 20000
NDS = 8


class View:
    __slots__ = ("key", "ap")

    def __init__(self, key, ap):
        self.key = key
        self.ap = ap


class Buf:
    def __init__(self, t, key):
        self.t = t
        self.key = key

    def __getitem__(self, idx):
        return View(self.key, self.t[idx])

    def sub(self, subkey, idx):
        return View((self.key, subkey), self.t[idx])


class Sched:
    ENG = ["pe", "act", "dve", "pool", "sp"]

    def __init__(self, nc, dry=False):
        self.nc = nc
        self.dry = dry
        self.stack = contextlib.ExitStack()
        self.prog = {e: [] for e in self.ENG}
        self.count = {e: 0 for e in self.ENG}
        self.waited = {e: {} for e in self.ENG}
        self.last_w = {}
        self.readers = {}
        self.dslot = {e: 0 for e in self.ENG}
        self.dval = {}
        self.sids = {}
        self.nbuf = 0
        self.psum_banks = []
        self.psum_i = 0
        self.nops = 0

    def sb(self, shape, dtype=F32, name=None):
        self.nbuf += 1
        name = name or f"sb{self.nbuf}"
        t = self.stack.enter_context(self.nc.sbuf_tensor(name, list(shape), dtype))
        return Buf(t, name)

    def init_psum(self, n=8):
        for i in range(n):
            t = self.stack.enter_context(self.nc.psum_tensor(f"ps{i}", [128, 512], F32))
            self.psum_banks.append(Buf(t, f"ps{i}"))

    def ps(self):
        b = self.psum_banks[self.psum_i % len(self.psum_banks)]
        self.psum_i += 1
        return b

    def _deps(self, eng, reads, writes):
        deps = set()
        for v in reads:
            t = self.last_w.get(v.key)
            if t is not None:
                deps.add(t)
        for v in writes:
            t = self.last_w.get(v.key)
            if t is not None:
                deps.add(t)
            for r in self.readers.get(v.key, ()):
                if r[2] != eng or r[2] == "dma":
                    deps.add(r)
        for (sid, val, deng) in sorted(deps, key=lambda d: str(d)):
            if deng == eng and eng == "pe":
                continue
            if self.waited[eng].get(sid, 0) >= val:
                continue
            self.waited[eng][sid] = val
            self.prog[eng].append(("wait", sid, val))

    def _commit(self, tok, reads, writes):
        for v in reads:
            self.readers.setdefault(v.key, []).append(tok)
        for v in writes:
            self.last_w[v.key] = tok
            self.readers[v.key] = []

    def op(self, eng, fn, reads=(), writes=()):
        if self.dry:
            return None
        self._deps(eng, reads, writes)
        n = self.count[eng]
        self.count[eng] += 1
        sid = ("c", eng, n // EPOCH)
        val = n % EPOCH + 1
        self.sids[sid] = 1
        tok = (sid, val, eng)
        self.prog[eng].append(("op", fn, sid, 1))
        self._commit(tok, reads, writes)
        self.nops += 1
        return tok

    def dma(self, q, out_ap, in_ap, reads=(), writes=(), **kw):
        if self.dry:
            return None
        eng = q
        self._deps(eng, reads, writes)
        slot = self.dslot[q]
        self.dslot[q] = (slot + 1) % NDS
        sid = ("d", q, slot)
        self.sids[sid] = 1
        prev = self.dval.get(sid, 0)
        if prev > 0 and self.waited[eng].get(sid, 0) < prev:
            self.waited[eng][sid] = prev
            self.prog[eng].append(("wait", sid, prev))
        val = prev + 16
        self.dval[sid] = val
        tok = (sid, val, "dma")
        self.prog[eng].append(("op", lambda e: e.dma_start(out_ap, in_ap, **kw), sid, 16))
        self._commit(tok, reads, writes)
        self.nops += 1
        return tok

    def barrier(self):
        if self.dry:
            return
        toks = []
        for e in self.ENG:
            n = self.count[e]
            if n > 0:
                toks.append((("c", e, (n - 1) // EPOCH), (n - 1) % EPOCH + 1, e))
        for sid, val in self.dval.items():
            toks.append((sid, val, "dma"))
        for e in self.ENG:
            for (sid, val, deng) in toks:
                if deng == e:
                    continue
                if self.waited[e].get(sid, 0) >= val:
                    continue
                self.waited[e][sid] = val
                self.prog[e].append(("wait", sid, val))

    def finish(self):
        if self.dry:
            self.stack.close()
            return
        for sid, val in self.dval.items():
            q = sid[1]
            if self.waited[q].get(sid, 0) < val:
                self.prog[q].append(("wait", sid, val))
        nc = self.nc
        sems = {}
        for i, sid in enumerate(self.sids):
            sems[sid] = self.stack.enter_context(nc.semaphore(f"s{i}"))
        prog = self.prog

        def replay(name, e):
            for it in prog[name]:
                if it[0] == "wait":
                    e.wait_ge(sems[it[1]], it[2])
                else:
                    it[1](e).then_inc(sems[it[2]], it[3])

        with nc.Block() as block:
            @block.tensor
            def _(e):
                replay("pe", e)

            @block.scalar
            def _(e):
                replay("act", e)

            @block.vector
            def _(e):
                replay("dve", e)

            @block.gpsimd
            def _(e):
                replay("pool", e)

            @block.sync
            def _(e):
                replay("sp", e)
        self.stack.close()

    def mm(self, out, lhsT, rhs, start=True, stop=True, r=False):
        la, ra = lhsT.ap, rhs.ap
        if r:
            la, ra = la.bitcast(F32R), ra.bitcast(F32R)
        rd = [lhsT, rhs] + ([] if start else [out])
        return self.op("pe", lambda e: e.matmul(out.ap, la, ra, start=start, stop=stop), rd, [out])

    def tr(self, out, in_, ident):
        return self.op("pe", lambda e: e.transpose(out.ap, in_.ap, ident.ap), [in_, ident], [out])

    def act(self, out, in_, func, bias=None, scale=None, accum=None, eng="act"):
        kw = {}
        rd = [in_]
        wr = [out]
        if bias is not None:
            if isinstance(bias, View):
                kw["bias"] = bias.ap
                rd.append(bias)
            else:
                kw["bias"] = bias
        if scale is not None:
            if isinstance(scale, View):
                kw["scale"] = scale.ap
                rd.append(scale)
            else:
                kw["scale"] = scale
        if accum is not None:
            kw["accum_out"] = accum.ap
            wr.append(accum)
        return self.op("act", lambda e: e.activation(out.ap, in_.ap, func, **kw), rd, wr)

    def tt(self, out, a, b, op, eng="dve"):
        return self.op(eng, lambda e: e.tensor_tensor(out.ap, a.ap, b.ap, op), [a, b], [out])

    def ts(self, out, a, s1, op0, s2=None, op1=None, accum=None, eng="dve"):
        rd = [a]
        wr = [out]
        s1a = s1.ap if isinstance(s1, View) else s1
        s2a = s2.ap if isinstance(s2, View) else s2
        if isinstance(s1, View):
            rd.append(s1)
        if isinstance(s2, View):
            rd.append(s2)
        kw = {}
        if op1 is not None:
            kw["op1"] = op1
        if accum is not None:
            kw["accum_out"] = accum.ap
            wr.append(accum)
        return self.op(eng, lambda e: e.tensor_scalar(out.ap, a.ap, s1a, s2a, op0, **kw), rd, wr)

    def stt(self, out, a, s, b, op0, op1, accum=None):
        rd = [a, b]
        wr = [out]
        sa = s.ap if isinstance(s, View) else s
        if isinstance(s, View):
            rd.append(s)
        kw = {}
        if accum is not None:
            kw["accum_out"] = accum.ap
            wr.append(accum)
        return self.op("dve", lambda e: e.scalar_tensor_tensor(out.ap, a.ap, sa, b.ap, op0, op1, **kw), rd, wr)

    def copy(self, out, in_, eng="dve"):
        if eng == "act":
            return self.op("act", lambda e: e.copy(out.ap, in_.ap), [in_], [out])
        return self.op(eng, lambda e: e.tensor_copy(out.ap, in_.ap), [in_], [out])

    def memset(self, out, val, eng="pool"):
        return self.op(eng, lambda e: e.memset(out.ap, val), [], [out])

    def recip(self, out, in_):
        return self.op("dve", lambda e: e.reciprocal(out.ap, in_.ap), [in_], [out])

    def rmax(self, out, in_, eng="dve"):
        return self.op(eng, lambda e: e.reduce_max(out.ap, in_.ap, AX.X), [in_], [out])

D = 2048
NIN = 20504
O_QA, O_KA, O_VA, O_ZA, O_BD = 0, 1024, 2048, 3072, 4096
O_GA, O_GG, O_BG, O_CG, O_HC = 4112, 5136, 6160, 7184, 8208
O_QD, O_KD, O_VD, O_OD, O_IF, O_GATE = 9232, 9744, 10256, 11280, 12304, 12312
ALPHA = 4 ** 0.25
NEG = -1.0e30
NS = 16
TBP = 256
CFG = {}


def dap(t, off, dims):
    return bass.AP(t, off, [list(d) for d in dims])


def build():
    specs = []
    _build(True, specs)
    return _build(False, specs)


def _build(DRY, WSPECS):
    nc = bass.Bass("TRN2", target_bir_lowering=False)

    def din(name, shape):
        return nc.dram_tensor(name, list(shape), F32, kind="ExternalInput")

    def dout(name, shape):
        return nc.dram_tensor(name, list(shape), F32, kind="ExternalOutput")

    I = {}
    for name, shape in [
        ("xp", (2048, D)), ("xs", (NS, D)), ("memp", (256, D)),
        ("cmk", (2, NS, 256, 512)), ("cmv", (2, NS, 256, 512)),
        ("sdc", (2, NS, 3, 3072)), ("sdS", (2, NS, 8, 128, 128)), ("sgc", (2, NS, 30, 1024)),
        ("ssc", (2, NS, 2, 1024)), ("smC", (2, NS, 4, 128, 256)), ("smn", (2, NS, 4, 128)), ("smm", (2, NS, 4)),
        ("w_in", (2, D, NIN)), ("b_in", (2, NIN)), ("a_conv_w", (2, 4, 3072)), ("a_A_log", (2, 8)),
        ("a_dt_bias", (2, 8)), ("a_norm_w", (2, 128)), ("b_conv_w", (2, 31, 1024)), ("b_conv_b", (2, 1024)),
        ("b_ln_g", (2, 1024)), ("b_ln_b", (2, 1024)), ("c_conv_w", (2, 3, 1024)), ("d_norm_w", (2, 256)),
        ("w_branch", (2, 4, 1024, D)), ("w_out", (2, D, D)), ("ln1_g", (2, D)), ("ln1_b", (2, D)),
        ("xq_w", (2, D, 512)), ("xk_w", (2, D, 512)), ("xv_w", (2, D, 512)), ("xo_w", (2, 512, D)),
        ("ln2_g", (2, D)), ("ln2_b", (2, D)), ("ffn_w1", (2, D, 4 * D)), ("ffn_b1", (2, 4 * D)),
        ("ffn_w2", (2, 4 * D, D)), ("ffn_b2", (2, D)), ("ln3_g", (2, D)), ("ln3_b", (2, D)),
    ]:
        I[name] = din(name, shape)
    O = {}
    for name, shape in [
        ("yp", (2048, D)), ("ys", (NS, D)), ("mkp", (2, 256, 512)), ("mvp", (2, 256, 512)),
        ("dcp", (2, 3, 3072)), ("dSp", (2, 8, 128, 128)), ("gcp", (2, 30, 1024)), ("scp", (2, 2, 1024)),
        ("mCp", (2, 4, 128, 256)), ("mnp", (2, 4, 128)), ("mmp", (2, 4)),
        ("dcs", (2, NS, 3, 3072)), ("dSs", (2, NS, 8, 128, 128)), ("gcs", (2, NS, 30, 1024)),
        ("scs", (2, NS, 2, 1024)), ("mCs", (2, NS, 4, 128, 256)), ("mns", (2, NS, 4, 128)), ("mms", (2, NS, 4)),
    ]:
        O[name] = dout(name, shape)

    DBG = {}
    if CFG.get('dump'):
        for nm in ['C', 'B', 'A', 'D']:
            DBG[nm] = dout('dbg_' + nm, (128, 8, TBP))
        for nm in ['mixed', 'x1', 'x2', 'x3']:
            DBG[nm] = dout('dbg_' + nm, (128, 16, TBP))
    dumped = set()
    S = Sched(nc, dry=DRY)
    S.init_psum()
    sb = S.sb

    def dump(nm, buf, nch):
        if not CFG.get('dump') or nm in dumped:
            return
        dumped.add(nm)
        S.dma("pool", dap(DBG[nm], 0, [[nch * TBP, 128], [TBP, nch], [1, TBP]]), buf.t[:, 0:nch, :],
              reads=[buf.sub(k, (slice(None), k, slice(None))) for k in range(nch)])

    ones = sb([128, 128], name="ones")
    ident = sb([128, 128], name="ident")
    triu = sb([128, 128], name="triu")
    maskS = sb([128, 128], name="maskS")
    maskL = sb([128, 128], name="maskL")
    zeros = sb([128, 128], name="zeros")
    S.memset(ones[:], 1.0)
    S.memset(zeros[:], 0.0)
    S.op("pool", lambda e: e.affine_select(ident.t[:], ones.t[:], [[-1, 128]], ALU.is_equal, 0.0, base=0, channel_multiplier=1), [ones[:]], [ident[:]])
    S.op("pool", lambda e: e.affine_select(triu.t[:], ones.t[:], [[1, 128]], ALU.is_ge, 0.0, base=0, channel_multiplier=-1), [ones[:]], [triu[:]])
    S.op("pool", lambda e: e.affine_select(maskS.t[:], zeros.t[:], [[1, 128]], ALU.is_gt, NEG, base=0, channel_multiplier=-1), [zeros[:]], [maskS[:]])
    S.op("pool", lambda e: e.affine_select(maskL.t[:], zeros.t[:], [[-1, 128]], ALU.is_ge, NEG, base=0, channel_multiplier=1), [zeros[:]], [maskL[:]])

    def colload(dst, dcol0, t, off, nchunk, n=128):
        S.dma("pool", dst.t[0:n, dcol0:dcol0 + nchunk], dap(t, off, [[1, n], [128, nchunk]]), writes=[dst[:]],
              allow_slow_non_contiguous=True)

    P = []
    for l in range(2):
        p = {}
        bi = sb([128, 162], name=f"bin{l}")
        colload(bi, 0, I["b_in"], l * NIN + 0, 32)
        colload(bi, 32, I["b_in"], l * NIN + O_BD, 1, n=16)
        colload(bi, 33, I["b_in"], l * NIN + O_GA, 64)
        colload(bi, 97, I["b_in"], l * NIN + O_IF, 1, n=8)
        colload(bi, 98, I["b_in"], l * NIN + O_GATE, 64)
        p["bin"] = bi
        acw = sb([128, 24, 4], name=f"acw{l}")
        for k in range(4):
            S.dma("pool", acw.t[:, :, k], dap(I["a_conv_w"], l * 4 * 3072 + k * 3072, [[1, 128], [128, 24]]), writes=[acw[:]], allow_slow_non_contiguous=True)
        bcw = sb([128, 8, 31], name=f"bcw{l}")
        for k in range(31):
            S.dma("pool", bcw.t[:, :, k], dap(I["b_conv_w"], l * 31 * 1024 + k * 1024, [[1, 128], [128, 8]]), writes=[bcw[:]], allow_slow_non_contiguous=True)
        ccw = sb([128, 8, 3], name=f"ccw{l}")
        for k in range(3):
            S.dma("pool", ccw.t[:, :, k], dap(I["c_conv_w"], l * 3 * 1024 + k * 1024, [[1, 128], [128, 8]]), writes=[ccw[:]], allow_slow_non_contiguous=True)
        p["acw"], p["bcw"], p["ccw"] = acw, bcw, ccw
        for nm, nch in [("b_conv_b", 8), ("b_ln_g", 8), ("b_ln_b", 8), ("a_norm_w", 1), ("d_norm_w", 2), ("ln1_g", 16), ("ln1_b", 16),
                        ("ln2_g", 16), ("ln2_b", 16), ("ln3_g", 16), ("ln3_b", 16), ("ffn_b1", 64), ("ffn_b2", 16)]:
            tl = sb([128, nch], name=f"{nm}{l}")
            colload(tl, 0, I[nm], l * nch * 128, nch)
            p[nm] = tl
        negA = sb([128, 8], name=f"negA{l}")
        dtb = sb([128, 8], name=f"dtb{l}")
        S.dma("pool", negA.t[:, :], dap(I["a_A_log"], l * 8, [[0, 128], [1, 8]]), writes=[negA[:]])
        S.dma("pool", dtb.t[:, :], dap(I["a_dt_bias"], l * 8, [[0, 128], [1, 8]]), writes=[dtb[:]])
        S.act(negA[:], negA[:], AF.Exp)
        S.ts(negA[:], negA[:], -1.0, ALU.mult)
        p["negA"], p["dtb"] = negA, dtb
        P.append(p)

    def bcol(l, col0, n=128):
        if col0 < O_BD:
            c = col0 // 128
        elif col0 == O_BD:
            c = 32
        elif col0 < O_IF:
            c = 33 + (col0 - O_GA) // 128
        elif col0 == O_IF:
            c = 97
        else:
            c = 98 + (col0 - O_GATE) // 128
        return P[l]["bin"][0:n, c:c + 1]

    xT = sb([128, 16, TBP], name="xT")
    mixed = sb([128, 16, TBP], name="mixed")
    outn = sb([128, 8, TBP], BF16, name="outn")
    xTb = sb([128, 16, TBP], BF16, name="xTb")
    mixedb = sb([128, 16, TBP], BF16, name="mixedb")
    hb = sb([128, 8, TBP], BF16, name="hb")
    ubuf = sb([128, 8, TBP], name="ubuf")
    tokbuf = sb([128, D], name="tokbuf")
    wsl = [sb([128, 2048], name=f"w{i}") for i in range(3)]
    wbf = [sb([128, 2048], BF16, name=f"wb{i}") for i in range(3)]
    wi = [0]
    NT = 10
    tmpA = [sb([128, TBP], name=f"tA{i}") for i in range(NT)]
    ti = [0]
    NQ = 12
    tmpQ = [sb([128, 128], name=f"tQ{i}") for i in range(NQ)]
    qi = [0]
    NC_ = 48
    tmpC = [sb([128, 8], name=f"tC{i}") for i in range(NC_)]
    ci_ = [0]
    extb = sb([128, TBP + 30], name="extb")
    persA = sb([128, 16, 24], name="persA")
    lnm = sb([128, TBP], name="lnm"); lnr = sb([128, TBP], name="lnr"); lnm2 = sb([128, TBP], name="lnm2")
    tmpQL = [sb([128, 128], name=f"tQL{i}") for i in range(12)]
    qli = [0]
    persD = sb([128, 16, 12], name="persD")
    exts = sb([128, NS, 31], name="exts")
    t258 = [sb([128, 258], name=f"t258_{i}") for i in range(4)]
    t258i = [0]

    def tA():
        ti[0] += 1
        return tmpA[ti[0] % NT]

    def tQ():
        qi[0] += 1
        return tmpQ[qi[0] % NQ]

    def tQL():
        qli[0] += 1
        return tmpQL[qli[0] % 12]

    def psrc(pv, L, shape_rows):
        if L != 1:
            return pv
        t = tQ()
        v = t[0:shape_rows, 0:1]
        S.act(v, pv, AF.Identity)
        return v

    def tC():
        ci_[0] += 1
        return tmpC[ci_[0] % NC_]

    def t258n():
        t258i[0] += 1
        return t258[t258i[0] % 4]

    def ch(buf, k, sl=slice(None)):
        return buf.sub(k, (slice(None), k, sl))

    ARN = 10240
    arena = sb([128, ARN], name="arena")
    apos = {"p": 0, "s": 0}

    def carve(phase, shape, name):
        n = 1
        for s_ in shape[1:]:
            n *= s_
        a0 = apos[phase]
        apos[phase] += n
        assert apos[phase] <= ARN, (phase, apos[phase])
        v = arena.t[:, a0:a0 + n]
        if len(shape) == 3:
            v = v.rearrange("p (a b) -> p a b", a=shape[1])
        elif len(shape) == 4:
            v = v.rearrange("p (a b c) -> p a b c", a=shape[1], b=shape[2])
        return Buf(v, name)

    histA = [carve("p", [128, 24, 3], f"hA{l}") for l in range(2)]
    histB = [carve("p", [128, 8, 30], f"hB{l}") for l in range(2)]
    histC = [carve("p", [128, 8, 2], f"hC{l}") for l in range(2)]
    Sa = [[carve("p", [128, 128], f"Sa{l}_{h}") for h in range(8)] for l in range(2)]
    Cn = [[carve("p", [128, 258], f"Cn{l}_{h}") for h in range(4)] for l in range(2)]
    mst = [carve("p", [128, 4], f"mst{l}") for l in range(2)]
    KT = [carve("p", [128, 4, 256], f"KT{l}") for l in range(2)]
    Vt = [carve("p", [128, 2, 512], f"Vt{l}") for l in range(2)]
    for l in range(2):
        S.memset(histA[l][:], 0.0); S.memset(histB[l][:], 0.0); S.memset(histC[l][:], 0.0)
        S.memset(mst[l][:], 0.0)
        for h in range(8):
            S.memset(Sa[l][h][:], 0.0)
        for h in range(4):
            S.memset(Cn[l][h][:], 0.0)
    SaS = [carve("s", [128, 128], f"SaS{i}") for i in range(3)]
    CnS = [carve("s", [128, 258], f"CnS{i}") for i in range(3)]
    mstS = [carve("s", [128, 4], f"mstS{i}") for i in range(3)]
    ssi = [0]
    KTs = carve("s", [128, 4, 256], "KTs")
    Kts = carve("s", [128, 2, 512], "Kts")
    Vts = carve("s", [128, 2, 512], "Vts")
    hsA = carve("s", [128, 24, NS, 3], "hsA")
    hsB = carve("s", [128, 8, NS, 30], "hsB")
    hsC = carve("s", [128, 8, NS, 2], "hsC")
    tmpB_new = carve("s", [128, 8, NS], "tmpBn")
    tmpA_new = carve("s", [128, 24, NS], "tmpAn")
    qTb = sb([128, 4, TBP], name="qTb")
    oTb = sb([128, 4, TBP], BF16, name="oTb")

    issued = [0]
    WDEPTH = 2

    def w_views(j, wide):
        kk = 4 if wide else 16
        w = wsl[j % 3]
        wb = wbf[j % 3]
        return (w, wb, w.t[:, :].rearrange("p (k n) -> p k n", k=kk), wb.t[:, :].rearrange("p (k n) -> p k n", k=kk))

    def w_issue(j):
        (t, base, rstride, row0, col0, n, kcn, bf, wide) = WSPECS[j]
        w, wb, wv, wbv = w_views(j, wide)
        S.dma("sp", wv[:, 0:kcn, 0:n], dap(t, base + row0 * rstride + col0, [[rstride, 128], [128 * rstride, kcn], [1, n]]),
              writes=[w[:]])
        if bf:
            if j % 2 == 0:
                S.op("act", lambda e: e.copy(wbv[:, 0:kcn, 0:n], wv[:, 0:kcn, 0:n]), [w[:]], [wb[:]])
            else:
                S.op("dve", lambda e: e.tensor_copy(wbv[:, 0:kcn, 0:n], wv[:, 0:kcn, 0:n]), [w[:]], [wb[:]])

    def load_w(t, base, rstride, row0, col0, n, kcn, bf=False, wide=False):
        i = wi[0]
        wi[0] += 1
        spec = (t, base, rstride, row0, col0, n, kcn, bf, wide)
        if DRY:
            WSPECS.append(spec)
        else:
            assert WSPECS[i][1:] == spec[1:], (i, WSPECS[i][1:], spec[1:])
            while issued[0] <= min(i + WDEPTH, len(WSPECS) - 1):
                w_issue(issued[0])
                issued[0] += 1
        w, wb, wv, wbv = w_views(i, wide)
        return Buf(wbv, wb.key) if bf else Buf(wv, w.key)

    def fm_proj(t, base, rstride, row0, col0, n, kcn, rhs_fn, TB):
        w = load_w(t, base, rstride, row0, col0, n, kcn, bf=True)
        p = S.ps()
        for k in range(kcn):
            S.mm(p[0:n, 0:TB], w[:, k, 0:n], rhs_fn(k), start=(k == 0), stop=(k == kcn - 1), r=False)
        return p

    def fm_group(t, base, rstride, row0, col0, ncols, kcn, rhs_fn, TB):
        nch = (ncols + 127) // 128
        pss = [S.ps() for _ in range(nch)]
        for kq in range(0, kcn, 4):
            nk = min(4, kcn - kq)
            w = load_w(t, base, rstride, row0 + kq * 128, col0, ncols, nk, bf=True, wide=True)
            for c in range(nch):
                n = min(128, ncols - c * 128)
                for kk in range(nk):
                    k = kq + kk
                    S.mm(pss[c][0:n, 0:TB], w[:, kk, c * 128:c * 128 + n], rhs_fn(k), start=(k == 0), stop=(k == kcn - 1), r=False)
        return pss

    def win_group(l, col0, ncols, TB):
        return fm_group(I["w_in"], l * D * NIN, NIN, 0, col0, ncols, 16, lambda k: ch(xTb, k, slice(0, TB)), TB)

    def win_proj(l, col0, n, TB):
        return fm_proj(I["w_in"], l * D * NIN, NIN, 0, col0, n, 16, lambda k: ch(xTb, k, slice(0, TB)), TB)

    def transpose_to(dst_view, src_view, rows, cols, eng="dve"):
        p = S.ps()
        S.tr(p[0:cols, 0:rows], src_view, ident[0:rows, 0:rows])
        S.copy(dst_view, p[0:cols, 0:rows], eng=eng)

    def layernorm(l, gname, bname, TB):
        pm = S.ps()
        for k in range(16):
            S.mm(pm[:, 0:TB], ones[:, :], ch(xT, k, slice(0, TB)), start=(k == 0), stop=(k == 15), r=False)
        pq = S.ps()
        for k in range(16):
            sq = tA()
            S.act(sq[:, 0:TB], ch(xT, k, slice(0, TB)), AF.Square)
            S.mm(pq[:, 0:TB], ones[:, :], sq[:, 0:TB], start=(k == 0), stop=(k == 15), r=False)
        mean, rstd, m2 = lnm, lnr, lnm2
        S.ts(mean[:, 0:TB], pm[:, 0:TB], 1.0 / D, ALU.mult)
        S.tt(m2[:, 0:TB], mean[:, 0:TB], mean[:, 0:TB], ALU.mult)
        S.stt(rstd[:, 0:TB], pq[:, 0:TB], 1.0 / D, m2[:, 0:TB], ALU.mult, ALU.subtract)
        S.ts(rstd[:, 0:TB], rstd[:, 0:TB], 0.0, ALU.max, 1e-5, ALU.add)
        S.act(rstd[:, 0:TB], rstd[:, 0:TB], AF.Sqrt)
        S.recip(rstd[:, 0:TB], rstd[:, 0:TB])
        for k in range(16):
            t = tA()
            S.tt(t[:, 0:TB], ch(xT, k, slice(0, TB)), mean[:, 0:TB], ALU.subtract)
            S.tt(t[:, 0:TB], t[:, 0:TB], rstd[:, 0:TB], ALU.mult)
            S.act(ch(xT, k, slice(0, TB)), t[:, 0:TB], AF.Identity, bias=P[l][bname][:, k:k + 1], scale=P[l][gname][:, k:k + 1])
            S.act(ch(xTb, k, slice(0, TB)), t[:, 0:TB], AF.Identity, bias=P[l][bname][:, k:k + 1], scale=P[l][gname][:, k:k + 1])

    def emit_rows(srcT_fn, nch, R, dst_ap_fn):
        for j0 in range(0, nch, 16):
            nj = min(16, nch - j0)
            for j in range(j0, j0 + nj):
                transpose_to(tokbuf[0:R, (j - j0) * 128:(j - j0 + 1) * 128], srcT_fn(j), 128, R)
            S.dma("pool", dst_ap_fn(j0 * 128, nj * 128), tokbuf.t[0:R, 0:nj * 128], reads=[tokbuf[:]])

    class Conv:
        def __init__(self, mode, W, TB):
            self.mode, self.W, self.TB, self.H = mode, W, TB, W - 1
            if mode == "p":
                self.newv = extb[:, self.H:self.H + TB]
                self.histv = extb[:, 0:self.H]
                self.tail = extb[:, TB:TB + self.H]
            else:
                self.newv = exts[:, :, self.H]
                self.histv = exts[:, :, 0:self.H]

        def tap(self, k):
            if self.mode == "p":
                return extb[:, k:k + self.TB]
            return exts[:, :, k]

    def conv_apply(cv, wtile, j, out_view, bias_view=None):
        W = cv.W
        acc = out_view
        if bias_view is not None:
            S.ts(acc, cv.tap(0), wtile[:, j, 0:1], ALU.mult, bias_view, ALU.add)
        else:
            S.ts(acc, cv.tap(0), wtile[:, j, 0:1], ALU.mult)
        for k in range(1, W):
            S.stt(acc, cv.tap(k), wtile[:, j, k:k + 1], acc, ALU.mult, ALU.add)

    def layer_block(l, TB, mode, chunks, last, hsA=None, hsB=None, hsC=None):
        p_ = P[l]
        shp = (lambda v: v)
        if mode == "p":
            ov = lambda buf: buf[:, 0:TB]
        else:
            ov = lambda buf: buf[:, 0:TB]
        first_branch = [True]

        def branch_merge(n):
            for dg in range(4):
                pgs = win_group(l, O_GATE + n * D + dg * 512, 512, TB)
                gts = []
                for c in range(4):
                    gt = tA()
                    S.act(gt[:, 0:TB], pgs[c][:, 0:TB], AF.Sigmoid, bias=bcol(l, O_GATE + n * D + (dg * 4 + c) * 128))
                    gts.append(gt)
                pbs = fm_group(I["w_branch"], (l * 4 + n) * 1024 * D, D, 0, dg * 512, 512, 8, lambda k: ch(outn, k, slice(0, TB)), TB)
                for c in range(4):
                    d = dg * 4 + c
                    pb, gt = pbs[c], gts[c]
                    if first_branch[0]:
                        S.tt(ch(mixed, d, slice(0, TB)), pb[:, 0:TB], gt[:, 0:TB], ALU.mult)
                    else:
                        S.tt(gt[:, 0:TB], pb[:, 0:TB], gt[:, 0:TB], ALU.mult)
                        if n == 3:
                            S.tt(ch(mixedb, d, slice(0, TB)), ch(mixed, d, slice(0, TB)), gt[:, 0:TB], ALU.add)
                        else:
                            S.tt(ch(mixed, d, slice(0, TB)), ch(mixed, d, slice(0, TB)), gt[:, 0:TB], ALU.add)
            first_branch[0] = False

        def newrow_out(vals_fn, nch, dst_t, lbase, H, C):
            emit_rows(vals_fn, nch, NS, lambda c0, n: dap(dst_t, lbase + (H - 1) * C + c0, [[H * C, NS], [1, n]]))

        def ph_C():
            cv = Conv(mode, 3, TB)
            newC = ubuf
            for jg in range(2):
                pbgs = win_group(l, O_BG + jg * 512, 512, TB)
                bgs = []
                for c in range(4):
                    t_ = tA()
                    S.act(t_[:, 0:TB], pbgs[c][:, 0:TB], AF.Identity, bias=bcol(l, O_BG + (jg * 4 + c) * 128))
                    bgs.append(t_)
                pcs = win_group(l, O_CG + jg * 512, 512, TB)
                cgs = []
                for c in range(4):
                    t_ = tA()
                    S.act(t_[:, 0:TB], pcs[c][:, 0:TB], AF.Identity, bias=bcol(l, O_CG + (jg * 4 + c) * 128))
                    cgs.append(t_)
                phs = win_group(l, O_HC + jg * 512, 512, TB)
                for c in range(4):
                    j = jg * 4 + c
                    if mode == "p":
                        S.copy(cv.histv, histC[l][:, j, :], eng="pool")
                    else:
                        S.copy(cv.histv, hsC.sub(j, (slice(None), j, slice(None), slice(None))), eng="pool")
                    S.stt(cv.newv, phs[c][:, 0:TB], bcol(l, O_HC + j * 128), cgs[c][:, 0:TB], ALU.add, ALU.mult)
                    conv_apply(cv, p_["ccw"], j, cgs[c][:, 0:TB])
                    if mode == "p":
                        S.copy(histC[l][:, j, :], cv.tail, eng="pool")
                    else:
                        S.copy(ch(newC, j, slice(0, NS)), cv.newv, eng="pool")
                    S.tt(ch(outn, j, slice(0, TB)), bgs[c][:, 0:TB], cgs[c][:, 0:TB], ALU.mult)
            if mode == "s":
                newrow_out(lambda j: ch(newC, j, slice(0, NS)), 8, O["scs"], l * NS * 2 * 1024, 2, 1024)
            branch_merge(2)

        def ph_B():
            cv = Conv(mode, 31, TB)
            newB = mixed
            for jg in range(2):
                pgs = win_group(l, O_GG + jg * 512, 512, TB)
                sgs = []
                for c in range(4):
                    t_ = tA()
                    S.act(t_[:, 0:TB], pgs[c][:, 0:TB], AF.Sigmoid, bias=bcol(l, O_GG + (jg * 4 + c) * 128))
                    sgs.append(t_)
                pas = win_group(l, O_GA + jg * 512, 512, TB)
                for c in range(4):
                    j = jg * 4 + c
                    if mode == "p":
                        S.copy(cv.histv, histB[l][:, j, :], eng="pool")
                    else:
                        S.copy(cv.histv, hsB.sub(j, (slice(None), j, slice(None), slice(None))), eng="pool")
                    S.stt(cv.newv, pas[c][:, 0:TB], bcol(l, O_GA + j * 128), sgs[c][:, 0:TB], ALU.add, ALU.mult)
                    conv_apply(cv, p_["bcw"], j, ch(ubuf, j, slice(0, TB)), bias_view=p_["b_conv_b"][:, j:j + 1])
                    if mode == "p":
                        S.copy(histB[l][:, j, :], cv.tail, eng="pool")
                    else:
                        S.copy(tmpB_new.sub(j, (slice(None), j, slice(None))), cv.newv, eng="pool")
            if mode == "s":
                newrow_out(lambda j: tmpB_new.sub(j, (slice(None), j, slice(None))), 8, O["gcs"], l * NS * 30 * 1024, 30, 1024)
            pm = S.ps()
            for k in range(8):
                S.mm(pm[:, 0:TB], ones[:, :], ch(ubuf, k, slice(0, TB)), start=(k == 0), stop=(k == 7), r=False)
            pq = S.ps()
            for k in range(8):
                sq = tA()
                S.act(sq[:, 0:TB], ch(ubuf, k, slice(0, TB)), AF.Square)
                S.mm(pq[:, 0:TB], ones[:, :], sq[:, 0:TB], start=(k == 0), stop=(k == 7), r=False)
            mean, rstd, m2 = lnm, lnr, lnm2
            S.ts(mean[:, 0:TB], pm[:, 0:TB], 1.0 / 1024, ALU.mult)
            S.tt(m2[:, 0:TB], mean[:, 0:TB], mean[:, 0:TB], ALU.mult)
            S.stt(rstd[:, 0:TB], pq[:, 0:TB], 1.0 / 1024, m2[:, 0:TB], ALU.mult, ALU.subtract)
            S.ts(rstd[:, 0:TB], rstd[:, 0:TB], 0.0, ALU.max, 1e-5, ALU.add)
            S.act(rstd[:, 0:TB], rstd[:, 0:TB], AF.Sqrt)
            S.recip(rstd[:, 0:TB], rstd[:, 0:TB])
            for k in range(8):
                t = tA()
                S.tt(t[:, 0:TB], ch(ubuf, k, slice(0, TB)), mean[:, 0:TB], ALU.subtract)
                S.tt(t[:, 0:TB], t[:, 0:TB], rstd[:, 0:TB], ALU.mult)
                S.act(ch(outn, k, slice(0, TB)), t[:, 0:TB], AF.Silu, bias=p_["b_ln_b"][:, k:k + 1], scale=p_["b_ln_g"][:, k:k + 1])
            branch_merge(1)

        def ph_A():
            pbd = win_proj(l, O_BD, 16, TB)
            bdT = tA()
            S.act(bdT[0:16, 0:TB], pbd[0:16, 0:TB], AF.Identity, bias=bcol(l, O_BD, 16))
            beta_t, g_t, gc_t = [], [], []
            for (c0, L) in chunks:
                p = S.ps()
                S.tr(p[0:L, 0:16], bdT[0:16, c0:c0 + L], ident[0:16, 0:16])
                ci_a = len(beta_t)
                be = Buf(persA.t[:, ci_a, 0:8], ("persA", ci_a, 0)); gg = Buf(persA.t[:, ci_a, 8:16], ("persA", ci_a, 1))
                gcx = Buf(persA.t[:, ci_a, 16:24], ("persA", ci_a, 2)); tmp = tC()
                S.act(be[0:L, 0:8], p[0:L, 0:8], AF.Sigmoid)
                S.tt(tmp[0:L, 0:8], p[0:L, 8:16], p_["dtb"][0:L, 0:8], ALU.add)
                S.act(tmp[0:L, 0:8], tmp[0:L, 0:8], AF.Exp)
                S.act(tmp[0:L, 0:8], tmp[0:L, 0:8], AF.Ln, bias=1.0)
                S.tt(gg[0:L, 0:8], tmp[0:L, 0:8], p_["negA"][0:L, 0:8], ALU.mult)
                pc = S.ps()
                S.mm(pc[0:L, 0:8], triu[0:L, 0:L], gg[0:L, 0:8], r=False)
                S.copy(gcx[0:L, 0:8], pc[0:L, 0:8])
                beta_t.append(be); g_t.append(gg); gc_t.append(gcx)
            newA = tmpA_new
            for h in range(8):
                qkv = []
                for part, off in enumerate((O_QA, O_KA, O_VA)):
                    jj = part * 8 + h
                    cv = Conv(mode, 4, TB)
                    pp = win_proj(l, off + h * 128, 128, TB)
                    if mode == "p":
                        S.copy(cv.histv, histA[l][:, jj, :], eng="pool")
                    else:
                        S.copy(cv.histv, hsA.sub(jj, (slice(None), jj, slice(None), slice(None))), eng="pool")
                    S.act(cv.newv, pp[:, 0:TB], AF.Identity, bias=bcol(l, off + h * 128))
                    acc = tA()
                    conv_apply(cv, p_["acw"], jj, acc[:, 0:TB])
                    if mode == "p":
                        S.copy(histA[l][:, jj, :], cv.tail, eng="pool")
                    else:
                        S.copy(newA.sub(jj, (slice(None), jj, slice(None))), cv.newv, eng="pool")
                    S.act(acc[:, 0:TB], acc[:, 0:TB], AF.Silu)
                    qkv.append(acc)
                qT_, kT_, vT_ = qkv
                for idx, t_ in enumerate((qT_, kT_)):
                    sq = tA()
                    S.act(sq[:, 0:TB], t_[:, 0:TB], AF.Square)
                    pn = S.ps()
                    S.mm(pn[:, 0:TB], ones[:, :], sq[:, 0:TB], r=False)
                    S.ts(sq[:, 0:TB], pn[:, 0:TB], 1e-6, ALU.add)
                    S.act(sq[:, 0:TB], sq[:, 0:TB], AF.Sqrt)
                    S.recip(sq[:, 0:TB], sq[:, 0:TB])
                    if idx == 0:
                        S.stt(t_[:, 0:TB], t_[:, 0:TB], 128 ** -0.5, sq[:, 0:TB], ALU.mult, ALU.mult)
                    else:
                        S.tt(t_[:, 0:TB], t_[:, 0:TB], sq[:, 0:TB], ALU.mult)
                pz = win_proj(l, O_ZA + h * 128, 128, TB)
                sz = tA()
                S.act(sz[:, 0:TB], pz[:, 0:TB], AF.Silu, bias=bcol(l, O_ZA + h * 128))
                for ci, (c0, L) in enumerate(chunks):
                    if mode == "p":
                        Sh = Sa[l][h]
                    else:
                        Sh = SaS[ssi[0] % 3]; ssi[0] += 1
                        S.dma("sp", Sh.t[:, :], dap(I["sdS"], ((l * NS + ci) * 8 + h) * 16384, [[128, 128], [1, 128]]), writes=[Sh[:]])
                    delta_chunk(l, h, ci, c0, L, qT_, kT_, vT_, sz, Sh, beta_t[ci], g_t[ci], gc_t[ci])
                    if mode == "s":
                        S.dma("pool", dap(O["dSs"], ((l * NS + ci) * 8 + h) * 16384, [[128, 128], [1, 128]]), Sh.t[:, :], reads=[Sh[:]])
                    elif last:
                        if ci == len(chunks) - 1:
                            S.dma("pool", dap(O["dSp"], (l * 8 + h) * 16384, [[128, 128], [1, 128]]), Sh.t[:, :], reads=[Sh[:]])
            if mode == "s":
                newrow_out(lambda j: newA.sub(j, (slice(None), j, slice(None))), 24, O["dcs"], l * NS * 3 * 3072, 3, 3072)
            branch_merge(0)

        def ph_D():
            pif = win_proj(l, O_IF, 8, TB)
            ifT = tA()
            S.act(ifT[0:8, 0:TB], pif[0:8, 0:TB], AF.Identity, bias=bcol(l, O_IF, 8))
            lf_t, b_t, ib_t = [], [], []
            for (c0, L) in chunks:
                p = S.ps()
                S.tr(p[0:L, 0:8], ifT[0:8, c0:c0 + L], ident[0:8, 0:8])
                ci_d = len(lf_t)
                lf = Buf(persD.t[:, ci_d, 0:4], ("persD", ci_d, 0)); bb = Buf(persD.t[:, ci_d, 4:8], ("persD", ci_d, 1))
                ib = Buf(persD.t[:, ci_d, 8:12], ("persD", ci_d, 2))
                S.act(lf[0:L, 0:4], p[0:L, 4:8], AF.Exp, scale=-1.0)
                S.act(lf[0:L, 0:4], lf[0:L, 0:4], AF.Ln, bias=1.0)
                S.ts(lf[0:L, 0:4], lf[0:L, 0:4], -1.0, ALU.mult)
                pc = S.ps()
                S.mm(pc[0:L, 0:4], triu[0:L, 0:L], lf[0:L, 0:4], r=False)
                S.copy(bb[0:L, 0:4], pc[0:L, 0:4])
                S.tt(ib[0:L, 0:4], p[0:L, 0:4], bb[0:L, 0:4], ALU.subtract)
                lf_t.append(lf); b_t.append(bb); ib_t.append(ib)
            for h in range(4):
                pq_ = win_proj(l, O_QD + h * 128, 128, TB)
                qT_ = tA()
                S.act(qT_[:, 0:TB], pq_[:, 0:TB], AF.Identity, bias=bcol(l, O_QD + h * 128))
                pk_ = win_proj(l, O_KD + h * 128, 128, TB)
                kT_ = tA()
                S.ts(kT_[:, 0:TB], pk_[:, 0:TB], bcol(l, O_KD + h * 128), ALU.add, 128 ** -0.5, ALU.mult)
                vT_, so_ = [], []
                for e in range(2):
                    pv_ = win_proj(l, O_VD + h * 256 + e * 128, 128, TB)
                    v_ = tA()
                    S.act(v_[:, 0:TB], pv_[:, 0:TB], AF.Identity, bias=bcol(l, O_VD + h * 256 + e * 128))
                    vT_.append(v_)
                for e in range(2):
                    po_ = win_proj(l, O_OD + h * 256 + e * 128, 128, TB)
                    o_ = tA()
                    S.act(o_[:, 0:TB], po_[:, 0:TB], AF.Sigmoid, bias=bcol(l, O_OD + h * 256 + e * 128))
                    so_.append(o_)
                for ci, (c0, L) in enumerate(chunks):
                    if mode == "p":
                        Ch, ms = Cn[l][h], mst[l]
                    else:
                        Ch = CnS[ssi[0] % 3]; ms = mstS[ssi[0] % 3]; ssi[0] += 1
                        base = (l * NS + ci) * 4 + h
                        S.dma("sp", Ch.t[:, 0:256], dap(I["smC"], base * 32768, [[256, 128], [1, 256]]), writes=[Ch[:]])
                        S.dma("sp", Ch.t[:, 256:257], dap(I["smn"], base * 128, [[1, 128], [1, 1]]), writes=[Ch[:]], allow_slow_non_contiguous=True)
                        S.dma("sp", ms.t[:, h:h + 1], dap(I["smm"], base, [[0, 128], [1, 1]]), writes=[ms[:]])
                    mlstm_chunk(l, h, c0, L, qT_, kT_, vT_, so_, Ch, ms, lf_t[ci], b_t[ci], ib_t[ci])
                    if mode == "s" or (last and ci == len(chunks) - 1):
                        if mode == "s":
                            base = (l * NS + ci) * 4 + h
                            oc, on_, om = O["mCs"], O["mns"], O["mms"]
                        else:
                            base = l * 4 + h
                            oc, on_, om = O["mCp"], O["mnp"], O["mmp"]
                        S.dma("pool", dap(oc, base * 32768, [[256, 128], [1, 256]]), Ch.t[:, 0:256], reads=[Ch[:]])
                        S.dma("pool", dap(on_, base * 128, [[1, 128], [1, 1]]), Ch.t[:, 256:257], reads=[Ch[:]], allow_slow_non_contiguous=True)
                        S.dma("pool", dap(om, base, [[1, 1], [1, 1]]), ms.t[0:1, h:h + 1], reads=[ms[:]])
            branch_merge(3)

        def ph_O():
            for dg in range(4):
                pys = fm_group(I["w_out"], l * D * D, D, 0, dg * 512, 512, 16, lambda k: ch(mixedb, k, slice(0, TB)), TB)
                for c in range(4):
                    d = dg * 4 + c
                    S.stt(ch(xT, d, slice(0, TB)), ch(xT, d, slice(0, TB)), ALPHA, pys[c][:, 0:TB], ALU.mult, ALU.add)
            layernorm(l, "ln1_g", "ln1_b", TB)

        def ph_X():
            pqs = fm_group(I["xq_w"], l * D * 512, 512, 0, 0, 512, 16, lambda k: ch(xTb, k, slice(0, TB)), TB)
            for h in range(4):
                S.copy(ch(qTb, h, slice(0, TB)), pqs[h][:, 0:TB], eng="act")
            for ci, (c0, L) in enumerate(chunks):
                if mode == "p":
                    KTl, Vl = KT[l], Vt[l]
                else:
                    KTl, Vl = KTs, Vts
                    S.dma("sp", Kts.t[:, :, :], dap(I["cmk"], (l * NS + ci) * 256 * 512, [[512, 128], [128 * 512, 2], [1, 512]]), writes=[Kts[:]])
                    S.dma("sp", Vts.t[:, :, :], dap(I["cmv"], (l * NS + ci) * 256 * 512, [[512, 128], [128 * 512, 2], [1, 512]]), writes=[Vts[:]])
                    for hh in range(4):
                        for mc in range(2):
                            transpose_to(KTs[:, hh, mc * 128:(mc + 1) * 128], Kts[:, mc, hh * 128:(hh + 1) * 128], 128, 128, eng="act")
                otok = tokbuf
                for h in range(4):
                    ps_ = S.ps()
                    S.mm(ps_[0:L, 0:256], ch(qTb, h, slice(c0, c0 + L)), KTl[:, h, :], r=False)
                    mx = tC()
                    S.rmax(mx[0:L, 0:1], ps_[0:L, 0:256])
                    S.ts(mx[0:L, 1:2], mx[0:L, 0:1], -(128 ** -0.5), ALU.mult)
                    es = tA()
                    S.act(es[0:L, 0:256], ps_[0:L, 0:256], AF.Exp, bias=mx[0:L, 1:2], scale=128 ** -0.5, accum=mx[0:L, 2:3])
                    S.recip(mx[0:L, 3:4], mx[0:L, 2:3])
                    aT = tA()
                    for mc in range(2):
                        transpose_to(aT[:, mc * 128:mc * 128 + L], es[0:L, mc * 128:(mc + 1) * 128], L, 128, eng="act")
                    po = S.ps()
                    for mc in range(2):
                        S.mm(po[0:L, 0:128], aT[:, mc * 128:mc * 128 + L], Vl[:, mc, h * 128:(h + 1) * 128], start=(mc == 0), stop=(mc == 1), r=False)
                    S.ts(otok[0:L, h * 128:(h + 1) * 128], po[0:L, 0:128], mx[0:L, 3:4], ALU.mult)
                for h in range(4):
                    transpose_to(ch(oTb, h, slice(c0, c0 + L)), otok[0:L, h * 128:(h + 1) * 128], L, 128, eng="act")
            for dg in range(4):
                pys = fm_group(I["xo_w"], l * 512 * D, D, 0, dg * 512, 512, 4, lambda k: ch(oTb, k, slice(0, TB)), TB)
                for c in range(4):
                    d = dg * 4 + c
                    S.stt(ch(xT, d, slice(0, TB)), ch(xT, d, slice(0, TB)), ALPHA, pys[c][:, 0:TB], ALU.mult, ALU.add)
            layernorm(l, "ln2_g", "ln2_b", TB)

        def ph_M():
            for g in range(8):
                for jg2 in range(2):
                    phs = fm_group(I["ffn_w1"], l * D * 4 * D, 4 * D, 0, (g * 8 + jg2 * 4) * 128, 512, 16, lambda k: ch(xTb, k, slice(0, TB)), TB)
                    for c in range(4):
                        jj = jg2 * 4 + c
                        j = g * 8 + jj
                        r_ = tA()
                        S.act(r_[:, 0:TB], phs[c][:, 0:TB], AF.Relu, bias=p_["ffn_b1"][:, j:j + 1])
                        S.act(ch(hb, jj, slice(0, TB)), r_[:, 0:TB], AF.Square)
                for dg in range(4):
                    pds = fm_group(I["ffn_w2"], l * 4 * D * D, D, g * 1024, dg * 512, 512, 8, lambda k: ch(hb, k, slice(0, TB)), TB)
                    for c in range(4):
                        d = dg * 4 + c
                        if g == 0:
                            S.copy(ch(mixed, d, slice(0, TB)), pds[c][:, 0:TB], eng="act")
                        else:
                            S.tt(ch(mixed, d, slice(0, TB)), ch(mixed, d, slice(0, TB)), pds[c][:, 0:TB], ALU.add)
            for d in range(16):
                S.ts(ch(mixed, d, slice(0, TB)), ch(mixed, d, slice(0, TB)), p_["ffn_b2"][:, d:d + 1], ALU.add)
                S.stt(ch(xT, d, slice(0, TB)), ch(xT, d, slice(0, TB)), ALPHA, ch(mixed, d, slice(0, TB)), ALU.mult, ALU.add)
            layernorm(l, "ln3_g", "ln3_b", TB)

        PH = CFG.get('phases', 'CBADOXM')
        if 'C' in PH:
            ph_C(); dump('C', outn, 8)
        if 'B' in PH:
            ph_B(); dump('B', outn, 8)
        if 'A' in PH:
            ph_A(); dump('A', outn, 8)
        if 'D' in PH:
            ph_D(); dump('D', outn, 8); dump('mixed', mixed, 16)
        if 'O' in PH:
            ph_O(); dump('x1', xT, 16)
        if 'X' in PH:
            ph_X(); dump('x2', xT, 16)
        if 'M' in PH:
            ph_M(); dump('x3', xT, 16)

    def delta_chunk(l, h, ci, c0, L, qT_, kT_, vT_, sz, Sh, be, gg, gcx):
        p_ = P[l]
        beta_c, g_c, gc_c = be[0:L, h:h + 1], gg[0:L, h:h + 1], gcx[0:L, h:h + 1]
        kc, qc, vc = kT_[:, c0:c0 + L], qT_[:, c0:c0 + L], vT_[:, c0:c0 + L]
        gb = tQ()
        S.ts(gb[0:L, :], ones[0:L, :], g_c, ALU.mult)
        pg = S.ps()
        S.mm(pg[:, 0:L], gb[0:L, :], triu[0:L, 0:L], r=False)
        gcr = tQL(); egr = tQL()
        S.copy(gcr[:, 0:L], pg[:, 0:L], eng="act")
        S.act(egr[:, 0:L], pg[:, 0:L], AF.Exp)
        cols = tC()
        eg_c, ekl_c, nb_c, nbk_c = cols[0:L, 0:1], cols[0:L, 1:2], cols[0:L, 2:3], cols[0:L, 3:4]
        S.act(eg_c, gc_c, AF.Exp)
        S.act(ekl_c, gc_c, AF.Exp, bias=gcr[0:L, L - 1:L], scale=-1.0)
        S.ts(nb_c, beta_c, -1.0, ALU.mult)
        S.tt(nbk_c, nb_c, ekl_c, ALU.mult)
        ktok = tQL(); vtok = tQL()
        transpose_to(ktok[0:L, :], kc, 128, L, eng="act")
        transpose_to(vtok[0:L, :], vc, 128, L, eng="act")
        pk = S.ps()
        S.mm(pk[0:L, 0:L], kc, kc, r=False)
        S.mm(pk[0:L, 128:128 + L], kc, qc, r=False)
        qkd = tQL()
        if L > 1:
            Dm = tQ(); Es = tQ(); B0 = tQ(); Ei = tQ()
            S.stt(Dm[0:L, 0:L], gcr[0:L, 0:L], gc_c, maskS[0:L, 0:L], ALU.subtract, ALU.add)
            S.act(Es[0:L, 0:L], Dm[0:L, 0:L], AF.Exp)
            S.stt(B0[0:L, 0:L], pk[0:L, 0:L], beta_c, Es[0:L, 0:L], ALU.mult, ALU.mult)
            S.tt(Ei[0:L, 0:L], Es[0:L, 0:L], ident[0:L, 0:L], ALU.add, eng="pool")
            S.tt(qkd[0:L, 0:L], pk[0:L, 128:128 + L], Ei[0:L, 0:L], ALU.mult)
        else:
            S.copy(qkd[0:L, 0:L], pk[0:L, 128:128 + L], eng="act")
        pks = S.ps()
        S.mm(pks[0:L, 0:128], kc, Sh[:, :], r=False)
        rneg = tQL()
        S.stt(rneg[0:L, :], pks[0:L, 0:128], eg_c, vtok[0:L, :], ALU.mult, ALU.subtract)
        dl = tQL(); dk = tQL()
        if L > 1:
            Bp = B0
            Ap = tQ()
            transpose_to(Ap[0:L, 0:L], B0[0:L, 0:L], L, L, eng="act")
            U = tQ(); Lw = tQ()
            S.tt(U[0:L, 0:L], ident[0:L, 0:L], Bp[0:L, 0:L], ALU.subtract)
            S.tt(Lw[0:L, 0:L], ident[0:L, 0:L], Ap[0:L, 0:L], ALU.subtract, eng="pool")
            nlev = 6
            for j in range(1, nlev + 1):
                pB = S.ps()
                S.mm(pB[0:L, 0:L], Ap[0:L, 0:L], Bp[0:L, 0:L], r=False)
                Bn = tQ()
                S.copy(Bn[0:L, 0:L], pB[0:L, 0:L], eng="act")
                An = None
                if j < nlev:
                    pA = S.ps()
                    S.mm(pA[0:L, 0:L], Bp[0:L, 0:L], Ap[0:L, 0:L], r=False)
                    An = tQ()
                    S.copy(An[0:L, 0:L], pA[0:L, 0:L])
                pU = S.ps()
                S.mm(pU[0:L, 0:L], Lw[0:L, 0:L], Bn[0:L, 0:L], r=False)
                Un = tQ()
                S.tt(Un[0:L, 0:L], U[0:L, 0:L], pU[0:L, 0:L], ALU.add)
                if j < nlev:
                    pL = S.ps()
                    S.mm(pL[0:L, 0:L], U[0:L, 0:L], An[0:L, 0:L], r=False)
                    Ln_ = tQ()
                    S.tt(Ln_[0:L, 0:L], Lw[0:L, 0:L], pL[0:L, 0:L], ALU.add)
                    Lw = Ln_
                    Ap = An
                U = Un
                Bp = Bn
            pT = S.ps()
            S.mm(pT[0:L, 0:128], U[0:L, 0:L], rneg[0:L, :], r=False)
            S.ts(dl[0:L, :], pT[0:L, 0:128], nb_c, ALU.mult)
            S.act(dk[0:L, :], pT[0:L, 0:128], AF.Copy, scale=nbk_c) if False else S.ts(dk[0:L, :], pT[0:L, 0:128], nbk_c, ALU.mult)
        else:
            S.ts(dl[0:L, :], rneg[0:L, :], nb_c, ALU.mult)
            S.ts(dk[0:L, :], rneg[0:L, :], nbk_c, ALU.mult)
        qd = tQL()
        S.tt(qd[:, 0:L], qc, egr[:, 0:L], ALU.mult, eng="pool")
        po = S.ps()
        S.mm(po[0:L, 0:128], qd[:, 0:L], Sh[:, :], start=True, stop=False, r=False)
        S.mm(po[0:L, 0:128], qkd[0:L, 0:L], dl[0:L, :], start=False, stop=True, r=False)
        pS = S.ps()
        S.mm(pS[:, 0:128], ktok[0:L, :], dk[0:L, :], r=False)
        S.stt(Sh[:, :], Sh[:, :], egr[:, L - 1:L], pS[:, 0:128], ALU.mult, ALU.add)
        junk = tQ(); c2 = tC()
        S.act(junk[0:L, :], po[0:L, 0:128], AF.Square, accum=c2[0:L, 0:1])
        S.ts(c2[0:L, 1:2], c2[0:L, 0:1], 1.0 / 128, ALU.mult, 1e-6, ALU.add)
        S.act(c2[0:L, 1:2], c2[0:L, 1:2], AF.Sqrt)
        S.recip(c2[0:L, 2:3], c2[0:L, 1:2])
        on_ = tQL()
        S.ts(on_[0:L, :], po[0:L, 0:128], c2[0:L, 2:3], ALU.mult)
        pT2 = S.ps()
        S.tr(pT2[:, 0:L], on_[0:L, :], ident[0:L, 0:L])
        S.stt(outn.sub(h, (slice(None), h, slice(c0, c0 + L))), psrc(pT2[:, 0:L], L, 128), p_["a_norm_w"][:, 0:1], sz[:, c0:c0 + L], ALU.mult, ALU.mult)

    def mlstm_chunk(l, h, c0, L, qT_, kT_, vT_, so_, Ch, ms, lf, bb, ib):
        p_ = P[l]
        b_c, ib_c, lf_c = bb[0:L, h:h + 1], ib[0:L, h:h + 1], lf[0:L, h:h + 1]
        qc, kc = qT_[:, c0:c0 + L], kT_[:, c0:c0 + L]
        ibb = tQ(); lfb = tQ()
        S.ts(ibb[0:L, :], ones[0:L, :], ib_c, ALU.mult)
        if CFG.get('dstop', 999) == 1: return
        S.ts(lfb[0:L, :], ones[0:L, :], lf_c, ALU.mult)
        if CFG.get('dstop', 999) == 2: return
        pr = S.ps()
        S.mm(pr[:, 0:L], ibb[0:L, :], ident[0:L, 0:L], r=False)
        if CFG.get('dstop', 999) == 3: return
        S.mm(pr[:, 128:128 + L], lfb[0:L, :], triu[0:L, 0:L], r=False)
        if CFG.get('dstop', 999) == 4: return
        ibr = tQ()
        S.copy(ibr[:, 0:L], pr[:, 0:L], eng="act")
        if CFG.get('dstop', 999) == 5: return
        cb = tC()
        S.act(cb[:, 0:1], pr[:, 128 + L - 1:128 + L], AF.Identity)
        if CFG.get('dstop', 999) == 6: return
        S.rmax(cb[:, 1:2], ibr[:, 0:L])
        if CFG.get('dstop', 999) == 7: return
        S.tt(cb[:, 1:2], cb[:, 1:2], cb[:, 0:1], ALU.add)
        if CFG.get('dstop', 999) == 8: return
        S.tt(cb[:, 2:3], cb[:, 0:1], ms[:, h:h + 1], ALU.add)
        if CFG.get('dstop', 999) == 9: return
        S.tt(cb[:, 3:4], cb[:, 2:3], cb[:, 1:2], ALU.max)
        if CFG.get('dstop', 999) == 10: return
        S.tt(cb[:, 5:6], cb[:, 2:3], cb[:, 3:4], ALU.subtract)
        if CFG.get('dstop', 999) == 11: return
        S.act(cb[:, 4:5], cb[:, 5:6], AF.Exp)
        if CFG.get('dstop', 999) == 12: return
        S.tt(cb[:, 5:6], cb[:, 0:1], cb[:, 3:4], ALU.subtract)
        if CFG.get('dstop', 999) == 13: return
        cc = tC()
        S.tt(cc[0:L, 0:1], b_c, ms[0:L, h:h + 1], ALU.add)
        if CFG.get('dstop', 999) == 14: return
        ld = tQ()
        S.stt(ld[0:L, 0:L], ibr[0:L, 0:L], b_c, maskL[0:L, 0:L], ALU.add, ALU.add)
        if CFG.get('dstop', 999) == 15: return
        S.rmax(cc[0:L, 1:2], ld[0:L, 0:L])
        if CFG.get('dstop', 999) == 16: return
        S.tt(cc[0:L, 2:3], cc[0:L, 0:1], cc[0:L, 1:2], ALU.max)
        if CFG.get('dstop', 999) == 17: return
        S.ts(cc[0:L, 3:4], cc[0:L, 2:3], -1.0, ALU.mult)
        if CFG.get('dstop', 999) == 18: return
        S.act(cc[0:L, 4:5], cc[0:L, 0:1], AF.Exp, bias=cc[0:L, 3:4])
        if CFG.get('dstop', 999) == 19: return
        S.act(cc[0:L, 5:6], cc[0:L, 3:4], AF.Exp)
        if CFG.get('dstop', 999) == 20: return
        S.act(cc[0:L, 6:7], ib_c, AF.Exp, bias=cb[0:L, 5:6])
        if CFG.get('dstop', 999) == 21: return
        ed = tQ()
        S.act(ed[0:L, 0:L], ld[0:L, 0:L], AF.Exp, bias=cc[0:L, 3:4])
        if CFG.get('dstop', 999) == 22: return
        pqk = S.ps()
        S.mm(pqk[0:L, 0:L], qc, kc, r=False)
        if CFG.get('dstop', 999) == 23: return
        dm = tQ()
        S.tt(dm[0:L, 0:L], ed[0:L, 0:L], psrc(pqk[0:L, 0:L], L, L), ALU.mult)
        if CFG.get('dstop', 999) == 24: return
        dmT = tQ()
        transpose_to(dmT[0:L, 0:L], dm[0:L, 0:L], L, L, eng="act")
        if CFG.get('dstop', 999) == 25: return
        va = t258n()
        for e in range(2):
            transpose_to(va[0:L, e * 128:(e + 1) * 128], vT_[e][:, c0:c0 + L], 128, L, eng="act")
        S.memset(va[0:L, 256:257], 1.0)
        if CFG.get('dstop', 999) == 26: return
        S.memset(va[0:L, 257:258], 0.0)
        if CFG.get('dstop', 999) == 27: return
        ktok = tQ()
        transpose_to(ktok[0:L, :], kc, 128, L)
        if CFG.get('dstop', 999) == 28: return
        p1 = S.ps()
        S.mm(p1[0:L, 0:258], qc, Ch[:, :], r=False)
        if CFG.get('dstop', 999) == 29: return
        t1 = t258n()
        S.ts(t1[0:L, :], p1[0:L, 0:258], cc[0:L, 4:5], ALU.mult)
        if CFG.get('dstop', 999) == 30: return
        p2 = S.ps()
        S.mm(p2[0:L, 0:258], dmT[0:L, 0:L], va[0:L, :], r=False)
        if CFG.get('dstop', 999) == 31: return
        S.tt(t1[0:L, :], t1[0:L, :], p2[0:L, 0:258], ALU.add)
        if CFG.get('dstop', 999) == 32: return
        c3 = tC()
        S.act(c3[0:L, 5:6], t1[0:L, 256:257], AF.Abs)
        if CFG.get('dstop', 999) == 33: return
        S.tt(c3[0:L, 0:1], c3[0:L, 5:6], cc[0:L, 5:6], ALU.max)
        if CFG.get('dstop', 999) == 34: return
        S.recip(c3[0:L, 1:2], c3[0:L, 0:1])
        if CFG.get('dstop', 999) == 35: return
        hh = t258n()
        S.ts(hh[0:L, 0:256], t1[0:L, 0:256], c3[0:L, 1:2], ALU.mult)
        if CFG.get('dstop', 999) == 36: return
        S.act(t1[0:L, 0:256], hh[0:L, 0:256], AF.Square, accum=c3[0:L, 2:3])
        if CFG.get('dstop', 999) == 37: return
        S.ts(c3[0:L, 3:4], c3[0:L, 2:3], 1.0 / 256, ALU.mult, 1e-6, ALU.add)
        if CFG.get('dstop', 999) == 38: return
        S.act(c3[0:L, 3:4], c3[0:L, 3:4], AF.Sqrt)
        if CFG.get('dstop', 999) == 39: return
        S.recip(c3[0:L, 4:5], c3[0:L, 3:4])
        if CFG.get('dstop', 999) == 40: return
        S.ts(hh[0:L, 0:256], hh[0:L, 0:256], c3[0:L, 4:5], ALU.mult)
        if CFG.get('dstop', 999) == 41: return
        for e in range(2):
            pT = S.ps()
            S.tr(pT[:, 0:L], hh[0:L, e * 128:(e + 1) * 128], ident[0:L, 0:L])
            S.stt(outn.sub(2 * h + e, (slice(None), 2 * h + e, slice(c0, c0 + L))), psrc(pT[:, 0:L], L, 128), p_["d_norm_w"][:, e:e + 1], so_[e][:, c0:c0 + L], ALU.mult, ALU.mult)
        S.ts(va[0:L, :], va[0:L, :], cc[0:L, 6:7], ALU.mult)
        if CFG.get('dstop', 999) == 42: return
        pC = S.ps()
        S.mm(pC[:, 0:258], ktok[0:L, :], va[0:L, :], r=False)
        if CFG.get('dstop', 999) == 43: return
        S.stt(Ch[:, :], Ch[:, :], cb[:, 4:5], pC[:, 0:258], ALU.mult, ALU.add)
        if CFG.get('dstop', 999) == 44: return
        S.copy(ms[:, h:h + 1], cb[:, 3:4])
        if CFG.get('dstop', 999) == 45: return


    memT = mixed
    for t in range(2):
        S.dma("sp", tokbuf.t[:, :], dap(I["memp"], t * 128 * D, [[D, 128], [1, D]]), writes=[tokbuf[:]])
        for k in range(16):
            transpose_to(memT.sub(k, (slice(None), k, slice(t * 128, (t + 1) * 128))), tokbuf[:, k * 128:(k + 1) * 128], 128, 128)
    for l in range(2 if CFG.get('kv', True) else 0):
        Ktok = Buf(ubuf.t[:, 0:4, :].rearrange("p (m x) b -> p m (x b)", m=2), "Ktok")
        for which, wname, oname in ((0, "xk_w", "mkp"), (1, "xv_w", "mvp")):
            for h in range(4):
                w = load_w(I[wname], l * D * 512, 512, 0, h * 128, 128, 16)
                for mc in range(2):
                    p = S.ps()
                    for k in range(16):
                        S.mm(p[:, 0:128], memT.sub(k, (slice(None), k, slice(mc * 128, (mc + 1) * 128))), w[:, k, 0:128], start=(k == 0), stop=(k == 15), r=False)
                    if which == 0:
                        S.copy(Ktok.sub(mc, (slice(None), mc, slice(h * 128, (h + 1) * 128))), p[:, 0:128])
                        transpose_to(KT[l][:, h, mc * 128:(mc + 1) * 128], Ktok.sub(mc, (slice(None), mc, slice(h * 128, (h + 1) * 128))), 128, 128)
                    else:
                        S.copy(Vt[l][:, mc, h * 128:(h + 1) * 128], p[:, 0:128])
            src = Ktok if which == 0 else Vt[l]
            rd = [Ktok.sub(0, (slice(None), 0, slice(None))), Ktok.sub(1, (slice(None), 1, slice(None)))] if which == 0 else [Vt[l][:]]
            S.dma("pool", dap(O[oname], l * 256 * 512, [[512, 128], [128 * 512, 2], [1, 512]]), src.t[:, 0:2, 0:512], reads=rd)

    S.barrier()
    NBLK = CFG.get('nblk', 2048 // TBP)
    NLAY = CFG.get('layers', 2)
    chunks_p = [(c * 128, 128) for c in range(TBP // 128)]
    for blk in range(NBLK):
        for t in range(TBP // 128):
            S.dma("sp", tokbuf.t[:, :], dap(I["xp"], (blk * TBP + t * 128) * D, [[D, 128], [1, D]]), writes=[tokbuf[:]])
            for k in range(16):
                transpose_to(ch(xT, k, slice(t * 128, (t + 1) * 128)), tokbuf[:, k * 128:(k + 1) * 128], 128, 128, eng=("act" if k % 2 else "dve"))
                S.copy(ch(xTb, k, slice(t * 128, (t + 1) * 128)), ch(xT, k, slice(t * 128, (t + 1) * 128)), eng="pool")
        for l in range(NLAY):
            layer_block(l, TBP, "p", chunks_p, (blk == NBLK - 1) and CFG.get("stateout", True))
        for t in range(TBP // 128):
            for k in range(16):
                transpose_to(tokbuf[:, k * 128:(k + 1) * 128], ch(xT, k, slice(t * 128, (t + 1) * 128)), 128, 128, eng=("act" if k % 2 else "dve"))
            S.dma("pool", dap(O["yp"], (blk * TBP + t * 128) * D, [[D, 128], [1, D]]), tokbuf.t[:, :], reads=[tokbuf[:]])
    for l in range(2):
        emit_rows(lambda j: histA[l][:, j, :], 24, 3, lambda c0, n: dap(O["dcp"], l * 3 * 3072 + c0, [[3072, 3], [1, n]]))
        emit_rows(lambda j: histB[l][:, j, :], 8, 30, lambda c0, n: dap(O["gcp"], l * 30 * 1024 + c0, [[1024, 30], [1, n]]))
        emit_rows(lambda j: histC[l][:, j, :], 8, 2, lambda c0, n: dap(O["scp"], l * 2 * 1024 + c0, [[1024, 2], [1, n]]))

    if CFG.get('sample', True):
        S.dma("sp", tokbuf.t[0:NS, :], dap(I["xs"], 0, [[D, NS], [1, D]]), writes=[tokbuf[:]])
        for k in range(16):
            transpose_to(ch(xT, k, slice(0, NS)), tokbuf[0:NS, k * 128:(k + 1) * 128], NS, 128)
            S.copy(ch(xTb, k, slice(0, NS)), ch(xT, k, slice(0, NS)), eng="pool")
        S.barrier()
        for i in range(3):
            S.memset(CnS[i][:], 0.0)
        chunks_s = [(s, 1) for s in range(NS)]
        for l in range(NLAY):
            for (src, H, C, hs, dst) in ((I["sdc"], 3, 3072, hsA, O["dcs"]), (I["sgc"], 30, 1024, hsB, O["gcs"]), (I["ssc"], 2, 1024, hsC, O["scs"])):
                S.dma("pool", dap(dst, l * NS * H * C, [[H * C, NS], [C, H - 1], [1, C]]), dap(src, l * NS * H * C + C, [[H * C, NS], [C, H - 1], [1, C]]))
                spt = max(1, min(NS, 128 // H))
                for s0 in range(0, NS, spt):
                    ns_ = min(spt, NS - s0)
                    R = ns_ * H
                    for cc0 in range(0, C, D):
                        ncol = min(D, C - cc0)
                        S.dma("sp", tokbuf.t[0:R, 0:ncol], dap(src, (l * NS + s0) * H * C + cc0, [[C, R], [1, ncol]]), writes=[tokbuf[:]])
                        for jj in range(ncol // 128):
                            j = cc0 // 128 + jj
                            p = S.ps()
                            S.tr(p[:, 0:R], tokbuf[0:R, jj * 128:(jj + 1) * 128], ident[0:R, 0:R])
                            S.copy(hs.sub(j, (slice(None), j, slice(s0, s0 + ns_), slice(None))), p[:, 0:R].ap.rearrange("p (s h) -> p s h", h=H) if False else View(p.key, p.t[:, 0:R].rearrange("p (s h) -> p s h", h=H)))
            layer_block(l, NS, "s", chunks_s, False, hsA, hsB, hsC)
        for k in range(16):
            transpose_to(tokbuf[0:NS, k * 128:(k + 1) * 128], ch(xT, k, slice(0, NS)), 128, NS)
        S.dma("pool", dap(O["ys"], 0, [[D, NS], [1, D]]), tokbuf.t[0:NS, :], reads=[tokbuf[:]])
    S.finish()
    return nc


_NC = [None]


def kernel(**inp):
    a = {k: np.ascontiguousarray(np.asarray(v, dtype=np.float32)) for k, v in inp.items()}
    if _NC[0] is None:
        _NC[0] = build()
    nc = _NC[0]
    wnames = ["w_in", "b_in", "a_conv_w", "a_A_log", "a_dt_bias", "a_norm_w", "b_conv_w", "b_conv_b", "b_ln_g", "b_ln_b",
              "c_conv_w", "d_norm_w", "w_branch", "w_out", "ln1_g", "ln1_b", "xq_w", "xk_w", "xv_w", "xo_w", "ln2_g", "ln2_b",
              "ffn_w1", "ffn_b1", "ffn_w2", "ffn_b2", "ln3_g", "ln3_b"]
    in_maps = []
    for c in range(8):
        b = c % 4
        sl = slice(c * NS, (c + 1) * NS)
        m = {"xp": a["x_prompt"][b], "xs": a["x_sample"][sl, 0, :], "memp": a["mem_prompt"][b],
             "cmk": a["cache_mem_k"][:, sl].reshape(2, NS, 256, 512), "cmv": a["cache_mem_v"][:, sl].reshape(2, NS, 256, 512),
             "sdc": a["state_delta_conv"][:, sl], "sdS": a["state_delta_S"][:, sl], "sgc": a["state_glu_conv"][:, sl],
             "ssc": a["state_short_conv"][:, sl], "smC": a["state_mlstm_C"][:, sl], "smn": a["state_mlstm_n"][:, sl],
             "smm": a["state_mlstm_m"][:, sl]}
        m = {k: np.ascontiguousarray(v) for k, v in m.items()}
        for w in wnames:
            m[w] = a[w]
        in_maps.append(m)
    res = run_bass_kernel_spmd(nc, in_maps, core_ids=list(range(8)))
    R = res.results

    def pst(name, shape):
        return np.stack([np.asarray(R[b][name]) for b in range(4)], axis=1).reshape(shape)

    def sst(name, shape):
        return np.concatenate([np.asarray(R[c][name]) for c in range(8)], axis=1).reshape(shape)

    y_prompt = np.stack([np.asarray(R[b]["yp"]) for b in range(4)], axis=0)
    y_sample = np.concatenate([np.asarray(R[c]["ys"]) for c in range(8)], axis=0).reshape(128, 1, D)
    outs = (y_prompt, y_sample,
            pst("mkp", (2, 4, 256, 4, 128)), pst("mvp", (2, 4, 256, 4, 128)),
            pst("dcp", (2, 4, 3, 3072)), pst("dSp", (2, 4, 8, 128, 128)), pst("gcp", (2, 4, 30, 1024)),
            pst("scp", (2, 4, 2, 1024)), pst("mCp", (2, 4, 4, 128, 256)), pst("mnp", (2, 4, 4, 128)), pst("mmp", (2, 4, 4)),
            sst("dcs", (2, 128, 3, 3072)), sst("dSs", (2, 128, 8, 128, 128)), sst("gcs", (2, 128, 30, 1024)),
            sst("scs", (2, 128, 2, 1024)), sst("mCs", (2, 128, 4, 128, 256)), sst("mns", (2, 128, 4, 128)), sst("mms", (2, 128, 4)))
    return tuple(np.ascontiguousarray(o.astype(np.float32)) for o in outs)
```

```python
import contextlib
import numpy as np
import concourse.bass as bass
import concourse.mybir as mybir
from concourse.bass_utils import run_bass_kernel_spmd

F32 = mybir.dt.float32
F32R = mybir.dt.float32r
BF16 = mybir.dt.bfloat16
AF = mybir.ActivationFunctionType
ALU = mybir.AluOpType
AX = mybir.AxisListType

EPOCH = 20000
NDS = 8


class View:
    __slots__ = ("key", "ap")

    def __init__(self, key, ap):
        self.key = key
        self.ap = ap


class Buf:
    def __init__(self, t, key):
        self.t = t
        self.key = key

    def __getitem__(self, idx):
        return View(self.key, self.t[idx])

    def sub(self, subkey, idx):
        return View((self.key, subkey), self.t[idx])


class Sched:
    ENG = ["pe", "act", "dve", "pool", "sp"]

    def __init__(self, nc, dry=False):
        self.nc = nc
        self.dry = dry
        self.stack = contextlib.ExitStack()
        self.prog = {e: [] for e in self.ENG}
        self.count = {e: 0 for e in self.ENG}
        self.waited = {e: {} for e in self.ENG}
        self.last_w = {}
        self.readers = {}
        self.dslot = {e: 0 for e in self.ENG}
        self.dval = {}
        self.sids = {}
        self.nbuf = 0
        self.psum_banks = []
        self.psum_i = 0
        self.nops = 0

    def sb(self, shape, dtype=F32, name=None):
        self.nbuf += 1
        name = name or f"sb{self.nbuf}"
        t = self.stack.enter_context(self.nc.sbuf_tensor(name, list(shape), dtype))
        return Buf(t, name)

    def init_psum(self, n=8):
        for i in range(n):
            t = self.stack.enter_context(self.nc.psum_tensor(f"ps{i}", [128, 512], F32))
            self.psum_banks.append(Buf(t, f"ps{i}"))

    def ps(self):
        b = self.psum_banks[self.psum_i % len(self.psum_banks)]
        self.psum_i += 1
        return b

    def _deps(self, eng, reads, writes):
        deps = set()
        for v in reads:
            t = self.last_w.get(v.key)
            if t is not None:
                deps.add(t)
        for v in writes:
            t = self.last_w.get(v.key)
            if t is not None:
                deps.add(t)
            for r in self.readers.get(v.key, ()):
                if r[2] != eng or r[2] == "dma":
                    deps.add(r)
        for (sid, val, deng) in sorted(deps, key=lambda d: str(d)):
            if deng == eng and eng == "pe":
                continue
            if self.waited[eng].get(sid, 0) >= val:
                continue
            self.waited[eng][sid] = val
            self.prog[eng].append(("wait", sid, val))

    def _commit(self, tok, reads, writes):
        for v in reads:
            self.readers.setdefault(v.key, []).append(tok)
        for v in writes:
            self.last_w[v.key] = tok
            self.readers[v.key] = []

    def op(self, eng, fn, reads=(), writes=()):
        if self.dry:
            return None
        self._deps(eng, reads, writes)
        n = self.count[eng]
        self.count[eng] += 1
        sid = ("c", eng, n // EPOCH)
        val = n % EPOCH + 1
        self.sids[sid] = 1
        tok = (sid, val, eng)
        self.prog[eng].append(("op", fn, sid, 1))
        self._commit(tok, reads, writes)
        self.nops += 1
        return tok

    def dma(self, q, out_ap, in_ap, reads=(), writes=(), **kw):
        if self.dry:
            return None
        eng = q
        self._deps(eng, reads, writes)
        slot = self.dslot[q]
        self.dslot[q] = (slot + 1) % NDS
        sid = ("d", q, slot)
        self.sids[sid] = 1
        prev = self.dval.get(sid, 0)
        if prev > 0 and self.waited[eng].get(sid, 0) < prev:
            self.waited[eng][sid] = prev
            self.prog[eng].append(("wait", sid, prev))
        val = prev + 16
        self.dval[sid] = val
        tok = (sid, val, "dma")
        self.prog[eng].append(("op", lambda e: e.dma_start(out_ap, in_ap, **kw), sid, 16))
        self._commit(tok, reads, writes)
        self.nops += 1
        return tok

    def barrier(self):
        if self.dry:
            return
        toks = []
        for e in self.ENG:
            n = self.count[e]
            if n > 0:
                toks.append((("c", e, (n - 1) // EPOCH), (n - 1) % EPOCH + 1, e))
        for sid, val in self.dval.items():
            toks.append((sid, val, "dma"))
        for e in self.ENG:
            for (sid, val, deng) in toks:
                if deng == e:
                    continue
                if self.waited[e].get(sid, 0) >= val:
                    continue
                self.waited[e][sid] = val
                self.prog[e].append(("wait", sid, val))

    def finish(self):
        if self.dry:
            self.stack.close()
            return
        for sid, val in self.dval.items():
            q = sid[1]
            if self.waited[q].get(sid, 0) < val:
                self.prog[q].append(("wait", sid, val))
        nc = self.nc
        sems = {}
        for i, sid in enumerate(self.sids):
            sems[sid] = self.stack.enter_context(nc.semaphore(f"s{i}"))
        prog = self.prog

        def replay(name, e):
            for it in prog[name]:
                if it[0] == "wait":
                    e.wait_ge(sems[it[1]], it[2])
                else:
                    it[1](e).then_inc(sems[it[2]], it[3])

        with nc.Block() as block:
            @block.tensor
            def _(e):
                replay("pe", e)

            @block.scalar
            def _(e):
                replay("act", e)

            @block.vector
            def _(e):
                replay("dve", e)

            @block.gpsimd
            def _(e):
                replay("pool", e)

            @block.sync
            def _(e):
                replay("sp", e)
        self.stack.close()

    def mm(self, out, lhsT, rhs, start=True, stop=True, r=False):
        la, ra = lhsT.ap, rhs.ap
        if r:
            la, ra = la.bitcast(F32R), ra.bitcast(F32R)
        rd = [lhsT, rhs] + ([] if start else [out])
        return self.op("pe", lambda e: e.matmul(out.ap, la, ra, start=start, stop=stop), rd, [out])

    def tr(self, out, in_, ident):
        return self.op("pe", lambda e: e.transpose(out.ap, in_.ap, ident.ap), [in_, ident], [out])

    def act(self, out, in_, func, bias=None, scale=None, accum=None, eng="act"):
        kw = {}
        rd = [in_]
        wr = [out]
        if bias is not None:
            if isinstance(bias, View):
                kw["bias"] = bias.ap
                rd.append(bias)
            else:
                kw["bias"] = bias
        if scale is not None:
            if isinstance(scale, View):
                kw["scale"] = scale.ap
                rd.append(scale)
            else:
                kw["scale"] = scale
        if accum is not None:
            kw["accum_out"] = accum.ap
            wr.append(accum)
        return self.op("act", lambda e: e.activation(out.ap, in_.ap, func, **kw), rd, wr)

    def tt(self, out, a, b, op, eng="dve"):
        return self.op(eng, lambda e: e.tensor_tensor(out.ap, a.ap, b.ap, op), [a, b], [out])

    def ts(self, out, a, s1, op0, s2=None, op1=None, accum=None, eng="dve"):
        rd = [a]
        wr = [out]
        s1a = s1.ap if isinstance(s1, View) else s1
        s2a = s2.ap if isinstance(s2, View) else s2
        if isinstance(s1, View):
            rd.append(s1)
        if isinstance(s2, View):
            rd.append(s2)
        kw = {}
        if op1 is not None:
            kw["op1"] = op1
        if accum is not None:
            kw["accum_out"] = accum.ap
            wr.append(accum)
        return self.op(eng, lambda e: e.tensor_scalar(out.ap, a.ap, s1a, s2a, op0, **kw), rd, wr)

    def stt(self, out, a, s, b, op0, op1, accum=None):
        rd = [a, b]
        wr = [out]
        sa = s.ap if isinstance(s, View) else s
        if isinstance(s, View):
            rd.append(s)
        kw = {}
        if accum is not None:
            kw["accum_out"] = accum.ap
            wr.append(accum)
        return self.op("dve", lambda e: e.scalar_tensor_tensor(out.ap, a.ap, sa, b.ap, op0, op1, **kw), rd, wr)

    def copy(self, out, in_, eng="dve"):
        if eng == "act":
            return self.op("act", lambda e: e.copy(out.ap, in_.ap), [in_], [out])
        return self.op(eng, lambda e: e.tensor_copy(out.ap, in_.ap), [in_], [out])

    def memset(self, out, val, eng="pool"):
        return self.op(eng, lambda e: e.memset(out.ap, val), [], [out])

    def recip(self, out, in_):
        return self.op("dve", lambda e: e.reciprocal(out.ap, in_.ap), [in_], [out])

    def rmax(self, out, in_, eng="dve"):
        return self.op(eng, lambda e: e.reduce_max(out.ap, in_.ap, AX.X), [in_], [out])

D = 2048
NIN = 20504
O_QA, O_KA, O_VA, O_ZA, O_BD = 0, 1024, 2048, 3072, 4096
O_GA, O_GG, O_BG, O_CG, O_HC = 4112, 5136, 6160, 7184, 8208
O_QD, O_KD, O_VD, O_OD, O_IF, O_GATE = 9232, 9744, 10256, 11280, 12304, 12312
ALPHA = 4 ** 0.25
NEG = -1.0e30
NS = 16
TBP = 256
CFG = {}


def dap(t, off, dims):
    return bass.AP(t, off, [list(d) for d in dims])


def build():
    specs = []
    _build(True, specs)
    return _build(False, specs)


def _build(DRY, WSPECS):
    nc = bass.Bass("TRN2", target_bir_lowering=False)

    def din(name, shape):
        return nc.dram_tensor(name, list(shape), F32, kind="ExternalInput")

    def dout(name, shape):
        return nc.dram_tensor(name, list(shape), F32, kind="ExternalOutput")

    I = {}
    for name, shape in [
        ("xp", (2048, D)), ("xs", (NS, D)), ("memp", (256, D)),
        ("cmk", (2, NS, 256, 512)), ("cmv", (2, NS, 256, 512)),
        ("sdc", (2, NS, 3, 3072)), ("sdS", (2, NS, 8, 128, 128)), ("sgc", (2, NS, 30, 1024)),
        ("ssc", (2, NS, 2, 1024)), ("smC", (2, NS, 4, 128, 256)), ("smn", (2, NS, 4, 128)), ("smm", (2, NS, 4)),
        ("w_in", (2, D, NIN)), ("b_in", (2, NIN)), ("a_conv_w", (2, 4, 3072)), ("a_A_log", (2, 8)),
        ("a_dt_bias", (2, 8)), ("a_norm_w", (2, 128)), ("b_conv_w", (2, 31, 1024)), ("b_conv_b", (2, 1024)),
        ("b_ln_g", (2, 1024)), ("b_ln_b", (2, 1024)), ("c_conv_w", (2, 3, 1024)), ("d_norm_w", (2, 256)),
        ("w_branch", (2, 4, 1024, D)), ("w_out", (2, D, D)), ("ln1_g", (2, D)), ("ln1_b", (2, D)),
        ("xq_w", (2, D, 512)), ("xk_w", (2, D, 512)), ("xv_w", (2, D, 512)), ("xo_w", (2, 512, D)),
        ("ln2_g", (2, D)), ("ln2_b", (2, D)), ("ffn_w1", (2, D, 4 * D)), ("ffn_b1", (2, 4 * D)),
        ("ffn_w2", (2, 4 * D, D)), ("ffn_b2", (2, D)), ("ln3_g", (2, D)), ("ln3_b", (2, D)),
    ]:
        I[name] = din(name, shape)
    O = {}
    for name, shape in [
        ("yp", (2048, D)), ("ys", (NS, D)), ("mkp", (2, 256, 512)), ("mvp", (2, 256, 512)),
        ("dcp", (2, 3, 3072)), ("dSp", (2, 8, 128, 128)), ("gcp", (2, 30, 1024)), ("scp", (2, 2, 1024)),
        ("mCp", (2, 4, 128, 256)), ("mnp", (2, 4, 128)), ("mmp", (2, 4)),
        ("dcs", (2, NS, 3, 3072)), ("dSs", (2, NS, 8, 128, 128)), ("gcs", (2, NS, 30, 1024)),
        ("scs", (2, NS, 2, 1024)), ("mCs", (2, NS, 4, 128, 256)), ("mns", (2, NS, 4, 128)), ("mms", (2, NS, 4)),
    ]:
        O[name] = dout(name, shape)

    DBG = {}
    if CFG.get('dump'):
        for nm in ['C', 'B', 'A', 'D']:
            DBG[nm] = dout('dbg_' + nm, (128, 8, TBP))
        for nm in ['mixed', 'x1', 'x2', 'x3']:
            DBG[nm] = dout('dbg_' + nm, (128, 16, TBP))
    dumped = set()
    S = Sched(nc, dry=DRY)
    S.init_psum()
    sb = S.sb

    def dump(nm, buf, nch):
        if not CFG.get('dump') or nm in dumped:
            return
        dumped.add(nm)
        S.dma("pool", dap(DBG[nm], 0, [[nch * TBP, 128], [TBP, nch], [1, TBP]]), buf.t[:, 0:nch, :],
              reads=[buf.sub(k, (slice(None), k, slice(None))) for k in range(nch)])

    ones = sb([128, 128], name="ones")
    ident = sb([128, 128], name="ident")
    triu = sb([128, 128], name="triu")
    maskS = sb([128, 128], name="maskS")
    maskL = sb([128, 128], name="maskL")
    zeros = sb([128, 128], name="zeros")
    S.memset(ones[:], 1.0)
    S.memset(zeros[:], 0.0)
    S.op("pool", lambda e: e.affine_select(ident.t[:], ones.t[:], [[-1, 128]], ALU.is_equal, 0.0, base=0, channel_multiplier=1), [ones[:]], [ident[:]])
    S.op("pool", lambda e: e.affine_select(triu.t[:], ones.t[:], [[1, 128]], ALU.is_ge, 0.0, base=0, channel_multiplier=-1), [ones[:]], [triu[:]])
    S.op("pool", lambda e: e.affine_select(maskS.t[:], zeros.t[:], [[1, 128]], ALU.is_gt, NEG, base=0, channel_multiplier=-1), [zeros[:]], [maskS[:]])
    S.op("pool", lambda e: e.affine_select(maskL.t[:], zeros.t[:], [[-1, 128]], ALU.is_ge, NEG, base=0, channel_multiplier=1), [zeros[:]], [maskL[:]])

    def colload(dst, dcol0, t, off, nchunk, n=128):
        S.dma("pool", dst.t[0:n, dcol0:dcol0 + nchunk], dap(t, off, [[1, n], [128, nchunk]]), writes=[dst[:]],
              allow_slow_non_contiguous=True)

    P = []
    for l in range(2):
        p = {}
        bi = sb([128, 162], name=f"bin{l}")
        colload(bi, 0, I["b_in"], l * NIN + 0, 32)
        colload(bi, 32, I["b_in"], l * NIN + O_BD, 1, n=16)
        colload(bi, 33, I["b_in"], l * NIN + O_GA, 64)
        colload(bi, 97, I["b_in"], l * NIN + O_IF, 1, n=8)
        colload(bi, 98, I["b_in"], l * NIN + O_GATE, 64)
        p["bin"] = bi
        acw = sb([128, 24, 4], name=f"acw{l}")
        for k in range(4):
            S.dma("pool", acw.t[:, :, k], dap(I["a_conv_w"], l * 4 * 3072 + k * 3072, [[1, 128], [128, 24]]), writes=[acw[:]], allow_slow_non_contiguous=True)
        bcw = sb([128, 8, 31], name=f"bcw{l}")
        for k in range(31):
            S.dma("pool", bcw.t[:, :, k], dap(I["b_conv_w"], l * 31 * 1024 + k * 1024, [[1, 128], [128, 8]]), writes=[bcw[:]], allow_slow_non_contiguous=True)
        ccw = sb([128, 8, 3], name=f"ccw{l}")
        for k in range(3):
            S.dma("pool", ccw.t[:, :, k], dap(I["c_conv_w"], l * 3 * 1024 + k * 1024, [[1, 128], [128, 8]]), writes=[ccw[:]], allow_slow_non_contiguous=True)
        p["acw"], p["bcw"], p["ccw"] = acw, bcw, ccw
        for nm, nch in [("b_conv_b", 8), ("b_ln_g", 8), ("b_ln_b", 8), ("a_norm_w", 1), ("d_norm_w", 2), ("ln1_g", 16), ("ln1_b", 16),
                        ("ln2_g", 16), ("ln2_b", 16), ("ln3_g", 16), ("ln3_b", 16), ("ffn_b1", 64), ("ffn_b2", 16)]:
            tl = sb([128, nch], name=f"{nm}{l}")
            colload(tl, 0, I[nm], l * nch * 128, nch)
            p[nm] = tl
        negA = sb([128, 8], name=f"negA{l}")
        dtb = sb([128, 8], name=f"dtb{l}")
        S.dma("pool", negA.t[:, :], dap(I["a_A_log"], l * 8, [[0, 128], [1, 8]]), writes=[negA[:]])
        S.dma("pool", dtb.t[:, :], dap(I["a_dt_bias"], l * 8, [[0, 128], [1, 8]]), writes=[dtb[:]])
        S.act(negA[:], negA[:], AF.Exp)
        S.ts(negA[:], negA[:], -1.0, ALU.mult)
        p["negA"], p["dtb"] = negA, dtb
        P.append(p)

    def bcol(l, col0, n=128):
        if col0 < O_BD:
            c = col0 // 128
        elif col0 == O_BD:
            c = 32
        elif col0 < O_IF:
            c = 33 + (col0 - O_GA) // 128
        elif col0 == O_IF:
            c = 97
        else:
            c = 98 + (col0 - O_GATE) // 128
        return P[l]["bin"][0:n, c:c + 1]

    xT = sb([128, 16, TBP], name="xT")
    mixed = sb([128, 16, TBP], name="mixed")
    outn = sb([128, 8, TBP], BF16, name="outn")
    xTb = sb([128, 16, TBP], BF16, name="xTb")
    mixedb = sb([128, 16, TBP], BF16, name="mixedb")
    hb = sb([128, 8, TBP], BF16, name="hb")
    ubuf = sb([128, 8, TBP], name="ubuf")
    tokbuf = sb([128, D], name="tokbuf")
    wsl = [sb([128, 2048], name=f"w{i}") for i in range(3)]
    wbf = [sb([128, 2048], BF16, name=f"wb{i}") for i in range(3)]
    wi = [0]
    NT = 10
    tmpA = [sb([128, TBP], name=f"tA{i}") for i in range(NT)]
    ti = [0]
    NQ = 12
    tmpQ = [sb([128, 128], name=f"tQ{i}") for i in range(NQ)]
    qi = [0]
    NC_ = 48
    tmpC = [sb([128, 8], name=f"tC{i}") for i in range(NC_)]
    ci_ = [0]
    extb = sb([128, TBP + 30], name="extb")
    persA = sb([128, 16, 24], name="persA")
    lnm = sb([128, TBP], name="lnm"); lnr = sb([128, TBP], name="lnr"); lnm2 = sb([128, TBP], name="lnm2")
    tmpQL = [sb([128, 128], name=f"tQL{i}") for i in range(12)]
    qli = [0]
    persD = sb([128, 16, 12], name="persD")
    exts = sb([128, NS, 31], name="exts")
    t258 = [sb([128, 258], name=f"t258_{i}") for i in range(4)]
    t258i = [0]

    def tA():
        ti[0] += 1
        return tmpA[ti[0] % NT]

    def tQ():
        qi[0] += 1
        return tmpQ[qi[0] % NQ]

    def tQL():
        qli[0] += 1
        return tmpQL[qli[0] % 12]

    def psrc(pv, L, shape_rows):
        if L != 1:
            return pv
        t = tQ()
        v = t[0:shape_rows, 0:1]
        S.act(v, pv, AF.Identity)
        return v

    def tC():
        ci_[0] += 1
        return tmpC[ci_[0] % NC_]

    def t258n():
        t258i[0] += 1
        return t258[t258i[0] % 4]

    def ch(buf, k, sl=slice(None)):
        return buf.sub(k, (slice(None), k, sl))

    ARN = 10240
    arena = sb([128, ARN], name="arena")
    apos = {"p": 0, "s": 0}

    def carve(phase, shape, name):
        n = 1
        for s_ in shape[1:]:
            n *= s_
        a0 = apos[phase]
        apos[phase] += n
        assert apos[phase] <= ARN, (phase, apos[phase])
        v = arena.t[:, a0:a0 + n]
        if len(shape) == 3:
            v = v.rearrange("p (a b) -> p a b", a=shape[1])
        elif len(shape) == 4:
            v = v.rearrange("p (a b c) -> p a b c", a=shape[1], b=shape[2])
        return Buf(v, name)

    histA = [carve("p", [128, 24, 3], f"hA{l}") for l in range(2)]
    histB = [carve("p", [128, 8, 30], f"hB{l}") for l in range(2)]
    histC = [carve("p", [128, 8, 2], f"hC{l}") for l in range(2)]
    Sa = [[carve("p", [128, 128], f"Sa{l}_{h}") for h in range(8)] for l in range(2)]
    Cn = [[carve("p", [128, 258], f"Cn{l}_{h}") for h in range(4)] for l in range(2)]
    mst = [carve("p", [128, 4], f"mst{l}") for l in range(2)]
    KT = [carve("p", [128, 4, 256], f"KT{l}") for l in range(2)]
    Vt = [carve("p", [128, 2, 512], f"Vt{l}") for l in range(2)]
    for l in range(2):
        S.memset(histA[l][:], 0.0); S.memset(histB[l][:], 0.0); S.memset(histC[l][:], 0.0)
        S.memset(mst[l][:], 0.0)
        for h in range(8):
            S.memset(Sa[l][h][:], 0.0)
        for h in range(4):
            S.memset(Cn[l][h][:], 0.0)
    SaS = [carve("s", [128, 128], f"SaS{i}") for i in range(3)]
    CnS = [carve("s", [128, 258], f"CnS{i}") for i in range(3)]
    mstS = [carve("s", [128, 4], f"mstS{i}") for i in range(3)]
    ssi = [0]
    KTs = carve("s", [128, 4, 256], "KTs")
    Kts = carve("s", [128, 2, 512], "Kts")
    Vts = carve("s", [128, 2, 512], "Vts")
    hsA = carve("s", [128, 24, NS, 3], "hsA")
    hsB = carve("s", [128, 8, NS, 30], "hsB")
    hsC = carve("s", [128, 8, NS, 2], "hsC")
    tmpB_new = carve("s", [128, 8, NS], "tmpBn")
    tmpA_new = carve("s", [128, 24, NS], "tmpAn")
    qTb = sb([128, 4, TBP], name="qTb")
    oTb = sb([128, 4, TBP], BF16, name="oTb")

    issued = [0]
    WDEPTH = 2
    scr_idx = {}
    USE_SCR = CFG.get('scr', True)
    wscr = None
    if not DRY and USE_SCR:
        nuniq = len({(s[0].name,) + tuple(s[1:]) for s in WSPECS if s[7]})
        wscr = [nc.dram_tensor(f"wscr{q}", [min(400, nuniq - q * 400), 128, 2048], BF16, kind="Internal") for q in range((nuniq + 399) // 400)]

    def w_views(j, wide):
        kk = 4 if wide else 16
        w = wsl[j % 3]
        wb = wbf[j % 3]
        return (w, wb, w.t[:, :].rearrange("p (k n) -> p k n", k=kk), wb.t[:, :].rearrange("p (k n) -> p k n", k=kk))

    def w_issue(j):
        (t, base, rstride, row0, col0, n, kcn, bf, wide) = WSPECS[j]
        w, wb, wv, wbv = w_views(j, wide)
        skey = (t.name,) + tuple(WSPECS[j][1:])
        if bf and USE_SCR and skey in scr_idx:
            idx = scr_idx[skey]
            S.dma("sp", wb.t[:, :], dap(wscr[idx // 400], (idx % 400) * 128 * 2048, [[2048, 128], [1, 2048]]),
                  reads=[View(("scr", idx), None)], writes=[wb[:]])
            return
        S.dma("sp", wv[:, 0:kcn, 0:n], dap(t, base + row0 * rstride + col0, [[rstride, 128], [128 * rstride, kcn], [1, n]]),
              writes=[w[:]])
        if bf:
            if j % 2 == 0:
                S.op("act", lambda e: e.copy(wbv[:, 0:kcn, 0:n], wv[:, 0:kcn, 0:n]), [w[:]], [wb[:]])
            else:
                S.op("dve", lambda e: e.tensor_copy(wbv[:, 0:kcn, 0:n], wv[:, 0:kcn, 0:n]), [w[:]], [wb[:]])
            if USE_SCR:
                idx = len(scr_idx)
                scr_idx[skey] = idx
                S.dma("pool", dap(wscr[idx // 400], (idx % 400) * 128 * 2048, [[2048, 128], [1, 2048]]), wb.t[:, :],
                      reads=[wb[:]], writes=[View(("scr", idx), None)])

    def load_w(t, base, rstride, row0, col0, n, kcn, bf=False, wide=False):
        i = wi[0]
        wi[0] += 1
        spec = (t, base, rstride, row0, col0, n, kcn, bf, wide)
        if DRY:
            WSPECS.append(spec)
        else:
            assert WSPECS[i][1:] == spec[1:], (i, WSPECS[i][1:], spec[1:])
            while issued[0] <= min(i + WDEPTH, len(WSPECS) - 1):
                w_issue(issued[0])
                issued[0] += 1
        w, wb, wv, wbv = w_views(i, wide)
        return Buf(wbv, wb.key) if bf else Buf(wv, w.key)

    def fm_proj(t, base, rstride, row0, col0, n, kcn, rhs_fn, TB):
        w = load_w(t, base, rstride, row0, col0, n, kcn, bf=True)
        p = S.ps()
        for k in range(kcn):
            S.mm(p[0:n, 0:TB], w[:, k, 0:n], rhs_fn(k), start=(k == 0), stop=(k == kcn - 1), r=False)
        return p

    def fm_group(t, base, rstride, row0, col0, ncols, kcn, rhs_fn, TB):
        nch = (ncols + 127) // 128
        pss = [S.ps() for _ in range(nch)]
        for kq in range(0, kcn, 4):
            nk = min(4, kcn - kq)
            w = load_w(t, base, rstride, row0 + kq * 128, col0, ncols, nk, bf=True, wide=True)
            for c in range(nch):
                n = min(128, ncols - c * 128)
                for kk in range(nk):
                    k = kq + kk
                    S.mm(pss[c][0:n, 0:TB], w[:, kk, c * 128:c * 128 + n], rhs_fn(k), start=(k == 0), stop=(k == kcn - 1), r=False)
        return pss

    def win_group(l, col0, ncols, TB):
        return fm_group(I["w_in"], l * D * NIN, NIN, 0, col0, ncols, 16, lambda k: ch(xTb, k, slice(0, TB)), TB)

    def win_proj(l, col0, n, TB):
        return fm_proj(I["w_in"], l * D * NIN, NIN, 0, col0, n, 16, lambda k: ch(xTb, k, slice(0, TB)), TB)

    def transpose_to(dst_view, src_view, rows, cols, eng="dve"):
        p = S.ps()
        S.tr(p[0:cols, 0:rows], src_view, ident[0:rows, 0:rows])
        S.copy(dst_view, p[0:cols, 0:rows], eng=eng)

    def layernorm(l, gname, bname, TB):
        pm = S.ps()
        for k in range(16):
            S.mm(pm[:, 0:TB], ones[:, :], ch(xT, k, slice(0, TB)), start=(k == 0), stop=(k == 15), r=False)
        pq = S.ps()
        for k in range(16):
            sq = tA()
            S.act(sq[:, 0:TB], ch(xT, k, slice(0, TB)), AF.Square)
            S.mm(pq[:, 0:TB], ones[:, :], sq[:, 0:TB], start=(k == 0), stop=(k == 15), r=False)
        mean, rstd, m2 = lnm, lnr, lnm2
        S.ts(mean[:, 0:TB], pm[:, 0:TB], 1.0 / D, ALU.mult)
        S.tt(m2[:, 0:TB], mean[:, 0:TB], mean[:, 0:TB], ALU.mult)
        S.stt(rstd[:, 0:TB], pq[:, 0:TB], 1.0 / D, m2[:, 0:TB], ALU.mult, ALU.subtract)
        S.ts(rstd[:, 0:TB], rstd[:, 0:TB], 0.0, ALU.max, 1e-5, ALU.add)
        S.act(rstd[:, 0:TB], rstd[:, 0:TB], AF.Sqrt)
        S.recip(rstd[:, 0:TB], rstd[:, 0:TB])
        for k in range(16):
            t = tA()
            S.tt(t[:, 0:TB], ch(xT, k, slice(0, TB)), mean[:, 0:TB], ALU.subtract)
            S.tt(t[:, 0:TB], t[:, 0:TB], rstd[:, 0:TB], ALU.mult)
            S.act(ch(xT, k, slice(0, TB)), t[:, 0:TB], AF.Identity, bias=P[l][bname][:, k:k + 1], scale=P[l][gname][:, k:k + 1])
            S.act(ch(xTb, k, slice(0, TB)), t[:, 0:TB], AF.Identity, bias=P[l][bname][:, k:k + 1], scale=P[l][gname][:, k:k + 1])

    def emit_rows(srcT_fn, nch, R, dst_ap_fn):
        for j0 in range(0, nch, 16):
            nj = min(16, nch - j0)
            for j in range(j0, j0 + nj):
                transpose_to(tokbuf[0:R, (j - j0) * 128:(j - j0 + 1) * 128], srcT_fn(j), 128, R)
            S.dma("pool", dst_ap_fn(j0 * 128, nj * 128), tokbuf.t[0:R, 0:nj * 128], reads=[tokbuf[:]])

    class Conv:
        def __init__(self, mode, W, TB):
            self.mode, self.W, self.TB, self.H = mode, W, TB, W - 1
            if mode == "p":
                self.newv = extb[:, self.H:self.H + TB]
                self.histv = extb[:, 0:self.H]
                self.tail = extb[:, TB:TB + self.H]
            else:
                self.newv = exts[:, :, self.H]
                self.histv = exts[:, :, 0:self.H]

        def tap(self, k):
            if self.mode == "p":
                return extb[:, k:k + self.TB]
            return exts[:, :, k]

    def conv_apply(cv, wtile, j, out_view, bias_view=None):
        W = cv.W
        acc = out_view
        if bias_view is not None:
            S.ts(acc, cv.tap(0), wtile[:, j, 0:1], ALU.mult, bias_view, ALU.add)
        else:
            S.ts(acc, cv.tap(0), wtile[:, j, 0:1], ALU.mult)
        for k in range(1, W):
            S.stt(acc, cv.tap(k), wtile[:, j, k:k + 1], acc, ALU.mult, ALU.add)

    def layer_block(l, TB, mode, chunks, last, hsA=None, hsB=None, hsC=None):
        p_ = P[l]
        shp = (lambda v: v)
        if mode == "p":
            ov = lambda buf: buf[:, 0:TB]
        else:
            ov = lambda buf: buf[:, 0:TB]
        first_branch = [True]

        def branch_merge(n):
            for dg in range(4):
                pgs = win_group(l, O_GATE + n * D + dg * 512, 512, TB)
                gts = []
                for c in range(4):
                    gt = tA()
                    S.act(gt[:, 0:TB], pgs[c][:, 0:TB], AF.Sigmoid, bias=bcol(l, O_GATE + n * D + (dg * 4 + c) * 128))
                    gts.append(gt)
                pbs = fm_group(I["w_branch"], (l * 4 + n) * 1024 * D, D, 0, dg * 512, 512, 8, lambda k: ch(outn, k, slice(0, TB)), TB)
                for c in range(4):
                    d = dg * 4 + c
                    pb, gt = pbs[c], gts[c]
                    if first_branch[0]:
                        S.tt(ch(mixed, d, slice(0, TB)), pb[:, 0:TB], gt[:, 0:TB], ALU.mult)
                    else:
                        S.tt(gt[:, 0:TB], pb[:, 0:TB], gt[:, 0:TB], ALU.mult)
                        if n == 3:
                            S.tt(ch(mixedb, d, slice(0, TB)), ch(mixed, d, slice(0, TB)), gt[:, 0:TB], ALU.add)
                        else:
                            S.tt(ch(mixed, d, slice(0, TB)), ch(mixed, d, slice(0, TB)), gt[:, 0:TB], ALU.add)
            first_branch[0] = False

        def newrow_out(vals_fn, nch, dst_t, lbase, H, C):
            emit_rows(vals_fn, nch, NS, lambda c0, n: dap(dst_t, lbase + (H - 1) * C + c0, [[H * C, NS], [1, n]]))

        def ph_C():
            cv = Conv(mode, 3, TB)
            newC = ubuf
            for jg in range(2):
                pbgs = win_group(l, O_BG + jg * 512, 512, TB)
                bgs = []
                for c in range(4):
                    t_ = tA()
                    S.act(t_[:, 0:TB], pbgs[c][:, 0:TB], AF.Identity, bias=bcol(l, O_BG + (jg * 4 + c) * 128))
                    bgs.append(t_)
                pcs = win_group(l, O_CG + jg * 512, 512, TB)
                cgs = []
                for c in range(4):
                    t_ = tA()
                    S.act(t_[:, 0:TB], pcs[c][:, 0:TB], AF.Identity, bias=bcol(l, O_CG + (jg * 4 + c) * 128))
                    cgs.append(t_)
                phs = win_group(l, O_HC + jg * 512, 512, TB)
                for c in range(4):
                    j = jg * 4 + c
                    if mode == "p":
                        S.copy(cv.histv, histC[l][:, j, :], eng="pool")
                    else:
                        S.copy(cv.histv, hsC.sub(j, (slice(None), j, slice(None), slice(None))), eng="pool")
                    S.stt(cv.newv, phs[c][:, 0:TB], bcol(l, O_HC + j * 128), cgs[c][:, 0:TB], ALU.add, ALU.mult)
                    conv_apply(cv, p_["ccw"], j, cgs[c][:, 0:TB])
                    if mode == "p":
                        S.copy(histC[l][:, j, :], cv.tail, eng="pool")
                    else:
                        S.copy(ch(newC, j, slice(0, NS)), cv.newv, eng="pool")
                    S.tt(ch(outn, j, slice(0, TB)), bgs[c][:, 0:TB], cgs[c][:, 0:TB], ALU.mult)
            if mode == "s":
                newrow_out(lambda j: ch(newC, j, slice(0, NS)), 8, O["scs"], l * NS * 2 * 1024, 2, 1024)
            branch_merge(2)

        def ph_B():
            cv = Conv(mode, 31, TB)
            newB = mixed
            for jg in range(2):
                pgs = win_group(l, O_GG + jg * 512, 512, TB)
                sgs = []
                for c in range(4):
                    t_ = tA()
                    S.act(t_[:, 0:TB], pgs[c][:, 0:TB], AF.Sigmoid, bias=bcol(l, O_GG + (jg * 4 + c) * 128))
                    sgs.append(t_)
                pas = win_group(l, O_GA + jg * 512, 512, TB)
                for c in range(4):
                    j = jg * 4 + c
                    if mode == "p":
                        S.copy(cv.histv, histB[l][:, j, :], eng="pool")
                    else:
                        S.copy(cv.histv, hsB.sub(j, (slice(None), j, slice(None), slice(None))), eng="pool")
                    S.stt(cv.newv, pas[c][:, 0:TB], bcol(l, O_GA + j * 128), sgs[c][:, 0:TB], ALU.add, ALU.mult)
                    conv_apply(cv, p_["bcw"], j, ch(ubuf, j, slice(0, TB)), bias_view=p_["b_conv_b"][:, j:j + 1])
                    if mode == "p":
                        S.copy(histB[l][:, j, :], cv.tail, eng="pool")
                    else:
                        S.copy(tmpB_new.sub(j, (slice(None), j, slice(None))), cv.newv, eng="pool")
            if mode == "s":
                newrow_out(lambda j: tmpB_new.sub(j, (slice(None), j, slice(None))), 8, O["gcs"], l * NS * 30 * 1024, 30, 1024)
            pm = S.ps()
            for k in range(8):
                S.mm(pm[:, 0:TB], ones[:, :], ch(ubuf, k, slice(0, TB)), start=(k == 0), stop=(k == 7), r=False)
            pq = S.ps()
            for k in range(8):
                sq = tA()
                S.act(sq[:, 0:TB], ch(ubuf, k, slice(0, TB)), AF.Square)
                S.mm(pq[:, 0:TB], ones[:, :], sq[:, 0:TB], start=(k == 0), stop=(k == 7), r=False)
            mean, rstd, m2 = lnm, lnr, lnm2
            S.ts(mean[:, 0:TB], pm[:, 0:TB], 1.0 / 1024, ALU.mult)
            S.tt(m2[:, 0:TB], mean[:, 0:TB], mean[:, 0:TB], ALU.mult)
            S.stt(rstd[:, 0:TB], pq[:, 0:TB], 1.0 / 1024, m2[:, 0:TB], ALU.mult, ALU.subtract)
            S.ts(rstd[:, 0:TB], rstd[:, 0:TB], 0.0, ALU.max, 1e-5, ALU.add)
            S.act(rstd[:, 0:TB], rstd[:, 0:TB], AF.Sqrt)
            S.recip(rstd[:, 0:TB], rstd[:, 0:TB])
            for k in range(8):
                t = tA()
                S.tt(t[:, 0:TB], ch(ubuf, k, slice(0, TB)), mean[:, 0:TB], ALU.subtract)
                S.tt(t[:, 0:TB], t[:, 0:TB], rstd[:, 0:TB], ALU.mult)
                S.act(ch(outn, k, slice(0, TB)), t[:, 0:TB], AF.Silu, bias=p_["b_ln_b"][:, k:k + 1], scale=p_["b_ln_g"][:, k:k + 1])
            branch_merge(1)

        def ph_A():
            pbd = win_proj(l, O_BD, 16, TB)
            bdT = tA()
            S.act(bdT[0:16, 0:TB], pbd[0:16, 0:TB], AF.Identity, bias=bcol(l, O_BD, 16))
            beta_t, g_t, gc_t = [], [], []
            for (c0, L) in chunks:
                p = S.ps()
                S.tr(p[0:L, 0:16], bdT[0:16, c0:c0 + L], ident[0:16, 0:16])
                ci_a = len(beta_t)
                be = Buf(persA.t[:, ci_a, 0:8], ("persA", ci_a, 0)); gg = Buf(persA.t[:, ci_a, 8:16], ("persA", ci_a, 1))
                gcx = Buf(persA.t[:, ci_a, 16:24], ("persA", ci_a, 2)); tmp = tC()
                S.act(be[0:L, 0:8], p[0:L, 0:8], AF.Sigmoid)
                S.tt(tmp[0:L, 0:8], p[0:L, 8:16], p_["dtb"][0:L, 0:8], ALU.add)
                S.act(tmp[0:L, 0:8], tmp[0:L, 0:8], AF.Exp)
                S.act(tmp[0:L, 0:8], tmp[0:L, 0:8], AF.Ln, bias=1.0)
                S.tt(gg[0:L, 0:8], tmp[0:L, 0:8], p_["negA"][0:L, 0:8], ALU.mult)
                pc = S.ps()
                S.mm(pc[0:L, 0:8], triu[0:L, 0:L], gg[0:L, 0:8], r=False)
                S.copy(gcx[0:L, 0:8], pc[0:L, 0:8])
                beta_t.append(be); g_t.append(gg); gc_t.append(gcx)
            newA = tmpA_new
            for h in range(8):
                qkv = []
                for part, off in enumerate((O_QA, O_KA, O_VA)):
                    jj = part * 8 + h
                    cv = Conv(mode, 4, TB)
                    pp = win_proj(l, off + h * 128, 128, TB)
                    if mode == "p":
                        S.copy(cv.histv, histA[l][:, jj, :], eng="pool")
                    else:
                        S.copy(cv.histv, hsA.sub(jj, (slice(None), jj, slice(None), slice(None))), eng="pool")
                    S.act(cv.newv, pp[:, 0:TB], AF.Identity, bias=bcol(l, off + h * 128))
                    acc = tA()
                    conv_apply(cv, p_["acw"], jj, acc[:, 0:TB])
                    if mode == "p":
                        S.copy(histA[l][:, jj, :], cv.tail, eng="pool")
                    else:
                        S.copy(newA.sub(jj, (slice(None), jj, slice(None))), cv.newv, eng="pool")
                    S.act(acc[:, 0:TB], acc[:, 0:TB], AF.Silu)
                    qkv.append(acc)
                qT_, kT_, vT_ = qkv
                for idx, t_ in enumerate((qT_, kT_)):
                    sq = tA()
                    S.act(sq[:, 0:TB], t_[:, 0:TB], AF.Square)
                    pn = S.ps()
                    S.mm(pn[:, 0:TB], ones[:, :], sq[:, 0:TB], r=False)
                    S.ts(sq[:, 0:TB], pn[:, 0:TB], 1e-6, ALU.add)
                    S.act(sq[:, 0:TB], sq[:, 0:TB], AF.Sqrt)
                    S.recip(sq[:, 0:TB], sq[:, 0:TB])
                    if idx == 0:
                        S.stt(t_[:, 0:TB], t_[:, 0:TB], 128 ** -0.5, sq[:, 0:TB], ALU.mult, ALU.mult)
                    else:
                        S.tt(t_[:, 0:TB], t_[:, 0:TB], sq[:, 0:TB], ALU.mult)
                pz = win_proj(l, O_ZA + h * 128, 128, TB)
                sz = tA()
                S.act(sz[:, 0:TB], pz[:, 0:TB], AF.Silu, bias=bcol(l, O_ZA + h * 128))
                for ci, (c0, L) in enumerate(chunks):
                    if mode == "p":
                        Sh = Sa[l][h]
                    else:
                        Sh = SaS[ssi[0] % 3]; ssi[0] += 1
                        S.dma("sp", Sh.t[:, :], dap(I["sdS"], ((l * NS + ci) * 8 + h) * 16384, [[128, 128], [1, 128]]), writes=[Sh[:]])
                    delta_chunk(l, h, ci, c0, L, qT_, kT_, vT_, sz, Sh, beta_t[ci], g_t[ci], gc_t[ci])
                    if mode == "s":
                        S.dma("pool", dap(O["dSs"], ((l * NS + ci) * 8 + h) * 16384, [[128, 128], [1, 128]]), Sh.t[:, :], reads=[Sh[:]])
                    elif last:
                        if ci == len(chunks) - 1:
                            S.dma("pool", dap(O["dSp"], (l * 8 + h) * 16384, [[128, 128], [1, 128]]), Sh.t[:, :], reads=[Sh[:]])
            if mode == "s":
                newrow_out(lambda j: newA.sub(j, (slice(None), j, slice(None))), 24, O["dcs"], l * NS * 3 * 3072, 3, 3072)
            branch_merge(0)

        def ph_D():
            pif = win_proj(l, O_IF, 8, TB)
            ifT = tA()
            S.act(ifT[0:8, 0:TB], pif[0:8, 0:TB], AF.Identity, bias=bcol(l, O_IF, 8))
            lf_t, b_t, ib_t = [], [], []
            for (c0, L) in chunks:
                p = S.ps()
                S.tr(p[0:L, 0:8], ifT[0:8, c0:c0 + L], ident[0:8, 0:8])
                ci_d = len(lf_t)
                lf = Buf(persD.t[:, ci_d, 0:4], ("persD", ci_d, 0)); bb = Buf(persD.t[:, ci_d, 4:8], ("persD", ci_d, 1))
                ib = Buf(persD.t[:, ci_d, 8:12], ("persD", ci_d, 2))
                S.act(lf[0:L, 0:4], p[0:L, 4:8], AF.Exp, scale=-1.0)
                S.act(lf[0:L, 0:4], lf[0:L, 0:4], AF.Ln, bias=1.0)
                S.ts(lf[0:L, 0:4], lf[0:L, 0:4], -1.0, ALU.mult)
                pc = S.ps()
                S.mm(pc[0:L, 0:4], triu[0:L, 0:L], lf[0:L, 0:4], r=False)
                S.copy(bb[0:L, 0:4], pc[0:L, 0:4])
                S.tt(ib[0:L, 0:4], p[0:L, 0:4], bb[0:L, 0:4], ALU.subtract)
                lf_t.append(lf); b_t.append(bb); ib_t.append(ib)
            for h in range(4):
                pq_ = win_proj(l, O_QD + h * 128, 128, TB)
                qT_ = tA()
                S.act(qT_[:, 0:TB], pq_[:, 0:TB], AF.Identity, bias=bcol(l, O_QD + h * 128))
                pk_ = win_proj(l, O_KD + h * 128, 128, TB)
                kT_ = tA()
                S.ts(kT_[:, 0:TB], pk_[:, 0:TB], bcol(l, O_KD + h * 128), ALU.add, 128 ** -0.5, ALU.mult)
                vT_, so_ = [], []
                for e in range(2):
                    pv_ = win_proj(l, O_VD + h * 256 + e * 128, 128, TB)
                    v_ = tA()
                    S.act(v_[:, 0:TB], pv_[:, 0:TB], AF.Identity, bias=bcol(l, O_VD + h * 256 + e * 128))
                    vT_.append(v_)
                for e in range(2):
                    po_ = win_proj(l, O_OD + h * 256 + e * 128, 128, TB)
                    o_ = tA()
                    S.act(o_[:, 0:TB], po_[:, 0:TB], AF.Sigmoid, bias=bcol(l, O_OD + h * 256 + e * 128))
                    so_.append(o_)
                for ci, (c0, L) in enumerate(chunks):
                    if mode == "p":
                        Ch, ms = Cn[l][h], mst[l]
                    else:
                        Ch = CnS[ssi[0] % 3]; ms = mstS[ssi[0] % 3]; ssi[0] += 1
                        base = (l * NS + ci) * 4 + h
                        S.dma("sp", Ch.t[:, 0:256], dap(I["smC"], base * 32768, [[256, 128], [1, 256]]), writes=[Ch[:]])
                        S.dma("sp", Ch.t[:, 256:257], dap(I["smn"], base * 128, [[1, 128], [1, 1]]), writes=[Ch[:]], allow_slow_non_contiguous=True)
                        S.dma("sp", ms.t[:, h:h + 1], dap(I["smm"], base, [[0, 128], [1, 1]]), writes=[ms[:]])
                    mlstm_chunk(l, h, c0, L, qT_, kT_, vT_, so_, Ch, ms, lf_t[ci], b_t[ci], ib_t[ci])
                    if mode == "s" or (last and ci == len(chunks) - 1):
                        if mode == "s":
                            base = (l * NS + ci) * 4 + h
                            oc, on_, om = O["mCs"], O["mns"], O["mms"]
                        else:
                            base = l * 4 + h
                            oc, on_, om = O["mCp"], O["mnp"], O["mmp"]
                        S.dma("pool", dap(oc, base * 32768, [[256, 128], [1, 256]]), Ch.t[:, 0:256], reads=[Ch[:]])
                        S.dma("pool", dap(on_, base * 128, [[1, 128], [1, 1]]), Ch.t[:, 256:257], reads=[Ch[:]], allow_slow_non_contiguous=True)
                        S.dma("pool", dap(om, base, [[1, 1], [1, 1]]), ms.t[0:1, h:h + 1], reads=[ms[:]])
            branch_merge(3)

        def ph_O():
            for dg in range(4):
                pys = fm_group(I["w_out"], l * D * D, D, 0, dg * 512, 512, 16, lambda k: ch(mixedb, k, slice(0, TB)), TB)
                for c in range(4):
                    d = dg * 4 + c
                    S.stt(ch(xT, d, slice(0, TB)), ch(xT, d, slice(0, TB)), ALPHA, pys[c][:, 0:TB], ALU.mult, ALU.add)
            layernorm(l, "ln1_g", "ln1_b", TB)

        def ph_X():
            pqs = fm_group(I["xq_w"], l * D * 512, 512, 0, 0, 512, 16, lambda k: ch(xTb, k, slice(0, TB)), TB)
            for h in range(4):
                S.copy(ch(qTb, h, slice(0, TB)), pqs[h][:, 0:TB], eng="act")
            for ci, (c0, L) in enumerate(chunks):
                if mode == "p":
                    KTl, Vl = KT[l], Vt[l]
                else:
                    KTl, Vl = KTs, Vts
                    S.dma("sp", Kts.t[:, :, :], dap(I["cmk"], (l * NS + ci) * 256 * 512, [[512, 128], [128 * 512, 2], [1, 512]]), writes=[Kts[:]])
                    S.dma("sp", Vts.t[:, :, :], dap(I["cmv"], (l * NS + ci) * 256 * 512, [[512, 128], [128 * 512, 2], [1, 512]]), writes=[Vts[:]])
                    for hh in range(4):
                        for mc in range(2):
                            transpose_to(KTs[:, hh, mc * 128:(mc + 1) * 128], Kts[:, mc, hh * 128:(hh + 1) * 128], 128, 128, eng="act")
                otok = tokbuf
                for h in range(4):
                    ps_ = S.ps()
                    S.mm(ps_[0:L, 0:256], ch(qTb, h, slice(c0, c0 + L)), KTl[:, h, :], r=False)
                    mx = tC()
                    S.rmax(mx[0:L, 0:1], ps_[0:L, 0:256])
                    S.ts(mx[0:L, 1:2], mx[0:L, 0:1], -(128 ** -0.5), ALU.mult)
                    es = tA()
                    S.act(es[0:L, 0:256], ps_[0:L, 0:256], AF.Exp, bias=mx[0:L, 1:2], scale=128 ** -0.5, accum=mx[0:L, 2:3])
                    S.recip(mx[0:L, 3:4], mx[0:L, 2:3])
                    aT = tA()
                    for mc in range(2):
                        transpose_to(aT[:, mc * 128:mc * 128 + L], es[0:L, mc * 128:(mc + 1) * 128], L, 128, eng="act")
                    po = S.ps()
                    for mc in range(2):
                        S.mm(po[0:L, 0:128], aT[:, mc * 128:mc * 128 + L], Vl[:, mc, h * 128:(h + 1) * 128], start=(mc == 0), stop=(mc == 1), r=False)
                    S.ts(otok[0:L, h * 128:(h + 1) * 128], po[0:L, 0:128], mx[0:L, 3:4], ALU.mult)
                for h in range(4):
                    transpose_to(ch(oTb, h, slice(c0, c0 + L)), otok[0:L, h * 128:(h + 1) * 128], L, 128, eng="act")
            for dg in range(4):
                pys = fm_group(I["xo_w"], l * 512 * D, D, 0, dg * 512, 512, 4, lambda k: ch(oTb, k, slice(0, TB)), TB)
                for c in range(4):
                    d = dg * 4 + c
                    S.stt(ch(xT, d, slice(0, TB)), ch(xT, d, slice(0, TB)), ALPHA, pys[c][:, 0:TB], ALU.mult, ALU.add)
            layernorm(l, "ln2_g", "ln2_b", TB)

        def ph_M():
            for g in range(8):
                for jg2 in range(2):
                    phs = fm_group(I["ffn_w1"], l * D * 4 * D, 4 * D, 0, (g * 8 + jg2 * 4) * 128, 512, 16, lambda k: ch(xTb, k, slice(0, TB)), TB)
                    for c in range(4):
                        jj = jg2 * 4 + c
                        j = g * 8 + jj
                        r_ = tA()
                        S.act(r_[:, 0:TB], phs[c][:, 0:TB], AF.Relu, bias=p_["ffn_b1"][:, j:j + 1])
                        S.act(ch(hb, jj, slice(0, TB)), r_[:, 0:TB], AF.Square)
                for dg in range(4):
                    pds = fm_group(I["ffn_w2"], l * 4 * D * D, D, g * 1024, dg * 512, 512, 8, lambda k: ch(hb, k, slice(0, TB)), TB)
                    for c in range(4):
                        d = dg * 4 + c
                        if g == 0:
                            S.copy(ch(mixed, d, slice(0, TB)), pds[c][:, 0:TB], eng="act")
                        else:
                            S.tt(ch(mixed, d, slice(0, TB)), ch(mixed, d, slice(0, TB)), pds[c][:, 0:TB], ALU.add)
            for d in range(16):
                S.ts(ch(mixed, d, slice(0, TB)), ch(mixed, d, slice(0, TB)), p_["ffn_b2"][:, d:d + 1], ALU.add)
                S.stt(ch(xT, d, slice(0, TB)), ch(xT, d, slice(0, TB)), ALPHA, ch(mixed, d, slice(0, TB)), ALU.mult, ALU.add)
            layernorm(l, "ln3_g", "ln3_b", TB)

        PH = CFG.get('phases', 'CBADOXM')
        if 'C' in PH:
            ph_C(); dump('C', outn, 8)
        if 'B' in PH:
            ph_B(); dump('B', outn, 8)
        if 'A' in PH:
            ph_A(); dump('A', outn, 8)
        if 'D' in PH:
            ph_D(); dump('D', outn, 8); dump('mixed', mixed, 16)
        if 'O' in PH:
            ph_O(); dump('x1', xT, 16)
        if 'X' in PH:
            ph_X(); dump('x2', xT, 16)
        if 'M' in PH:
            ph_M(); dump('x3', xT, 16)

    def delta_chunk(l, h, ci, c0, L, qT_, kT_, vT_, sz, Sh, be, gg, gcx):
        p_ = P[l]
        beta_c, g_c, gc_c = be[0:L, h:h + 1], gg[0:L, h:h + 1], gcx[0:L, h:h + 1]
        kc, qc, vc = kT_[:, c0:c0 + L], qT_[:, c0:c0 + L], vT_[:, c0:c0 + L]
        gb = tQ()
        S.ts(gb[0:L, :], ones[0:L, :], g_c, ALU.mult)
        pg = S.ps()
        S.mm(pg[:, 0:L], gb[0:L, :], triu[0:L, 0:L], r=False)
        gcr = tQL(); egr = tQL()
        S.copy(gcr[:, 0:L], pg[:, 0:L], eng="act")
        S.act(egr[:, 0:L], pg[:, 0:L], AF.Exp)
        cols = tC()
        eg_c, ekl_c, nb_c, nbk_c = cols[0:L, 0:1], cols[0:L, 1:2], cols[0:L, 2:3], cols[0:L, 3:4]
        S.act(eg_c, gc_c, AF.Exp)
        S.act(ekl_c, gc_c, AF.Exp, bias=gcr[0:L, L - 1:L], scale=-1.0)
        S.ts(nb_c, beta_c, -1.0, ALU.mult)
        S.tt(nbk_c, nb_c, ekl_c, ALU.mult)
        ktok = tQL(); vtok = tQL()
        transpose_to(ktok[0:L, :], kc, 128, L, eng="act")
        transpose_to(vtok[0:L, :], vc, 128, L, eng="act")
        pk = S.ps()
        S.mm(pk[0:L, 0:L], kc, kc, r=False)
        S.mm(pk[0:L, 128:128 + L], kc, qc, r=False)
        qkd = tQL()
        if L > 1:
            Dm = tQ(); Es = tQ(); B0 = tQ(); Ei = tQ()
            S.stt(Dm[0:L, 0:L], gcr[0:L, 0:L], gc_c, maskS[0:L, 0:L], ALU.subtract, ALU.add)
            S.act(Es[0:L, 0:L], Dm[0:L, 0:L], AF.Exp)
            S.stt(B0[0:L, 0:L], pk[0:L, 0:L], beta_c, Es[0:L, 0:L], ALU.mult, ALU.mult)
            S.tt(Ei[0:L, 0:L], Es[0:L, 0:L], ident[0:L, 0:L], ALU.add, eng="pool")
            S.tt(qkd[0:L, 0:L], pk[0:L, 128:128 + L], Ei[0:L, 0:L], ALU.mult)
        else:
            S.copy(qkd[0:L, 0:L], pk[0:L, 128:128 + L], eng="act")
        pks = S.ps()
        S.mm(pks[0:L, 0:128], kc, Sh[:, :], r=False)
        rneg = tQL()
        S.stt(rneg[0:L, :], pks[0:L, 0:128], eg_c, vtok[0:L, :], ALU.mult, ALU.subtract)
        dl = tQL(); dk = tQL()
        if L > 1:
            Bp = B0
            Ap = tQ()
            transpose_to(Ap[0:L, 0:L], B0[0:L, 0:L], L, L, eng="act")
            U = tQ(); Lw = tQ()
            S.tt(U[0:L, 0:L], ident[0:L, 0:L], Bp[0:L, 0:L], ALU.subtract)
            S.tt(Lw[0:L, 0:L], ident[0:L, 0:L], Ap[0:L, 0:L], ALU.subtract, eng="pool")
            nlev = 6
            for j in range(1, nlev + 1):
                pB = S.ps()
                S.mm(pB[0:L, 0:L], Ap[0:L, 0:L], Bp[0:L, 0:L], r=False)
                Bn = tQ()
                S.copy(Bn[0:L, 0:L], pB[0:L, 0:L], eng="act")
                An = None
                if j < nlev:
                    pA = S.ps()
                    S.mm(pA[0:L, 0:L], Bp[0:L, 0:L], Ap[0:L, 0:L], r=False)
                    An = tQ()
                    S.copy(An[0:L, 0:L], pA[0:L, 0:L])
                pU = S.ps()
                S.mm(pU[0:L, 0:L], Lw[0:L, 0:L], Bn[0:L, 0:L], r=False)
                Un = tQ()
                S.tt(Un[0:L, 0:L], U[0:L, 0:L], pU[0:L, 0:L], ALU.add)
                if j < nlev:
                    pL = S.ps()
                    S.mm(pL[0:L, 0:L], U[0:L, 0:L], An[0:L, 0:L], r=False)
                    Ln_ = tQ()
                    S.tt(Ln_[0:L, 0:L], Lw[0:L, 0:L], pL[0:L, 0:L], ALU.add)
                    Lw = Ln_
                    Ap = An
                U = Un
                Bp = Bn
            pT = S.ps()
            S.mm(pT[0:L, 0:128], U[0:L, 0:L], rneg[0:L, :], r=False)
            S.ts(dl[0:L, :], pT[0:L, 0:128], nb_c, ALU.mult)
            S.act(dk[0:L, :], pT[0:L, 0:128], AF.Copy, scale=nbk_c) if False else S.ts(dk[0:L, :], pT[0:L, 0:128], nbk_c, ALU.mult)
        else:
            S.ts(dl[0:L, :], rneg[0:L, :], nb_c, ALU.mult)
            S.ts(dk[0:L, :], rneg[0:L, :], nbk_c, ALU.mult)
        qd = tQL()
        S.tt(qd[:, 0:L], qc, egr[:, 0:L], ALU.mult, eng="pool")
        po = S.ps()
        S.mm(po[0:L, 0:128], qd[:, 0:L], Sh[:, :], start=True, stop=False, r=False)
        S.mm(po[0:L, 0:128], qkd[0:L, 0:L], dl[0:L, :], start=False, stop=True, r=False)
        pS = S.ps()
        S.mm(pS[:, 0:128], ktok[0:L, :], dk[0:L, :], r=False)
        S.stt(Sh[:, :], Sh[:, :], egr[:, L - 1:L], pS[:, 0:128], ALU.mult, ALU.add)
        junk = tQ(); c2 = tC()
        S.act(junk[0:L, :], po[0:L, 0:128], AF.Square, accum=c2[0:L, 0:1])
        S.ts(c2[0:L, 1:2], c2[0:L, 0:1], 1.0 / 128, ALU.mult, 1e-6, ALU.add)
        S.act(c2[0:L, 1:2], c2[0:L, 1:2], AF.Sqrt)
        S.recip(c2[0:L, 2:3], c2[0:L, 1:2])
        on_ = tQL()
        S.ts(on_[0:L, :], po[0:L, 0:128], c2[0:L, 2:3], ALU.mult)
        pT2 = S.ps()
        S.tr(pT2[:, 0:L], on_[0:L, :], ident[0:L, 0:L])
        S.stt(outn.sub(h, (slice(None), h, slice(c0, c0 + L))), psrc(pT2[:, 0:L], L, 128), p_["a_norm_w"][:, 0:1], sz[:, c0:c0 + L], ALU.mult, ALU.mult)

    def mlstm_chunk(l, h, c0, L, qT_, kT_, vT_, so_, Ch, ms, lf, bb, ib):
        p_ = P[l]
        b_c, ib_c, lf_c = bb[0:L, h:h + 1], ib[0:L, h:h + 1], lf[0:L, h:h + 1]
        qc, kc = qT_[:, c0:c0 + L], kT_[:, c0:c0 + L]
        ibb = tQ(); lfb = tQ()
        S.ts(ibb[0:L, :], ones[0:L, :], ib_c, ALU.mult)
        if CFG.get('dstop', 999) == 1: return
        S.ts(lfb[0:L, :], ones[0:L, :], lf_c, ALU.mult)
        if CFG.get('dstop', 999) == 2: return
        pr = S.ps()
        S.mm(pr[:, 0:L], ibb[0:L, :], ident[0:L, 0:L], r=False)
        if CFG.get('dstop', 999) == 3: return
        S.mm(pr[:, 128:128 + L], lfb[0:L, :], triu[0:L, 0:L], r=False)
        if CFG.get('dstop', 999) == 4: return
        ibr = tQ()
        S.copy(ibr[:, 0:L], pr[:, 0:L], eng="act")
        if CFG.get('dstop', 999) == 5: return
        cb = tC()
        S.act(cb[:, 0:1], pr[:, 128 + L - 1:128 + L], AF.Identity)
        if CFG.get('dstop', 999) == 6: return
        S.rmax(cb[:, 1:2], ibr[:, 0:L])
        if CFG.get('dstop', 999) == 7: return
        S.tt(cb[:, 1:2], cb[:, 1:2], cb[:, 0:1], ALU.add)
        if CFG.get('dstop', 999) == 8: return
        S.tt(cb[:, 2:3], cb[:, 0:1], ms[:, h:h + 1], ALU.add)
        if CFG.get('dstop', 999) == 9: return
        S.tt(cb[:, 3:4], cb[:, 2:3], cb[:, 1:2], ALU.max)
        if CFG.get('dstop', 999) == 10: return
        S.tt(cb[:, 5:6], cb[:, 2:3], cb[:, 3:4], ALU.subtract)
        if CFG.get('dstop', 999) == 11: return
        S.act(cb[:, 4:5], cb[:, 5:6], AF.Exp)
        if CFG.get('dstop', 999) == 12: return
        S.tt(cb[:, 5:6], cb[:, 0:1], cb[:, 3:4], ALU.subtract)
        if CFG.get('dstop', 999) == 13: return
        cc = tC()
        S.tt(cc[0:L, 0:1], b_c, ms[0:L, h:h + 1], ALU.add)
        if CFG.get('dstop', 999) == 14: return
        ld = tQ()
        S.stt(ld[0:L, 0:L], ibr[0:L, 0:L], b_c, maskL[0:L, 0:L], ALU.add, ALU.add)
        if CFG.get('dstop', 999) == 15: return
        S.rmax(cc[0:L, 1:2], ld[0:L, 0:L])
        if CFG.get('dstop', 999) == 16: return
        S.tt(cc[0:L, 2:3], cc[0:L, 0:1], cc[0:L, 1:2], ALU.max)
        if CFG.get('dstop', 999) == 17: return
        S.ts(cc[0:L, 3:4], cc[0:L, 2:3], -1.0, ALU.mult)
        if CFG.get('dstop', 999) == 18: return
        S.act(cc[0:L, 4:5], cc[0:L, 0:1], AF.Exp, bias=cc[0:L, 3:4])
        if CFG.get('dstop', 999) == 19: return
        S.act(cc[0:L, 5:6], cc[0:L, 3:4], AF.Exp)
        if CFG.get('dstop', 999) == 20: return
        S.act(cc[0:L, 6:7], ib_c, AF.Exp, bias=cb[0:L, 5:6])
        if CFG.get('dstop', 999) == 21: return
        ed = tQ()
        S.act(ed[0:L, 0:L], ld[0:L, 0:L], AF.Exp, bias=cc[0:L, 3:4])
        if CFG.get('dstop', 999) == 22: return
        pqk = S.ps()
        S.mm(pqk[0:L, 0:L], qc, kc, r=False)
        if CFG.get('dstop', 999) == 23: return
        dm = tQ()
        S.tt(dm[0:L, 0:L], ed[0:L, 0:L], psrc(pqk[0:L, 0:L], L, L), ALU.mult)
        if CFG.get('dstop', 999) == 24: return
        dmT = tQ()
        transpose_to(dmT[0:L, 0:L], dm[0:L, 0:L], L, L, eng="act")
        if CFG.get('dstop', 999) == 25: return
        va = t258n()
        for e in range(2):
            transpose_to(va[0:L, e * 128:(e + 1) * 128], vT_[e][:, c0:c0 + L], 128, L, eng="act")
        S.memset(va[0:L, 256:257], 1.0)
        if CFG.get('dstop', 999) == 26: return
        S.memset(va[0:L, 257:258], 0.0)
        if CFG.get('dstop', 999) == 27: return
        ktok = tQ()
        transpose_to(ktok[0:L, :], kc, 128, L)
        if CFG.get('dstop', 999) == 28: return
        p1 = S.ps()
        S.mm(p1[0:L, 0:258], qc, Ch[:, :], r=False)
        if CFG.get('dstop', 999) == 29: return
        t1 = t258n()
        S.ts(t1[0:L, :], p1[0:L, 0:258], cc[0:L, 4:5], ALU.mult)
        if CFG.get('dstop', 999) == 30: return
        p2 = S.ps()
        S.mm(p2[0:L, 0:258], dmT[0:L, 0:L], va[0:L, :], r=False)
        if CFG.get('dstop', 999) == 31: return
        S.tt(t1[0:L, :], t1[0:L, :], p2[0:L, 0:258], ALU.add)
        if CFG.get('dstop', 999) == 32: return
        c3 = tC()
        S.act(c3[0:L, 5:6], t1[0:L, 256:257], AF.Abs)
        if CFG.get('dstop', 999) == 33: return
        S.tt(c3[0:L, 0:1], c3[0:L, 5:6], cc[0:L, 5:6], ALU.max)
        if CFG.get('dstop', 999) == 34: return
        S.recip(c3[0:L, 1:2], c3[0:L, 0:1])
        if CFG.get('dstop', 999) == 35: return
        hh = t258n()
        S.ts(hh[0:L, 0:256], t1[0:L, 0:256], c3[0:L, 1:2], ALU.mult)
        if CFG.get('dstop', 999) == 36: return
        S.act(t1[0:L, 0:256], hh[0:L, 0:256], AF.Square, accum=c3[0:L, 2:3])
        if CFG.get('dstop', 999) == 37: return
        S.ts(c3[0:L, 3:4], c3[0:L, 2:3], 1.0 / 256, ALU.mult, 1e-6, ALU.add)
        if CFG.get('dstop', 999) == 38: return
        S.act(c3[0:L, 3:4], c3[0:L, 3:4], AF.Sqrt)
        if CFG.get('dstop', 999) == 39: return
        S.recip(c3[0:L, 4:5], c3[0:L, 3:4])
        if CFG.get('dstop', 999) == 40: return
        S.ts(hh[0:L, 0:256], hh[0:L, 0:256], c3[0:L, 4:5], ALU.mult)
        if CFG.get('dstop', 999) == 41: return
        for e in range(2):
            pT = S.ps()
            S.tr(pT[:, 0:L], hh[0:L, e * 128:(e + 1) * 128], ident[0:L, 0:L])
            S.stt(outn.sub(2 * h + e, (slice(None), 2 * h + e, slice(c0, c0 + L))), psrc(pT[:, 0:L], L, 128), p_["d_norm_w"][:, e:e + 1], so_[e][:, c0:c0 + L], ALU.mult, ALU.mult)
        S.ts(va[0:L, :], va[0:L, :], cc[0:L, 6:7], ALU.mult)
        if CFG.get('dstop', 999) == 42: return
        pC = S.ps()
        S.mm(pC[:, 0:258], ktok[0:L, :], va[0:L, :], r=False)
        if CFG.get('dstop', 999) == 43: return
        S.stt(Ch[:, :], Ch[:, :], cb[:, 4:5], pC[:, 0:258], ALU.mult, ALU.add)
        if CFG.get('dstop', 999) == 44: return
        S.copy(ms[:, h:h + 1], cb[:, 3:4])
        if CFG.get('dstop', 999) == 45: return


    memT = mixed
    for t in range(2):
        S.dma("sp", tokbuf.t[:, :], dap(I["memp"], t * 128 * D, [[D, 128], [1, D]]), writes=[tokbuf[:]])
        for k in range(16):
            transpose_to(memT.sub(k, (slice(None), k, slice(t * 128, (t + 1) * 128))), tokbuf[:, k * 128:(k + 1) * 128], 128, 128)
    for l in range(2 if CFG.get('kv', True) else 0):
        Ktok = Buf(ubuf.t[:, 0:4, :].rearrange("p (m x) b -> p m (x b)", m=2), "Ktok")
        for which, wname, oname in ((0, "xk_w", "mkp"), (1, "xv_w", "mvp")):
            for h in range(4):
                w = load_w(I[wname], l * D * 512, 512, 0, h * 128, 128, 16)
                for mc in range(2):
                    p = S.ps()
                    for k in range(16):
                        S.mm(p[:, 0:128], memT.sub(k, (slice(None), k, slice(mc * 128, (mc + 1) * 128))), w[:, k, 0:128], start=(k == 0), stop=(k == 15), r=False)
                    if which == 0:
                        S.copy(Ktok.sub(mc, (slice(None), mc, slice(h * 128, (h + 1) * 128))), p[:, 0:128])
                        transpose_to(KT[l][:, h, mc * 128:(mc + 1) * 128], Ktok.sub(mc, (slice(None), mc, slice(h * 128, (h + 1) * 128))), 128, 128)
                    else:
                        S.copy(Vt[l][:, mc, h * 128:(h + 1) * 128], p[:, 0:128])
            src = Ktok if which == 0 else Vt[l]
            rd = [Ktok.sub(0, (slice(None), 0, slice(None))), Ktok.sub(1, (slice(None), 1, slice(None)))] if which == 0 else [Vt[l][:]]
            S.dma("pool", dap(O[oname], l * 256 * 512, [[512, 128], [128 * 512, 2], [1, 512]]), src.t[:, 0:2, 0:512], reads=rd)

    S.barrier()
    NBLK = CFG.get('nblk', 2048 // TBP)
    NLAY = CFG.get('layers', 2)
    chunks_p = [(c * 128, 128) for c in range(TBP // 128)]
    for blk in range(NBLK):
        for t in range(TBP // 128):
            S.dma("sp", tokbuf.t[:, :], dap(I["xp"], (blk * TBP + t * 128) * D, [[D, 128], [1, D]]), writes=[tokbuf[:]])
            for k in range(16):
                transpose_to(ch(xT, k, slice(t * 128, (t + 1) * 128)), tokbuf[:, k * 128:(k + 1) * 128], 128, 128, eng=("act" if k % 2 else "dve"))
                S.copy(ch(xTb, k, slice(t * 128, (t + 1) * 128)), ch(xT, k, slice(t * 128, (t + 1) * 128)), eng="pool")
        for l in range(NLAY):
            layer_block(l, TBP, "p", chunks_p, (blk == NBLK - 1) and CFG.get("stateout", True))
        for t in range(TBP // 128):
            for k in range(16):
                transpose_to(tokbuf[:, k * 128:(k + 1) * 128], ch(xT, k, slice(t * 128, (t + 1) * 128)), 128, 128, eng=("act" if k % 2 else "dve"))
            S.dma("pool", dap(O["yp"], (blk * TBP + t * 128) * D, [[D, 128], [1, D]]), tokbuf.t[:, :], reads=[tokbuf[:]])
    for l in range(2):
        emit_rows(lambda j: histA[l][:, j, :], 24, 3, lambda c0, n: dap(O["dcp"], l * 3 * 3072 + c0, [[3072, 3], [1, n]]))
        emit_rows(lambda j: histB[l][:, j, :], 8, 30, lambda c0, n: dap(O["gcp"], l * 30 * 1024 + c0, [[1024, 30], [1, n]]))
        emit_rows(lambda j: histC[l][:, j, :], 8, 2, lambda c0, n: dap(O["scp"], l * 2 * 1024 + c0, [[1024, 2], [1, n]]))

    if CFG.get('sample', True):
        S.dma("sp", tokbuf.t[0:NS, :], dap(I["xs"], 0, [[D, NS], [1, D]]), writes=[tokbuf[:]])
        for k in range(16):
            transpose_to(ch(xT, k, slice(0, NS)), tokbuf[0:NS, k * 128:(k + 1) * 128], NS, 128)
            S.copy(ch(xTb, k, slice(0, NS)), ch(xT, k, slice(0, NS)), eng="pool")
        S.barrier()
        for i in range(3):
            S.memset(CnS[i][:], 0.0)
        chunks_s = [(s, 1) for s in range(NS)]
        for l in range(NLAY):
            for (src, H, C, hs, dst) in ((I["sdc"], 3, 3072, hsA, O["dcs"]), (I["sgc"], 30, 1024, hsB, O["gcs"]), (I["ssc"], 2, 1024, hsC, O["scs"])):
                S.dma("pool", dap(dst, l * NS * H * C, [[H * C, NS], [C, H - 1], [1, C]]), dap(src, l * NS * H * C + C, [[H * C, NS], [C, H - 1], [1, C]]))
                spt = max(1, min(NS, 128 // H))
                for s0 in range(0, NS, spt):
                    ns_ = min(spt, NS - s0)
                    R = ns_ * H
                    for cc0 in range(0, C, D):
                        ncol = min(D, C - cc0)
                        S.dma("sp", tokbuf.t[0:R, 0:ncol], dap(src, (l * NS + s0) * H * C + cc0, [[C, R], [1, ncol]]), writes=[tokbuf[:]])
                        for jj in range(ncol // 128):
                            j = cc0 // 128 + jj
                            p = S.ps()
                            S.tr(p[:, 0:R], tokbuf[0:R, jj * 128:(jj + 1) * 128], ident[0:R, 0:R])
                            S.copy(hs.sub(j, (slice(None), j, slice(s0, s0 + ns_), slice(None))), p[:, 0:R].ap.rearrange("p (s h) -> p s h", h=H) if False else View(p.key, p.t[:, 0:R].rearrange("p (s h) -> p s h", h=H)))
            layer_block(l, NS, "s", chunks_s, False, hsA, hsB, hsC)
        for k in range(16):
            transpose_to(tokbuf[0:NS, k * 128:(k + 1) * 128], ch(xT, k, slice(0, NS)), 128, NS)
        S.dma("pool", dap(O["ys"], 0, [[D, NS], [1, D]]), tokbuf.t[0:NS, :], reads=[tokbuf[:]])
    S.finish()
    return nc


_NC = [None]


def kernel(**inp):
    a = {k: np.ascontiguousarray(np.asarray(v, dtype=np.float32)) for k, v in inp.items()}
    if _NC[0] is None:
        _NC[0] = build()
    nc = _NC[0]
    wnames = ["w_in", "b_in", "a_conv_w", "a_A_log", "a_dt_bias", "a_norm_w", "b_conv_w", "b_conv_b", "b_ln_g", "b_ln_b",
              "c_conv_w", "d_norm_w", "w_branch", "w_out", "ln1_g", "ln1_b", "xq_w", "xk_w", "xv_w", "xo_w", "ln2_g", "ln2_b",
              "ffn_w1", "ffn_b1", "ffn_w2", "ffn_b2", "ln3_g", "ln3_b"]
    in_maps = []
    for c in range(8):
        b = c % 4
        sl = slice(c * NS, (c + 1) * NS)
        m = {"xp": a["x_prompt"][b], "xs": a["x_sample"][sl, 0, :], "memp": a["mem_prompt"][b],
             "cmk": a["cache_mem_k"][:, sl].reshape(2, NS, 256, 512), "cmv": a["cache_mem_v"][:, sl].reshape(2, NS, 256, 512),
             "sdc": a["state_delta_conv"][:, sl], "sdS": a["state_delta_S"][:, sl], "sgc": a["state_glu_conv"][:, sl],
             "ssc": a["state_short_conv"][:, sl], "smC": a["state_mlstm_C"][:, sl], "smn": a["state_mlstm_n"][:, sl],
             "smm": a["state_mlstm_m"][:, sl]}
        m = {k: np.ascontiguousarray(v) for k, v in m.items()}
        for w in wnames:
            m[w] = a[w]
        in_maps.append(m)
    res = run_bass_kernel_spmd(nc, in_maps, core_ids=list(range(8)))
    R = res.results

    def pst(name, shape):
        return np.stack([np.asarray(R[b][name]) for b in range(4)], axis=1).reshape(shape)

    def sst(name, shape):
        return np.concatenate([np.asarray(R[c][name]) for c in range(8)], axis=1).reshape(shape)

    y_prompt = np.stack([np.asarray(R[b]["yp"]) for b in range(4)], axis=0)
    y_sample = np.concatenate([np.asarray(R[c]["ys"]) for c in range(8)], axis=0).reshape(128, 1, D)
    outs = (y_prompt, y_sample,
            pst("mkp", (2, 4, 256, 4, 128)), pst("mvp", (2, 4, 256, 4, 128)),
            pst("dcp", (2, 4, 3, 3072)), pst("dSp", (2, 4, 8, 128, 128)), pst("gcp", (2, 4, 30, 1024)),
            pst("scp", (2, 4, 2, 1024)), pst("mCp", (2, 4, 4, 128, 256)), pst("mnp", (2, 4, 4, 128)), pst("mmp", (2, 4, 4)),
            sst("dcs", (2, 128, 3, 3072)), sst("dSs", (2, 128, 8, 128, 128)), sst("gcs", (2, 128, 30, 1024)),
            sst("scs", (2, 128, 2, 1024)), sst("mCs", (2, 128, 4, 128, 256)), sst("mns", (2, 128, 4, 128)), sst("mms", (2, 128, 4)))
    return tuple(np.ascontiguousarray(o.astype(np.float32)) for o in outs)
```

```python
import contextlib
import numpy as np
import concourse.bass as bass
import concourse.mybir as mybir
from concourse.bass_utils import run_bass_kernel_spmd

F32 = mybir.dt.float32
F32R = mybir.dt.float32r
BF16 = mybir.dt.bfloat16
AF = mybir.ActivationFunctionType
ALU = mybir.AluOpType
AX = mybir.AxisListType

EPOCH = 20000
NDS = 8


class View:
    __slots__ = ("key", "ap")

    def __init__(self, key, ap):
        self.key = key
        self.ap = ap


class Buf:
    def __init__(self, t, key):
        self.t = t
        self.key = key

    def __getitem__(self, idx):
        return View(self.key, self.t[idx])

    def sub(self, subkey, idx):
        return View((self.key, subkey), self.t[idx])


class Sched:
    ENG = ["pe", "act", "dve", "pool", "sp"]

    def __init__(self, nc, dry=False):
        self.nc = nc
        self.dry = dry
        self.stack = contextlib.ExitStack()
        self.prog = {e: [] for e in self.ENG}
        self.count = {e: 0 for e in self.ENG}
        self.waited = {e: {} for e in self.ENG}
        self.last_w = {}
        self.readers = {}
        self.dslot = {e: 0 for e in self.ENG}
        self.dval = {}
        self.sids = {}
        self.nbuf = 0
        self.psum_banks = []
        self.psum_i = 0
        self.nops = 0

    def sb(self, shape, dtype=F32, name=None):
        self.nbuf += 1
        name = name or f"sb{self.nbuf}"
        t = self.stack.enter_context(self.nc.sbuf_tensor(name, list(shape), dtype))
        return Buf(t, name)

    def init_psum(self, n=8):
        for i in range(n):
            t = self.stack.enter_context(self.nc.psum_tensor(f"ps{i}", [128, 512], F32))
            self.psum_banks.append(Buf(t, f"ps{i}"))

    def ps(self):
        b = self.psum_banks[self.psum_i % len(self.psum_banks)]
        self.psum_i += 1
        return b

    def _deps(self, eng, reads, writes):
        deps = set()
        for v in reads:
            t = self.last_w.get(v.key)
            if t is not None:
                deps.add(t)
        for v in writes:
            t = self.last_w.get(v.key)
            if t is not None:
                deps.add(t)
            for r in self.readers.get(v.key, ()):
                if r[2] != eng or r[2] == "dma":
                    deps.add(r)
        for (sid, val, deng) in sorted(deps, key=lambda d: str(d)):
            if deng == eng and eng == "pe":
                continue
            if self.waited[eng].get(sid, 0) >= val:
                continue
            self.waited[eng][sid] = val
            self.prog[eng].append(("wait", sid, val))

    def _commit(self, tok, reads, writes):
        for v in reads:
            self.readers.setdefault(v.key, []).append(tok)
        for v in writes:
            self.last_w[v.key] = tok
            self.readers[v.key] = []

    def op(self, eng, fn, reads=(), writes=()):
        if self.dry:
            return None
        self._deps(eng, reads, writes)
        n = self.count[eng]
        self.count[eng] += 1
        sid = ("c", eng, n // EPOCH)
        val = n % EPOCH + 1
        self.sids[sid] = 1
        tok = (sid, val, eng)
        self.prog[eng].append(("op", fn, sid, 1))
        self._commit(tok, reads, writes)
        self.nops += 1
        return tok

    def dma(self, q, out_ap, in_ap, reads=(), writes=(), **kw):
        if self.dry:
            return None
        eng = q
        self._deps(eng, reads, writes)
        slot = self.dslot[q]
        self.dslot[q] = (slot + 1) % NDS
        sid = ("d", q, slot)
        self.sids[sid] = 1
        prev = self.dval.get(sid, 0)
        if prev > 0 and self.waited[eng].get(sid, 0) < prev:
            self.waited[eng][sid] = prev
            self.prog[eng].append(("wait", sid, prev))
        val = prev + 16
        self.dval[sid] = val
        tok = (sid, val, "dma")
        self.prog[eng].append(("op", lambda e: e.dma_start(out_ap, in_ap, **kw), sid, 16))
        self._commit(tok, reads, writes)
        self.nops += 1
        return tok

    def barrier(self):
        if self.dry:
            return
        toks = []
        for e in self.ENG:
            n = self.count[e]
            if n > 0:
                toks.append((("c", e, (n - 1) // EPOCH), (n - 1) % EPOCH + 1, e))
        for sid, val in self.dval.items():
            toks.append((sid, val, "dma"))
        for e in self.ENG:
            for (sid, val, deng) in toks:
                if deng == e:
                    continue
                if self.waited[e].get(sid, 0) >= val:
                    continue
                self.waited[e][sid] = val
                self.prog[e].append(("wait", sid, val))

    def finish(self):
        if self.dry:
            self.stack.close()
            return
        for sid, val in self.dval.items():
            q = sid[1]
            if self.waited[q].get(sid, 0) < val:
                self.prog[q].append(("wait", sid, val))
        nc = self.nc
        sems = {}
        for i, sid in enumerate(self.sids):
            sems[sid] = self.stack.enter_context(nc.semaphore(f"s{i}"))
        prog = self.prog

        def replay(name, e):
            for it in prog[name]:
                if it[0] == "wait":
                    e.wait_ge(sems[it[1]], it[2])
                else:
                    it[1](e).then_inc(sems[it[2]], it[3])

        with nc.Block() as block:
            @block.tensor
            def _(e):
                replay("pe", e)

            @block.scalar
            def _(e):
                replay("act", e)

            @block.vector
            def _(e):
                replay("dve", e)

            @block.gpsimd
            def _(e):
                replay("pool", e)

            @block.sync
            def _(e):
                replay("sp", e)
        self.stack.close()

    def mm(self, out, lhsT, rhs, start=True, stop=True, r=False):
        la, ra = lhsT.ap, rhs.ap
        if r:
            la, ra = la.bitcast(F32R), ra.bitcast(F32R)
        rd = [lhsT, rhs] + ([] if start else [out])
        return self.op("pe", lambda e: e.matmul(out.ap, la, ra, start=start, stop=stop), rd, [out])

    def tr(self, out, in_, ident):
        return self.op("pe", lambda e: e.transpose(out.ap, in_.ap, ident.ap), [in_, ident], [out])

    def act(self, out, in_, func, bias=None, scale=None, accum=None, eng="act"):
        kw = {}
        rd = [in_]
        wr = [out]
        if bias is not None:
            if isinstance(bias, View):
                kw["bias"] = bias.ap
                rd.append(bias)
            else:
                kw["bias"] = bias
        if scale is not None:
            if isinstance(scale, View):
                kw["scale"] = scale.ap
                rd.append(scale)
            else:
                kw["scale"] = scale
        if accum is not None:
            kw["accum_out"] = accum.ap
            wr.append(accum)
        return self.op("act", lambda e: e.activation(out.ap, in_.ap, func, **kw), rd, wr)

    def tt(self, out, a, b, op, eng="dve"):
        return self.op(eng, lambda e: e.tensor_tensor(out.ap, a.ap, b.ap, op), [a, b], [out])

    def ts(self, out, a, s1, op0, s2=None, op1=None, accum=None, eng="dve"):
        rd = [a]
        wr = [out]
        s1a = s1.ap if isinstance(s1, View) else s1
        s2a = s2.ap if isinstance(s2, View) else s2
        if isinstance(s1, View):
            rd.append(s1)
        if isinstance(s2, View):
            rd.append(s2)
        kw = {}
        if op1 is not None:
            kw["op1"] = op1
        if accum is not None:
            kw["accum_out"] = accum.ap
            wr.append(accum)
        return self.op(eng, lambda e: e.tensor_scalar(out.ap, a.ap, s1a, s2a, op0, **kw), rd, wr)

    def stt(self, out, a, s, b, op0, op1, accum=None):
        rd = [a, b]
        wr = [out]
        sa = s.ap if isinstance(s, View) else s
        if isinstance(s, View):
            rd.append(s)
        kw = {}
        if accum is not None:
            kw["accum_out"] = accum.ap
            wr.append(accum)
        return self.op("dve", lambda e: e.scalar_tensor_tensor(out.ap, a.ap, sa, b.ap, op0, op1, **kw), rd, wr)

    def copy(self, out, in_, eng="dve"):
        if eng == "act":
            return self.op("act", lambda e: e.copy(out.ap, in_.ap), [in_], [out])
        return self.op(eng, lambda e: e.tensor_copy(out.ap, in_.ap), [in_], [out])

    def memset(self, out, val, eng="pool"):
        return self.op(eng, lambda e: e.memset(out.ap, val), [], [out])

    def recip(self, out, in_):
        return self.op("dve", lambda e: e.reciprocal(out.ap, in_.ap), [in_], [out])

    def rmax(self, out, in_, eng="dve"):
        return self.op(eng, lambda e: e.reduce_max(out.ap, in_.ap, AX.X), [in_], [out])

D = 2048
NIN = 20504
O_QA, O_KA, O_VA, O_ZA, O_BD = 0, 1024, 2048, 3072, 4096
O_GA, O_GG, O_BG, O_CG, O_HC = 4112, 5136, 6160, 7184, 8208
O_QD, O_KD, O_VD, O_OD, O_IF, O_GATE = 9232, 9744, 10256, 11280, 12304, 12312
ALPHA = 4 ** 0.25
NEG = -1.0e30
NS = 16
TBP = 256
CFG = {}


def dap(t, off, dims):
    return bass.AP(t, off, [list(d) for d in dims])


def build():
    specs = []
    _build(True, specs)
    return _build(False, specs)


def _build(DRY, WSPECS):
    nc = bass.Bass("TRN2", target_bir_lowering=False)

    def din(name, shape):
        return nc.dram_tensor(name, list(shape), F32, kind="ExternalInput")

    def dout(name, shape):
        return nc.dram_tensor(name, list(shape), F32, kind="ExternalOutput")

    I = {}
    for name, shape in [
        ("xp", (2048, D)), ("xs", (NS, D)), ("memp", (256, D)),
        ("cmk", (2, NS, 256, 512)), ("cmv", (2, NS, 256, 512)),
        ("sdc", (2, NS, 3, 3072)), ("sdS", (2, NS, 8, 128, 128)), ("sgc", (2, NS, 30, 1024)),
        ("ssc", (2, NS, 2, 1024)), ("smC", (2, NS, 4, 128, 256)), ("smn", (2, NS, 4, 128)), ("smm", (2, NS, 4)),
        ("w_in", (2, D, NIN)), ("b_in", (2, NIN)), ("a_conv_w", (2, 4, 3072)), ("a_A_log", (2, 8)),
        ("a_dt_bias", (2, 8)), ("a_norm_w", (2, 128)), ("b_conv_w", (2, 31, 1024)), ("b_conv_b", (2, 1024)),
        ("b_ln_g", (2, 1024)), ("b_ln_b", (2, 1024)), ("c_conv_w", (2, 3, 1024)), ("d_norm_w", (2, 256)),
        ("w_branch", (2, 4, 1024, D)), ("w_out", (2, D, D)), ("ln1_g", (2, D)), ("ln1_b", (2, D)),
        ("xq_w", (2, D, 512)), ("xk_w", (2, D, 512)), ("xv_w", (2, D, 512)), ("xo_w", (2, 512, D)),
        ("ln2_g", (2, D)), ("ln2_b", (2, D)), ("ffn_w1", (2, D, 4 * D)), ("ffn_b1", (2, 4 * D)),
        ("ffn_w2", (2, 4 * D, D)), ("ffn_b2", (2, D)), ("ln3_g", (2, D)), ("ln3_b", (2, D)),
    ]:
        I[name] = din(name, shape)
    O = {}
    for name, shape in [
        ("yp", (2048, D)), ("ys", (NS, D)), ("mkp", (2, 256, 512)), ("mvp", (2, 256, 512)),
        ("dcp", (2, 3, 3072)), ("dSp", (2, 8, 128, 128)), ("gcp", (2, 30, 1024)), ("scp", (2, 2, 1024)),
        ("mCp", (2, 4, 128, 256)), ("mnp", (2, 4, 128)), ("mmp", (2, 4)),
        ("dcs", (2, NS, 3, 3072)), ("dSs", (2, NS, 8, 128, 128)), ("gcs", (2, NS, 30, 1024)),
        ("scs", (2, NS, 2, 1024)), ("mCs", (2, NS, 4, 128, 256)), ("mns", (2, NS, 4, 128)), ("mms", (2, NS, 4)),
    ]:
        O[name] = dout(name, shape)

    DBG = {}
    if CFG.get('dump'):
        for nm in ['C', 'B', 'A', 'D']:
            DBG[nm] = dout('dbg_' + nm, (128, 8, TBP))
        for nm in ['mixed', 'x1', 'x2', 'x3']:
            DBG[nm] = dout('dbg_' + nm, (128, 16, TBP))
    dumped = set()
    S = Sched(nc, dry=DRY)
    S.init_psum()
    sb = S.sb

    def dump(nm, buf, nch):
        if not CFG.get('dump') or nm in dumped:
            return
        dumped.add(nm)
        S.dma("pool", dap(DBG[nm], 0, [[nch * TBP, 128], [TBP, nch], [1, TBP]]), buf.t[:, 0:nch, :],
              reads=[buf.sub(k, (slice(None), k, slice(None))) for k in range(nch)])

    ones = sb([128, 128], name="ones")
    ident = sb([128, 128], name="ident")
    triu = sb([128, 128], name="triu")
    maskS = sb([128, 128], name="maskS")
    maskL = sb([128, 128], name="maskL")
    zeros = sb([128, 128], name="zeros")
    S.memset(ones[:], 1.0)
    S.memset(zeros[:], 0.0)
    S.op("pool", lambda e: e.affine_select(ident.t[:], ones.t[:], [[-1, 128]], ALU.is_equal, 0.0, base=0, channel_multiplier=1), [ones[:]], [ident[:]])
    S.op("pool", lambda e: e.affine_select(triu.t[:], ones.t[:], [[1, 128]], ALU.is_ge, 0.0, base=0, channel_multiplier=-1), [ones[:]], [triu[:]])
    S.op("pool", lambda e: e.affine_select(maskS.t[:], zeros.t[:], [[1, 128]], ALU.is_gt, NEG, base=0, channel_multiplier=-1), [zeros[:]], [maskS[:]])
    S.op("pool", lambda e: e.affine_select(maskL.t[:], zeros.t[:], [[-1, 128]], ALU.is_ge, NEG, base=0, channel_multiplier=1), [zeros[:]], [maskL[:]])

    def colload(dst, dcol0, t, off, nchunk, n=128):
        S.dma("pool", dst.t[0:n, dcol0:dcol0 + nchunk], dap(t, off, [[1, n], [128, nchunk]]), writes=[dst[:]],
              allow_slow_non_contiguous=True)

    P = []
    for l in range(2):
        p = {}
        bi = sb([128, 162], name=f"bin{l}")
        colload(bi, 0, I["b_in"], l * NIN + 0, 32)
        colload(bi, 32, I["b_in"], l * NIN + O_BD, 1, n=16)
        colload(bi, 33, I["b_in"], l * NIN + O_GA, 64)
        colload(bi, 97, I["b_in"], l * NIN + O_IF, 1, n=8)
        colload(bi, 98, I["b_in"], l * NIN + O_GATE, 64)
        p["bin"] = bi
        acw = sb([128, 24, 4], name=f"acw{l}")
        for k in range(4):
            S.dma("pool", acw.t[:, :, k], dap(I["a_conv_w"], l * 4 * 3072 + k * 3072, [[1, 128], [128, 24]]), writes=[acw[:]], allow_slow_non_contiguous=True)
        bcw = sb([128, 8, 31], name=f"bcw{l}")
        for k in range(31):
            S.dma("pool", bcw.t[:, :, k], dap(I["b_conv_w"], l * 31 * 1024 + k * 1024, [[1, 128], [128, 8]]), writes=[bcw[:]], allow_slow_non_contiguous=True)
        ccw = sb([128, 8, 3], name=f"ccw{l}")
        for k in range(3):
            S.dma("pool", ccw.t[:, :, k], dap(I["c_conv_w"], l * 3 * 1024 + k * 1024, [[1, 128], [128, 8]]), writes=[ccw[:]], allow_slow_non_contiguous=True)
        p["acw"], p["bcw"], p["ccw"] = acw, bcw, ccw
        for nm, nch in [("b_conv_b", 8), ("b_ln_g", 8), ("b_ln_b", 8), ("a_norm_w", 1), ("d_norm_w", 2), ("ln1_g", 16), ("ln1_b", 16),
                        ("ln2_g", 16), ("ln2_b", 16), ("ln3_g", 16), ("ln3_b", 16), ("ffn_b1", 64), ("ffn_b2", 16)]:
            tl = sb([128, nch], name=f"{nm}{l}")
            colload(tl, 0, I[nm], l * nch * 128, nch)
            p[nm] = tl
        negA = sb([128, 8], name=f"negA{l}")
        dtb = sb([128, 8], name=f"dtb{l}")
        S.dma("pool", negA.t[:, :], dap(I["a_A_log"], l * 8, [[0, 128], [1, 8]]), writes=[negA[:]])
        S.dma("pool", dtb.t[:, :], dap(I["a_dt_bias"], l * 8, [[0, 128], [1, 8]]), writes=[dtb[:]])
        S.act(negA[:], negA[:], AF.Exp)
        S.ts(negA[:], negA[:], -1.0, ALU.mult)
        p["negA"], p["dtb"] = negA, dtb
        P.append(p)

    def bcol(l, col0, n=128):
        if col0 < O_BD:
            c = col0 // 128
        elif col0 == O_BD:
            c = 32
        elif col0 < O_IF:
            c = 33 + (col0 - O_GA) // 128
        elif col0 == O_IF:
            c = 97
        else:
            c = 98 + (col0 - O_GATE) // 128
        return P[l]["bin"][0:n, c:c + 1]

    xT = sb([128, 16, TBP], name="xT")
    mixed = sb([128, 16, TBP], name="mixed")
    outn = sb([128, 8, TBP], BF16, name="outn")
    xTb = sb([128, 16, TBP], BF16, name="xTb")
    mixedb = sb([128, 16, TBP], BF16, name="mixedb")
    hb = sb([128, 8, TBP], BF16, name="hb")
    ubuf = sb([128, 8, TBP], name="ubuf")
    tokbuf = sb([128, D], name="tokbuf")
    wsl = [sb([128, 2048], name=f"w{i}") for i in range(3)]
    NWB = 5
    wbf = [sb([128, 2048], BF16, name=f"wb{i}") for i in range(NWB)]
    wi = [0]
    NT = 10
    tmpA = [sb([128, TBP], name=f"tA{i}") for i in range(NT)]
    ti = [0]
    NQ = 12
    tmpQ = [sb([128, 128], name=f"tQ{i}") for i in range(NQ)]
    qi = [0]
    NC_ = 48
    tmpC = [sb([128, 8], name=f"tC{i}") for i in range(NC_)]
    ci_ = [0]
    extb = sb([128, TBP + 30], name="extb")
    persA = sb([128, 16, 24], name="persA")
    lnm = sb([128, TBP], name="lnm"); lnr = sb([128, TBP], name="lnr"); lnm2 = sb([128, TBP], name="lnm2")
    tmpQL = [sb([128, 128], name=f"tQL{i}") for i in range(12)]
    qli = [0]
    persD = sb([128, 16, 12], name="persD")
    exts = sb([128, NS, 31], name="exts")
    t258 = [sb([128, 258], name=f"t258_{i}") for i in range(4)]
    t258i = [0]

    def tA():
        ti[0] += 1
        return tmpA[ti[0] % NT]

    def tQ():
        qi[0] += 1
        return tmpQ[qi[0] % NQ]

    def tQL():
        qli[0] += 1
        return tmpQL[qli[0] % 12]

    def psrc(pv, L, shape_rows):
        if L != 1:
            return pv
        t = tQ()
        v = t[0:shape_rows, 0:1]
        S.act(v, pv, AF.Identity)
        return v

    def tC():
        ci_[0] += 1
        return tmpC[ci_[0] % NC_]

    def t258n():
        t258i[0] += 1
        return t258[t258i[0] % 4]

    def ch(buf, k, sl=slice(None)):
        return buf.sub(k, (slice(None), k, sl))

    ARN = 10240
    arena = sb([128, ARN], name="arena")
    apos = {"p": 0, "s": 0}

    def carve(phase, shape, name):
        n = 1
        for s_ in shape[1:]:
            n *= s_
        a0 = apos[phase]
        apos[phase] += n
        assert apos[phase] <= ARN, (phase, apos[phase])
        v = arena.t[:, a0:a0 + n]
        if len(shape) == 3:
            v = v.rearrange("p (a b) -> p a b", a=shape[1])
        elif len(shape) == 4:
            v = v.rearrange("p (a b c) -> p a b c", a=shape[1], b=shape[2])
        return Buf(v, name)

    histA = [carve("p", [128, 24, 3], f"hA{l}") for l in range(2)]
    histB = [carve("p", [128, 8, 30], f"hB{l}") for l in range(2)]
    histC = [carve("p", [128, 8, 2], f"hC{l}") for l in range(2)]
    Sa = [[carve("p", [128, 128], f"Sa{l}_{h}") for h in range(8)] for l in range(2)]
    Cn = [[carve("p", [128, 258], f"Cn{l}_{h}") for h in range(4)] for l in range(2)]
    mst = [carve("p", [128, 4], f"mst{l}") for l in range(2)]
    KT = [carve("p", [128, 4, 256], f"KT{l}") for l in range(2)]
    Vt = [carve("p", [128, 2, 512], f"Vt{l}") for l in range(2)]
    for l in range(2):
        S.memset(histA[l][:], 0.0); S.memset(histB[l][:], 0.0); S.memset(histC[l][:], 0.0)
        S.memset(mst[l][:], 0.0)
        for h in range(8):
            S.memset(Sa[l][h][:], 0.0)
        for h in range(4):
            S.memset(Cn[l][h][:], 0.0)
    SaS = [carve("s", [128, 128], f"SaS{i}") for i in range(3)]
    CnS = [carve("s", [128, 258], f"CnS{i}") for i in range(3)]
    mstS = [carve("s", [128, 4], f"mstS{i}") for i in range(3)]
    ssi = [0]
    KTs = carve("s", [128, 4, 256], "KTs")
    Kts = carve("s", [128, 2, 512], "Kts")
    Vts = carve("s", [128, 2, 512], "Vts")
    hsA = carve("s", [128, 24, NS, 3], "hsA")
    hsB = carve("s", [128, 8, NS, 30], "hsB")
    hsC = carve("s", [128, 8, NS, 2], "hsC")
    tmpB_new = carve("s", [128, 8, NS], "tmpBn")
    tmpA_new = carve("s", [128, 24, NS], "tmpAn")
    qTb = sb([128, 4, TBP], name="qTb")
    oTb = sb([128, 4, TBP], BF16, name="oTb")

    issued = [0]
    WDEPTH = 4
    scr_idx = {}
    USE_SCR = CFG.get('scr', True)
    wscr = None
    if not DRY and USE_SCR:
        nuniq = len({(s[0].name,) + tuple(s[1:]) for s in WSPECS if s[7]})
        wscr = [nc.dram_tensor(f"wscr{q}", [min(400, nuniq - q * 400), 128, 2048], BF16, kind="Internal") for q in range((nuniq + 399) // 400)]

    def w_views(j, wide):
        kk = 4 if wide else 16
        w = wsl[j % 3]
        wb = wbf[j % NWB]
        return (w, wb, w.t[:, :].rearrange("p (k n) -> p k n", k=kk), wb.t[:, :].rearrange("p (k n) -> p k n", k=kk))

    def w_issue(j):
        (t, base, rstride, row0, col0, n, kcn, bf, wide) = WSPECS[j]
        w, wb, wv, wbv = w_views(j, wide)
        skey = (t.name,) + tuple(WSPECS[j][1:])
        if bf and USE_SCR and skey in scr_idx:
            idx = scr_idx[skey]
            S.dma("sp", wb.t[:, :], dap(wscr[idx // 400], (idx % 400) * 128 * 2048, [[2048, 128], [1, 2048]]),
                  reads=[View(("scr", idx), None)], writes=[wb[:]])
            return
        S.dma("sp", wv[:, 0:kcn, 0:n], dap(t, base + row0 * rstride + col0, [[rstride, 128], [128 * rstride, kcn], [1, n]]),
              writes=[w[:]])
        if bf:
            if j % 2 == 0:
                S.op("act", lambda e: e.copy(wbv[:, 0:kcn, 0:n], wv[:, 0:kcn, 0:n]), [w[:]], [wb[:]])
            else:
                S.op("dve", lambda e: e.tensor_copy(wbv[:, 0:kcn, 0:n], wv[:, 0:kcn, 0:n]), [w[:]], [wb[:]])
            if USE_SCR:
                idx = len(scr_idx)
                scr_idx[skey] = idx
                S.dma("pool", dap(wscr[idx // 400], (idx % 400) * 128 * 2048, [[2048, 128], [1, 2048]]), wb.t[:, :],
                      reads=[wb[:]], writes=[View(("scr", idx), None)])

    def load_w(t, base, rstride, row0, col0, n, kcn, bf=False, wide=False):
        i = wi[0]
        wi[0] += 1
        spec = (t, base, rstride, row0, col0, n, kcn, bf, wide)
        if DRY:
            WSPECS.append(spec)
        else:
            assert WSPECS[i][1:] == spec[1:], (i, WSPECS[i][1:], spec[1:])
            while issued[0] <= min(i + WDEPTH, len(WSPECS) - 1):
                jn = issued[0]
                if jn - 3 >= i and not WSPECS[jn - 3][7]:
                    break
                w_issue(jn)
                issued[0] += 1
        w, wb, wv, wbv = w_views(i, wide)
        return Buf(wbv, wb.key) if bf else Buf(wv, w.key)

    def fm_proj(t, base, rstride, row0, col0, n, kcn, rhs_fn, TB):
        w = load_w(t, base, rstride, row0, col0, n, kcn, bf=True)
        p = S.ps()
        for k in range(kcn):
            S.mm(p[0:n, 0:TB], w[:, k, 0:n], rhs_fn(k), start=(k == 0), stop=(k == kcn - 1), r=False)
        return p

    def fm_group(t, base, rstride, row0, col0, ncols, kcn, rhs_fn, TB):
        nch = (ncols + 127) // 128
        pss = [S.ps() for _ in range(nch)]
        for kq in range(0, kcn, 4):
            nk = min(4, kcn - kq)
            w = load_w(t, base, rstride, row0 + kq * 128, col0, ncols, nk, bf=True, wide=True)
            for c in range(nch):
                n = min(128, ncols - c * 128)
                for kk in range(nk):
                    k = kq + kk
                    S.mm(pss[c][0:n, 0:TB], w[:, kk, c * 128:c * 128 + n], rhs_fn(k), start=(k == 0), stop=(k == kcn - 1), r=False)
        return pss

    def win_group(l, col0, ncols, TB):
        return fm_group(I["w_in"], l * D * NIN, NIN, 0, col0, ncols, 16, lambda k: ch(xTb, k, slice(0, TB)), TB)

    def win_proj(l, col0, n, TB):
        return fm_proj(I["w_in"], l * D * NIN, NIN, 0, col0, n, 16, lambda k: ch(xTb, k, slice(0, TB)), TB)

    def transpose_to(dst_view, src_view, rows, cols, eng="dve"):
        p = S.ps()
        S.tr(p[0:cols, 0:rows], src_view, ident[0:rows, 0:rows])
        S.copy(dst_view, p[0:cols, 0:rows], eng=eng)

    def layernorm(l, gname, bname, TB):
        pm = S.ps()
        for k in range(16):
            S.mm(pm[:, 0:TB], ones[:, :], ch(xT, k, slice(0, TB)), start=(k == 0), stop=(k == 15), r=False)
        pq = S.ps()
        for k in range(16):
            sq = tA()
            S.act(sq[:, 0:TB], ch(xT, k, slice(0, TB)), AF.Square)
            S.mm(pq[:, 0:TB], ones[:, :], sq[:, 0:TB], start=(k == 0), stop=(k == 15), r=False)
        mean, rstd, m2 = lnm, lnr, lnm2
        S.ts(mean[:, 0:TB], pm[:, 0:TB], 1.0 / D, ALU.mult)
        S.tt(m2[:, 0:TB], mean[:, 0:TB], mean[:, 0:TB], ALU.mult)
        S.stt(rstd[:, 0:TB], pq[:, 0:TB], 1.0 / D, m2[:, 0:TB], ALU.mult, ALU.subtract)
        S.ts(rstd[:, 0:TB], rstd[:, 0:TB], 0.0, ALU.max, 1e-5, ALU.add)
        S.act(rstd[:, 0:TB], rstd[:, 0:TB], AF.Sqrt)
        S.recip(rstd[:, 0:TB], rstd[:, 0:TB])
        for k in range(16):
            t = tA()
            S.tt(t[:, 0:TB], ch(xT, k, slice(0, TB)), mean[:, 0:TB], ALU.subtract)
            S.tt(t[:, 0:TB], t[:, 0:TB], rstd[:, 0:TB], ALU.mult)
            S.act(ch(xT, k, slice(0, TB)), t[:, 0:TB], AF.Identity, bias=P[l][bname][:, k:k + 1], scale=P[l][gname][:, k:k + 1])
            S.act(ch(xTb, k, slice(0, TB)), t[:, 0:TB], AF.Identity, bias=P[l][bname][:, k:k + 1], scale=P[l][gname][:, k:k + 1])

    def emit_rows(srcT_fn, nch, R, dst_ap_fn):
        for j0 in range(0, nch, 16):
            nj = min(16, nch - j0)
            for j in range(j0, j0 + nj):
                transpose_to(tokbuf[0:R, (j - j0) * 128:(j - j0 + 1) * 128], srcT_fn(j), 128, R)
            S.dma("pool", dst_ap_fn(j0 * 128, nj * 128), tokbuf.t[0:R, 0:nj * 128], reads=[tokbuf[:]])

    class Conv:
        def __init__(self, mode, W, TB):
            self.mode, self.W, self.TB, self.H = mode, W, TB, W - 1
            if mode == "p":
                self.newv = extb[:, self.H:self.H + TB]
                self.histv = extb[:, 0:self.H]
                self.tail = extb[:, TB:TB + self.H]
            else:
                self.newv = exts[:, :, self.H]
                self.histv = exts[:, :, 0:self.H]

        def tap(self, k):
            if self.mode == "p":
                return extb[:, k:k + self.TB]
            return exts[:, :, k]

    def conv_apply(cv, wtile, j, out_view, bias_view=None):
        W = cv.W
        acc = out_view
        if bias_view is not None:
            S.ts(acc, cv.tap(0), wtile[:, j, 0:1], ALU.mult, bias_view, ALU.add)
        else:
            S.ts(acc, cv.tap(0), wtile[:, j, 0:1], ALU.mult)
        for k in range(1, W):
            S.stt(acc, cv.tap(k), wtile[:, j, k:k + 1], acc, ALU.mult, ALU.add)

    def layer_block(l, TB, mode, chunks, last, hsA=None, hsB=None, hsC=None):
        p_ = P[l]
        shp = (lambda v: v)
        if mode == "p":
            ov = lambda buf: buf[:, 0:TB]
        else:
            ov = lambda buf: buf[:, 0:TB]
        first_branch = [True]

        def branch_merge(n):
            for dg in range(4):
                pgs = win_group(l, O_GATE + n * D + dg * 512, 512, TB)
                gts = []
                for c in range(4):
                    gt = tA()
                    S.act(gt[:, 0:TB], pgs[c][:, 0:TB], AF.Sigmoid, bias=bcol(l, O_GATE + n * D + (dg * 4 + c) * 128))
                    gts.append(gt)
                pbs = fm_group(I["w_branch"], (l * 4 + n) * 1024 * D, D, 0, dg * 512, 512, 8, lambda k: ch(outn, k, slice(0, TB)), TB)
                for c in range(4):
                    d = dg * 4 + c
                    pb, gt = pbs[c], gts[c]
                    if first_branch[0]:
                        S.tt(ch(mixed, d, slice(0, TB)), pb[:, 0:TB], gt[:, 0:TB], ALU.mult)
                    else:
                        S.tt(gt[:, 0:TB], pb[:, 0:TB], gt[:, 0:TB], ALU.mult)
                        if n == 3:
                            S.tt(ch(mixedb, d, slice(0, TB)), ch(mixed, d, slice(0, TB)), gt[:, 0:TB], ALU.add)
                        else:
                            S.tt(ch(mixed, d, slice(0, TB)), ch(mixed, d, slice(0, TB)), gt[:, 0:TB], ALU.add)
            first_branch[0] = False

        def newrow_out(vals_fn, nch, dst_t, lbase, H, C):
            emit_rows(vals_fn, nch, NS, lambda c0, n: dap(dst_t, lbase + (H - 1) * C + c0, [[H * C, NS], [1, n]]))

        def ph_C():
            cv = Conv(mode, 3, TB)
            newC = ubuf
            for jg in range(2):
                pbgs = win_group(l, O_BG + jg * 512, 512, TB)
                bgs = []
                for c in range(4):
                    t_ = tA()
                    S.act(t_[:, 0:TB], pbgs[c][:, 0:TB], AF.Identity, bias=bcol(l, O_BG + (jg * 4 + c) * 128))
                    bgs.append(t_)
                pcs = win_group(l, O_CG + jg * 512, 512, TB)
                cgs = []
                for c in range(4):
                    t_ = tA()
                    S.act(t_[:, 0:TB], pcs[c][:, 0:TB], AF.Identity, bias=bcol(l, O_CG + (jg * 4 + c) * 128))
                    cgs.append(t_)
                phs = win_group(l, O_HC + jg * 512, 512, TB)
                for c in range(4):
                    j = jg * 4 + c
                    if mode == "p":
                        S.copy(cv.histv, histC[l][:, j, :], eng="pool")
                    else:
                        S.copy(cv.histv, hsC.sub(j, (slice(None), j, slice(None), slice(None))), eng="pool")
                    S.stt(cv.newv, phs[c][:, 0:TB], bcol(l, O_HC + j * 128), cgs[c][:, 0:TB], ALU.add, ALU.mult)
                    conv_apply(cv, p_["ccw"], j, cgs[c][:, 0:TB])
                    if mode == "p":
                        S.copy(histC[l][:, j, :], cv.tail, eng="pool")
                    else:
                        S.copy(ch(newC, j, slice(0, NS)), cv.newv, eng="pool")
                    S.tt(ch(outn, j, slice(0, TB)), bgs[c][:, 0:TB], cgs[c][:, 0:TB], ALU.mult)
            if mode == "s":
                newrow_out(lambda j: ch(newC, j, slice(0, NS)), 8, O["scs"], l * NS * 2 * 1024, 2, 1024)
            branch_merge(2)

        def ph_B():
            cv = Conv(mode, 31, TB)
            newB = mixed
            for jg in range(2):
                pgs = win_group(l, O_GG + jg * 512, 512, TB)
                sgs = []
                for c in range(4):
                    t_ = tA()
                    S.act(t_[:, 0:TB], pgs[c][:, 0:TB], AF.Sigmoid, bias=bcol(l, O_GG + (jg * 4 + c) * 128))
                    sgs.append(t_)
                pas = win_group(l, O_GA + jg * 512, 512, TB)
                for c in range(4):
                    j = jg * 4 + c
                    if mode == "p":
                        S.copy(cv.histv, histB[l][:, j, :], eng="pool")
                    else:
                        S.copy(cv.histv, hsB.sub(j, (slice(None), j, slice(None), slice(None))), eng="pool")
                    S.stt(cv.newv, pas[c][:, 0:TB], bcol(l, O_GA + j * 128), sgs[c][:, 0:TB], ALU.add, ALU.mult)
                    conv_apply(cv, p_["bcw"], j, ch(ubuf, j, slice(0, TB)), bias_view=p_["b_conv_b"][:, j:j + 1])
                    if mode == "p":
                        S.copy(histB[l][:, j, :], cv.tail, eng="pool")
                    else:
                        S.copy(tmpB_new.sub(j, (slice(None), j, slice(None))), cv.newv, eng="pool")
            if mode == "s":
                newrow_out(lambda j: tmpB_new.sub(j, (slice(None), j, slice(None))), 8, O["gcs"], l * NS * 30 * 1024, 30, 1024)
            pm = S.ps()
            for k in range(8):
                S.mm(pm[:, 0:TB], ones[:, :], ch(ubuf, k, slice(0, TB)), start=(k == 0), stop=(k == 7), r=False)
            pq = S.ps()
            for k in range(8):
                sq = tA()
                S.act(sq[:, 0:TB], ch(ubuf, k, slice(0, TB)), AF.Square)
                S.mm(pq[:, 0:TB], ones[:, :], sq[:, 0:TB], start=(k == 0), stop=(k == 7), r=False)
            mean, rstd, m2 = lnm, lnr, lnm2
            S.ts(mean[:, 0:TB], pm[:, 0:TB], 1.0 / 1024, ALU.mult)
            S.tt(m2[:, 0:TB], mean[:, 0:TB], mean[:, 0:TB], ALU.mult)
            S.stt(rstd[:, 0:TB], pq[:, 0:TB], 1.0 / 1024, m2[:, 0:TB], ALU.mult, ALU.subtract)
            S.ts(rstd[:, 0:TB], rstd[:, 0:TB], 0.0, ALU.max, 1e-5, ALU.add)
            S.act(rstd[:, 0:TB], rstd[:, 0:TB], AF.Sqrt)
            S.recip(rstd[:, 0:TB], rstd[:, 0:TB])
            for k in range(8):
                t = tA()
                S.tt(t[:, 0:TB], ch(ubuf, k, slice(0, TB)), mean[:, 0:TB], ALU.subtract)
                S.tt(t[:, 0:TB], t[:, 0:TB], rstd[:, 0:TB], ALU.mult)
                S.act(ch(outn, k, slice(0, TB)), t[:, 0:TB], AF.Silu, bias=p_["b_ln_b"][:, k:k + 1], scale=p_["b_ln_g"][:, k:k + 1])
            branch_merge(1)

        def ph_A():
            pbd = win_proj(l, O_BD, 16, TB)
            bdT = tA()
            S.act(bdT[0:16, 0:TB], pbd[0:16, 0:TB], AF.Identity, bias=bcol(l, O_BD, 16))
            beta_t, g_t, gc_t = [], [], []
            for (c0, L) in chunks:
                p = S.ps()
                S.tr(p[0:L, 0:16], bdT[0:16, c0:c0 + L], ident[0:16, 0:16])
                ci_a = len(beta_t)
                be = Buf(persA.t[:, ci_a, 0:8], ("persA", ci_a, 0)); gg = Buf(persA.t[:, ci_a, 8:16], ("persA", ci_a, 1))
                gcx = Buf(persA.t[:, ci_a, 16:24], ("persA", ci_a, 2)); tmp = tC()
                S.act(be[0:L, 0:8], p[0:L, 0:8], AF.Sigmoid)
                S.tt(tmp[0:L, 0:8], p[0:L, 8:16], p_["dtb"][0:L, 0:8], ALU.add)
                S.act(tmp[0:L, 0:8], tmp[0:L, 0:8], AF.Exp)
                S.act(tmp[0:L, 0:8], tmp[0:L, 0:8], AF.Ln, bias=1.0)
                S.tt(gg[0:L, 0:8], tmp[0:L, 0:8], p_["negA"][0:L, 0:8], ALU.mult)
                pc = S.ps()
                S.mm(pc[0:L, 0:8], triu[0:L, 0:L], gg[0:L, 0:8], r=False)
                S.copy(gcx[0:L, 0:8], pc[0:L, 0:8])
                beta_t.append(be); g_t.append(gg); gc_t.append(gcx)
            newA = tmpA_new
            for h in range(8):
                qkv = []
                for part, off in enumerate((O_QA, O_KA, O_VA)):
                    jj = part * 8 + h
                    cv = Conv(mode, 4, TB)
                    pp = win_proj(l, off + h * 128, 128, TB)
                    if mode == "p":
                        S.copy(cv.histv, histA[l][:, jj, :], eng="pool")
                    else:
                        S.copy(cv.histv, hsA.sub(jj, (slice(None), jj, slice(None), slice(None))), eng="pool")
                    S.act(cv.newv, pp[:, 0:TB], AF.Identity, bias=bcol(l, off + h * 128))
                    acc = tA()
                    conv_apply(cv, p_["acw"], jj, acc[:, 0:TB])
                    if mode == "p":
                        S.copy(histA[l][:, jj, :], cv.tail, eng="pool")
                    else:
                        S.copy(newA.sub(jj, (slice(None), jj, slice(None))), cv.newv, eng="pool")
                    S.act(acc[:, 0:TB], acc[:, 0:TB], AF.Silu)
                    qkv.append(acc)
                qT_, kT_, vT_ = qkv
                for idx, t_ in enumerate((qT_, kT_)):
                    sq = tA()
                    S.act(sq[:, 0:TB], t_[:, 0:TB], AF.Square)
                    pn = S.ps()
                    S.mm(pn[:, 0:TB], ones[:, :], sq[:, 0:TB], r=False)
                    S.ts(sq[:, 0:TB], pn[:, 0:TB], 1e-6, ALU.add)
                    S.act(sq[:, 0:TB], sq[:, 0:TB], AF.Sqrt)
                    S.recip(sq[:, 0:TB], sq[:, 0:TB])
                    if idx == 0:
                        S.stt(t_[:, 0:TB], t_[:, 0:TB], 128 ** -0.5, sq[:, 0:TB], ALU.mult, ALU.mult)
                    else:
                        S.tt(t_[:, 0:TB], t_[:, 0:TB], sq[:, 0:TB], ALU.mult)
                pz = win_proj(l, O_ZA + h * 128, 128, TB)
                sz = tA()
                S.act(sz[:, 0:TB], pz[:, 0:TB], AF.Silu, bias=bcol(l, O_ZA + h * 128))
                for ci, (c0, L) in enumerate(chunks):
                    if mode == "p":
                        Sh = Sa[l][h]
                    else:
                        Sh = SaS[ssi[0] % 3]; ssi[0] += 1
                        S.dma("sp", Sh.t[:, :], dap(I["sdS"], ((l * NS + ci) * 8 + h) * 16384, [[128, 128], [1, 128]]), writes=[Sh[:]])
                    delta_chunk(l, h, ci, c0, L, qT_, kT_, vT_, sz, Sh, beta_t[ci], g_t[ci], gc_t[ci])
                    if mode == "s":
                        S.dma("pool", dap(O["dSs"], ((l * NS + ci) * 8 + h) * 16384, [[128, 128], [1, 128]]), Sh.t[:, :], reads=[Sh[:]])
                    elif last:
                        if ci == len(chunks) - 1:
                            S.dma("pool", dap(O["dSp"], (l * 8 + h) * 16384, [[128, 128], [1, 128]]), Sh.t[:, :], reads=[Sh[:]])
            if mode == "s":
                newrow_out(lambda j: newA.sub(j, (slice(None), j, slice(None))), 24, O["dcs"], l * NS * 3 * 3072, 3, 3072)
            branch_merge(0)

        def ph_D():
            pif = win_proj(l, O_IF, 8, TB)
            ifT = tA()
            S.act(ifT[0:8, 0:TB], pif[0:8, 0:TB], AF.Identity, bias=bcol(l, O_IF, 8))
            lf_t, b_t, ib_t = [], [], []
            for (c0, L) in chunks:
                p = S.ps()
                S.tr(p[0:L, 0:8], ifT[0:8, c0:c0 + L], ident[0:8, 0:8])
                ci_d = len(lf_t)
                lf = Buf(persD.t[:, ci_d, 0:4], ("persD", ci_d, 0)); bb = Buf(persD.t[:, ci_d, 4:8], ("persD", ci_d, 1))
                ib = Buf(persD.t[:, ci_d, 8:12], ("persD", ci_d, 2))
                S.act(lf[0:L, 0:4], p[0:L, 4:8], AF.Exp, scale=-1.0)
                S.act(lf[0:L, 0:4], lf[0:L, 0:4], AF.Ln, bias=1.0)
                S.ts(lf[0:L, 0:4], lf[0:L, 0:4], -1.0, ALU.mult)
                pc = S.ps()
                S.mm(pc[0:L, 0:4], triu[0:L, 0:L], lf[0:L, 0:4], r=False)
                S.copy(bb[0:L, 0:4], pc[0:L, 0:4])
                S.tt(ib[0:L, 0:4], p[0:L, 0:4], bb[0:L, 0:4], ALU.subtract)
                lf_t.append(lf); b_t.append(bb); ib_t.append(ib)
            for h in range(4):
                pq_ = win_proj(l, O_QD + h * 128, 128, TB)
                qT_ = tA()
                S.act(qT_[:, 0:TB], pq_[:, 0:TB], AF.Identity, bias=bcol(l, O_QD + h * 128))
                pk_ = win_proj(l, O_KD + h * 128, 128, TB)
                kT_ = tA()
                S.ts(kT_[:, 0:TB], pk_[:, 0:TB], bcol(l, O_KD + h * 128), ALU.add, 128 ** -0.5, ALU.mult)
                vT_, so_ = [], []
                for e in range(2):
                    pv_ = win_proj(l, O_VD + h * 256 + e * 128, 128, TB)
                    v_ = tA()
                    S.act(v_[:, 0:TB], pv_[:, 0:TB], AF.Identity, bias=bcol(l, O_VD + h * 256 + e * 128))
                    vT_.append(v_)
                for e in range(2):
                    po_ = win_proj(l, O_OD + h * 256 + e * 128, 128, TB)
                    o_ = tA()
                    S.act(o_[:, 0:TB], po_[:, 0:TB], AF.Sigmoid, bias=bcol(l, O_OD + h * 256 + e * 128))
                    so_.append(o_)
                for ci, (c0, L) in enumerate(chunks):
                    if mode == "p":
                        Ch, ms = Cn[l][h], mst[l]
                    else:
                        Ch = CnS[ssi[0] % 3]; ms = mstS[ssi[0] % 3]; ssi[0] += 1
                        base = (l * NS + ci) * 4 + h
                        S.dma("sp", Ch.t[:, 0:256], dap(I["smC"], base * 32768, [[256, 128], [1, 256]]), writes=[Ch[:]])
                        S.dma("sp", Ch.t[:, 256:257], dap(I["smn"], base * 128, [[1, 128], [1, 1]]), writes=[Ch[:]], allow_slow_non_contiguous=True)
                        S.dma("sp", ms.t[:, h:h + 1], dap(I["smm"], base, [[0, 128], [1, 1]]), writes=[ms[:]])
                    mlstm_chunk(l, h, c0, L, qT_, kT_, vT_, so_, Ch, ms, lf_t[ci], b_t[ci], ib_t[ci])
                    if mode == "s" or (last and ci == len(chunks) - 1):
                        if mode == "s":
                            base = (l * NS + ci) * 4 + h
                            oc, on_, om = O["mCs"], O["mns"], O["mms"]
                        else:
                            base = l * 4 + h
                            oc, on_, om = O["mCp"], O["mnp"], O["mmp"]
                        S.dma("pool", dap(oc, base * 32768, [[256, 128], [1, 256]]), Ch.t[:, 0:256], reads=[Ch[:]])
                        S.dma("pool", dap(on_, base * 128, [[1, 128], [1, 1]]), Ch.t[:, 256:257], reads=[Ch[:]], allow_slow_non_contiguous=True)
                        S.dma("pool", dap(om, base, [[1, 1], [1, 1]]), ms.t[0:1, h:h + 1], reads=[ms[:]])
            branch_merge(3)

        def ph_O():
            for dg in range(4):
                pys = fm_group(I["w_out"], l * D * D, D, 0, dg * 512, 512, 16, lambda k: ch(mixedb, k, slice(0, TB)), TB)
                for c in range(4):
                    d = dg * 4 + c
                    S.stt(ch(xT, d, slice(0, TB)), ch(xT, d, slice(0, TB)), ALPHA, pys[c][:, 0:TB], ALU.mult, ALU.add)
            layernorm(l, "ln1_g", "ln1_b", TB)

        def ph_X():
            pqs = fm_group(I["xq_w"], l * D * 512, 512, 0, 0, 512, 16, lambda k: ch(xTb, k, slice(0, TB)), TB)
            for h in range(4):
                S.copy(ch(qTb, h, slice(0, TB)), pqs[h][:, 0:TB], eng="act")
            for ci, (c0, L) in enumerate(chunks):
                if mode == "p":
                    KTl, Vl = KT[l], Vt[l]
                else:
                    KTl, Vl = KTs, Vts
                    S.dma("sp", Kts.t[:, :, :], dap(I["cmk"], (l * NS + ci) * 256 * 512, [[512, 128], [128 * 512, 2], [1, 512]]), writes=[Kts[:]])
                    S.dma("sp", Vts.t[:, :, :], dap(I["cmv"], (l * NS + ci) * 256 * 512, [[512, 128], [128 * 512, 2], [1, 512]]), writes=[Vts[:]])
                    for hh in range(4):
                        for mc in range(2):
                            transpose_to(KTs[:, hh, mc * 128:(mc + 1) * 128], Kts[:, mc, hh * 128:(hh + 1) * 128], 128, 128, eng="act")
                otok = tokbuf
                for h in range(4):
                    ps_ = S.ps()
                    S.mm(ps_[0:L, 0:256], ch(qTb, h, slice(c0, c0 + L)), KTl[:, h, :], r=False)
                    mx = tC()
                    S.rmax(mx[0:L, 0:1], ps_[0:L, 0:256])
                    S.ts(mx[0:L, 1:2], mx[0:L, 0:1], -(128 ** -0.5), ALU.mult)
                    es = tA()
                    S.act(es[0:L, 0:256], ps_[0:L, 0:256], AF.Exp, bias=mx[0:L, 1:2], scale=128 ** -0.5, accum=mx[0:L, 2:3])
                    S.recip(mx[0:L, 3:4], mx[0:L, 2:3])
                    aT = tA()
                    for mc in range(2):
                        transpose_to(aT[:, mc * 128:mc * 128 + L], es[0:L, mc * 128:(mc + 1) * 128], L, 128, eng="act")
                    po = S.ps()
                    for mc in range(2):
                        S.mm(po[0:L, 0:128], aT[:, mc * 128:mc * 128 + L], Vl[:, mc, h * 128:(h + 1) * 128], start=(mc == 0), stop=(mc == 1), r=False)
                    S.ts(otok[0:L, h * 128:(h + 1) * 128], po[0:L, 0:128], mx[0:L, 3:4], ALU.mult)
                for h in range(4):
                    transpose_to(ch(oTb, h, slice(c0, c0 + L)), otok[0:L, h * 128:(h + 1) * 128], L, 128, eng="act")
            for dg in range(4):
                pys = fm_group(I["xo_w"], l * 512 * D, D, 0, dg * 512, 512, 4, lambda k: ch(oTb, k, slice(0, TB)), TB)
                for c in range(4):
                    d = dg * 4 + c
                    S.stt(ch(xT, d, slice(0, TB)), ch(xT, d, slice(0, TB)), ALPHA, pys[c][:, 0:TB], ALU.mult, ALU.add)
            layernorm(l, "ln2_g", "ln2_b", TB)

        def ph_M():
            for g in range(8):
                for jg2 in range(2):
                    phs = fm_group(I["ffn_w1"], l * D * 4 * D, 4 * D, 0, (g * 8 + jg2 * 4) * 128, 512, 16, lambda k: ch(xTb, k, slice(0, TB)), TB)
                    for c in range(4):
                        jj = jg2 * 4 + c
                        j = g * 8 + jj
                        r_ = tA()
                        S.act(r_[:, 0:TB], phs[c][:, 0:TB], AF.Relu, bias=p_["ffn_b1"][:, j:j + 1])
                        S.act(ch(hb, jj, slice(0, TB)), r_[:, 0:TB], AF.Square)
                for dg in range(4):
                    pds = fm_group(I["ffn_w2"], l * 4 * D * D, D, g * 1024, dg * 512, 512, 8, lambda k: ch(hb, k, slice(0, TB)), TB)
                    for c in range(4):
                        d = dg * 4 + c
                        if g == 0:
                            S.copy(ch(mixed, d, slice(0, TB)), pds[c][:, 0:TB], eng="act")
                        else:
                            S.tt(ch(mixed, d, slice(0, TB)), ch(mixed, d, slice(0, TB)), pds[c][:, 0:TB], ALU.add)
            for d in range(16):
                S.ts(ch(mixed, d, slice(0, TB)), ch(mixed, d, slice(0, TB)), p_["ffn_b2"][:, d:d + 1], ALU.add)
                S.stt(ch(xT, d, slice(0, TB)), ch(xT, d, slice(0, TB)), ALPHA, ch(mixed, d, slice(0, TB)), ALU.mult, ALU.add)
            layernorm(l, "ln3_g", "ln3_b", TB)

        PH = CFG.get('phases', 'CBADOXM')
        if 'C' in PH:
            ph_C(); dump('C', outn, 8)
        if 'B' in PH:
            ph_B(); dump('B', outn, 8)
        if 'A' in PH:
            ph_A(); dump('A', outn, 8)
        if 'D' in PH:
            ph_D(); dump('D', outn, 8); dump('mixed', mixed, 16)
        if 'O' in PH:
            ph_O(); dump('x1', xT, 16)
        if 'X' in PH:
            ph_X(); dump('x2', xT, 16)
        if 'M' in PH:
            ph_M(); dump('x3', xT, 16)

    def delta_chunk(l, h, ci, c0, L, qT_, kT_, vT_, sz, Sh, be, gg, gcx):
        p_ = P[l]
        beta_c, g_c, gc_c = be[0:L, h:h + 1], gg[0:L, h:h + 1], gcx[0:L, h:h + 1]
        kc, qc, vc = kT_[:, c0:c0 + L], qT_[:, c0:c0 + L], vT_[:, c0:c0 + L]
        gb = tQ()
        S.ts(gb[0:L, :], ones[0:L, :], g_c, ALU.mult)
        pg = S.ps()
        S.mm(pg[:, 0:L], gb[0:L, :], triu[0:L, 0:L], r=False)
        gcr = tQL(); egr = tQL()
        S.copy(gcr[:, 0:L], pg[:, 0:L], eng="act")
        S.act(egr[:, 0:L], pg[:, 0:L], AF.Exp)
        cols = tC()
        eg_c, ekl_c, nb_c, nbk_c = cols[0:L, 0:1], cols[0:L, 1:2], cols[0:L, 2:3], cols[0:L, 3:4]
        S.act(eg_c, gc_c, AF.Exp)
        S.act(ekl_c, gc_c, AF.Exp, bias=gcr[0:L, L - 1:L], scale=-1.0)
        S.ts(nb_c, beta_c, -1.0, ALU.mult)
        S.tt(nbk_c, nb_c, ekl_c, ALU.mult)
        ktok = tQL(); vtok = tQL()
        transpose_to(ktok[0:L, :], kc, 128, L, eng="act")
        transpose_to(vtok[0:L, :], vc, 128, L, eng="act")
        pk = S.ps()
        S.mm(pk[0:L, 0:L], kc, kc, r=False)
        S.mm(pk[0:L, 128:128 + L], kc, qc, r=False)
        qkd = tQL()
        if L > 1:
            Dm = tQ(); Es = tQ(); B0 = tQ(); Ei = tQ()
            S.stt(Dm[0:L, 0:L], gcr[0:L, 0:L], gc_c, maskS[0:L, 0:L], ALU.subtract, ALU.add)
            S.act(Es[0:L, 0:L], Dm[0:L, 0:L], AF.Exp)
            S.stt(B0[0:L, 0:L], pk[0:L, 0:L], beta_c, Es[0:L, 0:L], ALU.mult, ALU.mult)
            S.tt(Ei[0:L, 0:L], Es[0:L, 0:L], ident[0:L, 0:L], ALU.add, eng="pool")
            S.tt(qkd[0:L, 0:L], pk[0:L, 128:128 + L], Ei[0:L, 0:L], ALU.mult)
        else:
            S.copy(qkd[0:L, 0:L], pk[0:L, 128:128 + L], eng="act")
        pks = S.ps()
        S.mm(pks[0:L, 0:128], kc, Sh[:, :], r=False)
        rneg = tQL()
        S.stt(rneg[0:L, :], pks[0:L, 0:128], eg_c, vtok[0:L, :], ALU.mult, ALU.subtract)
        dl = tQL(); dk = tQL()
        if L > 1:
            Bp = B0
            Ap = tQ()
            transpose_to(Ap[0:L, 0:L], B0[0:L, 0:L], L, L, eng="act")
            U = tQ(); Lw = tQ()
            S.tt(U[0:L, 0:L], ident[0:L, 0:L], Bp[0:L, 0:L], ALU.subtract)
            S.tt(Lw[0:L, 0:L], ident[0:L, 0:L], Ap[0:L, 0:L], ALU.subtract, eng="pool")
            nlev = 6
            for j in range(1, nlev + 1):
                pB = S.ps()
                S.mm(pB[0:L, 0:L], Ap[0:L, 0:L], Bp[0:L, 0:L], r=False)
                Bn = tQ()
                S.copy(Bn[0:L, 0:L], pB[0:L, 0:L], eng="act")
                An = None
                if j < nlev:
                    pA = S.ps()
                    S.mm(pA[0:L, 0:L], Bp[0:L, 0:L], Ap[0:L, 0:L], r=False)
                    An = tQ()
                    S.copy(An[0:L, 0:L], pA[0:L, 0:L])
                pU = S.ps()
                S.mm(pU[0:L, 0:L], Lw[0:L, 0:L], Bn[0:L, 0:L], r=False)
                Un = tQ()
                S.tt(Un[0:L, 0:L], U[0:L, 0:L], pU[0:L, 0:L], ALU.add)
                if j < nlev:
                    pL = S.ps()
                    S.mm(pL[0:L, 0:L], U[0:L, 0:L], An[0:L, 0:L], r=False)
                    Ln_ = tQ()
                    S.tt(Ln_[0:L, 0:L], Lw[0:L, 0:L], pL[0:L, 0:L], ALU.add)
                    Lw = Ln_
                    Ap = An
                U = Un
                Bp = Bn
            pT = S.ps()
            S.mm(pT[0:L, 0:128], U[0:L, 0:L], rneg[0:L, :], r=False)
            S.ts(dl[0:L, :], pT[0:L, 0:128], nb_c, ALU.mult)
            S.act(dk[0:L, :], pT[0:L, 0:128], AF.Copy, scale=nbk_c) if False else S.ts(dk[0:L, :], pT[0:L, 0:128], nbk_c, ALU.mult)
        else:
            S.ts(dl[0:L, :], rneg[0:L, :], nb_c, ALU.mult)
            S.ts(dk[0:L, :], rneg[0:L, :], nbk_c, ALU.mult)
        qd = tQL()
        S.tt(qd[:, 0:L], qc, egr[:, 0:L], ALU.mult, eng="pool")
        po = S.ps()
        S.mm(po[0:L, 0:128], qd[:, 0:L], Sh[:, :], start=True, stop=False, r=False)
        S.mm(po[0:L, 0:128], qkd[0:L, 0:L], dl[0:L, :], start=False, stop=True, r=False)
        pS = S.ps()
        S.mm(pS[:, 0:128], ktok[0:L, :], dk[0:L, :], r=False)
        S.stt(Sh[:, :], Sh[:, :], egr[:, L - 1:L], pS[:, 0:128], ALU.mult, ALU.add)
        junk = tQ(); c2 = tC()
        S.act(junk[0:L, :], po[0:L, 0:128], AF.Square, accum=c2[0:L, 0:1])
        S.ts(c2[0:L, 1:2], c2[0:L, 0:1], 1.0 / 128, ALU.mult, 1e-6, ALU.add)
        S.act(c2[0:L, 1:2], c2[0:L, 1:2], AF.Sqrt)
        S.recip(c2[0:L, 2:3], c2[0:L, 1:2])
        on_ = tQL()
        S.ts(on_[0:L, :], po[0:L, 0:128], c2[0:L, 2:3], ALU.mult)
        pT2 = S.ps()
        S.tr(pT2[:, 0:L], on_[0:L, :], ident[0:L, 0:L])
        S.stt(outn.sub(h, (slice(None), h, slice(c0, c0 + L))), psrc(pT2[:, 0:L], L, 128), p_["a_norm_w"][:, 0:1], sz[:, c0:c0 + L], ALU.mult, ALU.mult)

    def mlstm_chunk(l, h, c0, L, qT_, kT_, vT_, so_, Ch, ms, lf, bb, ib):
        p_ = P[l]
        b_c, ib_c, lf_c = bb[0:L, h:h + 1], ib[0:L, h:h + 1], lf[0:L, h:h + 1]
        qc, kc = qT_[:, c0:c0 + L], kT_[:, c0:c0 + L]
        ibb = tQ(); lfb = tQ()
        S.ts(ibb[0:L, :], ones[0:L, :], ib_c, ALU.mult)
        if CFG.get('dstop', 999) == 1: return
        S.ts(lfb[0:L, :], ones[0:L, :], lf_c, ALU.mult)
        if CFG.get('dstop', 999) == 2: return
        pr = S.ps()
        S.mm(pr[:, 0:L], ibb[0:L, :], ident[0:L, 0:L], r=False)
        if CFG.get('dstop', 999) == 3: return
        S.mm(pr[:, 128:128 + L], lfb[0:L, :], triu[0:L, 0:L], r=False)
        if CFG.get('dstop', 999) == 4: return
        ibr = tQ()
        S.copy(ibr[:, 0:L], pr[:, 0:L], eng="act")
        if CFG.get('dstop', 999) == 5: return
        cb = tC()
        S.act(cb[:, 0:1], pr[:, 128 + L - 1:128 + L], AF.Identity)
        if CFG.get('dstop', 999) == 6: return
        S.rmax(cb[:, 1:2], ibr[:, 0:L])
        if CFG.get('dstop', 999) == 7: return
        S.tt(cb[:, 1:2], cb[:, 1:2], cb[:, 0:1], ALU.add)
        if CFG.get('dstop', 999) == 8: return
        S.tt(cb[:, 2:3], cb[:, 0:1], ms[:, h:h + 1], ALU.add)
        if CFG.get('dstop', 999) == 9: return
        S.tt(cb[:, 3:4], cb[:, 2:3], cb[:, 1:2], ALU.max)
        if CFG.get('dstop', 999) == 10: return
        S.tt(cb[:, 5:6], cb[:, 2:3], cb[:, 3:4], ALU.subtract)
        if CFG.get('dstop', 999) == 11: return
        S.act(cb[:, 4:5], cb[:, 5:6], AF.Exp)
        if CFG.get('dstop', 999) == 12: return
        S.tt(cb[:, 5:6], cb[:, 0:1], cb[:, 3:4], ALU.subtract)
        if CFG.get('dstop', 999) == 13: return
        cc = tC()
        S.tt(cc[0:L, 0:1], b_c, ms[0:L, h:h + 1], ALU.add)
        if CFG.get('dstop', 999) == 14: return
        ld = tQ()
        S.stt(ld[0:L, 0:L], ibr[0:L, 0:L], b_c, maskL[0:L, 0:L], ALU.add, ALU.add)
        if CFG.get('dstop', 999) == 15: return
        S.rmax(cc[0:L, 1:2], ld[0:L, 0:L])
        if CFG.get('dstop', 999) == 16: return
        S.tt(cc[0:L, 2:3], cc[0:L, 0:1], cc[0:L, 1:2], ALU.max)
        if CFG.get('dstop', 999) == 17: return
        S.ts(cc[0:L, 3:4], cc[0:L, 2:3], -1.0, ALU.mult)
        if CFG.get('dstop', 999) == 18: return
        S.act(cc[0:L, 4:5], cc[0:L, 0:1], AF.Exp, bias=cc[0:L, 3:4])
        if CFG.get('dstop', 999) == 19: return
        S.act(cc[0:L, 5:6], cc[0:L, 3:4], AF.Exp)
        if CFG.get('dstop', 999) == 20: return
        S.act(cc[0:L, 6:7], ib_c, AF.Exp, bias=cb[0:L, 5:6])
        if CFG.get('dstop', 999) == 21: return
        ed = tQ()
        S.act(ed[0:L, 0:L], ld[0:L, 0:L], AF.Exp, bias=cc[0:L, 3:4])
        if CFG.get('dstop', 999) == 22: return
        pqk = S.ps()
        S.mm(pqk[0:L, 0:L], qc, kc, r=False)
        if CFG.get('dstop', 999) == 23: return
        dm = tQ()
        S.tt(dm[0:L, 0:L], ed[0:L, 0:L], psrc(pqk[0:L, 0:L], L, L), ALU.mult)
        if CFG.get('dstop', 999) == 24: return
        dmT = tQ()
        transpose_to(dmT[0:L, 0:L], dm[0:L, 0:L], L, L, eng="act")
        if CFG.get('dstop', 999) == 25: return
        va = t258n()
        for e in range(2):
            transpose_to(va[0:L, e * 128:(e + 1) * 128], vT_[e][:, c0:c0 + L], 128, L, eng="act")
        S.memset(va[0:L, 256:257], 1.0)
        if CFG.get('dstop', 999) == 26: return
        S.memset(va[0:L, 257:258], 0.0)
        if CFG.get('dstop', 999) == 27: return
        ktok = tQ()
        transpose_to(ktok[0:L, :], kc, 128, L)
        if CFG.get('dstop', 999) == 28: return
        p1 = S.ps()
        S.mm(p1[0:L, 0:258], qc, Ch[:, :], r=False)
        if CFG.get('dstop', 999) == 29: return
        t1 = t258n()
        S.ts(t1[0:L, :], p1[0:L, 0:258], cc[0:L, 4:5], ALU.mult)
        if CFG.get('dstop', 999) == 30: return
        p2 = S.ps()
        S.mm(p2[0:L, 0:258], dmT[0:L, 0:L], va[0:L, :], r=False)
        if CFG.get('dstop', 999) == 31: return
        S.tt(t1[0:L, :], t1[0:L, :], p2[0:L, 0:258], ALU.add)
        if CFG.get('dstop', 999) == 32: return
        c3 = tC()
        S.act(c3[0:L, 5:6], t1[0:L, 256:257], AF.Abs)
        if CFG.get('dstop', 999) == 33: return
        S.tt(c3[0:L, 0:1], c3[0:L, 5:6], cc[0:L, 5:6], ALU.max)
        if CFG.get('dstop', 999) == 34: return
        S.recip(c3[0:L, 1:2], c3[0:L, 0:1])
        if CFG.get('dstop', 999) == 35: return
        hh = t258n()
        S.ts(hh[0:L, 0:256], t1[0:L, 0:256], c3[0:L, 1:2], ALU.mult)
        if CFG.get('dstop', 999) == 36: return
        S.act(t1[0:L, 0:256], hh[0:L, 0:256], AF.Square, accum=c3[0:L, 2:3])
        if CFG.get('dstop', 999) == 37: return
        S.ts(c3[0:L, 3:4], c3[0:L, 2:3], 1.0 / 256, ALU.mult, 1e-6, ALU.add)
        if CFG.get('dstop', 999) == 38: return
        S.act(c3[0:L, 3:4], c3[0:L, 3:4], AF.Sqrt)
        if CFG.get('dstop', 999) == 39: return
        S.recip(c3[0:L, 4:5], c3[0:L, 3:4])
        if CFG.get('dstop', 999) == 40: return
        S.ts(hh[0:L, 0:256], hh[0:L, 0:256], c3[0:L, 4:5], ALU.mult)
        if CFG.get('dstop', 999) == 41: return
        for e in range(2):
            pT = S.ps()
            S.tr(pT[:, 0:L], hh[0:L, e * 128:(e + 1) * 128], ident[0:L, 0:L])
            S.stt(outn.sub(2 * h + e, (slice(None), 2 * h + e, slice(c0, c0 + L))), psrc(pT[:, 0:L], L, 128), p_["d_norm_w"][:, e:e + 1], so_[e][:, c0:c0 + L], ALU.mult, ALU.mult)
        S.ts(va[0:L, :], va[0:L, :], cc[0:L, 6:7], ALU.mult)
        if CFG.get('dstop', 999) == 42: return
        pC = S.ps()
        S.mm(pC[:, 0:258], ktok[0:L, :], va[0:L, :], r=False)
        if CFG.get('dstop', 999) == 43: return
        S.stt(Ch[:, :], Ch[:, :], cb[:, 4:5], pC[:, 0:258], ALU.mult, ALU.add)
        if CFG.get('dstop', 999) == 44: return
        S.copy(ms[:, h:h + 1], cb[:, 3:4])
        if CFG.get('dstop', 999) == 45: return


    memT = mixed
    for t in range(2):
        S.dma("sp", tokbuf.t[:, :], dap(I["memp"], t * 128 * D, [[D, 128], [1, D]]), writes=[tokbuf[:]])
        for k in range(16):
            transpose_to(memT.sub(k, (slice(None), k, slice(t * 128, (t + 1) * 128))), tokbuf[:, k * 128:(k + 1) * 128], 128, 128)
    for l in range(2 if CFG.get('kv', True) else 0):
        Ktok = Buf(ubuf.t[:, 0:4, :].rearrange("p (m x) b -> p m (x b)", m=2), "Ktok")
        for which, wname, oname in ((0, "xk_w", "mkp"), (1, "xv_w", "mvp")):
            for h in range(4):
                w = load_w(I[wname], l * D * 512, 512, 0, h * 128, 128, 16)
                for mc in range(2):
                    p = S.ps()
                    for k in range(16):
                        S.mm(p[:, 0:128], memT.sub(k, (slice(None), k, slice(mc * 128, (mc + 1) * 128))), w[:, k, 0:128], start=(k == 0), stop=(k == 15), r=False)
                    if which == 0:
                        S.copy(Ktok.sub(mc, (slice(None), mc, slice(h * 128, (h + 1) * 128))), p[:, 0:128])
                        transpose_to(KT[l][:, h, mc * 128:(mc + 1) * 128], Ktok.sub(mc, (slice(None), mc, slice(h * 128, (h + 1) * 128))), 128, 128)
                    else:
                        S.copy(Vt[l][:, mc, h * 128:(h + 1) * 128], p[:, 0:128])
            src = Ktok if which == 0 else Vt[l]
            rd = [Ktok.sub(0, (slice(None), 0, slice(None))), Ktok.sub(1, (slice(None), 1, slice(None)))] if which == 0 else [Vt[l][:]]
            S.dma("pool", dap(O[oname], l * 256 * 512, [[512, 128], [128 * 512, 2], [1, 512]]), src.t[:, 0:2, 0:512], reads=rd)

    S.barrier()
    NBLK = CFG.get('nblk', 2048 // TBP)
    NLAY = CFG.get('layers', 2)
    chunks_p = [(c * 128, 128) for c in range(TBP // 128)]
    for blk in range(NBLK):
        for t in range(TBP // 128):
            S.dma("sp", tokbuf.t[:, :], dap(I["xp"], (blk * TBP + t * 128) * D, [[D, 128], [1, D]]), writes=[tokbuf[:]])
            for k in range(16):
                transpose_to(ch(xT, k, slice(t * 128, (t + 1) * 128)), tokbuf[:, k * 128:(k + 1) * 128], 128, 128, eng=("act" if k % 2 else "dve"))
                S.copy(ch(xTb, k, slice(t * 128, (t + 1) * 128)), ch(xT, k, slice(t * 128, (t + 1) * 128)), eng="pool")
        for l in range(NLAY):
            layer_block(l, TBP, "p", chunks_p, (blk == NBLK - 1) and CFG.get("stateout", True))
        for t in range(TBP // 128):
            for k in range(16):
                transpose_to(tokbuf[:, k * 128:(k + 1) * 128], ch(xT, k, slice(t * 128, (t + 1) * 128)), 128, 128, eng=("act" if k % 2 else "dve"))
            S.dma("pool", dap(O["yp"], (blk * TBP + t * 128) * D, [[D, 128], [1, D]]), tokbuf.t[:, :], reads=[tokbuf[:]])
    for l in range(2):
        emit_rows(lambda j: histA[l][:, j, :], 24, 3, lambda c0, n: dap(O["dcp"], l * 3 * 3072 + c0, [[3072, 3], [1, n]]))
        emit_rows(lambda j: histB[l][:, j, :], 8, 30, lambda c0, n: dap(O["gcp"], l * 30 * 1024 + c0, [[1024, 30], [1, n]]))
        emit_rows(lambda j: histC[l][:, j, :], 8, 2, lambda c0, n: dap(O["scp"], l * 2 * 1024 + c0, [[1024, 2], [1, n]]))

    if CFG.get('sample', True):
        S.dma("sp", tokbuf.t[0:NS, :], dap(I["xs"], 0, [[D, NS], [1, D]]), writes=[tokbuf[:]])
        for k in range(16):
            transpose_to(ch(xT, k, slice(0, NS)), tokbuf[0:NS, k * 128:(k + 1) * 128], NS, 128)
            S.copy(ch(xTb, k, slice(0, NS)), ch(xT, k, slice(0, NS)), eng="pool")
        S.barrier()
        for i in range(3):
            S.memset(CnS[i][:], 0.0)
        chunks_s = [(s, 1) for s in range(NS)]
        for l in range(NLAY):
            for (src, H, C, hs, dst) in ((I["sdc"], 3, 3072, hsA, O["dcs"]), (I["sgc"], 30, 1024, hsB, O["gcs"]), (I["ssc"], 2, 1024, hsC, O["scs"])):
                S.dma("pool", dap(dst, l * NS * H * C, [[H * C, NS], [C, H - 1], [1, C]]), dap(src, l * NS * H * C + C, [[H * C, NS], [C, H - 1], [1, C]]))
                spt = max(1, min(NS, 128 // H))
                for s0 in range(0, NS, spt):
                    ns_ = min(spt, NS - s0)
                    R = ns_ * H
                    for cc0 in range(0, C, D):
                        ncol = min(D, C - cc0)
                        S.dma("sp", tokbuf.t[0:R, 0:ncol], dap(src, (l * NS + s0) * H * C + cc0, [[C, R], [1, ncol]]), writes=[tokbuf[:]])
                        for jj in range(ncol // 128):
                            j = cc0 // 128 + jj
                            p = S.ps()
                            S.tr(p[:, 0:R], tokbuf[0:R, jj * 128:(jj + 1) * 128], ident[0:R, 0:R])
                            S.copy(hs.sub(j, (slice(None), j, slice(s0, s0 + ns_), slice(None))), p[:, 0:R].ap.rearrange("p (s h) -> p s h", h=H) if False else View(p.key, p.t[:, 0:R].rearrange("p (s h) -> p s h", h=H)))
            layer_block(l, NS, "s", chunks_s, False, hsA, hsB, hsC)
        for k in range(16):
            transpose_to(tokbuf[0:NS, k * 128:(k + 1) * 128], ch(xT, k, slice(0, NS)), 128, NS)
        S.dma("pool", dap(O["ys"], 0, [[D, NS], [1, D]]), tokbuf.t[0:NS, :], reads=[tokbuf[:]])
    S.finish()
    return nc


_NC = [None]


def kernel(**inp):
    a = {k: np.ascontiguousarray(np.asarray(v, dtype=np.float32)) for k, v in inp.items()}
    if _NC[0] is None:
        _NC[0] = build()
    nc = _NC[0]
    wnames = ["w_in", "b_in", "a_conv_w", "a_A_log", "a_dt_bias", "a_norm_w", "b_conv_w", "b_conv_b", "b_ln_g", "b_ln_b",
              "c_conv_w", "d_norm_w", "w_branch", "w_out", "ln1_g", "ln1_b", "xq_w", "xk_w", "xv_w", "xo_w", "ln2_g", "ln2_b",
              "ffn_w1", "ffn_b1", "ffn_w2", "ffn_b2", "ln3_g", "ln3_b"]
    in_maps = []
    for c in range(8):
        b = c % 4
        sl = slice(c * NS, (c + 1) * NS)
        m = {"xp": a["x_prompt"][b], "xs": a["x_sample"][sl, 0, :], "memp": a["mem_prompt"][b],
             "cmk": a["cache_mem_k"][:, sl].reshape(2, NS, 256, 512), "cmv": a["cache_mem_v"][:, sl].reshape(2, NS, 256, 512),
             "sdc": a["state_delta_conv"][:, sl], "sdS": a["state_delta_S"][:, sl], "sgc": a["state_glu_conv"][:, sl],
             "ssc": a["state_short_conv"][:, sl], "smC": a["state_mlstm_C"][:, sl], "smn": a["state_mlstm_n"][:, sl],
             "smm": a["state_mlstm_m"][:, sl]}
        m = {k: np.ascontiguousarray(v) for k, v in m.items()}
        for w in wnames:
            m[w] = a[w]
        in_maps.append(m)
    res = run_bass_kernel_spmd(nc, in_maps, core_ids=list(range(8)))
    R = res.results

    def pst(name, shape):
        return np.stack([np.asarray(R[b][name]) for b in range(4)], axis=1).reshape(shape)

    def sst(name, shape):
        return np.concatenate([np.asarray(R[c][name]) for c in range(8)], axis=1).reshape(shape)

    y_prompt = np.stack([np.asarray(R[b]["yp"]) for b in range(4)], axis=0)
    y_sample = np.concatenate([np.asarray(R[c]["ys"]) for c in range(8)], axis=0).reshape(128, 1, D)
    outs = (y_prompt, y_sample,
            pst("mkp", (2, 4, 256, 4, 128)), pst("mvp", (2, 4, 256, 4, 128)),
            pst("dcp", (2, 4, 3, 3072)), pst("dSp", (2, 4, 8, 128, 128)), pst("gcp", (2, 4, 30, 1024)),
            pst("scp", (2, 4, 2, 1024)), pst("mCp", (2, 4, 4, 128, 256)), pst("mnp", (2, 4, 4, 128)), pst("mmp", (2, 4, 4)),
            sst("dcs", (2, 128, 3, 3072)), sst("dSs", (2, 128, 8, 128, 128)), sst("gcs", (2, 128, 30, 1024)),
            sst("scs", (2, 128, 2, 1024)), sst("mCs", (2, 128, 4, 128, 256)), sst("mns", (2, 128, 4, 128)), sst("mms", (2, 128, 4)))
    return tuple(np.ascontiguousarray(o.astype(np.float32)) for o in outs)
```
